# Optimizing a Trainium2 kernel written in Bass

```python
import math
import numpy as np
import jax
import jax.numpy as jnp
from jax import lax

D_MODEL = 1024
BATCH = 4
SEQ = 8192
DEPTH = 1

CTX_LEN = 256
GRID_W = 64
EPS = 1e-6
CONV_K = 3
CHUNK = 64
N_DIR = 2
N_BRANCH = 2
GDN_HEADS = 8
GDN_DK = 128
GDN_DV = 128
GDN_QK = GDN_HEADS * GDN_DK
GDN_WIDTH = GDN_HEADS * GDN_DV
SSM_INNER = 2 * D_MODEL
SSM_HEAD_DIM = 64
SSM_HEADS = SSM_INNER // SSM_HEAD_DIM
SSM_GROUPS = 4
SSM_STATE = 128
SSM_XBC = SSM_INNER + 2 * SSM_GROUPS * SSM_STATE
D_FF = ((8 * D_MODEL // 3 + 255) // 256) * 256
IN_SIZES = (
    2 * GDN_QK + GDN_WIDTH,
    GDN_WIDTH,
    N_DIR * GDN_HEADS,
    N_DIR * GDN_HEADS,
    SSM_INNER,
    SSM_XBC,
    N_DIR * SSM_HEADS,
    N_BRANCH * D_MODEL,
)
D_IN_PROJ = sum(IN_SIZES)

kernel_name = "bidir_gdn_mamba2_griffin_merge_prefix_ctx"


def _split_cols(t, sizes):
    idx = np.cumsum(np.array(sizes))[:-1].tolist()
    return jnp.split(t, idx, axis=-1)


def _rmsnorm(x, w):
    x32 = x.astype(jnp.float32)
    y = x32 * lax.rsqrt(jnp.mean(x32 * x32, axis=-1, keepdims=True) + EPS)
    return y.astype(x.dtype) * w


def _l2norm(x):
    x32 = x.astype(jnp.float32)
    return (x32 * lax.rsqrt(jnp.sum(x32 * x32, axis=-1, keepdims=True) + EPS)).astype(x.dtype)


def _flip(t):
    return jnp.flip(t, axis=1)


def _dwconv_centred(u, w, b):
    pad = CONV_K // 2
    length = u.shape[-2]
    up = jnp.pad(u, [(0, 0)] * (u.ndim - 2) + [(pad, pad), (0, 0)])
    out = b
    for j in range(CONV_K):
        out = out + up[..., j:j + length, :] * w[j]
    return out


def _short_conv(u, w, b, latent):
    if latent:
        bsz, length, ch = u.shape
        rows = length // GRID_W
        return _dwconv_centred(u.reshape(bsz, rows, GRID_W, ch), w, b).reshape(bsz, length, ch)
    return _dwconv_centred(u, w, b)


def _to_chunks(t):
    bsz, length = t.shape[:2]
    return jnp.moveaxis(t.reshape(bsz, length // CHUNK, CHUNK, *t.shape[2:]), 1, 0)


def _from_chunks(t):
    t = jnp.moveaxis(t, 0, 1)
    return t.reshape(t.shape[0], t.shape[1] * t.shape[2], *t.shape[3:])


def _gdn_chunked(q, k, v, g, beta, s0):
    out_dtype = v.dtype
    q, k, v, g, beta = (t.astype(jnp.float32) for t in (q, k, v, g, beta))
    idx = jnp.arange(CHUNK)
    incl = idx[:, None] >= idx[None, :]
    strict = idx[:, None] > idx[None, :]
    eye = jnp.eye(CHUNK, dtype=jnp.float32)

    def step(s, inp):
        qc, kc, vc, gc, bc = inp
        qh, kh, vh = (jnp.swapaxes(t, 1, 2) for t in (qc, kc, vc))
        gcum = jnp.cumsum(jnp.swapaxes(gc, 1, 2), axis=-1)
        bh = jnp.swapaxes(bc, 1, 2)[..., None]
        decay = jnp.exp(jnp.where(incl, gcum[..., :, None] - gcum[..., None, :], -jnp.inf))
        kb = kh * bh
        lower = jnp.where(strict, jnp.einsum('bhid,bhjd->bhij', kb, kh) * decay, 0.0) + eye
        rhs = jnp.concatenate([kb * jnp.exp(gcum)[..., None], vh * bh], axis=-1)
        sol = lax.linalg.triangular_solve(lower, rhs, left_side=True, lower=True, unit_diagonal=True)
        w_c, u_c = sol[..., :GDN_DK], sol[..., GDN_DK:]
        v_new = u_c - jnp.einsum('bhck,bhkv->bhcv', w_c, s)
        attn = jnp.einsum('bhik,bhjk->bhij', qh, kh) * decay
        o = (jnp.einsum('bhck,bhkv->bhcv', qh * jnp.exp(gcum)[..., None], s)
             + jnp.einsum('bhij,bhjv->bhiv', attn, v_new))
        g_last = gcum[..., -1:]
        s = (s * jnp.exp(g_last)[..., None]
             + jnp.einsum('bhck,bhcv->bhkv', kh * jnp.exp(g_last - gcum)[..., None], v_new))
        return s, jnp.swapaxes(o, 1, 2)

    s_fin, o = lax.scan(step, s0, tuple(_to_chunks(t) for t in (q, k, v, g, beta)))
    return _from_chunks(o).astype(out_dtype), s_fin


def _ssd_chunked(x, dt, a, bm, cm, h0):
    out_dtype = x.dtype
    x, dt, bm, cm = (t.astype(jnp.float32) for t in (x, dt, bm, cm))
    bsz = x.shape[0]
    rep = SSM_HEADS // SSM_GROUPS
    idx = jnp.arange(CHUNK)
    incl = idx[:, None] >= idx[None, :]

    def step(h, inp):
        xc, dtc, bc, cc = inp
        acum = jnp.cumsum(jnp.swapaxes(dtc * a, 1, 2), axis=-1).reshape(bsz, SSM_GROUPS, rep, CHUNK)
        seg = jnp.exp(jnp.where(incl, acum[..., :, None] - acum[..., None, :], -jnp.inf))
        xdt = (xc * dtc[..., None]).reshape(bsz, CHUNK, SSM_GROUPS, rep, SSM_HEAD_DIM)
        cb = jnp.einsum('bign,bjgn->bgij', cc, bc)
        y_diag = jnp.einsum('bgij,bgrij,bjgrp->bigrp', cb, seg, xdt)
        hg = h.reshape(bsz, SSM_GROUPS, rep, SSM_HEAD_DIM, SSM_STATE)
        y_off = jnp.einsum('bign,bgrpn,bgri->bigrp', cc, hg, jnp.exp(acum))
        a_last = acum[..., -1:]
        new = jnp.einsum('bjgn,bgrj,bjgrp->bgrpn', bc, jnp.exp(a_last - acum), xdt)
        hg = hg * jnp.exp(a_last)[..., None] + new
        y = (y_diag + y_off).reshape(bsz, CHUNK, SSM_HEADS, SSM_HEAD_DIM)
        return hg.reshape(h.shape), y

    h_fin, y = lax.scan(step, h0, tuple(_to_chunks(t) for t in (x, dt, bm, cm)))
    return _from_chunks(y).astype(out_dtype), h_fin


def _gdn_bidir(q, k, v, g, beta, s_f, s_b):
    o_f, s_f = _gdn_chunked(q, k, v, g[:, :, 0], beta[:, :, 0], s_f)
    o_b, s_b = _gdn_chunked(_flip(q), _flip(k), _flip(v), _flip(g[:, :, 1]), _flip(beta[:, :, 1]), s_b)
    return o_f + _flip(o_b), s_f, s_b


def _ssd_bidir(xs, dt, a_log, bm, cm, h_f, h_b):
    a = -jnp.exp(a_log.astype(jnp.float32))
    y_f, h_f = _ssd_chunked(xs, dt[:, :, 0], a[0], bm, cm, h_f)
    y_b, h_b = _ssd_chunked(_flip(xs), _flip(dt[:, :, 1]), a[1], _flip(bm), _flip(cm), h_b)
    return y_f + _flip(y_b), h_f, h_b


def _mixer_inputs(h, latent, w_in, gdn_conv_w, gdn_conv_b, gdn_a_log, gdn_dt_bias,
                  ssm_conv_w, ssm_conv_b, ssm_dt_bias):
    bsz, length, _ = h.shape
    qkv, z_gdn, a_gdn, b_gdn, z_ssm, xbc, dt_raw, br_gate = _split_cols(h @ w_in, IN_SIZES)
    qkv = jax.nn.silu(_short_conv(qkv, gdn_conv_w, gdn_conv_b, latent))
    q, k, v = _split_cols(qkv, (GDN_QK, GDN_QK, GDN_WIDTH))
    q = _l2norm(q.reshape(bsz, length, GDN_HEADS, GDN_DK)) * GDN_DK ** -0.5
    k = _l2norm(k.reshape(bsz, length, GDN_HEADS, GDN_DK))
    v = v.reshape(bsz, length, GDN_HEADS, GDN_DV)
    g = -jnp.exp(gdn_a_log) * jax.nn.softplus(a_gdn.reshape(bsz, length, N_DIR, GDN_HEADS) + gdn_dt_bias)
    beta = jax.nn.sigmoid(b_gdn.reshape(bsz, length, N_DIR, GDN_HEADS))
    xbc = jax.nn.silu(_short_conv(xbc, ssm_conv_w, ssm_conv_b, latent))
    xs, bm, cm = _split_cols(xbc, (SSM_INNER, SSM_GROUPS * SSM_STATE, SSM_GROUPS * SSM_STATE))
    xs = xs.reshape(bsz, length, SSM_HEADS, SSM_HEAD_DIM)
    bm = bm.reshape(bsz, length, SSM_GROUPS, SSM_STATE)
    cm = cm.reshape(bsz, length, SSM_GROUPS, SSM_STATE)
    dt = jax.nn.softplus(dt_raw.reshape(bsz, length, N_DIR, SSM_HEADS) + ssm_dt_bias)
    return q, k, v, g, beta, z_gdn, xs, bm, cm, dt, z_ssm, br_gate


def _mixer_output(o_gdn, z_gdn, y_ssm, xs, z_ssm, br_gate, gdn_norm_w, ssm_d, ssm_norm_w,
                  w_br_gdn, w_br_ssm, w_out):
    bsz, length = o_gdn.shape[:2]
    o = _rmsnorm(o_gdn, gdn_norm_w) * jax.nn.silu(z_gdn.reshape(bsz, length, GDN_HEADS, GDN_DV))
    p_gdn = o.reshape(bsz, length, GDN_WIDTH) @ w_br_gdn
    y = (y_ssm + ssm_d[:, None] * xs).reshape(bsz, length, SSM_INNER) * jax.nn.silu(z_ssm)
    y = _rmsnorm(y.reshape(bsz, length, SSM_GROUPS, SSM_INNER // SSM_GROUPS),
                 ssm_norm_w.reshape(SSM_GROUPS, SSM_INNER // SSM_GROUPS)).reshape(bsz, length, SSM_INNER)
    p_ssm = y @ w_br_ssm
    gate_gdn, gate_ssm = jnp.split(br_gate, N_BRANCH, axis=-1)
    merged = jax.nn.sigmoid(gate_gdn) * p_gdn + jax.nn.sigmoid(gate_ssm) * p_ssm
    return merged @ w_out


def _swiglu(h, w_ffn_in, w_ffn_out):
    gate, up = jnp.split(h @ w_ffn_in, 2, axis=-1)
    return (jax.nn.silu(gate) * up) @ w_ffn_out


def setup_inputs(seed: int = 0) -> dict:
    key = jax.random.key(seed)
    ks = jax.random.split(key, 26)

    def nrm(i, shape, scale):
        return scale * jax.random.normal(ks[i], shape, jnp.float32)

    def gain(i, shape):
        return 1.0 + 0.02 * jax.random.normal(ks[i], shape, jnp.float32)

    def a_log(i, shape):
        return jnp.log(jax.random.uniform(ks[i], shape, jnp.float32, 1.0, 16.0))

    def dt_bias(i, shape):
        dt = jnp.exp(jax.random.uniform(ks[i], shape, jnp.float32, math.log(1e-3), math.log(1e-1)))
        return dt + jnp.log(-jnp.expm1(-dt))

    return {
        "x": nrm(0, (BATCH, SEQ, D_MODEL), 1.0),
        "c": nrm(1, (BATCH, D_MODEL), 1.0),
        "ctx": nrm(2, (BATCH, CTX_LEN, D_MODEL), 1.0),
        "c_ctx": nrm(3, (D_MODEL,), 1.0),
        "ada_w": nrm(4, (DEPTH, D_MODEL, 6 * D_MODEL), 0.5 * D_MODEL ** -0.5),
        "ada_b": nrm(5, (DEPTH, 6 * D_MODEL), 0.01),
        "norm1_w": gain(6, (DEPTH, D_MODEL)),
        "w_in": nrm(7, (DEPTH, D_MODEL, D_IN_PROJ), D_MODEL ** -0.5),
        "gdn_conv_w": nrm(8, (DEPTH, CONV_K, 2 * GDN_QK + GDN_WIDTH), CONV_K ** -0.5),
        "gdn_conv_b": nrm(9, (DEPTH, 2 * GDN_QK + GDN_WIDTH), 0.01),
        "gdn_a_log": a_log(10, (DEPTH, N_DIR, GDN_HEADS)),
        "gdn_dt_bias": dt_bias(11, (DEPTH, N_DIR, GDN_HEADS)),
        "gdn_norm_w": gain(12, (DEPTH, GDN_DV)),
        "ssm_conv_w": nrm(13, (DEPTH, CONV_K, SSM_XBC), CONV_K ** -0.5),
        "ssm_conv_b": nrm(14, (DEPTH, SSM_XBC), 0.01),
        "ssm_a_log": a_log(15, (DEPTH, N_DIR, SSM_HEADS)),
        "ssm_dt_bias": dt_bias(16, (DEPTH, N_DIR, SSM_HEADS)),
        "ssm_d": gain(17, (DEPTH, SSM_HEADS)),
        "ssm_norm_w": gain(18, (DEPTH, SSM_INNER)),
        "w_br_gdn": nrm(19, (DEPTH, GDN_WIDTH, D_MODEL), GDN_WIDTH ** -0.5),
        "w_br_ssm": nrm(20, (DEPTH, SSM_INNER, D_MODEL), SSM_INNER ** -0.5),
        "w_out": nrm(21, (DEPTH, D_MODEL, D_MODEL), D_MODEL ** -0.5),
        "norm2_w": gain(22, (DEPTH, D_MODEL)),
        "w_ffn_in": nrm(23, (DEPTH, D_MODEL, 2 * D_FF), D_MODEL ** -0.5),
        "w_ffn_out": nrm(24, (DEPTH, D_FF, D_MODEL), D_FF ** -0.5),
        "norm_f_w": gain(25, (D_MODEL,)),
    }


def reference(x, c, ctx, c_ctx, ada_w, ada_b, norm1_w, w_in, gdn_conv_w, gdn_conv_b, gdn_a_log,
              gdn_dt_bias, gdn_norm_w, ssm_conv_w, ssm_conv_b, ssm_a_log, ssm_dt_bias, ssm_d,
              ssm_norm_w, w_br_gdn, w_br_ssm, w_out, norm2_w, w_ffn_in, w_ffn_out, norm_f_w):
    bsz = x.shape[0]
    silu_c = jax.nn.silu(c)[:, None, :]
    silu_cc = jax.nn.silu(c_ctx)
    h_lat, h_ctx = x, ctx
    for i in range(DEPTH):
        sh1, sc1, g1, sh2, sc2, g2 = jnp.split(silu_c @ ada_w[i] + ada_b[i], 6, axis=-1)
        csh1, csc1, cg1, csh2, csc2, cg2 = jnp.split(silu_cc @ ada_w[i] + ada_b[i], 6, axis=-1)
        prm = (w_in[i], gdn_conv_w[i], gdn_conv_b[i], gdn_a_log[i], gdn_dt_bias[i],
               ssm_conv_w[i], ssm_conv_b[i], ssm_dt_bias[i])
        a_lat = _rmsnorm(h_lat, norm1_w[i]) * (1 + sc1) + sh1
        a_ctx = _rmsnorm(h_ctx, norm1_w[i]) * (1 + csc1) + csh1
        (cq, ck, cv, cgd, cbeta, cz_gdn, cxs, cbm, ccm, cdt, cz_ssm, cgate) = _mixer_inputs(a_ctx, False, *prm)
        (lq, lk, lv, lgd, lbeta, lz_gdn, lxs, lbm, lcm, ldt, lz_ssm, lgate) = _mixer_inputs(a_lat, True, *prm)
        s0 = jnp.zeros((bsz, GDN_HEADS, GDN_DK, GDN_DV), jnp.float32)
        h0 = jnp.zeros((bsz, SSM_HEADS, SSM_HEAD_DIM, SSM_STATE), jnp.float32)
        co_gdn, s_f, s_b = _gdn_bidir(cq, ck, cv, cgd, cbeta, s0, s0)
        lo_gdn, _, _ = _gdn_bidir(lq, lk, lv, lgd, lbeta, s_f, s_b)
        cy_ssm, hf, hb = _ssd_bidir(cxs, cdt, ssm_a_log[i], cbm, ccm, h0, h0)
        ly_ssm, _, _ = _ssd_bidir(lxs, ldt, ssm_a_log[i], lbm, lcm, hf, hb)
        out_prm = (gdn_norm_w[i], ssm_d[i], ssm_norm_w[i], w_br_gdn[i], w_br_ssm[i], w_out[i])
        h_lat = h_lat + g1 * _mixer_output(lo_gdn, lz_gdn, ly_ssm, lxs, lz_ssm, lgate, *out_prm)
        f_lat = _rmsnorm(h_lat, norm2_w[i]) * (1 + sc2) + sh2
        h_lat = h_lat + g2 * _swiglu(f_lat, w_ffn_in[i], w_ffn_out[i])
        if i < DEPTH - 1:
            h_ctx = h_ctx + cg1 * _mixer_output(co_gdn, cz_gdn, cy_ssm, cxs, cz_ssm, cgate, *out_prm)
            f_ctx = _rmsnorm(h_ctx, norm2_w[i]) * (1 + csc2) + csh2
            h_ctx = h_ctx + cg2 * _swiglu(f_ctx, w_ffn_in[i], w_ffn_out[i])
    return _rmsnorm(h_lat, norm_f_w)
```

```python
import numpy as np
import ml_dtypes
from contextlib import ExitStack
import concourse.bass as bass
import concourse.mybir as mybir
from concourse.bass_utils import run_bass_kernel_spmd

F32 = mybir.dt.float32
BF16 = mybir.dt.bfloat16
AF = mybir.ActivationFunctionType
ALU = mybir.AluOpType
AX = mybir.AxisListType

D = 1024
NLAT = 4096
NCTX = 256
NTOK = NLAT + NCTX
C = 128
NCH = NTOK // C
EPS = 1e-6
DFF = 2816
O_QKV, O_XBC, O_GATE, O_SMALL, O_ZG, O_ZS = 0, 3072, 6144, 8192, 8320, 9344
W_IN_COLS = 11392
SEM_LIMIT = 30000
GDN_LIMIT = 0


class Sem:
    __slots__ = ("h", "count", "dma", "name")

    def __init__(self, h, dma, name):
        self.h, self.count, self.dma, self.name = h, 0, dma, name


class Res:
    __slots__ = ("name", "w", "r", "dsem")

    def __init__(self, name):
        self.name, self.w, self.r, self.dsem = name, None, {}, None


class Eng:
    def __init__(self, name, e, same_wait):
        self.name, self.e, self.sem, self.waited, self.same_wait = name, e, None, {}, same_wait
        self.nsem = 0


class K:
    def __init__(self, nc):
        self.nc = nc
        self.eng = {
            "pe": Eng("pe", nc.tensor, False),
            "act": Eng("act", nc.scalar, True),
            "dve": Eng("dve", nc.vector, True),
            "pool": Eng("pool", nc.gpsimd, True),
            "sp": Eng("sp", nc.sync, False),
        }
        self.nsem = 0
        self.ninst = 0
        self.all_sems = []
        self.free_dma = []

    def new_sem(self, dma, name):
        if dma and self.free_dma:
            self.free_dma.sort(key=lambda x: x.count)
            return self.free_dma.pop(0)
        self.nsem += 1
        h = self.nc.alloc_semaphore(f"s{self.nsem}_{name}")
        sm = Sem(h, dma, name)
        self.all_sems.append(sm)
        return sm

    def res(self, name):
        return Res(name)

    def _cur_sem(self, E):
        if E.sem is None or E.sem.count >= SEM_LIMIT:
            E.nsem += 1
            E.sem = self.new_sem(False, f"{E.name}{E.nsem}")
        return E.sem

    def _wait(self, E, evs):
        need = {}
        for (sem, val) in evs:
            if sem.dma:
                val = sem.count
            if need.get(sem, 0) < val:
                need[sem] = val
        for sem, val in need.items():
            if (not E.same_wait) and (sem is E.sem) and not sem.dma:
                continue
            if E.waited.get(sem, 0) >= val:
                continue
            E.e.wait_ge(sem.h, val)
            E.waited[sem] = val

    def _deps(self, r, w):
        evs = []
        for x in r:
            if x.w is not None:
                evs.append(x.w)
        for x in w:
            if x.w is not None:
                evs.append(x.w)
            evs.extend(x.r.items())
        return evs

    def _record(self, ev, r, w):
        sem, val = ev
        for x in r:
            if x.r.get(sem, 0) < val:
                x.r[sem] = val
        for x in w:
            x.w = ev
            x.r = {}

    def ins(self, en, fn, r=(), w=()):
        E = self.eng[en]
        self._wait(E, self._deps(r, w))
        inst = fn()
        sem = self._cur_sem(E)
        sem.count += 1
        inst.then_inc(sem.h, 1)
        self._record((sem, sem.count), r, w)
        self.ninst += 1
        return inst

    def group(self, en, fns, r=(), w=()):
        E = self.eng[en]
        self._wait(E, self._deps(r, w))
        inst = None
        for fn in fns:
            inst = fn()
        sem = self._cur_sem(E)
        sem.count += 1
        inst.then_inc(sem.h, 1)
        self._record((sem, sem.count), r, w)
        self.ninst += len(fns)
        return inst

    def dma(self, q, out, in_, r=(), w=(), dres=None):
        E = self.eng[q]
        self._wait(E, self._deps(r, w))
        if dres.dsem is None:
            dres.dsem = self.new_sem(True, "d" + dres.name)
        sem = dres.dsem
        inst = E.e.dma_start(out=out, in_=in_)
        sem.count += 16
        inst.then_inc(sem.h, 16)
        self._record((sem, sem.count), r, w)
        self.ninst += 1
        return inst

    def dmaop(self, q, fn, r=(), w=(), dres=None):
        E = self.eng[q]
        self._wait(E, self._deps(r, w))
        if dres.dsem is None:
            dres.dsem = self.new_sem(True, "d" + dres.name)
        sem = dres.dsem
        inst = fn()
        sem.count += 16
        inst.then_inc(sem.h, 16)
        self._record((sem, sem.count), r, w)
        self.ninst += 1
        return inst

    def barrier(self):
        sems = list(self.all_sems)
        for E in self.eng.values():
            for sem in sems:
                if sem.count == 0 or E.waited.get(sem, 0) >= sem.count:
                    continue
                if (sem is E.sem) and not E.same_wait:
                    continue
                E.e.wait_ge(sem.h, sem.count)
                E.waited[sem] = sem.count

    def wait_all(self, en, ress):
        E = self.eng[en]
        evs = []
        for x in ress:
            if x.w is not None:
                evs.append(x.w)
            evs.extend(x.r.items())
        self._wait(E, evs)


class Buf:
    def __init__(self, k, t, name):
        self.t, self.res, self.name = t, k.res(name), name

    def __getitem__(self, idx):
        return self.t[idx]


class Ring:
    def __init__(self, bufs):
        self.bufs, self.i = bufs, 0

    def next(self):
        b = self.bufs[self.i % len(self.bufs)]
        self.i += 1
        return b


class Ctx:
    def __init__(self, nc):
        self.nc = nc
        self.k = K(nc)
        self.stack = [ExitStack()]
        self.uid = 0
        self.phase_bufs = [[]]

    def push(self):
        self.stack.append(ExitStack())
        self.phase_bufs.append([])

    def pop(self):
        self.k.barrier()
        for b in self.phase_bufs.pop():
            if b.res.dsem is not None:
                self.k.free_dma.append(b.res.dsem)
                b.res.dsem = None
        self.stack.pop().close()

    def sb(self, name, shape, dtype):
        self.uid += 1
        t = self.stack[-1].enter_context(self.nc.sbuf_tensor(f"sb{self.uid}_{name}", list(shape), dtype))
        b = Buf(self.k, t, name)
        self.phase_bufs[-1].append(b)
        return b

    def psum(self, name, shape, dtype):
        self.uid += 1
        t = self.stack[-1].enter_context(self.nc.psum_tensor(f"ps{self.uid}_{name}", list(shape), dtype))
        return Buf(self.k, t, name)

    def sbring(self, name, n, shape, dtype):
        return Ring([self.sb(f"{name}{i}", shape, dtype) for i in range(n)])

    def dram(self, name, shape, dtype, kind="Internal"):
        t = self.nc.dram_tensor(name, list(shape), dtype, kind=kind)
        return t.ap()


def build_program(debug=False, phases=("0", "A", "R", "B1", "S1", "B2", "S2", "C1", "C2"), n_cores=8):
    nc = bass.Bass("TRN2", target_bir_lowering=False)
    cx = Ctx(nc)
    k = cx.k
    dk = "ExternalOutput" if debug else "Internal"

    xin = cx.dram("xin", [NTOK, D], F32, "ExternalInput")
    cvec = cx.dram("cvec", [128, 8, 2], F32, "ExternalInput")
    ada_w = cx.dram("ada_w", [D, 6 * D], F32, "ExternalInput")
    ada_b = cx.dram("ada_b", [1, 6 * D], F32, "ExternalInput")
    w_in = cx.dram("w_in", [D, W_IN_COLS], F32, "ExternalInput")
    xin2 = cx.dram("xin2", [NTOK, D], F32, "ExternalInput")
    convp2 = cx.dram("convp2", [128, 48, 4], F32, "ExternalInput")
    convp = cx.dram("convp", [128, 48, 4], F32, "ExternalInput")
    smallp = cx.dram("smallp", [128, 4], F32, "ExternalInput")
    n1w = cx.dram("n1w", [128, 8], F32, "ExternalInput")
    ident_d = cx.dram("ident", [128, 128], F32, "ExternalInput")
    wbg_d = cx.dram("w_brg", [1024, 1024], F32, "ExternalInput")
    wbs_d = cx.dram("w_brs", [2048, 1024], F32, "ExternalInput")
    wo_d = cx.dram("w_o", [1024, 1024], F32, "ExternalInput")
    wfi_d = cx.dram("w_fi", [1024, 2 * DFF], F32, "ExternalInput")
    wfo_d = cx.dram("w_fo", [DFF, 1024], F32, "ExternalInput")
    gnw_d = cx.dram("gnw", [128, 1], F32, "ExternalInput")
    dskip_d = cx.dram("dskip", [1, 2048], F32, "ExternalInput")
    snw_d = cx.dram("snw", [1, 2048], F32, "ExternalInput")
    n2w_d = cx.dram("n2w", [128, 8], F32, "ExternalInput")
    nfw_d = cx.dram("nfw", [1, 1024], F32, "ExternalInput")
    salog_d = cx.dram("salog", [128, 2, 32], F32, "ExternalInput")
    tri_d = cx.dram("tri", [128, 4, 128], F32, "ExternalInput")
    msk_d = cx.dram("msk", [128, 4, 128], F32, "ExternalInput")
    out_d = cx.dram("out", [NLAT, D], F32, "ExternalOutput")

    mod_d = cx.dram("mod_d", [2, 6 * D], F32, dk)
    qT_d = cx.dram("qT_d", [8, 128, NTOK], BF16, dk)
    kT_d = cx.dram("kT_d", [8, 128, NTOK], BF16, dk)
    k_d = cx.dram("k_d", [NTOK, 1024], BF16, dk)
    v_d = cx.dram("v_d", [NTOK, 1024], BF16, dk)
    x_d = cx.dram("x_d", [NTOK, 2048], BF16, dk)
    BT_d = cx.dram("BT_d", [4, 128, NTOK], BF16, dk)
    CT_d = cx.dram("CT_d", [4, 128, NTOK], BF16, dk)
    B_d = cx.dram("B_d", [NTOK, 512], BF16, dk)
    sm_d = cx.dram("sm_d", [NTOK, 128], F32, dk)
    kT2_d = cx.dram("kT2_d", [8, 128, NTOK], BF16)
    k2_d = cx.dram("k2_d", [NTOK, 1024], BF16)
    v2_d = cx.dram("v2_d", [NTOK, 1024], BF16)
    x2_d = cx.dram("x2_d", [NTOK, 2048], BF16)
    B2_d = cx.dram("B2_d", [NTOK, 512], BF16)
    sm2_d = cx.dram("sm2_d", [NTOK, 128], F32)
    sgT_d = cx.dram("sgT_d", [16, 128, NLAT], F32, dk)
    aT_d = cx.dram("aT_d", [8, 128, NLAT], BF16, dk)
    o1_d = cx.dram("o1_d", [8, 128, NLAT], F32, dk)
    oT_d = cx.dram("oT_d", [8, 128, NLAT], F32, dk)
    st1_d = cx.dram("st1_d", [128, 3072], F32)
    st2_d = cx.dram("st2_d", [128, 3072], F32)
    y1_d = cx.dram("y1_d", [NLAT, 2048], F32, dk)
    y_d = cx.dram("y_d", [NLAT, 2048], F32, dk)
    R_y1 = [k.res(f"y1_{c}") for c in range(NCH)]
    R_y = k.res("y")
    R_o1 = [k.res(f"o1_{c}") for c in range(NCH)]
    R_oT = k.res("oT")
    R_st1 = k.res("st1")
    R_st2 = k.res("st2")
    h1_d = cx.dram("h1_d", [NLAT, D], F32, dk)
    R_h1 = k.res("h1")
    R_out = k.res("out")
    R_scr = k.res("scratchA")
    R_mod = k.res("mod_d")

    ident = cx.sb("ident", [128, 128], F32)
    identb = cx.sb("identb", [128, 128], BF16)
    onesf = cx.sb("onesf", [128, 128], F32)
    k.dma("sp", ident[:], ident_d, w=[ident.res], dres=ident.res)
    k.ins("dve", lambda: nc.vector.tensor_copy(identb[:], ident[:]), r=[ident.res], w=[identb.res])
    k.ins("dve", lambda: nc.vector.memset(onesf[:], 1.0), w=[onesf.res])
    epsc = cx.sb("epsc", [128, 1], F32)
    k.ins("dve", lambda: nc.vector.memset(epsc[:], EPS), w=[epsc.res])


    if "0" in phases:
        cx.push()
        ps = [cx.psum(f"ps{i}", [128, 512], F32) for i in range(2)]
        cv = cx.sb("cv", [128, 8, 2], F32)
        cvs = cx.sb("cvs", [128, 8, 2], F32)
        k.dma("sp", cv[:], cvec, w=[cv.res], dres=cv.res)
        k.ins("act", lambda: nc.scalar.activation(out=cvs[:], in_=cv[:], func=AF.Exp, scale=-1.0), r=[cv.res], w=[cvs.res])
        k.ins("dve", lambda: nc.vector.tensor_scalar(out=cvs[:], in0=cvs[:], scalar1=1.0, scalar2=None, op0=ALU.add),
              r=[cvs.res], w=[cvs.res])
        k.ins("dve", lambda: nc.vector.reciprocal(out=cvs[:], in_=cvs[:]), r=[cvs.res], w=[cvs.res])
        k.ins("dve", lambda: nc.vector.tensor_tensor(out=cvs[:], in0=cvs[:], in1=cv[:], op=ALU.mult),
              r=[cvs.res, cv.res], w=[cvs.res])
        adab = cx.sb("adab", [2, 6 * D], F32)
        k.dma("sp", adab[0:1, :], ada_b, w=[adab.res], dres=adab.res)
        k.dma("sp", adab[1:2, :], ada_b, w=[adab.res], dres=adab.res)
        modsb = cx.sb("modsb", [2, 6 * D], F32)
        awring = cx.sbring("aw", 2, [128, 8, 512], F32)
        for cg in range(12):
            aw = awring.next()
            k.dma("sp", aw[:], ada_w[:, cg * 512:(cg + 1) * 512].rearrange("(kt p) n -> p kt n", p=128),
                  w=[aw.res], dres=aw.res)
            pb = ps[cg % 2]
            k.group("pe", [
                (lambda kt=kt, aw=aw, pb=pb: nc.tensor.matmul(pb[0:2, :], cvs[:, kt, :], aw[:, kt, :],
                                                               start=(kt == 0), stop=(kt == 7)))
                for kt in range(8)], r=[cvs.res, aw.res], w=[pb.res])
            k.ins("dve", lambda cg=cg, pb=pb: nc.vector.tensor_tensor(
                out=modsb[:, cg * 512:(cg + 1) * 512], in0=pb[0:2, :], in1=adab[:, cg * 512:(cg + 1) * 512],
                op=ALU.add), r=[pb.res, adab.res], w=[modsb.res])
        k.dma("sp", mod_d, modsb[:], r=[modsb.res], w=[R_mod], dres=modsb.res)
        cx.pop()

    modf = cx.sb("modf", [128, 2, 6, 8], F32)
    with nc.allow_non_contiguous_dma("small modulation vector relayout"):
        for r_ in range(2):
            k.dma("sp", modf[:, r_, :, :], mod_d[r_, :].rearrange("(j kt p) -> p j kt", p=128, kt=8),
                  r=[R_mod], w=[modf.res], dres=modf.res)
    n1 = cx.sb("n1", [128, 8], F32)
    k.dma("sp", n1[:], n1w, w=[n1.res], dres=n1.res)
    S1 = cx.sb("S1", [128, 2, 8], F32)
    for r_ in range(2):
        k.ins("dve", lambda r_=r_: nc.vector.scalar_tensor_tensor(
            out=S1[:, r_, :], in0=modf[:, r_, 1, :], scalar=1.0, in1=n1[:], op0=ALU.add, op1=ALU.mult),
            r=[modf.res, n1.res], w=[S1.res])

    if "A" in phases:
        cx.push()
        ps = [cx.psum(f"ps{i}", [128, 512], F32) for i in range(6)]
        NWC = O_ZG
        wA = cx.sb("wA", [128, 8, NWC], BF16)
        for kt in range(8):
            for c0 in range(0, NWC, 2080):
                k.dma("pool", wA[:, kt, c0:c0 + 2080], w_in[kt * 128:(kt + 1) * 128, c0:c0 + 2080],
                      w=[wA.res], dres=wA.res)
        cp = cx.sb("cp", [128, 48, 4], F32)
        k.dma("sp", cp[:], convp, w=[cp.res], dres=cp.res)
        smp = cx.sb("smp", [128, 4], F32)
        k.dma("sp", smp[:], smallp, w=[smp.res], dres=smp.res)
        smult = cx.sb("smult", [128, 1], F32)
        k.ins("act", lambda: nc.scalar.activation(out=smult[:], in_=smp[:, 2:3], func=AF.Exp),
              r=[smp.res], w=[smult.res])
        k.ins("dve", lambda: nc.vector.tensor_tensor(out=smult[:], in0=smult[:], in1=smp[:, 3:4], op=ALU.mult),
              r=[smp.res, smult.res], w=[smult.res])

        TT = 256
        xring = cx.sbring("xt", 2, [128, 2, D], F32)
        xnring = cx.sbring("xn", 2, [128, D], F32)
        junk = cx.sb("junk", [128, D], BF16)
        ssr = cx.sbring("ss", 2, [128, 2], F32)
        aTring = cx.sbring("aT", 2, [128, 8, TT], BF16)
        cring = cx.sbring("cv_", 3, [128, TT], F32)
        sring = cx.sbring("so_", 4, [128, TT], BF16)
        sqring = cx.sbring("sq_", 2, [128, TT], F32)
        rsring = cx.sbring("rs_", 2, [128, TT], F32)
        sgring = cx.sbring("sg_", 3, [128, TT], F32)
        ering = cx.sbring("ee_", 3, [128, TT], F32)
        neg1 = cx.sb("neg1", [128, TT], F32)
        k.ins("pool", lambda: nc.gpsimd.memset(neg1[:], -1.0), w=[neg1.res])
        smring = cx.sbring("smf", 2, [128, TT], F32)
        ktok = cx.sbring("ktok", 1, [128, 2, 1024], BF16)
        vtok = cx.sbring("vtok", 1, [128, 2, 1024], BF16)
        xtok = cx.sbring("xtok", 1, [128, 2, 2048], BF16)
        btok = cx.sbring("btok", 1, [128, 2, 512], BF16)
        smtok = cx.sbring("smtok", 2, [128, 2, 128], F32)
        pst = [ps[0], ps[1]]
        psa = [ps[2], ps[3]]
        pss = ps[4]
        pso = [cx.psum(f"pso{i}", [128, 1024], BF16) for i in range(2)]
        psm = ps[5]
        n_acc = 0
        n_tr = 0
        n_ot = 0

        own = dict(qT_d=qT_d, kT_d=kT_d, k_d=k_d, v_d=v_d, x_d=x_d, BT_d=BT_d, CT_d=CT_d, B_d=B_d, sm_d=sm_d)
        par = dict(qT_d=None, kT_d=kT2_d, k_d=k2_d, v_d=v2_d, x_d=x2_d, BT_d=None, CT_d=None, B_d=B2_d, sm_d=sm2_d)
        cp2 = cx.sb("cp2", [128, 48, 4], F32)
        k.dma("sp", cp2[:], convp2, w=[cp2.res], dres=cp2.res)
        runs = [(False, xin, cp, own)]
        if "R" in phases:
            runs.append((True, xin2, cp2, par))
        for red, xsrc, cpt, DD in runs:
            for ti in range(NTOK // TT):
                t0 = ti * TT
                is_ctx = ti == 0
                mr = 1 if is_ctx else 0
                rowlen = 256 if is_ctx else 64
                nrow = TT // rowlen
                l0 = t0 - NCTX
                xt = xring.next()
                k.dma("sp", xt[:], xsrc[t0:t0 + TT, :].rearrange("(j p) d -> p j d", p=128), w=[xt.res], dres=xt.res)
                ss = ssr.next()
                for j in range(2):
                    k.ins("act", lambda j=j, xt=xt, ss=ss: nc.scalar.activation(
                        out=junk[:], in_=xt[:, j, :], func=AF.Square, accum_out=ss[:, j:j + 1]),
                        r=[xt.res], w=[junk.res, ss.res])
                k.ins("act", lambda ss=ss: nc.scalar.activation(out=ss[:], in_=ss[:], func=AF.Ln, scale=1.0 / D, bias=epsc[:, 0:1]),
                      r=[ss.res, epsc.res], w=[ss.res])
                k.ins("act", lambda ss=ss: nc.scalar.activation(out=ss[:], in_=ss[:], func=AF.Exp, scale=-0.5),
                      r=[ss.res], w=[ss.res])
                aT = aTring.next()
                for j in range(2):
                    xn = xnring.next()
                    k.ins("act", lambda j=j, xt=xt, ss=ss, xn=xn: nc.scalar.activation(
                        out=xn[:], in_=xt[:, j, :], func=AF.Identity, scale=ss[:, j:j + 1]),
                        r=[xt.res, ss.res], w=[xn.res])
                    for half in range(2):
                        pb = pst[n_tr % 2]
                        n_tr += 1
                        k.group("pe", [
                            (lambda q=q, pb=pb, xn=xn, half=half: nc.tensor.transpose(
                                pb[:, q * 128:(q + 1) * 128], xn[:, (half * 4 + q) * 128:(half * 4 + q + 1) * 128],
                                ident[:]))
                            for q in range(4)], r=[xn.res, ident.res], w=[pb.res])
                        for q in range(4):
                            kt = half * 4 + q
                            k.ins("act", lambda q=q, kt=kt, pb=pb, aT=aT, j=j: nc.scalar.activation(
                                out=aT[:, kt, j * 128:(j + 1) * 128], in_=pb[:, q * 128:(q + 1) * 128],
                                func=AF.Identity, scale=S1[:, mr, kt:kt + 1], bias=modf[:, mr, 0, kt:kt + 1]),
                                r=[pb.res, S1.res, modf.res], w=[aT.res])
                if not is_ctx and not red:
                    k.dma("sp", aT_d[:, :, l0:l0 + TT].rearrange("kt p t -> p kt t"), aT[:],
                          r=[aT.res], w=[R_scr], dres=aT.res)

                kt_ = ktok.next(); vt_ = vtok.next(); xk_ = xtok.next(); bt_ = btok.next(); st_ = smtok.next()
                ncol_tiles = 65
                for ct in range(ncol_tiles):
                    if (is_ctx or red) and 48 <= ct < 64:
                        continue
                    if red and (ct < 8 or 44 <= ct < 48):
                        continue
                    pa = psa[n_acc % 2]
                    n_acc += 1
                    k.group("pe", [
                        (lambda kt=kt, pa=pa, ct=ct, aT=aT: nc.tensor.matmul(
                            pa[:, 0:TT], wA[:, kt, ct * 128:(ct + 1) * 128], aT[:, kt, :],
                            start=(kt == 0), stop=(kt == 7)))
                        for kt in range(8)], r=[wA.res, aT.res], w=[pa.res])
                    if ct < 48:
                        cb = cring.next()
                        k.ins("act", lambda cb=cb, pa=pa, ct=ct: nc.scalar.activation(
                            out=cb[:], in_=pa[:, 0:TT], func=AF.Identity, scale=cpt[:, ct, 1:2], bias=cpt[:, ct, 3:4]),
                            r=[pa.res, cpt.res], w=[cb.res])
                        pv = pa[:, 0:TT].rearrange("p (r t) -> p r t", t=rowlen)
                        cv3 = cb[:].rearrange("p (r t) -> p r t", t=rowlen)
                        k.ins("dve", lambda pv=pv, cv3=cv3, ct=ct: nc.vector.scalar_tensor_tensor(
                            out=cv3[:, :, 1:], in0=pv[:, :, 0:rowlen - 1], scalar=cpt[:, ct, 0:1], in1=cv3[:, :, 1:],
                            op0=ALU.mult, op1=ALU.add), r=[pa.res, cpt.res, cb.res], w=[cb.res])
                        k.ins("dve", lambda pv=pv, cv3=cv3, ct=ct: nc.vector.scalar_tensor_tensor(
                            out=cv3[:, :, 0:rowlen - 1], in0=pv[:, :, 1:], scalar=cpt[:, ct, 2:3],
                            in1=cv3[:, :, 0:rowlen - 1], op0=ALU.mult, op1=ALU.add),
                            r=[pa.res, cpt.res, cb.res], w=[cb.res])
                        so = sring.next()
                        ee = ering.next()
                        k.ins("act", lambda cb=cb, ee=ee: nc.scalar.activation(out=ee[:], in_=cb[:], func=AF.Exp, scale=-1.0),
                              r=[cb.res], w=[ee.res])
                        k.ins("dve", lambda ee=ee: nc.vector.tensor_scalar(out=ee[:], in0=ee[:], scalar1=1.0, scalar2=None,
                                                                        op0=ALU.add), r=[ee.res], w=[ee.res])
                        k.ins("pool", lambda ee=ee: nc.gpsimd.tensor_tensor(out=ee[:], in0=ee[:], in1=neg1[:], op=ALU.pow),
                              r=[ee.res, neg1.res], w=[ee.res])
                        if ct < 16:
                            sf = sgring.next()
                            k.ins("pool", lambda cb=cb, sf=sf, ee=ee: nc.gpsimd.tensor_tensor(out=sf[:], in0=cb[:], in1=ee[:],
                                                                                              op=ALU.mult),
                                  r=[cb.res, ee.res], w=[sf.res])
                            sq = sqring.next()
                            k.ins("pool", lambda sq=sq, sf=sf: nc.gpsimd.tensor_tensor(out=sq[:], in0=sf[:], in1=sf[:],
                                                                                       op=ALU.mult),
                                  r=[sf.res], w=[sq.res])
                            k.ins("pe", lambda sq=sq: nc.tensor.matmul(pss[:, 0:TT], onesf[:], sq[:], start=True, stop=True),
                                  r=[onesf.res, sq.res], w=[pss.res])
                            rs = rsring.next()
                            k.ins("act", lambda rs=rs: nc.scalar.activation(out=rs[:], in_=pss[:, 0:TT], func=AF.Ln,
                                                                            bias=epsc[:, 0:1]),
                                  r=[pss.res, epsc.res], w=[rs.res])
                            k.ins("act", lambda rs=rs: nc.scalar.activation(out=rs[:], in_=rs[:], func=AF.Exp, scale=-0.5),
                                  r=[rs.res], w=[rs.res])
                            qscale = (128 ** -0.5) if ct < 8 else 1.0
                            k.ins("dve", lambda so=so, sf=sf, rs=rs, qscale=qscale: nc.vector.scalar_tensor_tensor(
                                out=so[:], in0=sf[:], scalar=qscale, in1=rs[:], op0=ALU.mult, op1=ALU.mult),
                                r=[sf.res, rs.res], w=[so.res])
                            dst = DD["qT_d"] if ct < 8 else DD["kT_d"]
                            k.dma("sp", dst[ct % 8, :, t0:t0 + TT], so[:], r=[so.res], w=[R_scr], dres=so.res)
                        else:
                            k.ins("pool", lambda cb=cb, so=so, ee=ee: nc.gpsimd.tensor_tensor(out=so[:], in0=cb[:], in1=ee[:],
                                                                                              op=ALU.mult),
                                  r=[cb.res, ee.res], w=[so.res])
                            if 40 <= ct < 44 and not red:
                                k.dma("sp", DD["BT_d"][ct - 40, :, t0:t0 + TT], so[:], r=[so.res], w=[R_scr], dres=so.res)
                            if 44 <= ct < 48:
                                k.dma("sp", DD["CT_d"][ct - 44, :, t0:t0 + TT], so[:], r=[so.res], w=[R_scr], dres=so.res)
                        tgt = None
                        if 8 <= ct < 16:
                            tgt = (kt_, (ct - 8) * 128)
                        elif 16 <= ct < 24:
                            tgt = (vt_, (ct - 16) * 128)
                        elif 24 <= ct < 40:
                            tgt = (xk_, (ct - 24) * 128)
                        elif 40 <= ct < 44:
                            tgt = (bt_, (ct - 40) * 128)
                        if tgt is not None:
                            po = pso[n_ot % 2]
                            n_ot += 1
                            pob = po
                            k.group("pe", [
                                (lambda j=j, so=so, pob=pob: nc.tensor.transpose(
                                    pob[:, j * 128:(j + 1) * 128], so[:, j * 128:(j + 1) * 128], identb[:]))
                                for j in range(2)], r=[so.res, identb.res], w=[po.res])
                            tb, off = tgt
                            k.ins("dve", lambda tb=tb, off=off, pob=pob: nc.vector.tensor_copy(
                                tb[:, :, off:off + 128], pob[:, 0:256].rearrange("p (j c) -> p j c", c=128)),
                                r=[po.res], w=[tb.res])
                    elif ct < 64:
                        sg = sgring.next()
                        k.ins("act", lambda sg=sg, pa=pa: nc.scalar.activation(out=sg[:], in_=pa[:, 0:TT], func=AF.Exp, scale=-1.0),
                              r=[pa.res], w=[sg.res])
                        k.ins("dve", lambda sg=sg: nc.vector.tensor_scalar(out=sg[:], in0=sg[:], scalar1=1.0, scalar2=None,
                                                                        op0=ALU.add), r=[sg.res], w=[sg.res])
                        k.ins("pool", lambda sg=sg: nc.gpsimd.tensor_tensor(out=sg[:], in0=sg[:], in1=neg1[:], op=ALU.pow),
                              r=[sg.res, neg1.res], w=[sg.res])
                        k.dma("sp", sgT_d[ct - 48, :, l0:l0 + TT], sg[:], r=[sg.res], w=[R_scr], dres=sg.res)
                    else:
                        sm = smring.next()
                        k.ins("act", lambda sm=sm, pa=pa: nc.scalar.activation(
                            out=sm[:], in_=pa[:, 0:TT], func=AF.Exp, scale=smp[:, 0:1], bias=smp[:, 1:2]),
                            r=[pa.res, smp.res], w=[sm.res])
                        k.ins("act", lambda sm=sm: nc.scalar.activation(out=sm[:], in_=sm[:], func=AF.Ln, bias=1.0),
                              r=[sm.res], w=[sm.res])
                        k.ins("dve", lambda sm=sm: nc.vector.tensor_scalar(out=sm[:], in0=sm[:], scalar1=smult[:, 0:1],
                                                                        scalar2=None, op0=ALU.mult),
                              r=[sm.res, smult.res], w=[sm.res])
                        k.group("pe", [
                            (lambda j=j, sm=sm: nc.tensor.transpose(psm[:, j * 128:(j + 1) * 128],
                                                                    sm[:, j * 128:(j + 1) * 128], ident[:]))
                            for j in range(2)], r=[sm.res, ident.res], w=[psm.res])
                        k.ins("dve", lambda st_=st_: nc.vector.tensor_copy(
                            st_[:], psm[:, 0:256].rearrange("p (j c) -> p j c", c=128)), r=[psm.res], w=[st_.res])
                rows = lambda d_: d_[t0:t0 + TT, :].rearrange("(j p) c -> p j c", p=128)
                k.dma("sp", rows(DD["k_d"]), kt_[:], r=[kt_.res], w=[R_scr], dres=kt_.res)
                k.dma("sp", rows(DD["v_d"]), vt_[:], r=[vt_.res], w=[R_scr], dres=vt_.res)
                k.dma("sp", rows(DD["x_d"]), xk_[:], r=[xk_.res], w=[R_scr], dres=xk_.res)
                k.dma("sp", rows(DD["B_d"]), bt_[:], r=[bt_.res], w=[R_scr], dres=bt_.res)
                k.dma("sp", rows(DD["sm_d"]), st_[:], r=[st_.res], w=[R_scr], dres=st_.res)
        cx.pop()


    def load_scan_consts():
        tri = cx.sb("tri", [128, 4, 128], F32)
        k.dma("sp", tri[:], tri_d, w=[tri.res], dres=tri.res)
        mskf = cx.sb("mskf", [128, 4, 128], F32)
        k.dma("sp", mskf[:], msk_d, w=[mskf.res], dres=mskf.res)
        mskb = cx.sb("mskb", [128, 4, 128], BF16)
        k.ins("dve", lambda: nc.vector.tensor_copy(mskb[:], mskf[:]), r=[mskf.res], w=[mskb.res])
        id4 = cx.sb("id4", [128, 4, 128], F32)
        for j in range(4):
            k.ins("dve", lambda j=j: nc.vector.tensor_copy(id4[:, j, :], ident[:]), r=[ident.res], w=[id4.res])
        return tri, mskb, id4

    def gdn_pass(mode):
        cx.push()
        fwd = mode != "own2"
        pi = 0 if mode == "own1" else 1
        red = mode == "red"
        s_kT, s_k, s_v, s_sm = (kT2_d, k2_d, v2_d, sm2_d) if red else (kT_d, k_d, v_d, sm_d)
        tri, mskb, id4 = load_scan_consts()
        cumL = tri[:, 0, :] if fwd else tri[:, 1, :]
        negR = tri[:, 2, :] if fwd else tri[:, 3, :]
        m_s = mskb[:, 0, :] if fwd else mskb[:, 1, :]
        mT_s = mskb[:, 1, :] if fwd else mskb[:, 0, :]
        mT_i = mskb[:, 3, :] if fwd else mskb[:, 2, :]
        g0 = pi * 8
        l0c = 32 + pi * 8
        banks = Ring([cx.psum(f"gb{i}", [128, 512], F32) for i in range(8)])
        S32 = cx.sb("S32", [128, 8, 128], F32)
        Sbf = cx.sb("Sbf", [128, 8, 128], BF16)
        if fwd:
            k.ins("pool", lambda: nc.gpsimd.memset(S32[:], 0.0), w=[S32.res])
        else:
            k.dma("sp", S32[:].rearrange("p h d -> p (h d)"), st2_d[:, 0:1024], r=[R_st2], w=[S32.res], dres=S32.res)
        k.ins("act", lambda: nc.scalar.copy(out=Sbf[:], in_=S32[:]), r=[S32.res], w=[Sbf.res])
        NB = 2
        qTr = cx.sbring("qTc", NB, [128, 8, 128], BF16)
        kTr = cx.sbring("kTc", NB, [128, 8, 128], BF16)
        kr = cx.sbring("kc", NB, [128, 8, 128], BF16)
        vr = cx.sbring("vc", NB, [128, 8, 128], BF16)
        smr = cx.sbring("smc", NB, [128, 128], F32)
        o1r = cx.sbring("o1c", NB, [128, 8, 128], F32)
        kbgr = cx.sbring("kbg", NB, [128, 8, 128], BF16)
        vbr = cx.sbring("vb", NB, [128, 8, 128], BF16)
        kdr = cx.sbring("kd", NB, [128, 8, 128], BF16)
        smallr = cx.sbring("gsm", NB, [128, 6, 8], F32)
        eglr = cx.sbring("egl", NB, [128, 8], F32)
        expr = cx.sbring("exps", NB, [128, 3, 8], F32)
        Ear = cx.sbring("Ea", 2, [128, 4, 128], F32)
        Ebr = cx.sbring("Eb", 2, [128, 4, 128], F32)
        Ecr = cx.sbring("Ec", 2, [128, 4, 128], F32)
        Edr = cx.sbring("Ed", 2, [128, 4, 128], F32)
        Pr = cx.sbring("Pp", 4, [128, 4, 128], F32)
        PTr = cx.sbring("PTp", 4, [128, 4, 128], F32)
        Yr = cx.sbring("Yp", 4, [128, 4, 128], F32)
        Yfr = cx.sbring("Yf", 2 * NB, [128, 4, 128], BF16)
        nWTr = cx.sbring("nWT", 2 * NB, [128, 4, 128], BF16)
        attr = cx.sbring("att", 2 * NB, [128, 4, 128], BF16)
        qgr = cx.sbring("qg", 2 * NB, [128, 4, 128], BF16)
        vnr = cx.sbring("vn", 2, [128, 4, 128], BF16)
        oTr = cx.sbring("oTs", 2, [128, 8, 128], F32)

        def prep(c):
            lat = c >= 2 and not red
            t0 = c * C
            l0 = t0 - NCTX
            qT_c = qTr.next(); kT_c = kTr.next(); k_c = kr.next(); v_c = vr.next(); sm_c = smr.next()
            if lat:
                k.dma("sp", qT_c[:], qT_d[:, :, t0:t0 + C].rearrange("h p t -> p h t"), r=[R_scr], w=[qT_c.res], dres=qT_c.res)
            k.dma("sp", kT_c[:], s_kT[:, :, t0:t0 + C].rearrange("h p t -> p h t"), r=[R_scr], w=[kT_c.res], dres=kT_c.res)
            k.dma("sp", k_c[:], s_k[t0:t0 + C, :].rearrange("t (h d) -> t h d", d=128), r=[R_scr], w=[k_c.res], dres=k_c.res)
            k.dma("sp", v_c[:], s_v[t0:t0 + C, :].rearrange("t (h d) -> t h d", d=128), r=[R_scr], w=[v_c.res], dres=v_c.res)
            k.dma("sp", sm_c[:], s_sm[t0:t0 + C, :], r=[R_scr], w=[sm_c.res], dres=sm_c.res)
            o1_c = None
            if lat and not fwd:
                o1_c = o1r.next()
                k.dma("sp", o1_c[:], o1_d[:, :, l0:l0 + C].rearrange("h p t -> p h t"), r=[R_o1[c]], w=[o1_c.res], dres=o1_c.res)
            gcols = sm_c[:, g0:g0 + 8]
            lcols = sm_c[:, l0c:l0c + 8]
            pb = banks.next()
            k.group("pe", [
                lambda: nc.tensor.matmul(pb[:, 0:8], cumL, gcols, start=True, stop=True),
                lambda: nc.tensor.matmul(pb[:, 8:16], onesf[:], gcols, start=True, stop=True)],
                r=[tri.res, onesf.res, sm_c.res], w=[pb.res])
            sm6 = smallr.next()
            gc, gcl, ngc, tmp = sm6[:, 0, :], sm6[:, 1, :], sm6[:, 2, :], sm6[:, 3, :]
            k.ins("dve", lambda: nc.vector.tensor_copy(gc, pb[:, 0:8]), r=[pb.res], w=[sm6.res])
            k.ins("dve", lambda: nc.vector.tensor_tensor(out=gcl, in0=gc, in1=lcols, op=ALU.add), r=[sm6.res, sm_c.res], w=[sm6.res])
            k.ins("dve", lambda: nc.vector.tensor_scalar(out=ngc, in0=gc, scalar1=-1.0, scalar2=None, op0=ALU.mult),
                  r=[sm6.res], w=[sm6.res])
            k.ins("dve", lambda: nc.vector.tensor_tensor(out=tmp, in0=pb[:, 8:16], in1=gc, op=ALU.subtract),
                  r=[pb.res, sm6.res], w=[sm6.res])
            ex = expr.next()
            egl = eglr.next()
            k.ins("act", lambda: nc.scalar.activation(out=ex[:, 0, :], in_=gcl, func=AF.Exp), r=[sm6.res], w=[ex.res])
            k.ins("act", lambda: nc.scalar.activation(out=ex[:, 1, :], in_=lcols, func=AF.Exp), r=[sm_c.res], w=[ex.res])
            k.ins("act", lambda: nc.scalar.activation(out=ex[:, 2, :], in_=tmp, func=AF.Exp), r=[sm6.res], w=[ex.res])
            k.ins("act", lambda: nc.scalar.activation(out=egl[:], in_=pb[:, 8:16], func=AF.Exp), r=[pb.res], w=[egl.res])
            kbg = kbgr.next(); vb = vbr.next(); kd = kdr.next()
            bc = lambda col: ex[:, col, :].unsqueeze(2).to_broadcast([128, 8, 128])
            k.ins("pool", lambda: nc.gpsimd.tensor_tensor(out=kbg[:], in0=k_c[:], in1=bc(0), op=ALU.mult),
                  r=[k_c.res, ex.res], w=[kbg.res])
            k.ins("pool", lambda: nc.gpsimd.tensor_tensor(out=vb[:], in0=v_c[:], in1=bc(1), op=ALU.mult),
                  r=[v_c.res, ex.res], w=[vb.res])
            k.ins("pool", lambda: nc.gpsimd.tensor_tensor(out=kd[:], in0=k_c[:], in1=bc(2), op=ALU.mult),
                  r=[k_c.res, ex.res], w=[kd.res])
            groups = []
            for gi in range(2):
                h0 = gi * 4
                gb = lambda h: sm_c[:, g0 + h:g0 + h + 1].to_broadcast([128, 128])
                lb = lambda h: sm_c[:, l0c + h:l0c + h + 1].to_broadcast([128, 128])
                KK = banks.next()
                k.group("pe", [(lambda j=j: nc.tensor.matmul(KK[:, j * 128:(j + 1) * 128], kT_c[:, h0 + j, :], kT_c[:, h0 + j, :],
                                                             start=True, stop=True)) for j in range(4)],
                        r=[kT_c.res], w=[KK.res])
                Da = banks.next()
                fl = []
                for j in range(4):
                    fl.append(lambda j=j: nc.tensor.matmul(Da[:, j * 128:(j + 1) * 128], gb(h0 + j), negR, start=True, stop=False))
                    fl.append(lambda j=j: nc.tensor.matmul(Da[:, j * 128:(j + 1) * 128], identb[:], m_s, start=False, stop=True))
                k.group("pe", fl, r=[sm_c.res, tri.res, identb.res, mskb.res], w=[Da.res])
                Ea = Ear.next()
                for j in range(4):
                    k.ins("act", lambda j=j: nc.scalar.activation(out=Ea[:, j, :], in_=Da[:, j * 128:(j + 1) * 128], func=AF.Exp,
                                                                  bias=sm6[:, 1, h0 + j:h0 + j + 1]),
                          r=[Da.res, sm6.res], w=[Ea.res])
                Db = banks.next()
                fl = []
                for j in range(4):
                    sl = slice(j * 128, (j + 1) * 128)
                    fl.append(lambda j=j, sl=sl: nc.tensor.matmul(Db[:, sl], gb(h0 + j), cumL, start=True, stop=False))
                    fl.append(lambda j=j, sl=sl: nc.tensor.matmul(Db[:, sl], lb(h0 + j), ident[:], start=False, stop=False))
                    fl.append(lambda j=j, sl=sl: nc.tensor.matmul(Db[:, sl], identb[:], mT_s, start=False, stop=True))
                k.group("pe", fl, r=[sm_c.res, tri.res, ident.res, identb.res, mskb.res], w=[Db.res])
                Eb = Ebr.next()
                for j in range(4):
                    k.ins("act", lambda j=j: nc.scalar.activation(out=Eb[:, j, :], in_=Db[:, j * 128:(j + 1) * 128], func=AF.Exp,
                                                                  bias=sm6[:, 2, h0 + j:h0 + j + 1]),
                          r=[Db.res, sm6.res], w=[Eb.res])
                P0 = Pr.next(); P0T = PTr.next()
                KK3 = KK[:].rearrange("p (j c) -> p j c", c=128)
                k.ins("dve", lambda: nc.vector.scalar_tensor_tensor(out=P0[:], in0=KK3, scalar=-1.0, in1=Ea[:],
                                                                    op0=ALU.mult, op1=ALU.mult),
                      r=[KK.res, Ea.res], w=[P0.res])
                k.ins("dve", lambda: nc.vector.scalar_tensor_tensor(out=P0T[:], in0=KK3, scalar=-1.0, in1=Eb[:],
                                                                    op0=ALU.mult, op1=ALU.mult),
                      r=[KK.res, Eb.res], w=[P0T.res])
                att = None; qg = None
                if lat:
                    QK = banks.next()
                    k.group("pe", [(lambda j=j: nc.tensor.matmul(QK[:, j * 128:(j + 1) * 128], kT_c[:, h0 + j, :], qT_c[:, h0 + j, :],
                                                                 start=True, stop=True)) for j in range(4)],
                            r=[kT_c.res, qT_c.res], w=[QK.res])
                    Dc = banks.next()
                    fl = []
                    for j in range(4):
                        sl = slice(j * 128, (j + 1) * 128)
                        fl.append(lambda j=j, sl=sl: nc.tensor.matmul(Dc[:, sl], gb(h0 + j), cumL, start=True, stop=False))
                        fl.append(lambda j=j, sl=sl: nc.tensor.matmul(Dc[:, sl], identb[:], mT_i, start=False, stop=True))
                    k.group("pe", fl, r=[sm_c.res, tri.res, identb.res, mskb.res], w=[Dc.res])
                    Ec = Ecr.next()
                    for j in range(4):
                        k.ins("act", lambda j=j: nc.scalar.activation(out=Ec[:, j, :], in_=Dc[:, j * 128:(j + 1) * 128], func=AF.Exp,
                                                                      bias=sm6[:, 2, h0 + j:h0 + j + 1]),
                              r=[Dc.res, sm6.res], w=[Ec.res])
                    Dd = banks.next()
                    k.group("pe", [(lambda j=j: nc.tensor.matmul(Dd[:, j * 128:(j + 1) * 128], gb(h0 + j), cumL, start=True, stop=True))
                                   for j in range(4)], r=[sm_c.res, tri.res], w=[Dd.res])
                    Ed = Edr.next()
                    k.ins("act", lambda: nc.scalar.activation(out=Ed[:].rearrange("p j c -> p (j c)"), in_=Dd[:], func=AF.Exp),
                          r=[Dd.res], w=[Ed.res])
                    att = attr.next()
                    k.ins("dve", lambda: nc.vector.tensor_tensor(out=att[:], in0=QK[:].rearrange("p (j c) -> p j c", c=128),
                                                                 in1=Ec[:], op=ALU.mult), r=[QK.res, Ec.res], w=[att.res])
                    qg = qgr.next()
                    k.ins("pool", lambda: nc.gpsimd.tensor_tensor(out=qg[:], in0=qT_c[:, h0:h0 + 4, :], in1=Ed[:], op=ALU.mult),
                          r=[qT_c.res, Ed.res], w=[qg.res])
                Y = Yr.next()
                k.ins("pool", lambda: nc.gpsimd.tensor_tensor(out=Y[:], in0=P0T[:], in1=id4[:], op=ALU.add),
                      r=[P0T.res, id4.res], w=[Y.res])
                Pp, PTp = P0, P0T
                for lev in range(1, 7):
                    last = lev == 6
                    Pb = banks.next()
                    k.group("pe", [(lambda j=j, Pp=Pp, PTp=PTp, Pb=Pb: nc.tensor.matmul(
                        Pb[:, j * 128:(j + 1) * 128], PTp[:, j, :], Pp[:, j, :], start=True, stop=True)) for j in range(4)],
                        r=[Pp.res, PTp.res], w=[Pb.res])
                    Pn = Pr.next()
                    k.ins("act", lambda Pn=Pn, Pb=Pb: nc.scalar.copy(out=Pn[:].rearrange("p j c -> p (j c)"), in_=Pb[:]),
                          r=[Pb.res], w=[Pn.res])
                    PTn = None
                    if not last:
                        PTb = banks.next()
                        k.group("pe", [(lambda j=j, Pp=Pp, PTp=PTp, PTb=PTb: nc.tensor.matmul(
                            PTb[:, j * 128:(j + 1) * 128], Pp[:, j, :], PTp[:, j, :], start=True, stop=True)) for j in range(4)],
                            r=[Pp.res, PTp.res], w=[PTb.res])
                        PTn = PTr.next()
                        k.ins("act", lambda PTn=PTn, PTb=PTb: nc.scalar.copy(out=PTn[:].rearrange("p j c -> p (j c)"), in_=PTb[:]),
                              r=[PTb.res], w=[PTn.res])
                    Yb = banks.next()
                    k.group("pe", [(lambda j=j, Yb=Yb, Y=Y, Pn=Pn: nc.tensor.matmul(
                        Yb[:, j * 128:(j + 1) * 128], Pn[:, j, :], Y[:, j, :], start=True, stop=True)) for j in range(4)],
                        r=[Y.res, Pn.res], w=[Yb.res])
                    if last:
                        Yn = Yfr.next()
                        k.ins("dve", lambda Yn=Yn, Yb=Yb, Y=Y: nc.vector.tensor_tensor(
                            out=Yn[:].rearrange("p j c -> p (j c)"), in0=Yb[:], in1=Y[:].rearrange("p j c -> p (j c)"), op=ALU.add),
                            r=[Yb.res, Y.res], w=[Yn.res])
                    else:
                        Yn = Yr.next()
                        k.ins("dve", lambda Yn=Yn, Yb=Yb, Y=Y: nc.vector.tensor_tensor(
                            out=Yn[:].rearrange("p j c -> p (j c)"), in0=Yb[:], in1=Y[:].rearrange("p j c -> p (j c)"), op=ALU.add),
                            r=[Yb.res, Y.res], w=[Yn.res])
                    Y = Yn
                    Pp, PTp = Pn, PTn
                Wb = banks.next()
                k.group("pe", [(lambda j=j: nc.tensor.matmul(Wb[:, j * 128:(j + 1) * 128], kbg[:, h0 + j, :], Y[:, j, :],
                                                             start=True, stop=True)) for j in range(4)],
                        r=[kbg.res, Y.res], w=[Wb.res])
                nWT = nWTr.next()
                k.ins("act", lambda: nc.scalar.activation(out=nWT[:].rearrange("p j c -> p (j c)"), in_=Wb[:], func=AF.Identity,
                                                          scale=-1.0), r=[Wb.res], w=[nWT.res])
                groups.append((Y, nWT, att, qg))
            return dict(c=c, lat=lat, groups=groups, vb=vb, kd=kd, egl=egl, o1=o1_c)

        def chain(pp):
            c, lat = pp["c"], pp["lat"]
            l0 = c * C - NCTX
            vb, kd, egl = pp["vb"], pp["kd"], pp["egl"]
            oTs = oTr.next() if lat else None
            for gi in range(2):
                h0 = gi * 4
                Y, nWT, att, qg = pp["groups"][gi]
                VN = banks.next()
                fl = []
                for j in range(4):
                    sl = slice(j * 128, (j + 1) * 128)
                    fl.append(lambda j=j, sl=sl: nc.tensor.matmul(VN[:, sl], Y[:, j, :], vb[:, h0 + j, :], start=True, stop=False))
                    fl.append(lambda j=j, sl=sl: nc.tensor.matmul(VN[:, sl], nWT[:, j, :], Sbf[:, h0 + j, :], start=False, stop=True))
                k.group("pe", fl, r=[Y.res, vb.res, nWT.res, Sbf.res], w=[VN.res])
                vn = vnr.next()
                k.ins("dve", lambda: nc.vector.tensor_copy(vn[:].rearrange("p j c -> p (j c)"), VN[:]), r=[VN.res], w=[vn.res])
                if lat:
                    OT = banks.next()
                    fl = []
                    for j in range(4):
                        sl = slice(j * 128, (j + 1) * 128)
                        fl.append(lambda j=j, sl=sl: nc.tensor.matmul(OT[:, sl], Sbf[:, h0 + j, :], qg[:, j, :], start=True, stop=False))
                        fl.append(lambda j=j, sl=sl: nc.tensor.matmul(OT[:, sl], vn[:, j, :], att[:, j, :], start=False, stop=True))
                    k.group("pe", fl, r=[Sbf.res, qg.res, vn.res, att.res], w=[OT.res])
                    osl = oTs[:, h0:h0 + 4, :].rearrange("p j c -> p (j c)")
                    if fwd:
                        k.ins("act", lambda: nc.scalar.copy(out=osl, in_=OT[:]), r=[OT.res], w=[oTs.res])
                    else:
                        o1_c = pp["o1"]
                        k.ins("dve", lambda: nc.vector.tensor_tensor(
                            out=osl, in0=OT[:], in1=o1_c[:, h0:h0 + 4, :].rearrange("p j c -> p (j c)"), op=ALU.add),
                            r=[OT.res, o1_c.res], w=[oTs.res])
                DS = banks.next()
                k.group("pe", [(lambda j=j: nc.tensor.matmul(DS[:, j * 128:(j + 1) * 128], kd[:, h0 + j, :], vn[:, j, :],
                                                             start=True, stop=True)) for j in range(4)],
                        r=[kd.res, vn.res], w=[DS.res])
                ssl = S32[:, h0:h0 + 4, :]
                k.ins("pool", lambda: nc.gpsimd.tensor_tensor(
                    out=ssl, in0=ssl, in1=egl[:, h0:h0 + 4].unsqueeze(2).to_broadcast([128, 4, 128]), op=ALU.mult),
                    r=[S32.res, egl.res], w=[S32.res])
                k.ins("dve", lambda: nc.vector.tensor_tensor(out=ssl, in0=ssl, in1=DS[:].rearrange("p (j c) -> p j c", c=128),
                                                             op=ALU.add), r=[S32.res, DS.res], w=[S32.res])
                k.ins("act", lambda: nc.scalar.copy(out=Sbf[:, h0:h0 + 4, :], in_=ssl), r=[S32.res], w=[Sbf.res])
            if lat:
                if fwd:
                    k.dma("sp", o1_d[:, :, l0:l0 + C].rearrange("h p t -> p h t"), oTs[:], r=[oTs.res], w=[R_o1[c]], dres=oTs.res)
                else:
                    k.dma("sp", oT_d[:, :, l0:l0 + C].rearrange("h p t -> p h t"), oTs[:], r=[oTs.res], w=[R_oT], dres=oTs.res)

        order = list(range(NCH)) if fwd else list(range(NCH - 1, 1, -1))
        if GDN_LIMIT:
            order = order[:GDN_LIMIT]
        pend = prep(order[0])
        for idx in range(len(order)):
            nxt = prep(order[idx + 1]) if idx + 1 < len(order) else None
            chain(pend)
            pend = nxt
        if red:
            k.dma("sp", st2_d[:, 0:1024], S32[:].rearrange("p h d -> p (h d)"), r=[S32.res], w=[R_st2], dres=S32.res)
        cx.pop()

    def ssd_pass(mode):
        cx.push()
        fwd = mode != "own2"
        pi = 0 if mode == "own1" else 1
        red = mode == "red"
        s_x, s_B, s_sm = (x2_d, B2_d, sm2_d) if red else (x_d, B_d, sm_d)
        tri, mskb, id4 = load_scan_consts()
        cumL = tri[:, 0, :] if fwd else tri[:, 1, :]
        mT_i = mskb[:, 3, :] if fwd else mskb[:, 2, :]
        d0 = 64 + pi * 32
        banks = Ring([cx.psum(f"sbk{i}", [128, 512], F32) for i in range(8)])
        aB = cx.sb("aB", [128, 32], F32)
        k.dma("sp", aB[:], salog_d[:, pi, :], w=[aB.res], dres=aB.res)
        k.ins("act", lambda: nc.scalar.activation(out=aB[:], in_=aB[:], func=AF.Exp), r=[aB.res], w=[aB.res])
        k.ins("dve", lambda: nc.vector.tensor_scalar(out=aB[:], in0=aB[:], scalar1=-1.0, scalar2=None, op0=ALU.mult),
              r=[aB.res], w=[aB.res])
        hT32 = cx.sb("hT32", [128, 32, 64], F32)
        hTbf = cx.sb("hTbf", [128, 32, 64], BF16)
        if fwd:
            k.ins("pool", lambda: nc.gpsimd.memset(hT32[:], 0.0), w=[hT32.res])
        else:
            k.dma("sp", hT32[:].rearrange("p h d -> p (h d)"), st2_d[:, 1024:3072], r=[R_st2], w=[hT32.res], dres=hT32.res)
        k.ins("act", lambda: nc.scalar.copy(out=hTbf[:], in_=hT32[:]), r=[hT32.res], w=[hTbf.res])
        xr = cx.sbring("xc", 2, [128, 32, 64], BF16)
        BTr = cx.sbring("BTc", 2, [128, 4, 128], BF16)
        CTr = cx.sbring("CTc", 2, [128, 4, 128], BF16)
        Br = cx.sbring("Bc", 2, [128, 4, 128], BF16)
        smr = cx.sbring("smc", 2, [128, 128], F32)
        y1r = cx.sbring("y1c", 2, [128, 2048], F32)
        dAr = cx.sbring("dA", 2, [128, 32], F32)
        s5r = cx.sbring("s5", 2, [128, 6, 32], F32)
        xdtr = cx.sbring("xdt", 2, [128, 32, 64], BF16)
        xdtsr = cx.sbring("xdts", 2, [128, 32, 64], BF16)
        ytr = cx.sbring("ytmp", 2, [128, 32, 64], F32)
        ycr = cx.sbring("yc", 2, [128, 2048], F32)
        segr = cx.sbring("seg", 3, [128, 4, 128], F32)
        GTr = cx.sbring("GT", 2, [128, 4, 128], F32)
        MTr = cx.sbring("MT", 3, [128, 4, 128], BF16)
        order = list(range(NCH)) if fwd else list(range(NCH - 1, 1, -1))
        if GDN_LIMIT:
            order = order[:GDN_LIMIT]
        for c in order:
            lat = c >= 2 and not red
            t0 = c * C
            l0 = t0 - NCTX
            x_c = xr.next(); BT_c = BTr.next(); CT_c = CTr.next(); B_c = Br.next(); sm_c = smr.next()
            k.dma("sp", x_c[:].rearrange("p h d -> p (h d)"), s_x[t0:t0 + C, :], r=[R_scr], w=[x_c.res], dres=x_c.res)
            if lat:
                k.dma("sp", BT_c[:], BT_d[:, :, t0:t0 + C].rearrange("g p t -> p g t"), r=[R_scr], w=[BT_c.res], dres=BT_c.res)
                k.dma("sp", CT_c[:], CT_d[:, :, t0:t0 + C].rearrange("g p t -> p g t"), r=[R_scr], w=[CT_c.res], dres=CT_c.res)
            k.dma("sp", B_c[:], s_B[t0:t0 + C, :].rearrange("t (g n) -> t g n", n=128), r=[R_scr], w=[B_c.res], dres=B_c.res)
            k.dma("sp", sm_c[:], s_sm[t0:t0 + C, :], r=[R_scr], w=[sm_c.res], dres=sm_c.res)
            y1_c = None
            if lat and not fwd:
                y1_c = y1r.next()
                k.dma("sp", y1_c[:], y1_d[l0:l0 + C, :], r=[R_y1[c]], w=[y1_c.res], dres=y1_c.res)
            dtc = sm_c[:, d0:d0 + 32]
            dA = dAr.next()
            k.ins("dve", lambda: nc.vector.tensor_tensor(out=dA[:], in0=dtc, in1=aB[:], op=ALU.mult),
                  r=[sm_c.res, aB.res], w=[dA.res])
            pb = banks.next()
            k.group("pe", [
                lambda: nc.tensor.matmul(pb[:, 0:32], cumL, dA[:], start=True, stop=True),
                lambda: nc.tensor.matmul(pb[:, 32:64], onesf[:], dA[:], start=True, stop=True)],
                r=[tri.res, onesf.res, dA.res], w=[pb.res])
            s5 = s5r.next()
            ac, nac, tmp, ea, w2, eat = (s5[:, i, :] for i in range(6))
            k.ins("dve", lambda: nc.vector.tensor_copy(ac, pb[:, 0:32]), r=[pb.res], w=[s5.res])
            k.ins("dve", lambda: nc.vector.tensor_scalar(out=nac, in0=ac, scalar1=-1.0, scalar2=None, op0=ALU.mult),
                  r=[s5.res], w=[s5.res])
            k.ins("dve", lambda: nc.vector.tensor_tensor(out=tmp, in0=pb[:, 32:64], in1=ac, op=ALU.subtract),
                  r=[pb.res, s5.res], w=[s5.res])
            k.ins("act", lambda: nc.scalar.activation(out=ea, in_=ac, func=AF.Exp), r=[s5.res], w=[s5.res])
            k.ins("act", lambda: nc.scalar.activation(out=w2, in_=tmp, func=AF.Exp), r=[s5.res], w=[s5.res])
            k.ins("act", lambda: nc.scalar.activation(out=eat, in_=pb[:, 32:64], func=AF.Exp), r=[pb.res], w=[s5.res])
            k.ins("dve", lambda: nc.vector.tensor_tensor(out=w2, in0=w2, in1=dtc, op=ALU.mult), r=[s5.res, sm_c.res], w=[s5.res])
            bc32 = lambda ap: ap.unsqueeze(2).to_broadcast([128, 32, 64])
            xdts = xdtsr.next()
            k.ins("pool", lambda: nc.gpsimd.tensor_tensor(out=xdts[:], in0=x_c[:], in1=bc32(w2), op=ALU.mult),
                  r=[x_c.res, s5.res], w=[xdts.res])
            ytmp = None
            if lat:
                xdt = xdtr.next()
                k.ins("pool", lambda: nc.gpsimd.tensor_tensor(out=xdt[:], in0=x_c[:], in1=bc32(dtc), op=ALU.mult),
                      r=[x_c.res, sm_c.res], w=[xdt.res])
                ytmp = ytr.next()
                for g in range(4):
                    YO = banks.next()
                    k.ins("pe", lambda g=g, YO=YO: nc.tensor.matmul(
                        YO[:], CT_c[:, g, :], hTbf[:, g * 8:(g + 1) * 8, :].rearrange("p h d -> p (h d)"), start=True, stop=True),
                        r=[CT_c.res, hTbf.res], w=[YO.res])
                    k.ins("dve", lambda g=g, YO=YO: nc.vector.tensor_tensor(
                        out=ytmp[:, g * 8:(g + 1) * 8, :], in0=YO[:].rearrange("p (h d) -> p h d", d=64),
                        in1=ea[:, g * 8:(g + 1) * 8].unsqueeze(2).to_broadcast([128, 8, 64]), op=ALU.mult),
                        r=[YO.res, s5.res], w=[ytmp.res])
            k.ins("pool", lambda: nc.gpsimd.tensor_tensor(out=hT32[:], in0=hT32[:], in1=bc32(eat), op=ALU.mult),
                  r=[hT32.res, s5.res], w=[hT32.res])
            for g in range(4):
                NS = banks.next()
                k.ins("pe", lambda g=g, NS=NS: nc.tensor.matmul(
                    NS[:], B_c[:, g, :], xdts[:, g * 8:(g + 1) * 8, :].rearrange("p h d -> p (h d)"), start=True, stop=True),
                    r=[B_c.res, xdts.res], w=[NS.res])
                hs = hT32[:, g * 8:(g + 1) * 8, :]
                k.ins("dve", lambda g=g, NS=NS, hs=hs: nc.vector.tensor_tensor(
                    out=hs, in0=hs, in1=NS[:].rearrange("p (h d) -> p h d", d=64), op=ALU.add),
                    r=[hT32.res, NS.res], w=[hT32.res])
            k.ins("act", lambda: nc.scalar.copy(out=hTbf[:], in_=hT32[:]), r=[hT32.res], w=[hTbf.res])
            if not lat:
                continue
            GTb = banks.next()
            k.group("pe", [(lambda g=g: nc.tensor.matmul(GTb[:, g * 128:(g + 1) * 128], BT_c[:, g, :], CT_c[:, g, :],
                                                         start=True, stop=True)) for g in range(4)],
                    r=[BT_c.res, CT_c.res], w=[GTb.res])
            GT = GTr.next()
            k.ins("act", lambda: nc.scalar.copy(out=GT[:].rearrange("p g c -> p (g c)"), in_=GTb[:]), r=[GTb.res], w=[GT.res])
            y_c = ycr.next()
            for g in range(4):
                YD = banks.next()
                for qd in range(2):
                    hq = g * 8 + qd * 4
                    Dq = banks.next()
                    fl = []
                    for j in range(4):
                        sl = slice(j * 128, (j + 1) * 128)
                        fl.append(lambda j=j, sl=sl, Dq=Dq: nc.tensor.matmul(
                            Dq[:, sl], dA[:, hq + j:hq + j + 1].to_broadcast([128, 128]), cumL, start=True, stop=False))
                        fl.append(lambda j=j, sl=sl, Dq=Dq: nc.tensor.matmul(Dq[:, sl], identb[:], mT_i, start=False, stop=True))
                    k.group("pe", fl, r=[dA.res, tri.res, identb.res, mskb.res], w=[Dq.res])
                    seg = segr.next()
                    for j in range(4):
                        k.ins("act", lambda j=j, seg=seg, Dq=Dq: nc.scalar.activation(
                            out=seg[:, j, :], in_=Dq[:, j * 128:(j + 1) * 128], func=AF.Exp, bias=s5[:, 1, hq + j:hq + j + 1]),
                            r=[Dq.res, s5.res], w=[seg.res])
                    MT = MTr.next()
                    k.ins("dve", lambda seg=seg, MT=MT, g=g: nc.vector.tensor_tensor(
                        out=MT[:], in0=seg[:], in1=GT[:, g:g + 1, :].to_broadcast([128, 4, 128]), op=ALU.mult),
                        r=[seg.res, GT.res], w=[MT.res])
                    k.group("pe", [(lambda j=j, MT=MT, YD=YD: nc.tensor.matmul(
                        YD[:, (qd * 4 + j) * 64:(qd * 4 + j + 1) * 64], MT[:, j, :], xdt[:, hq + j, :], start=True, stop=True))
                        for j in range(4)], r=[MT.res, xdt.res], w=[YD.res])
                ysl = y_c[:, g * 512:(g + 1) * 512]
                k.ins("dve", lambda g=g, YD=YD, ysl=ysl: nc.vector.tensor_tensor(
                    out=ysl, in0=YD[:], in1=ytmp[:, g * 8:(g + 1) * 8, :].rearrange("p h d -> p (h d)"), op=ALU.add),
                    r=[YD.res, ytmp.res], w=[y_c.res])
            if fwd:
                k.dma("sp", y1_d[l0:l0 + C, :], y_c[:], r=[y_c.res], w=[R_y1[c]], dres=y_c.res)
            else:
                k.ins("pool", lambda: nc.gpsimd.tensor_tensor(out=y_c[:], in0=y_c[:], in1=y1_c[:], op=ALU.add),
                      r=[y_c.res, y1_c.res], w=[y_c.res])
                k.dma("sp", y_d[l0:l0 + C, :], y_c[:], r=[y_c.res], w=[R_y], dres=y_c.res)
        if red:
            k.dma("sp", st2_d[:, 1024:3072], hT32[:].rearrange("p h d -> p (h d)"), r=[hT32.res], w=[R_st2], dres=hT32.res)
        cx.pop()

    with nc.allow_non_contiguous_dma("scan chunk relayout"):
        if "R" in phases:
            gdn_pass("red")
            ssd_pass("red")
        if "B1" in phases:
            gdn_pass("own1")
        if "S1" in phases:
            ssd_pass("own1")
        if "B2" in phases:
            gdn_pass("own2")
        if "S2" in phases:
            ssd_pass("own2")


    def silu_parts(src_ap, shape, ring_e, ring_z, src_res):
        ez = ring_e.next()
        zc = ring_z.next()
        k.ins("act", lambda: nc.scalar.activation(out=ez[:], in_=src_ap, func=AF.Exp, scale=-1.0), r=[src_res], w=[ez.res])
        k.ins("act", lambda: nc.scalar.copy(out=zc[:], in_=src_ap), r=[src_res], w=[zc.res])
        k.ins("dve", lambda: nc.vector.tensor_scalar(out=ez[:], in0=ez[:], scalar1=1.0, scalar2=None, op0=ALU.add),
              r=[ez.res], w=[ez.res])
        return zc, ez

    if "C1" in phases:
        cx.push()
        T1 = 128
        wZ = cx.sb("wZ", [128, 8, 3072], BF16)
        wbg = cx.sb("wbg", [128, 8, 1024], BF16)
        wbs = cx.sb("wbs", [128, 16, 1024], BF16)
        wo = cx.sb("wo", [128, 8, 1024], BF16)
        for kt in range(8):
            k.dma("pool", wZ[:, kt, :], w_in[kt * 128:(kt + 1) * 128, O_ZG:O_ZG + 3072], w=[wZ.res], dres=wZ.res)
            k.dma("pool", wbg[:, kt, :], wbg_d[kt * 128:(kt + 1) * 128, :], w=[wbg.res], dres=wbg.res)
            k.dma("pool", wo[:, kt, :], wo_d[kt * 128:(kt + 1) * 128, :], w=[wo.res], dres=wo.res)
        for kt in range(16):
            k.dma("pool", wbs[:, kt, :], wbs_d[kt * 128:(kt + 1) * 128, :], w=[wbs.res], dres=wbs.res)
        gnw = cx.sb("gnw", [128, 1], F32)
        k.dma("sp", gnw[:], gnw_d, w=[gnw.res], dres=gnw.res)
        dskB = cx.sb("dskB", [128, 2048], F32)
        snwB = cx.sb("snwB", [128, 2048], F32)
        g1B = cx.sb("g1B", [128, 1024], F32)
        k.dma("sp", dskB[:], dskip_d[0, :].partition_broadcast(128), w=[dskB.res], dres=dskB.res)
        k.dma("sp", snwB[:], snw_d[0, :].partition_broadcast(128), w=[snwB.res], dres=snwB.res)
        k.dma("sp", g1B[:], mod_d[0, 2048:3072].partition_broadcast(128), r=[R_mod], w=[g1B.res], dres=g1B.res)
        neg1c = cx.sb("neg1c", [128, 512], F32)
        k.ins("pool", lambda: nc.gpsimd.memset(neg1c[:], -1.0), w=[neg1c.res])
        banks = Ring([cx.psum(f"c1b{i}", [128, 512], F32) for i in range(6)])
        ptr = Ring([cx.psum(f"c1t{i}", [128, 1024], BF16) for i in range(2)])
        aTr = cx.sbring("c_aT", 2, [128, 8, T1], BF16)
        oTr_ = cx.sbring("c_oT", 1, [128, 8, T1], F32)
        yr_ = cx.sbring("c_y", 1, [128, 2048], F32)
        xsr_ = cx.sbring("c_xs", 1, [128, 2048], BF16)
        sgr_ = cx.sbring("c_sg", 1, [128, 16, T1], F32)
        xr_ = cx.sbring("c_x", 1, [128, 1024], F32)
        onTr = cx.sbring("c_on", 1, [128, 8, T1], BF16)
        ynTr = cx.sbring("c_yn", 1, [128, 16, T1], BF16)
        mTr = cx.sbring("c_mT", 1, [128, 8, T1], BF16)
        h1r = cx.sbring("c_h1", 1, [128, 1024], F32)
        e1r = cx.sbring("c_e1", 2, [128, T1], F32)
        z1r = cx.sbring("c_z1", 2, [128, T1], F32)
        t1r = cx.sbring("c_t1", 3, [128, T1], F32)
        e5r = cx.sbring("c_e5", 2, [128, 512], F32)
        z5r = cx.sbring("c_z5", 2, [128, 512], F32)
        t5r = cx.sbring("c_t5", 3, [128, 512], F32)
        ynr = cx.sbring("c_ynb", 2, [128, 512], BF16)
        ssr_ = cx.sbring("c_ss", 4, [128, 2], F32)
        for ti in range(NLAT // T1):
            l0 = ti * T1
            aT = aTr.next(); oT = oTr_.next(); yt = yr_.next(); xs_ = xsr_.next(); sg = sgr_.next(); xt = xr_.next()
            k.dma("sp", aT[:], aT_d[:, :, l0:l0 + T1].rearrange("kt p t -> p kt t"), r=[R_scr], w=[aT.res], dres=aT.res)
            k.dma("sp", oT[:], oT_d[:, :, l0:l0 + T1].rearrange("h p t -> p h t"), r=[R_oT], w=[oT.res], dres=oT.res)
            k.dma("sp", yt[:], y_d[l0:l0 + T1, :], r=[R_y], w=[yt.res], dres=yt.res)
            k.dma("sp", xs_[:], x_d[NCTX + l0:NCTX + l0 + T1, :], r=[R_scr], w=[xs_.res], dres=xs_.res)
            k.dma("sp", sg[:], sgT_d[:, :, l0:l0 + T1].rearrange("c p t -> p c t"), r=[R_scr], w=[sg.res], dres=sg.res)
            k.dma("sp", xt[:], xin[NCTX + l0:NCTX + l0 + T1, :], w=[xt.res], dres=xt.res)
            onT = onTr.next()
            for h in range(8):
                pz = banks.next()
                k.group("pe", [(lambda kt=kt: nc.tensor.matmul(pz[:, 0:T1], wZ[:, kt, h * 128:(h + 1) * 128], aT[:, kt, :],
                                                               start=(kt == 0), stop=(kt == 7))) for kt in range(8)],
                        r=[wZ.res, aT.res], w=[pz.res])
                zc, ez = silu_parts(pz[:, 0:T1], None, e1r, z1r, pz.res)
                k.ins("pool", lambda: nc.gpsimd.tensor_tensor(out=ez[:], in0=ez[:], in1=neg1c[:, 0:T1], op=ALU.pow),
                      r=[ez.res, neg1c.res], w=[ez.res])
                k.ins("pool", lambda: nc.gpsimd.tensor_tensor(out=zc[:], in0=zc[:], in1=ez[:], op=ALU.mult),
                      r=[zc.res, ez.res], w=[zc.res])
                o2 = t1r.next()
                k.ins("pool", lambda: nc.gpsimd.tensor_tensor(out=o2[:], in0=oT[:, h, :], in1=oT[:, h, :], op=ALU.mult),
                      r=[oT.res], w=[o2.res])
                pn = banks.next()
                k.ins("pe", lambda: nc.tensor.matmul(pn[:, 0:T1], onesf[:], o2[:], start=True, stop=True),
                      r=[onesf.res, o2.res], w=[pn.res])
                rs = t1r.next()
                k.ins("act", lambda: nc.scalar.activation(out=rs[:], in_=pn[:, 0:T1], func=AF.Ln, scale=1.0 / 128, bias=epsc[:, 0:1]),
                      r=[pn.res, epsc.res], w=[rs.res])
                k.ins("act", lambda: nc.scalar.activation(out=rs[:], in_=rs[:], func=AF.Exp, scale=-0.5), r=[rs.res], w=[rs.res])
                k.ins("dve", lambda: nc.vector.tensor_tensor(out=rs[:], in0=rs[:], in1=oT[:, h, :], op=ALU.mult),
                      r=[rs.res, oT.res], w=[rs.res])
                k.ins("dve", lambda: nc.vector.scalar_tensor_tensor(out=onT[:, h, :], in0=rs[:], scalar=gnw[:, 0:1], in1=zc[:],
                                                                    op0=ALU.mult, op1=ALU.mult),
                      r=[rs.res, gnw.res, zc.res], w=[onT.res])
            ynT = ynTr.next()
            for cg in range(4):
                csl = slice(cg * 512, (cg + 1) * 512)
                pz = banks.next()
                k.group("pe", [(lambda kt=kt: nc.tensor.matmul(pz[:], aT[:, kt, :], wZ[:, kt, 1024 + cg * 512:1024 + (cg + 1) * 512],
                                                               start=(kt == 0), stop=(kt == 7))) for kt in range(8)],
                        r=[wZ.res, aT.res], w=[pz.res])
                zc, ez = silu_parts(pz[:], None, e5r, z5r, pz.res)
                k.ins("pool", lambda: nc.gpsimd.tensor_tensor(out=ez[:], in0=ez[:], in1=neg1c[:], op=ALU.pow),
                      r=[ez.res, neg1c.res], w=[ez.res])
                k.ins("pool", lambda: nc.gpsimd.tensor_tensor(out=zc[:], in0=zc[:], in1=ez[:], op=ALU.mult),
                      r=[zc.res, ez.res], w=[zc.res])
                yy = t5r.next()
                k.ins("pool", lambda: nc.gpsimd.tensor_tensor(out=yy[:], in0=xs_[:, csl], in1=dskB[:, csl], op=ALU.mult),
                      r=[xs_.res, dskB.res], w=[yy.res])
                k.ins("dve", lambda: nc.vector.tensor_tensor(out=yy[:], in0=yy[:], in1=yt[:, csl], op=ALU.add),
                      r=[yy.res, yt.res], w=[yy.res])
                k.ins("dve", lambda: nc.vector.tensor_tensor(out=yy[:], in0=yy[:], in1=zc[:], op=ALU.mult),
                      r=[yy.res, zc.res], w=[yy.res])
                ss = ssr_.next()
                jk = t5r.next()
                k.ins("act", lambda: nc.scalar.activation(out=jk[:], in_=yy[:], func=AF.Square, accum_out=ss[:, 0:1]),
                      r=[yy.res], w=[jk.res, ss.res])
                k.ins("act", lambda: nc.scalar.activation(out=ss[:, 1:2], in_=ss[:, 0:1], func=AF.Ln, scale=1.0 / 512, bias=epsc[:, 0:1]),
                      r=[ss.res, epsc.res], w=[ss.res])
                k.ins("act", lambda: nc.scalar.activation(out=ss[:, 1:2], in_=ss[:, 1:2], func=AF.Exp, scale=-0.5),
                      r=[ss.res], w=[ss.res])
                ynb = ynr.next()
                k.ins("dve", lambda: nc.vector.scalar_tensor_tensor(out=ynb[:], in0=yy[:], scalar=ss[:, 1:2], in1=snwB[:, csl],
                                                                    op0=ALU.mult, op1=ALU.mult),
                      r=[yy.res, ss.res, snwB.res], w=[ynb.res])
                pt = ptr.next()
                k.group("pe", [(lambda j=j: nc.tensor.transpose(pt[:, j * 128:(j + 1) * 128], ynb[:, j * 128:(j + 1) * 128], identb[:]))
                               for j in range(4)], r=[ynb.res, identb.res], w=[pt.res])
                k.ins("act", lambda: nc.scalar.copy(out=ynT[:, cg * 4:(cg + 1) * 4, :],
                                                    in_=pt[:, 0:512].rearrange("p (j c) -> p j c", c=128)),
                      r=[pt.res], w=[ynT.res])
            mT = mTr.next()
            for oc in range(8):
                pg = banks.next()
                k.group("pe", [(lambda h=h: nc.tensor.matmul(pg[:, 0:T1], wbg[:, h, oc * 128:(oc + 1) * 128], onT[:, h, :],
                                                             start=(h == 0), stop=(h == 7))) for h in range(8)],
                        r=[wbg.res, onT.res], w=[pg.res])
                pp = banks.next()
                k.group("pe", [(lambda ct=ct: nc.tensor.matmul(pp[:, 0:T1], wbs[:, ct, oc * 128:(oc + 1) * 128], ynT[:, ct, :],
                                                               start=(ct == 0), stop=(ct == 15))) for ct in range(16)],
                        r=[wbs.res, ynT.res], w=[pp.res])
                m1 = t1r.next()
                k.ins("dve", lambda: nc.vector.tensor_tensor(out=m1[:], in0=pg[:, 0:T1], in1=sg[:, oc, :], op=ALU.mult),
                      r=[pg.res, sg.res], w=[m1.res])
                m2 = t1r.next()
                k.ins("dve", lambda: nc.vector.tensor_tensor(out=m2[:], in0=pp[:, 0:T1], in1=sg[:, 8 + oc, :], op=ALU.mult),
                      r=[pp.res, sg.res], w=[m2.res])
                k.ins("pool", lambda: nc.gpsimd.tensor_tensor(out=mT[:, oc, :], in0=m1[:], in1=m2[:], op=ALU.add),
                      r=[m1.res, m2.res], w=[mT.res])
            h1 = h1r.next()
            for cg in range(2):
                csl = slice(cg * 512, (cg + 1) * 512)
                pm = banks.next()
                k.group("pe", [(lambda kt=kt: nc.tensor.matmul(pm[:], mT[:, kt, :], wo[:, kt, csl],
                                                               start=(kt == 0), stop=(kt == 7))) for kt in range(8)],
                        r=[mT.res, wo.res], w=[pm.res])
                k.ins("dve", lambda: nc.vector.tensor_tensor(out=h1[:, csl], in0=pm[:], in1=g1B[:, csl], op=ALU.mult),
                      r=[pm.res, g1B.res], w=[h1.res])
                k.ins("pool", lambda: nc.gpsimd.tensor_tensor(out=h1[:, csl], in0=h1[:, csl], in1=xt[:, csl], op=ALU.add),
                      r=[h1.res, xt.res], w=[h1.res])
            k.dma("sp", h1_d[l0:l0 + T1, :], h1[:], r=[h1.res], w=[R_h1], dres=h1.res)
        cx.pop()

    if "C2" in phases:
        cx.push()
        T2 = 256
        NF = DFF // 128
        wfi = cx.sb("wfi", [128, 8, 2 * DFF], BF16)
        wfo = cx.sb("wfo", [128, NF, 1024], BF16)
        for kt in range(8):
            for c0 in range(0, 2 * DFF, 2816):
                k.dma("pool", wfi[:, kt, c0:c0 + 2816], wfi_d[kt * 128:(kt + 1) * 128, c0:c0 + 2816], w=[wfi.res], dres=wfi.res)
        for ft in range(NF):
            k.dma("pool", wfo[:, ft, :], wfo_d[ft * 128:(ft + 1) * 128, :], w=[wfo.res], dres=wfo.res)
        g2B = cx.sb("g2B", [128, 1024], F32)
        nfB = cx.sb("nfB", [128, 1024], F32)
        k.dma("sp", g2B[:], mod_d[0, 5120:6144].partition_broadcast(128), r=[R_mod], w=[g2B.res], dres=g2B.res)
        k.dma("sp", nfB[:], nfw_d[0, :].partition_broadcast(128), w=[nfB.res], dres=nfB.res)
        n2 = cx.sb("n2", [128, 8], F32)
        k.dma("sp", n2[:], n2w_d, w=[n2.res], dres=n2.res)
        S2 = cx.sb("S2", [128, 8], F32)
        k.ins("dve", lambda: nc.vector.scalar_tensor_tensor(out=S2[:], in0=modf[:, 0, 4, :], scalar=1.0, in1=n2[:],
                                                            op0=ALU.add, op1=ALU.mult), r=[modf.res, n2.res], w=[S2.res])
        neg1d = cx.sb("neg1d", [128, T2], F32)
        k.ins("pool", lambda: nc.gpsimd.memset(neg1d[:], -1.0), w=[neg1d.res])
        banks = Ring([cx.psum(f"c2b{i}", [128, 512], F32) for i in range(8)])
        h1r = cx.sbring("d_h1", 2, [128, 2, 1024], F32)
        xnr = cx.sbring("d_xn", 1, [128, 1024], F32)
        fTr = cx.sbring("d_fT", 2, [128, 8, T2], BF16)
        actr = cx.sbring("d_act", 1, [128, NF, T2], BF16)
        outr = cx.sbring("d_out", 1, [128, 2, 1024], F32)
        ssr2 = cx.sbring("d_ss", 4, [128, 2], F32)
        junk2 = cx.sb("junk2", [128, 1024], BF16)
        e2r = cx.sbring("d_e", 3, [128, T2], F32)
        a2r = cx.sbring("d_a", 3, [128, T2], F32)
        for ti in range(NLAT // T2):
            l0 = ti * T2
            h1 = h1r.next()
            k.dma("sp", h1[:], h1_d[l0:l0 + T2, :].rearrange("(j p) d -> p j d", p=128), r=[R_h1], w=[h1.res], dres=h1.res)
            ss = ssr2.next()
            for j in range(2):
                k.ins("act", lambda j=j: nc.scalar.activation(out=junk2[:], in_=h1[:, j, :], func=AF.Square, accum_out=ss[:, j:j + 1]),
                      r=[h1.res], w=[junk2.res, ss.res])
            k.ins("act", lambda: nc.scalar.activation(out=ss[:], in_=ss[:], func=AF.Ln, scale=1.0 / D, bias=epsc[:, 0:1]),
                  r=[ss.res, epsc.res], w=[ss.res])
            k.ins("act", lambda: nc.scalar.activation(out=ss[:], in_=ss[:], func=AF.Exp, scale=-0.5), r=[ss.res], w=[ss.res])
            fT = fTr.next()
            for j in range(2):
                xn = xnr.next()
                k.ins("act", lambda j=j, xn=xn: nc.scalar.activation(out=xn[:], in_=h1[:, j, :], func=AF.Identity, scale=ss[:, j:j + 1]),
                      r=[h1.res, ss.res], w=[xn.res])
                for half in range(2):
                    pb = banks.next()
                    k.group("pe", [(lambda q=q, xn=xn, pb=pb: nc.tensor.transpose(
                        pb[:, q * 128:(q + 1) * 128], xn[:, (half * 4 + q) * 128:(half * 4 + q + 1) * 128], ident[:]))
                        for q in range(4)], r=[xn.res, ident.res], w=[pb.res])
                    for q in range(4):
                        kt = half * 4 + q
                        k.ins("act", lambda q=q, kt=kt, pb=pb, j=j: nc.scalar.activation(
                            out=fT[:, kt, j * 128:(j + 1) * 128], in_=pb[:, q * 128:(q + 1) * 128], func=AF.Identity,
                            scale=S2[:, kt:kt + 1], bias=modf[:, 0, 3, kt:kt + 1]),
                            r=[pb.res, S2.res, modf.res], w=[fT.res])
            act = actr.next()
            for ft in range(NF):
                pgt = banks.next()
                k.group("pe", [(lambda kt=kt: nc.tensor.matmul(pgt[:, 0:T2], wfi[:, kt, ft * 128:(ft + 1) * 128], fT[:, kt, :],
                                                               start=(kt == 0), stop=(kt == 7))) for kt in range(8)],
                        r=[wfi.res, fT.res], w=[pgt.res])
                pup = banks.next()
                k.group("pe", [(lambda kt=kt: nc.tensor.matmul(pup[:, 0:T2], wfi[:, kt, DFF + ft * 128:DFF + (ft + 1) * 128], fT[:, kt, :],
                                                               start=(kt == 0), stop=(kt == 7))) for kt in range(8)],
                        r=[wfi.res, fT.res], w=[pup.res])
                ez = e2r.next()
                k.ins("act", lambda: nc.scalar.activation(out=ez[:], in_=pgt[:, 0:T2], func=AF.Exp, scale=-1.0), r=[pgt.res], w=[ez.res])
                k.ins("dve", lambda: nc.vector.tensor_scalar(out=ez[:], in0=ez[:], scalar1=1.0, scalar2=None, op0=ALU.add),
                      r=[ez.res], w=[ez.res])
                k.ins("pool", lambda: nc.gpsimd.tensor_tensor(out=ez[:], in0=ez[:], in1=neg1d[:], op=ALU.pow),
                      r=[ez.res, neg1d.res], w=[ez.res])
                a1 = a2r.next()
                k.ins("dve", lambda: nc.vector.tensor_tensor(out=a1[:], in0=pgt[:, 0:T2], in1=ez[:], op=ALU.mult),
                      r=[pgt.res, ez.res], w=[a1.res])
                k.ins("dve", lambda: nc.vector.tensor_tensor(out=act[:, ft, :], in0=pup[:, 0:T2], in1=a1[:], op=ALU.mult),
                      r=[pup.res, a1.res], w=[act.res])
            ot = outr.next()
            ss2 = ssr2.next()
            for j in range(2):
                for cg in range(2):
                    csl = slice(cg * 512, (cg + 1) * 512)
                    pf = banks.next()
                    k.group("pe", [(lambda ft=ft: nc.tensor.matmul(pf[:], act[:, ft, j * 128:(j + 1) * 128], wfo[:, ft, csl],
                                                                   start=(ft == 0), stop=(ft == NF - 1))) for ft in range(NF)],
                            r=[act.res, wfo.res], w=[pf.res])
                    k.ins("dve", lambda: nc.vector.tensor_tensor(out=ot[:, j, csl], in0=pf[:], in1=g2B[:, csl], op=ALU.mult),
                          r=[pf.res, g2B.res], w=[ot.res])
                    k.ins("pool", lambda: nc.gpsimd.tensor_tensor(out=ot[:, j, csl], in0=ot[:, j, csl], in1=h1[:, j, csl], op=ALU.add),
                          r=[ot.res, h1.res], w=[ot.res])
                k.ins("act", lambda j=j: nc.scalar.activation(out=junk2[:], in_=ot[:, j, :], func=AF.Square, accum_out=ss2[:, j:j + 1]),
                      r=[ot.res], w=[junk2.res, ss2.res])
            k.ins("act", lambda: nc.scalar.activation(out=ss2[:], in_=ss2[:], func=AF.Ln, scale=1.0 / D, bias=epsc[:, 0:1]),
                  r=[ss2.res, epsc.res], w=[ss2.res])
            k.ins("act", lambda: nc.scalar.activation(out=ss2[:], in_=ss2[:], func=AF.Exp, scale=-0.5), r=[ss2.res], w=[ss2.res])
            for j in range(2):
                k.ins("dve", lambda j=j: nc.vector.scalar_tensor_tensor(out=ot[:, j, :], in0=ot[:, j, :], scalar=ss2[:, j:j + 1],
                                                                        in1=nfB[:], op0=ALU.mult, op1=ALU.mult),
                      r=[ot.res, ss2.res, nfB.res], w=[ot.res])
            k.dma("sp", out_d[l0:l0 + T2, :].rearrange("(j p) d -> p j d", p=128), ot[:], r=[ot.res], w=[R_out], dres=ot.res)
        cx.pop()

    k.wait_all("sp", [R_scr, R_mod, R_oT, R_st1, R_st2, R_y, R_h1, R_out] + R_o1 + R_y1)
    return nc


def _consts():
    p = np.arange(128)[:, None]
    f = np.arange(128)[None, :]
    U = (p <= f).astype(np.float32)
    Lo = (p >= f).astype(np.float32)
    tri = np.stack([U, Lo, -U, -Lo], axis=1)
    NEG = -30000.0
    m = lambda ok: np.where(ok, 0.0, NEG).astype(np.float32)
    msk = np.stack([m(p > f), m(p < f), m(p >= f), m(p <= f)], axis=1)
    return np.ascontiguousarray(tri), np.ascontiguousarray(msk)


TRI, MSK = _consts()


def host_prepare(inputs, core):
    b, s = core // 2, core % 2
    x = inputs["x"][b, s * NLAT:(s + 1) * NLAT]
    ctx = inputs["ctx"][b]
    if s == 1:
        x = x[::-1]
        ctx = ctx[::-1]
    xin = np.ascontiguousarray(np.concatenate([ctx, x], axis=0))
    x2 = inputs["x"][b, (1 - s) * NLAT:(2 - s) * NLAT]
    ctx2 = inputs["ctx"][b]
    if s == 0:
        x2 = x2[::-1]
        ctx2 = ctx2[::-1]
    xin2 = np.ascontiguousarray(np.concatenate([ctx2, x2], axis=0))
    cv = np.stack([inputs["c"][b], inputs["c_ctx"]], axis=-1)
    cvec = np.ascontiguousarray(cv.reshape(8, 128, 2).transpose(1, 0, 2))
    d1, d2 = (0, 1) if s == 0 else (1, 0)
    w = inputs["w_in"][0]
    offs = np.cumsum([0, 3072, 1024, 16, 16, 2048, 3072, 64, 2048])
    qkv = w[:, offs[0]:offs[1]]
    zg = w[:, offs[1]:offs[2]]
    a_ = w[:, offs[2]:offs[3]].reshape(D, 2, 8)
    b_ = w[:, offs[3]:offs[4]].reshape(D, 2, 8)
    zs = w[:, offs[4]:offs[5]]
    xbc = w[:, offs[5]:offs[6]]
    dt_ = w[:, offs[6]:offs[7]].reshape(D, 2, 32)
    gate = w[:, offs[7]:offs[8]]
    z16 = np.zeros((D, 16), np.float32)
    small = np.concatenate([a_[:, d1], a_[:, d2], z16, b_[:, d1], b_[:, d2], z16, dt_[:, d1], dt_[:, d2]], axis=1)
    w_perm = np.ascontiguousarray(np.concatenate([qkv, xbc, gate, small, zg, zs], axis=1))
    assert w_perm.shape[1] == W_IN_COLS
    cw = np.concatenate([inputs["gdn_conv_w"][0], inputs["ssm_conv_w"][0]], axis=1)
    cbias = np.concatenate([inputs["gdn_conv_b"][0], inputs["ssm_conv_b"][0]], axis=0)
    mk = lambda w_: np.ascontiguousarray(np.concatenate([w_, cbias[None]], axis=0).reshape(4, 48, 128).transpose(2, 1, 0))
    convp = mk(cw[::-1] if s == 1 else cw)
    convp2 = mk(cw[::-1] if s == 0 else cw)
    smallp = np.zeros((128, 4), np.float32)
    smallp[:, 0] = 1.0
    smallp[32:64, 0] = -1.0
    gb = inputs["gdn_dt_bias"][0]
    sbias = inputs["ssm_dt_bias"][0]
    smallp[0:8, 1] = gb[d1]; smallp[8:16, 1] = gb[d2]
    smallp[64:96, 1] = sbias[d1]; smallp[96:128, 1] = sbias[d2]
    ga = inputs["gdn_a_log"][0]
    smallp[0:8, 2] = ga[d1]; smallp[8:16, 2] = ga[d2]
    smallp[:, 3] = -1.0
    smallp[64:128, 3] = 1.0
    n1w = np.ascontiguousarray(inputs["norm1_w"][0].reshape(8, 128).T)
    return {
        "xin": xin, "xin2": xin2, "convp2": convp2, "cvec": cvec, "ada_w": np.ascontiguousarray(inputs["ada_w"][0]),
        "ada_b": np.ascontiguousarray(inputs["ada_b"][0][None]), "w_in": w_perm, "convp": convp,
        "smallp": smallp, "n1w": n1w, "ident": np.eye(128, dtype=np.float32),
        "tri": TRI, "msk": MSK,
        "w_brg": np.ascontiguousarray(inputs["w_br_gdn"][0]), "w_brs": np.ascontiguousarray(inputs["w_br_ssm"][0]),
        "w_o": np.ascontiguousarray(inputs["w_out"][0]), "w_fi": np.ascontiguousarray(inputs["w_ffn_in"][0]),
        "w_fo": np.ascontiguousarray(inputs["w_ffn_out"][0]),
        "gnw": np.ascontiguousarray(inputs["gdn_norm_w"][0].reshape(128, 1)),
        "dskip": np.ascontiguousarray(np.repeat(inputs["ssm_d"][0], 64)[None]),
        "snw": np.ascontiguousarray(inputs["ssm_norm_w"][0][None]),
        "n2w": np.ascontiguousarray(inputs["norm2_w"][0].reshape(8, 128).T),
        "nfw": np.ascontiguousarray(inputs["norm_f_w"][None]),
        "salog": np.ascontiguousarray(np.broadcast_to(inputs["ssm_a_log"][0][[d1, d2]][None], (128, 2, 32))),
    }


def kernel(**inputs):
    inputs = {k_: np.asarray(v) for k_, v in inputs.items()}
    nc = build_program()
    in_maps = [host_prepare(inputs, c) for c in range(8)]
    res = run_bass_kernel_spmd(nc, in_maps, core_ids=list(range(8)))
    out = np.zeros((4, 8192, D), np.float32)
    for c in range(8):
        b, s = c // 2, c % 2
        o = res.results[c]["out"]
        if s == 1:
            o = o[::-1]
        out[b, s * NLAT:(s + 1) * NLAT] = o
    return out
```

```python
import numpy as np
import ml_dtypes
from contextlib import ExitStack
import concourse.bass as bass
import concourse.mybir as mybir
from concourse.bass_utils import run_bass_kernel_spmd

F32 = mybir.dt.float32
BF16 = mybir.dt.bfloat16
AF = mybir.ActivationFunctionType
ALU = mybir.AluOpType
AX = mybir.AxisListType

D = 1024
NLAT = 4096
NCTX = 256
NTOK = NLAT + NCTX
C = 128
NCH = NTOK // C
EPS = 1e-6
DFF = 2816
O_QKV, O_XBC, O_GATE, O_SMALL, O_ZG, O_ZS = 0, 3072, 6144, 8192, 8320, 9344
W_IN_COLS = 11392
SEM_LIMIT = 30000
GDN_LIMIT = 0


class Sem:
    __slots__ = ("h", "count", "dma", "name")

    def __init__(self, h, dma, name):
        self.h, self.count, self.dma, self.name = h, 0, dma, name


class Res:
    __slots__ = ("name", "w", "r", "dsem")

    def __init__(self, name):
        self.name, self.w, self.r, self.dsem = name, None, {}, None


class Eng:
    def __init__(self, name, e, same_wait):
        self.name, self.e, self.sem, self.waited, self.same_wait = name, e, None, {}, same_wait
        self.nsem = 0


class K:
    def __init__(self, nc):
        self.nc = nc
        self.eng = {
            "pe": Eng("pe", nc.tensor, False),
            "act": Eng("act", nc.scalar, True),
            "dve": Eng("dve", nc.vector, True),
            "pool": Eng("pool", nc.gpsimd, True),
            "sp": Eng("sp", nc.sync, False),
        }
        self.nsem = 0
        self.ninst = 0
        self.all_sems = []
        self.free_dma = []

    def new_sem(self, dma, name):
        if dma and self.free_dma:
            self.free_dma.sort(key=lambda x: x.count)
            return self.free_dma.pop(0)
        self.nsem += 1
        h = self.nc.alloc_semaphore(f"s{self.nsem}_{name}")
        sm = Sem(h, dma, name)
        self.all_sems.append(sm)
        return sm

    def res(self, name):
        return Res(name)

    def _cur_sem(self, E):
        if E.sem is None or E.sem.count >= SEM_LIMIT:
            E.nsem += 1
            E.sem = self.new_sem(False, f"{E.name}{E.nsem}")
        return E.sem

    def _wait(self, E, evs):
        need = {}
        for (sem, val) in evs:
            if sem.dma:
                val = sem.count
            if need.get(sem, 0) < val:
                need[sem] = val
        for sem, val in need.items():
            if (not E.same_wait) and (sem is E.sem) and not sem.dma:
                continue
            if E.waited.get(sem, 0) >= val:
                continue
            E.e.wait_ge(sem.h, val)
            E.waited[sem] = val

    def _deps(self, r, w):
        evs = []
        for x in r:
            if x.w is not None:
                evs.append(x.w)
        for x in w:
            if x.w is not None:
                evs.append(x.w)
            evs.extend(x.r.items())
        return evs

    def _record(self, ev, r, w):
        sem, val = ev
        for x in r:
            if x.r.get(sem, 0) < val:
                x.r[sem] = val
        for x in w:
            x.w = ev
            x.r = {}

    def ins(self, en, fn, r=(), w=()):
        E = self.eng[en]
        self._wait(E, self._deps(r, w))
        inst = fn()
        sem = self._cur_sem(E)
        sem.count += 1
        inst.then_inc(sem.h, 1)
        self._record((sem, sem.count), r, w)
        self.ninst += 1
        return inst

    def group(self, en, fns, r=(), w=()):
        E = self.eng[en]
        self._wait(E, self._deps(r, w))
        inst = None
        for fn in fns:
            inst = fn()
        sem = self._cur_sem(E)
        sem.count += 1
        inst.then_inc(sem.h, 1)
        self._record((sem, sem.count), r, w)
        self.ninst += len(fns)
        return inst

    def dma(self, q, out, in_, r=(), w=(), dres=None):
        E = self.eng[q]
        self._wait(E, self._deps(r, w))
        if dres.dsem is None:
            dres.dsem = self.new_sem(True, "d" + dres.name)
        sem = dres.dsem
        inst = E.e.dma_start(out=out, in_=in_)
        sem.count += 16
        inst.then_inc(sem.h, 16)
        self._record((sem, sem.count), r, w)
        self.ninst += 1
        return inst

    def dmaop(self, q, fn, r=(), w=(), dres=None):
        E = self.eng[q]
        self._wait(E, self._deps(r, w))
        if dres.dsem is None:
            dres.dsem = self.new_sem(True, "d" + dres.name)
        sem = dres.dsem
        inst = fn()
        sem.count += 16
        inst.then_inc(sem.h, 16)
        self._record((sem, sem.count), r, w)
        self.ninst += 1
        return inst

    def barrier(self):
        sems = list(self.all_sems)
        for E in self.eng.values():
            for sem in sems:
                if sem.count == 0 or E.waited.get(sem, 0) >= sem.count:
                    continue
                if (sem is E.sem) and not E.same_wait:
                    continue
                E.e.wait_ge(sem.h, sem.count)
                E.waited[sem] = sem.count

    def wait_all(self, en, ress):
        E = self.eng[en]
        evs = []
        for x in ress:
            if x.w is not None:
                evs.append(x.w)
            evs.extend(x.r.items())
        self._wait(E, evs)


class Buf:
    def __init__(self, k, t, name):
        self.t, self.res, self.name = t, k.res(name), name

    def __getitem__(self, idx):
        return self.t[idx]


class Ring:
    def __init__(self, bufs):
        self.bufs, self.i = bufs, 0

    def next(self):
        b = self.bufs[self.i % len(self.bufs)]
        self.i += 1
        return b


class Ctx:
    def __init__(self, nc):
        self.nc = nc
        self.k = K(nc)
        self.stack = [ExitStack()]
        self.uid = 0
        self.phase_bufs = [[]]

    def push(self):
        self.stack.append(ExitStack())
        self.phase_bufs.append([])

    def pop(self):
        self.k.barrier()
        for b in self.phase_bufs.pop():
            if b.res.dsem is not None:
                self.k.free_dma.append(b.res.dsem)
                b.res.dsem = None
        self.stack.pop().close()

    def sb(self, name, shape, dtype):
        self.uid += 1
        t = self.stack[-1].enter_context(self.nc.sbuf_tensor(f"sb{self.uid}_{name}", list(shape), dtype))
        b = Buf(self.k, t, name)
        self.phase_bufs[-1].append(b)
        return b

    def psum(self, name, shape, dtype):
        self.uid += 1
        t = self.stack[-1].enter_context(self.nc.psum_tensor(f"ps{self.uid}_{name}", list(shape), dtype))
        return Buf(self.k, t, name)

    def sbring(self, name, n, shape, dtype):
        return Ring([self.sb(f"{name}{i}", shape, dtype) for i in range(n)])

    def dram(self, name, shape, dtype, kind="Internal"):
        t = self.nc.dram_tensor(name, list(shape), dtype, kind=kind)
        return t.ap()


def build_program(debug=False, phases=("0", "A", "R", "B1", "S1", "B2", "S2", "C1", "C2"), n_cores=8):
    nc = bass.Bass("TRN2", target_bir_lowering=False)
    cx = Ctx(nc)
    k = cx.k
    dk = "ExternalOutput" if debug else "Internal"

    xin = cx.dram("xin", [NTOK, D], F32, "ExternalInput")
    cvec = cx.dram("cvec", [128, 8, 2], F32, "ExternalInput")
    ada_w = cx.dram("ada_w", [D, 6 * D], F32, "ExternalInput")
    ada_b = cx.dram("ada_b", [1, 6 * D], F32, "ExternalInput")
    w_in = cx.dram("w_in", [D, W_IN_COLS], F32, "ExternalInput")
    xin2 = cx.dram("xin2", [NTOK, D], F32, "ExternalInput")
    convp2 = cx.dram("convp2", [128, 48, 4], F32, "ExternalInput")
    convp = cx.dram("convp", [128, 48, 4], F32, "ExternalInput")
    smallp = cx.dram("smallp", [128, 4], F32, "ExternalInput")
    n1w = cx.dram("n1w", [128, 8], F32, "ExternalInput")
    ident_d = cx.dram("ident", [128, 128], F32, "ExternalInput")
    wbg_d = cx.dram("w_brg", [1024, 1024], F32, "ExternalInput")
    wbs_d = cx.dram("w_brs", [2048, 1024], F32, "ExternalInput")
    wo_d = cx.dram("w_o", [1024, 1024], F32, "ExternalInput")
    wfi_d = cx.dram("w_fi", [1024, 2 * DFF], F32, "ExternalInput")
    wfo_d = cx.dram("w_fo", [DFF, 1024], F32, "ExternalInput")
    gnw_d = cx.dram("gnw", [128, 1], F32, "ExternalInput")
    dskip_d = cx.dram("dskip", [1, 2048], F32, "ExternalInput")
    snw_d = cx.dram("snw", [1, 2048], F32, "ExternalInput")
    n2w_d = cx.dram("n2w", [128, 8], F32, "ExternalInput")
    nfw_d = cx.dram("nfw", [1, 1024], F32, "ExternalInput")
    salog_d = cx.dram("salog", [128, 2, 32], F32, "ExternalInput")
    tri_d = cx.dram("tri", [128, 4, 128], F32, "ExternalInput")
    msk_d = cx.dram("msk", [128, 4, 128], F32, "ExternalInput")
    out_d = cx.dram("out", [NLAT, D], F32, "ExternalOutput")

    mod_d = cx.dram("mod_d", [2, 6 * D], F32, dk)
    qT_d = cx.dram("qT_d", [8, 128, NTOK], BF16, dk)
    kT_d = cx.dram("kT_d", [8, 128, NTOK], BF16, dk)
    k_d = cx.dram("k_d", [NTOK, 1024], BF16, dk)
    v_d = cx.dram("v_d", [NTOK, 1024], BF16, dk)
    x_d = cx.dram("x_d", [NTOK, 2048], BF16, dk)
    BT_d = cx.dram("BT_d", [4, 128, NTOK], BF16, dk)
    CT_d = cx.dram("CT_d", [4, 128, NTOK], BF16, dk)
    B_d = cx.dram("B_d", [NTOK, 512], BF16, dk)
    sm_d = cx.dram("sm_d", [NTOK, 128], F32, dk)
    kT2_d = cx.dram("kT2_d", [8, 128, NTOK], BF16)
    k2_d = cx.dram("k2_d", [NTOK, 1024], BF16)
    v2_d = cx.dram("v2_d", [NTOK, 1024], BF16)
    x2_d = cx.dram("x2_d", [NTOK, 2048], BF16)
    B2_d = cx.dram("B2_d", [NTOK, 512], BF16)
    sm2_d = cx.dram("sm2_d", [NTOK, 128], F32)
    sgT_d = cx.dram("sgT_d", [16, 128, NLAT], F32, dk)
    aT_d = cx.dram("aT_d", [8, 128, NLAT], BF16, dk)
    o1_d = cx.dram("o1_d", [8, 128, NLAT], F32, dk)
    oT_d = cx.dram("oT_d", [8, 128, NLAT], F32, dk)
    st1_d = cx.dram("st1_d", [128, 3072], F32)
    st2_d = cx.dram("st2_d", [128, 3072], F32)
    y1_d = cx.dram("y1_d", [NLAT, 2048], F32, dk)
    y_d = cx.dram("y_d", [NLAT, 2048], F32, dk)
    R_y1 = [k.res(f"y1_{c}") for c in range(NCH)]
    R_y = k.res("y")
    R_o1 = [k.res(f"o1_{c}") for c in range(NCH)]
    R_oT = k.res("oT")
    R_st1 = k.res("st1")
    R_st2 = k.res("st2")
    h1_d = cx.dram("h1_d", [NLAT, D], F32, dk)
    R_h1 = k.res("h1")
    R_out = k.res("out")
    R_scr = k.res("scratchA")
    R_mod = k.res("mod_d")

    ident = cx.sb("ident", [128, 128], F32)
    identb = cx.sb("identb", [128, 128], BF16)
    onesf = cx.sb("onesf", [128, 128], F32)
    k.dma("sp", ident[:], ident_d, w=[ident.res], dres=ident.res)
    k.ins("dve", lambda: nc.vector.tensor_copy(identb[:], ident[:]), r=[ident.res], w=[identb.res])
    k.ins("dve", lambda: nc.vector.memset(onesf[:], 1.0), w=[onesf.res])
    epsc = cx.sb("epsc", [128, 1], F32)
    k.ins("dve", lambda: nc.vector.memset(epsc[:], EPS), w=[epsc.res])


    if "0" in phases:
        cx.push()
        ps = [cx.psum(f"ps{i}", [128, 512], F32) for i in range(2)]
        cv = cx.sb("cv", [128, 8, 2], F32)
        cvs = cx.sb("cvs", [128, 8, 2], F32)
        k.dma("sp", cv[:], cvec, w=[cv.res], dres=cv.res)
        k.ins("act", lambda: nc.scalar.activation(out=cvs[:], in_=cv[:], func=AF.Exp, scale=-1.0), r=[cv.res], w=[cvs.res])
        k.ins("dve", lambda: nc.vector.tensor_scalar(out=cvs[:], in0=cvs[:], scalar1=1.0, scalar2=None, op0=ALU.add),
              r=[cvs.res], w=[cvs.res])
        k.ins("dve", lambda: nc.vector.reciprocal(out=cvs[:], in_=cvs[:]), r=[cvs.res], w=[cvs.res])
        k.ins("dve", lambda: nc.vector.tensor_tensor(out=cvs[:], in0=cvs[:], in1=cv[:], op=ALU.mult),
              r=[cvs.res, cv.res], w=[cvs.res])
        adab = cx.sb("adab", [2, 6 * D], F32)
        k.dma("sp", adab[0:1, :], ada_b, w=[adab.res], dres=adab.res)
        k.dma("sp", adab[1:2, :], ada_b, w=[adab.res], dres=adab.res)
        modsb = cx.sb("modsb", [2, 6 * D], F32)
        awring = cx.sbring("aw", 2, [128, 8, 512], F32)
        for cg in range(12):
            aw = awring.next()
            k.dma("sp", aw[:], ada_w[:, cg * 512:(cg + 1) * 512].rearrange("(kt p) n -> p kt n", p=128),
                  w=[aw.res], dres=aw.res)
            pb = ps[cg % 2]
            k.group("pe", [
                (lambda kt=kt, aw=aw, pb=pb: nc.tensor.matmul(pb[0:2, :], cvs[:, kt, :], aw[:, kt, :],
                                                               start=(kt == 0), stop=(kt == 7)))
                for kt in range(8)], r=[cvs.res, aw.res], w=[pb.res])
            k.ins("dve", lambda cg=cg, pb=pb: nc.vector.tensor_tensor(
                out=modsb[:, cg * 512:(cg + 1) * 512], in0=pb[0:2, :], in1=adab[:, cg * 512:(cg + 1) * 512],
                op=ALU.add), r=[pb.res, adab.res], w=[modsb.res])
        k.dma("sp", mod_d, modsb[:], r=[modsb.res], w=[R_mod], dres=modsb.res)
        cx.pop()

    modf = cx.sb("modf", [128, 2, 6, 8], F32)
    with nc.allow_non_contiguous_dma("small modulation vector relayout"):
        for r_ in range(2):
            k.dma("sp", modf[:, r_, :, :], mod_d[r_, :].rearrange("(j kt p) -> p j kt", p=128, kt=8),
                  r=[R_mod], w=[modf.res], dres=modf.res)
    n1 = cx.sb("n1", [128, 8], F32)
    k.dma("sp", n1[:], n1w, w=[n1.res], dres=n1.res)
    S1 = cx.sb("S1", [128, 2, 8], F32)
    for r_ in range(2):
        k.ins("dve", lambda r_=r_: nc.vector.scalar_tensor_tensor(
            out=S1[:, r_, :], in0=modf[:, r_, 1, :], scalar=1.0, in1=n1[:], op0=ALU.add, op1=ALU.mult),
            r=[modf.res, n1.res], w=[S1.res])

    if "A" in phases:
        cx.push()
        ps = [cx.psum(f"ps{i}", [128, 512], F32) for i in range(6)]
        NWC = O_ZG
        wA = cx.sb("wA", [128, 8, NWC], BF16)
        for kt in range(8):
            for c0 in range(0, NWC, 2080):
                k.dma("pool", wA[:, kt, c0:c0 + 2080], w_in[kt * 128:(kt + 1) * 128, c0:c0 + 2080],
                      w=[wA.res], dres=wA.res)
        cp = cx.sb("cp", [128, 48, 4], F32)
        k.dma("sp", cp[:], convp, w=[cp.res], dres=cp.res)
        smp = cx.sb("smp", [128, 4], F32)
        k.dma("sp", smp[:], smallp, w=[smp.res], dres=smp.res)
        smult = cx.sb("smult", [128, 1], F32)
        k.ins("act", lambda: nc.scalar.activation(out=smult[:], in_=smp[:, 2:3], func=AF.Exp),
              r=[smp.res], w=[smult.res])
        k.ins("dve", lambda: nc.vector.tensor_tensor(out=smult[:], in0=smult[:], in1=smp[:, 3:4], op=ALU.mult),
              r=[smp.res, smult.res], w=[smult.res])

        TT = 256
        xring = cx.sbring("xt", 2, [128, 2, D], F32)
        xnring = cx.sbring("xn", 2, [128, D], F32)
        junk = cx.sb("junk", [128, D], BF16)
        ssr = cx.sbring("ss", 2, [128, 2], F32)
        aTring = cx.sbring("aT", 2, [128, 8, TT], BF16)
        cring = cx.sbring("cv_", 3, [128, TT], F32)
        sring = cx.sbring("so_", 4, [128, TT], BF16)
        sqring = cx.sbring("sq_", 2, [128, TT], F32)
        rsring = cx.sbring("rs_", 2, [128, TT], F32)
        sgring = cx.sbring("sg_", 3, [128, TT], F32)
        ering = cx.sbring("ee_", 3, [128, TT], F32)
        neg1 = cx.sb("neg1", [128, TT], F32)
        k.ins("pool", lambda: nc.gpsimd.memset(neg1[:], -1.0), w=[neg1.res])
        smring = cx.sbring("smf", 2, [128, TT], F32)
        ktok = cx.sbring("ktok", 1, [128, 2, 1024], BF16)
        vtok = cx.sbring("vtok", 1, [128, 2, 1024], BF16)
        xtok = cx.sbring("xtok", 1, [128, 2, 2048], BF16)
        btok = cx.sbring("btok", 1, [128, 2, 512], BF16)
        smtok = cx.sbring("smtok", 2, [128, 2, 128], F32)
        pst = [ps[0], ps[1]]
        psa = [ps[2], ps[3]]
        pss = ps[4]
        pso = [cx.psum(f"pso{i}", [128, 1024], BF16) for i in range(2)]
        psm = ps[5]
        n_acc = 0
        n_tr = 0
        n_ot = 0

        own = dict(qT_d=qT_d, kT_d=kT_d, k_d=k_d, v_d=v_d, x_d=x_d, BT_d=BT_d, CT_d=CT_d, B_d=B_d, sm_d=sm_d)
        par = dict(qT_d=None, kT_d=kT2_d, k_d=k2_d, v_d=v2_d, x_d=x2_d, BT_d=None, CT_d=None, B_d=B2_d, sm_d=sm2_d)
        cp2 = cx.sb("cp2", [128, 48, 4], F32)
        k.dma("sp", cp2[:], convp2, w=[cp2.res], dres=cp2.res)
        runs = [(False, xin, cp, own)]
        if "R" in phases:
            runs.append((True, xin2, cp2, par))
        for red, xsrc, cpt, DD in runs:
            for ti in range(NTOK // TT):
                t0 = ti * TT
                is_ctx = ti == 0
                mr = 1 if is_ctx else 0
                rowlen = 256 if is_ctx else 64
                nrow = TT // rowlen
                l0 = t0 - NCTX
                xt = xring.next()
                k.dma("sp", xt[:], xsrc[t0:t0 + TT, :].rearrange("(j p) d -> p j d", p=128), w=[xt.res], dres=xt.res)
                ss = ssr.next()
                for j in range(2):
                    k.ins("act", lambda j=j, xt=xt, ss=ss: nc.scalar.activation(
                        out=junk[:], in_=xt[:, j, :], func=AF.Square, accum_out=ss[:, j:j + 1]),
                        r=[xt.res], w=[junk.res, ss.res])
                k.ins("act", lambda ss=ss: nc.scalar.activation(out=ss[:], in_=ss[:], func=AF.Ln, scale=1.0 / D, bias=epsc[:, 0:1]),
                      r=[ss.res, epsc.res], w=[ss.res])
                k.ins("act", lambda ss=ss: nc.scalar.activation(out=ss[:], in_=ss[:], func=AF.Exp, scale=-0.5),
                      r=[ss.res], w=[ss.res])
                aT = aTring.next()
                for j in range(2):
                    xn = xnring.next()
                    k.ins("act", lambda j=j, xt=xt, ss=ss, xn=xn: nc.scalar.activation(
                        out=xn[:], in_=xt[:, j, :], func=AF.Identity, scale=ss[:, j:j + 1]),
                        r=[xt.res, ss.res], w=[xn.res])
                    for half in range(2):
                        pb = pst[n_tr % 2]
                        n_tr += 1
                        k.group("pe", [
                            (lambda q=q, pb=pb, xn=xn, half=half: nc.tensor.transpose(
                                pb[:, q * 128:(q + 1) * 128], xn[:, (half * 4 + q) * 128:(half * 4 + q + 1) * 128],
                                ident[:]))
                            for q in range(4)], r=[xn.res, ident.res], w=[pb.res])
                        for q in range(4):
                            kt = half * 4 + q
                            k.ins("act", lambda q=q, kt=kt, pb=pb, aT=aT, j=j: nc.scalar.activation(
                                out=aT[:, kt, j * 128:(j + 1) * 128], in_=pb[:, q * 128:(q + 1) * 128],
                                func=AF.Identity, scale=S1[:, mr, kt:kt + 1], bias=modf[:, mr, 0, kt:kt + 1]),
                                r=[pb.res, S1.res, modf.res], w=[aT.res])
                if not is_ctx and not red:
                    k.dma("sp", aT_d[:, :, l0:l0 + TT].rearrange("kt p t -> p kt t"), aT[:],
                          r=[aT.res], w=[R_scr], dres=aT.res)

                kt_ = ktok.next(); vt_ = vtok.next(); xk_ = xtok.next(); bt_ = btok.next(); st_ = smtok.next()
                ncol_tiles = 65
                for ct in range(ncol_tiles):
                    if (is_ctx or red) and 48 <= ct < 64:
                        continue
                    if red and (ct < 8 or 44 <= ct < 48):
                        continue
                    pa = psa[n_acc % 2]
                    n_acc += 1
                    k.group("pe", [
                        (lambda kt=kt, pa=pa, ct=ct, aT=aT: nc.tensor.matmul(
                            pa[:, 0:TT], wA[:, kt, ct * 128:(ct + 1) * 128], aT[:, kt, :],
                            start=(kt == 0), stop=(kt == 7)))
                        for kt in range(8)], r=[wA.res, aT.res], w=[pa.res])
                    if ct < 48:
                        cb = cring.next()
                        k.ins("act", lambda cb=cb, pa=pa, ct=ct: nc.scalar.activation(
                            out=cb[:], in_=pa[:, 0:TT], func=AF.Identity, scale=cpt[:, ct, 1:2], bias=cpt[:, ct, 3:4]),
                            r=[pa.res, cpt.res], w=[cb.res])
                        pv = pa[:, 0:TT].rearrange("p (r t) -> p r t", t=rowlen)
                        cv3 = cb[:].rearrange("p (r t) -> p r t", t=rowlen)
                        k.ins("dve", lambda pv=pv, cv3=cv3, ct=ct: nc.vector.scalar_tensor_tensor(
                            out=cv3[:, :, 1:], in0=pv[:, :, 0:rowlen - 1], scalar=cpt[:, ct, 0:1], in1=cv3[:, :, 1:],
                            op0=ALU.mult, op1=ALU.add), r=[pa.res, cpt.res, cb.res], w=[cb.res])
                        k.ins("dve", lambda pv=pv, cv3=cv3, ct=ct: nc.vector.scalar_tensor_tensor(
                            out=cv3[:, :, 0:rowlen - 1], in0=pv[:, :, 1:], scalar=cpt[:, ct, 2:3],
                            in1=cv3[:, :, 0:rowlen - 1], op0=ALU.mult, op1=ALU.add),
                            r=[pa.res, cpt.res, cb.res], w=[cb.res])
                        so = sring.next()
                        ee = ering.next()
                        k.ins("act", lambda cb=cb, ee=ee: nc.scalar.activation(out=ee[:], in_=cb[:], func=AF.Exp, scale=-1.0),
                              r=[cb.res], w=[ee.res])
                        k.ins("dve", lambda ee=ee: nc.vector.tensor_scalar(out=ee[:], in0=ee[:], scalar1=1.0, scalar2=None,
                                                                        op0=ALU.add), r=[ee.res], w=[ee.res])
                        k.ins("dve", lambda ee=ee: nc.vector.reciprocal(out=ee[:], in_=ee[:]), r=[ee.res], w=[ee.res])
                        if ct < 16:
                            sf = sgring.next()
                            k.ins("pool", lambda cb=cb, sf=sf, ee=ee: nc.gpsimd.tensor_tensor(out=sf[:], in0=cb[:], in1=ee[:],
                                                                                              op=ALU.mult),
                                  r=[cb.res, ee.res], w=[sf.res])
                            sq = sqring.next()
                            k.ins("pool", lambda sq=sq, sf=sf: nc.gpsimd.tensor_tensor(out=sq[:], in0=sf[:], in1=sf[:],
                                                                                       op=ALU.mult),
                                  r=[sf.res], w=[sq.res])
                            k.ins("pe", lambda sq=sq: nc.tensor.matmul(pss[:, 0:TT], onesf[:], sq[:], start=True, stop=True),
                                  r=[onesf.res, sq.res], w=[pss.res])
                            rs = rsring.next()
                            k.ins("act", lambda rs=rs: nc.scalar.activation(out=rs[:], in_=pss[:, 0:TT], func=AF.Ln,
                                                                            bias=epsc[:, 0:1]),
                                  r=[pss.res, epsc.res], w=[rs.res])
                            k.ins("act", lambda rs=rs: nc.scalar.activation(out=rs[:], in_=rs[:], func=AF.Exp, scale=-0.5),
                                  r=[rs.res], w=[rs.res])
                            qscale = (128 ** -0.5) if ct < 8 else 1.0
                            k.ins("dve", lambda so=so, sf=sf, rs=rs, qscale=qscale: nc.vector.scalar_tensor_tensor(
                                out=so[:], in0=sf[:], scalar=qscale, in1=rs[:], op0=ALU.mult, op1=ALU.mult),
                                r=[sf.res, rs.res], w=[so.res])
                            dst = DD["qT_d"] if ct < 8 else DD["kT_d"]
                            k.dma("sp", dst[ct % 8, :, t0:t0 + TT], so[:], r=[so.res], w=[R_scr], dres=so.res)
                        else:
                            k.ins("pool", lambda cb=cb, so=so, ee=ee: nc.gpsimd.tensor_tensor(out=so[:], in0=cb[:], in1=ee[:],
                                                                                              op=ALU.mult),
                                  r=[cb.res, ee.res], w=[so.res])
                            if 40 <= ct < 44 and not red:
                                k.dma("sp", DD["BT_d"][ct - 40, :, t0:t0 + TT], so[:], r=[so.res], w=[R_scr], dres=so.res)
                            if 44 <= ct < 48:
                                k.dma("sp", DD["CT_d"][ct - 44, :, t0:t0 + TT], so[:], r=[so.res], w=[R_scr], dres=so.res)
                        tgt = None
                        if 8 <= ct < 16:
                            tgt = (kt_, (ct - 8) * 128)
                        elif 16 <= ct < 24:
                            tgt = (vt_, (ct - 16) * 128)
                        elif 24 <= ct < 40:
                            tgt = (xk_, (ct - 24) * 128)
                        elif 40 <= ct < 44:
                            tgt = (bt_, (ct - 40) * 128)
                        if tgt is not None:
                            po = pso[n_ot % 2]
                            n_ot += 1
                            pob = po
                            k.group("pe", [
                                (lambda j=j, so=so, pob=pob: nc.tensor.transpose(
                                    pob[:, j * 128:(j + 1) * 128], so[:, j * 128:(j + 1) * 128], identb[:]))
                                for j in range(2)], r=[so.res, identb.res], w=[po.res])
                            tb, off = tgt
                            k.ins("dve", lambda tb=tb, off=off, pob=pob: nc.vector.tensor_copy(
                                tb[:, :, off:off + 128], pob[:, 0:256].rearrange("p (j c) -> p j c", c=128)),
                                r=[po.res], w=[tb.res])
                    elif ct < 64:
                        sg = sgring.next()
                        k.ins("act", lambda sg=sg, pa=pa: nc.scalar.activation(out=sg[:], in_=pa[:, 0:TT], func=AF.Exp, scale=-1.0),
                              r=[pa.res], w=[sg.res])
                        k.ins("dve", lambda sg=sg: nc.vector.tensor_scalar(out=sg[:], in0=sg[:], scalar1=1.0, scalar2=None,
                                                                        op0=ALU.add), r=[sg.res], w=[sg.res])
                        k.ins("dve", lambda sg=sg: nc.vector.reciprocal(out=sg[:], in_=sg[:]), r=[sg.res], w=[sg.res])
                        k.dma("sp", sgT_d[ct - 48, :, l0:l0 + TT], sg[:], r=[sg.res], w=[R_scr], dres=sg.res)
                    else:
                        sm = smring.next()
                        k.ins("act", lambda sm=sm, pa=pa: nc.scalar.activation(
                            out=sm[:], in_=pa[:, 0:TT], func=AF.Exp, scale=smp[:, 0:1], bias=smp[:, 1:2]),
                            r=[pa.res, smp.res], w=[sm.res])
                        k.ins("act", lambda sm=sm: nc.scalar.activation(out=sm[:], in_=sm[:], func=AF.Ln, bias=1.0),
                              r=[sm.res], w=[sm.res])
                        k.ins("dve", lambda sm=sm: nc.vector.tensor_scalar(out=sm[:], in0=sm[:], scalar1=smult[:, 0:1],
                                                                        scalar2=None, op0=ALU.mult),
                              r=[sm.res, smult.res], w=[sm.res])
                        k.group("pe", [
                            (lambda j=j, sm=sm: nc.tensor.transpose(psm[:, j * 128:(j + 1) * 128],
                                                                    sm[:, j * 128:(j + 1) * 128], ident[:]))
                            for j in range(2)], r=[sm.res, ident.res], w=[psm.res])
                        k.ins("dve", lambda st_=st_: nc.vector.tensor_copy(
                            st_[:], psm[:, 0:256].rearrange("p (j c) -> p j c", c=128)), r=[psm.res], w=[st_.res])
                rows = lambda d_: d_[t0:t0 + TT, :].rearrange("(j p) c -> p j c", p=128)
                k.dma("sp", rows(DD["k_d"]), kt_[:], r=[kt_.res], w=[R_scr], dres=kt_.res)
                k.dma("sp", rows(DD["v_d"]), vt_[:], r=[vt_.res], w=[R_scr], dres=vt_.res)
                k.dma("sp", rows(DD["x_d"]), xk_[:], r=[xk_.res], w=[R_scr], dres=xk_.res)
                k.dma("sp", rows(DD["B_d"]), bt_[:], r=[bt_.res], w=[R_scr], dres=bt_.res)
                k.dma("sp", rows(DD["sm_d"]), st_[:], r=[st_.res], w=[R_scr], dres=st_.res)
        cx.pop()


    def load_scan_consts():
        tri = cx.sb("tri", [128, 4, 128], F32)
        k.dma("sp", tri[:], tri_d, w=[tri.res], dres=tri.res)
        mskf = cx.sb("mskf", [128, 4, 128], F32)
        k.dma("sp", mskf[:], msk_d, w=[mskf.res], dres=mskf.res)
        mskb = cx.sb("mskb", [128, 4, 128], BF16)
        k.ins("dve", lambda: nc.vector.tensor_copy(mskb[:], mskf[:]), r=[mskf.res], w=[mskb.res])
        id4 = cx.sb("id4", [128, 4, 128], F32)
        for j in range(4):
            k.ins("dve", lambda j=j: nc.vector.tensor_copy(id4[:, j, :], ident[:]), r=[ident.res], w=[id4.res])
        return tri, mskb, id4

    def gdn_pass(mode):
        cx.push()
        fwd = mode != "own2"
        pi = 0 if mode == "own1" else 1
        red = mode == "red"
        s_kT, s_k, s_v, s_sm = (kT2_d, k2_d, v2_d, sm2_d) if red else (kT_d, k_d, v_d, sm_d)
        tri, mskb, id4 = load_scan_consts()
        cumL = tri[:, 0, :] if fwd else tri[:, 1, :]
        negR = tri[:, 2, :] if fwd else tri[:, 3, :]
        m_s = mskb[:, 0, :] if fwd else mskb[:, 1, :]
        mT_s = mskb[:, 1, :] if fwd else mskb[:, 0, :]
        mT_i = mskb[:, 3, :] if fwd else mskb[:, 2, :]
        g0 = pi * 8
        l0c = 32 + pi * 8
        banks = Ring([cx.psum(f"gb{i}", [128, 512], F32) for i in range(8)])
        S32 = cx.sb("S32", [128, 8, 128], F32)
        Sbf = cx.sb("Sbf", [128, 8, 128], BF16)
        if fwd:
            k.ins("pool", lambda: nc.gpsimd.memset(S32[:], 0.0), w=[S32.res])
        else:
            k.dma("sp", S32[:].rearrange("p h d -> p (h d)"), st2_d[:, 0:1024], r=[R_st2], w=[S32.res], dres=S32.res)
        k.ins("act", lambda: nc.scalar.copy(out=Sbf[:], in_=S32[:]), r=[S32.res], w=[Sbf.res])
        NB = 2
        qTr = cx.sbring("qTc", NB, [128, 8, 128], BF16)
        kTr = cx.sbring("kTc", NB, [128, 8, 128], BF16)
        kr = cx.sbring("kc", NB, [128, 8, 128], BF16)
        vr = cx.sbring("vc", NB, [128, 8, 128], BF16)
        smr = cx.sbring("smc", NB, [128, 128], F32)
        o1r = cx.sbring("o1c", NB, [128, 8, 128], F32)
        kbgr = cx.sbring("kbg", NB, [128, 8, 128], BF16)
        vbr = cx.sbring("vb", NB, [128, 8, 128], BF16)
        kdr = cx.sbring("kd", NB, [128, 8, 128], BF16)
        smallr = cx.sbring("gsm", NB, [128, 6, 8], F32)
        eglr = cx.sbring("egl", NB, [128, 8], F32)
        expr = cx.sbring("exps", NB, [128, 3, 8], F32)
        Ear = cx.sbring("Ea", 2, [128, 4, 128], F32)
        Ebr = cx.sbring("Eb", 2, [128, 4, 128], F32)
        Ecr = cx.sbring("Ec", 2, [128, 4, 128], F32)
        Edr = cx.sbring("Ed", 2, [128, 4, 128], F32)
        Pr = cx.sbring("Pp", 4, [128, 4, 128], F32)
        PTr = cx.sbring("PTp", 4, [128, 4, 128], F32)
        Yr = cx.sbring("Yp", 4, [128, 4, 128], F32)
        Yfr = cx.sbring("Yf", 2 * NB, [128, 4, 128], BF16)
        nWTr = cx.sbring("nWT", 2 * NB, [128, 4, 128], BF16)
        attr = cx.sbring("att", 2 * NB, [128, 4, 128], BF16)
        qgr = cx.sbring("qg", 2 * NB, [128, 4, 128], BF16)
        vnr = cx.sbring("vn", 2, [128, 4, 128], BF16)
        oTr = cx.sbring("oTs", 2, [128, 8, 128], F32)

        def prep(c):
            lat = c >= 2 and not red
            t0 = c * C
            l0 = t0 - NCTX
            qT_c = qTr.next(); kT_c = kTr.next(); k_c = kr.next(); v_c = vr.next(); sm_c = smr.next()
            if lat:
                k.dma("sp", qT_c[:], qT_d[:, :, t0:t0 + C].rearrange("h p t -> p h t"), r=[R_scr], w=[qT_c.res], dres=qT_c.res)
            k.dma("sp", kT_c[:], s_kT[:, :, t0:t0 + C].rearrange("h p t -> p h t"), r=[R_scr], w=[kT_c.res], dres=kT_c.res)
            k.dma("sp", k_c[:], s_k[t0:t0 + C, :].rearrange("t (h d) -> t h d", d=128), r=[R_scr], w=[k_c.res], dres=k_c.res)
            k.dma("sp", v_c[:], s_v[t0:t0 + C, :].rearrange("t (h d) -> t h d", d=128), r=[R_scr], w=[v_c.res], dres=v_c.res)
            k.dma("sp", sm_c[:], s_sm[t0:t0 + C, :], r=[R_scr], w=[sm_c.res], dres=sm_c.res)
            o1_c = None
            if lat and not fwd:
                o1_c = o1r.next()
                k.dma("sp", o1_c[:], o1_d[:, :, l0:l0 + C].rearrange("h p t -> p h t"), r=[R_o1[c]], w=[o1_c.res], dres=o1_c.res)
            gcols = sm_c[:, g0:g0 + 8]
            lcols = sm_c[:, l0c:l0c + 8]
            pb = banks.next()
            k.group("pe", [
                lambda: nc.tensor.matmul(pb[:, 0:8], cumL, gcols, start=True, stop=True),
                lambda: nc.tensor.matmul(pb[:, 8:16], onesf[:], gcols, start=True, stop=True)],
                r=[tri.res, onesf.res, sm_c.res], w=[pb.res])
            sm6 = smallr.next()
            gc, gcl, ngc, tmp = sm6[:, 0, :], sm6[:, 1, :], sm6[:, 2, :], sm6[:, 3, :]
            k.ins("dve", lambda: nc.vector.tensor_copy(gc, pb[:, 0:8]), r=[pb.res], w=[sm6.res])
            k.ins("dve", lambda: nc.vector.tensor_tensor(out=gcl, in0=gc, in1=lcols, op=ALU.add), r=[sm6.res, sm_c.res], w=[sm6.res])
            k.ins("dve", lambda: nc.vector.tensor_scalar(out=ngc, in0=gc, scalar1=-1.0, scalar2=None, op0=ALU.mult),
                  r=[sm6.res], w=[sm6.res])
            k.ins("dve", lambda: nc.vector.tensor_tensor(out=tmp, in0=pb[:, 8:16], in1=gc, op=ALU.subtract),
                  r=[pb.res, sm6.res], w=[sm6.res])
            ex = expr.next()
            egl = eglr.next()
            k.ins("act", lambda: nc.scalar.activation(out=ex[:, 0, :], in_=gcl, func=AF.Exp), r=[sm6.res], w=[ex.res])
            k.ins("act", lambda: nc.scalar.activation(out=ex[:, 1, :], in_=lcols, func=AF.Exp), r=[sm_c.res], w=[ex.res])
            k.ins("act", lambda: nc.scalar.activation(out=ex[:, 2, :], in_=tmp, func=AF.Exp), r=[sm6.res], w=[ex.res])
            k.ins("act", lambda: nc.scalar.activation(out=egl[:], in_=pb[:, 8:16], func=AF.Exp), r=[pb.res], w=[egl.res])
            kbg = kbgr.next(); vb = vbr.next(); kd = kdr.next()
            bc = lambda col: ex[:, col, :].unsqueeze(2).to_broadcast([128, 8, 128])
            k.ins("pool", lambda: nc.gpsimd.tensor_tensor(out=kbg[:], in0=k_c[:], in1=bc(0), op=ALU.mult),
                  r=[k_c.res, ex.res], w=[kbg.res])
            k.ins("pool", lambda: nc.gpsimd.tensor_tensor(out=vb[:], in0=v_c[:], in1=bc(1), op=ALU.mult),
                  r=[v_c.res, ex.res], w=[vb.res])
            k.ins("pool", lambda: nc.gpsimd.tensor_tensor(out=kd[:], in0=k_c[:], in1=bc(2), op=ALU.mult),
                  r=[k_c.res, ex.res], w=[kd.res])
            groups = []
            for gi in range(2):
                h0 = gi * 4
                gb = lambda h: sm_c[:, g0 + h:g0 + h + 1].to_broadcast([128, 128])
                lb = lambda h: sm_c[:, l0c + h:l0c + h + 1].to_broadcast([128, 128])
                KK = banks.next()
                k.group("pe", [(lambda j=j: nc.tensor.matmul(KK[:, j * 128:(j + 1) * 128], kT_c[:, h0 + j, :], kT_c[:, h0 + j, :],
                                                             start=True, stop=True)) for j in range(4)],
                        r=[kT_c.res], w=[KK.res])
                Da = banks.next()
                fl = []
                for j in range(4):
                    fl.append(lambda j=j: nc.tensor.matmul(Da[:, j * 128:(j + 1) * 128], gb(h0 + j), negR, start=True, stop=False))
                    fl.append(lambda j=j: nc.tensor.matmul(Da[:, j * 128:(j + 1) * 128], identb[:], m_s, start=False, stop=True))
                k.group("pe", fl, r=[sm_c.res, tri.res, identb.res, mskb.res], w=[Da.res])
                Ea = Ear.next()
                for j in range(4):
                    k.ins("act", lambda j=j: nc.scalar.activation(out=Ea[:, j, :], in_=Da[:, j * 128:(j + 1) * 128], func=AF.Exp,
                                                                  bias=sm6[:, 1, h0 + j:h0 + j + 1]),
                          r=[Da.res, sm6.res], w=[Ea.res])
                Db = banks.next()
                fl = []
                for j in range(4):
                    sl = slice(j * 128, (j + 1) * 128)
                    fl.append(lambda j=j, sl=sl: nc.tensor.matmul(Db[:, sl], gb(h0 + j), cumL, start=True, stop=False))
                    fl.append(lambda j=j, sl=sl: nc.tensor.matmul(Db[:, sl], lb(h0 + j), ident[:], start=False, stop=False))
                    fl.append(lambda j=j, sl=sl: nc.tensor.matmul(Db[:, sl], identb[:], mT_s, start=False, stop=True))
                k.group("pe", fl, r=[sm_c.res, tri.res, ident.res, identb.res, mskb.res], w=[Db.res])
                Eb = Ebr.next()
                for j in range(4):
                    k.ins("act", lambda j=j: nc.scalar.activation(out=Eb[:, j, :], in_=Db[:, j * 128:(j + 1) * 128], func=AF.Exp,
                                                                  bias=sm6[:, 2, h0 + j:h0 + j + 1]),
                          r=[Db.res, sm6.res], w=[Eb.res])
                P0 = Pr.next(); P0T = PTr.next()
                KK3 = KK[:].rearrange("p (j c) -> p j c", c=128)
                k.ins("dve", lambda: nc.vector.scalar_tensor_tensor(out=P0[:], in0=KK3, scalar=-1.0, in1=Ea[:],
                                                                    op0=ALU.mult, op1=ALU.mult),
                      r=[KK.res, Ea.res], w=[P0.res])
                k.ins("dve", lambda: nc.vector.scalar_tensor_tensor(out=P0T[:], in0=KK3, scalar=-1.0, in1=Eb[:],
                                                                    op0=ALU.mult, op1=ALU.mult),
                      r=[KK.res, Eb.res], w=[P0T.res])
                att = None; qg = None
                if lat:
                    QK = banks.next()
                    k.group("pe", [(lambda j=j: nc.tensor.matmul(QK[:, j * 128:(j + 1) * 128], kT_c[:, h0 + j, :], qT_c[:, h0 + j, :],
                                                                 start=True, stop=True)) for j in range(4)],
                            r=[kT_c.res, qT_c.res], w=[QK.res])
                    Dc = banks.next()
                    fl = []
                    for j in range(4):
                        sl = slice(j * 128, (j + 1) * 128)
                        fl.append(lambda j=j, sl=sl: nc.tensor.matmul(Dc[:, sl], gb(h0 + j), cumL, start=True, stop=False))
                        fl.append(lambda j=j, sl=sl: nc.tensor.matmul(Dc[:, sl], identb[:], mT_i, start=False, stop=True))
                    k.group("pe", fl, r=[sm_c.res, tri.res, identb.res, mskb.res], w=[Dc.res])
                    Ec = Ecr.next()
                    for j in range(4):
                        k.ins("act", lambda j=j: nc.scalar.activation(out=Ec[:, j, :], in_=Dc[:, j * 128:(j + 1) * 128], func=AF.Exp,
                                                                      bias=sm6[:, 2, h0 + j:h0 + j + 1]),
                              r=[Dc.res, sm6.res], w=[Ec.res])
                    Dd = banks.next()
                    k.group("pe", [(lambda j=j: nc.tensor.matmul(Dd[:, j * 128:(j + 1) * 128], gb(h0 + j), cumL, start=True, stop=True))
                                   for j in range(4)], r=[sm_c.res, tri.res], w=[Dd.res])
                    Ed = Edr.next()
                    k.ins("act", lambda: nc.scalar.activation(out=Ed[:].rearrange("p j c -> p (j c)"), in_=Dd[:], func=AF.Exp),
                          r=[Dd.res], w=[Ed.res])
                    att = attr.next()
                    k.ins("dve", lambda: nc.vector.tensor_tensor(out=att[:], in0=QK[:].rearrange("p (j c) -> p j c", c=128),
                                                                 in1=Ec[:], op=ALU.mult), r=[QK.res, Ec.res], w=[att.res])
                    qg = qgr.next()
                    k.ins("pool", lambda: nc.gpsimd.tensor_tensor(out=qg[:], in0=qT_c[:, h0:h0 + 4, :], in1=Ed[:], op=ALU.mult),
                          r=[qT_c.res, Ed.res], w=[qg.res])
                Y = Yr.next()
                k.ins("pool", lambda: nc.gpsimd.tensor_tensor(out=Y[:], in0=P0T[:], in1=id4[:], op=ALU.add),
                      r=[P0T.res, id4.res], w=[Y.res])
                Pp, PTp = P0, P0T
                for lev in range(1, 7):
                    last = lev == 6
                    Pb = banks.next()
                    k.group("pe", [(lambda j=j, Pp=Pp, PTp=PTp, Pb=Pb: nc.tensor.matmul(
                        Pb[:, j * 128:(j + 1) * 128], PTp[:, j, :], Pp[:, j, :], start=True, stop=True)) for j in range(4)],
                        r=[Pp.res, PTp.res], w=[Pb.res])
                    Pn = Pr.next()
                    k.ins("act", lambda Pn=Pn, Pb=Pb: nc.scalar.copy(out=Pn[:].rearrange("p j c -> p (j c)"), in_=Pb[:]),
                          r=[Pb.res], w=[Pn.res])
                    PTn = None
                    if not last:
                        PTb = banks.next()
                        k.group("pe", [(lambda j=j, Pp=Pp, PTp=PTp, PTb=PTb: nc.tensor.matmul(
                            PTb[:, j * 128:(j + 1) * 128], Pp[:, j, :], PTp[:, j, :], start=True, stop=True)) for j in range(4)],
                            r=[Pp.res, PTp.res], w=[PTb.res])
                        PTn = PTr.next()
                        k.ins("act", lambda PTn=PTn, PTb=PTb: nc.scalar.copy(out=PTn[:].rearrange("p j c -> p (j c)"), in_=PTb[:]),
                              r=[PTb.res], w=[PTn.res])
                    Yb = banks.next()
                    k.group("pe", [(lambda j=j, Yb=Yb, Y=Y, Pn=Pn: nc.tensor.matmul(
                        Yb[:, j * 128:(j + 1) * 128], Pn[:, j, :], Y[:, j, :], start=True, stop=True)) for j in range(4)],
                        r=[Y.res, Pn.res], w=[Yb.res])
                    if last:
                        Yn = Yfr.next()
                        k.ins("dve", lambda Yn=Yn, Yb=Yb, Y=Y: nc.vector.tensor_tensor(
                            out=Yn[:].rearrange("p j c -> p (j c)"), in0=Yb[:], in1=Y[:].rearrange("p j c -> p (j c)"), op=ALU.add),
                            r=[Yb.res, Y.res], w=[Yn.res])
                    else:
                        Yn = Yr.next()
                        k.ins("dve", lambda Yn=Yn, Yb=Yb, Y=Y: nc.vector.tensor_tensor(
                            out=Yn[:].rearrange("p j c -> p (j c)"), in0=Yb[:], in1=Y[:].rearrange("p j c -> p (j c)"), op=ALU.add),
                            r=[Yb.res, Y.res], w=[Yn.res])
                    Y = Yn
                    Pp, PTp = Pn, PTn
                Wb = banks.next()
                k.group("pe", [(lambda j=j: nc.tensor.matmul(Wb[:, j * 128:(j + 1) * 128], kbg[:, h0 + j, :], Y[:, j, :],
                                                             start=True, stop=True)) for j in range(4)],
                        r=[kbg.res, Y.res], w=[Wb.res])
                nWT = nWTr.next()
                k.ins("act", lambda: nc.scalar.activation(out=nWT[:].rearrange("p j c -> p (j c)"), in_=Wb[:], func=AF.Identity,
                                                          scale=-1.0), r=[Wb.res], w=[nWT.res])
                groups.append((Y, nWT, att, qg))
            return dict(c=c, lat=lat, groups=groups, vb=vb, kd=kd, egl=egl, o1=o1_c)

        def chain(pp):
            c, lat = pp["c"], pp["lat"]
            l0 = c * C - NCTX
            vb, kd, egl = pp["vb"], pp["kd"], pp["egl"]
            oTs = oTr.next() if lat else None
            for gi in range(2):
                h0 = gi * 4
                Y, nWT, att, qg = pp["groups"][gi]
                VN = banks.next()
                fl = []
                for j in range(4):
                    sl = slice(j * 128, (j + 1) * 128)
                    fl.append(lambda j=j, sl=sl: nc.tensor.matmul(VN[:, sl], Y[:, j, :], vb[:, h0 + j, :], start=True, stop=False))
                    fl.append(lambda j=j, sl=sl: nc.tensor.matmul(VN[:, sl], nWT[:, j, :], Sbf[:, h0 + j, :], start=False, stop=True))
                k.group("pe", fl, r=[Y.res, vb.res, nWT.res, Sbf.res], w=[VN.res])
                vn = vnr.next()
                k.ins("dve", lambda: nc.vector.tensor_copy(vn[:].rearrange("p j c -> p (j c)"), VN[:]), r=[VN.res], w=[vn.res])
                if lat:
                    OT = banks.next()
                    fl = []
                    for j in range(4):
                        sl = slice(j * 128, (j + 1) * 128)
                        fl.append(lambda j=j, sl=sl: nc.tensor.matmul(OT[:, sl], Sbf[:, h0 + j, :], qg[:, j, :], start=True, stop=False))
                        fl.append(lambda j=j, sl=sl: nc.tensor.matmul(OT[:, sl], vn[:, j, :], att[:, j, :], start=False, stop=True))
                    k.group("pe", fl, r=[Sbf.res, qg.res, vn.res, att.res], w=[OT.res])
                    osl = oTs[:, h0:h0 + 4, :].rearrange("p j c -> p (j c)")
                    if fwd:
                        k.ins("act", lambda: nc.scalar.copy(out=osl, in_=OT[:]), r=[OT.res], w=[oTs.res])
                    else:
                        o1_c = pp["o1"]
                        k.ins("dve", lambda: nc.vector.tensor_tensor(
                            out=osl, in0=OT[:], in1=o1_c[:, h0:h0 + 4, :].rearrange("p j c -> p (j c)"), op=ALU.add),
                            r=[OT.res, o1_c.res], w=[oTs.res])
                DS = banks.next()
                k.group("pe", [(lambda j=j: nc.tensor.matmul(DS[:, j * 128:(j + 1) * 128], kd[:, h0 + j, :], vn[:, j, :],
                                                             start=True, stop=True)) for j in range(4)],
                        r=[kd.res, vn.res], w=[DS.res])
                ssl = S32[:, h0:h0 + 4, :]
                k.ins("pool", lambda: nc.gpsimd.tensor_tensor(
                    out=ssl, in0=ssl, in1=egl[:, h0:h0 + 4].unsqueeze(2).to_broadcast([128, 4, 128]), op=ALU.mult),
                    r=[S32.res, egl.res], w=[S32.res])
                k.ins("dve", lambda: nc.vector.tensor_tensor(out=ssl, in0=ssl, in1=DS[:].rearrange("p (j c) -> p j c", c=128),
                                                             op=ALU.add), r=[S32.res, DS.res], w=[S32.res])
                k.ins("act", lambda: nc.scalar.copy(out=Sbf[:, h0:h0 + 4, :], in_=ssl), r=[S32.res], w=[Sbf.res])
            if lat:
                if fwd:
                    k.dma("sp", o1_d[:, :, l0:l0 + C].rearrange("h p t -> p h t"), oTs[:], r=[oTs.res], w=[R_o1[c]], dres=oTs.res)
                else:
                    k.dma("sp", oT_d[:, :, l0:l0 + C].rearrange("h p t -> p h t"), oTs[:], r=[oTs.res], w=[R_oT], dres=oTs.res)

        order = list(range(NCH)) if fwd else list(range(NCH - 1, 1, -1))
        if GDN_LIMIT:
            order = order[:GDN_LIMIT]
        pend = prep(order[0])
        for idx in range(len(order)):
            nxt = prep(order[idx + 1]) if idx + 1 < len(order) else None
            chain(pend)
            pend = nxt
        if red:
            k.dma("sp", st2_d[:, 0:1024], S32[:].rearrange("p h d -> p (h d)"), r=[S32.res], w=[R_st2], dres=S32.res)
        cx.pop()

    def ssd_pass(mode):
        cx.push()
        fwd = mode != "own2"
        pi = 0 if mode == "own1" else 1
        red = mode == "red"
        s_x, s_B, s_sm = (x2_d, B2_d, sm2_d) if red else (x_d, B_d, sm_d)
        tri, mskb, id4 = load_scan_consts()
        cumL = tri[:, 0, :] if fwd else tri[:, 1, :]
        mT_i = mskb[:, 3, :] if fwd else mskb[:, 2, :]
        d0 = 64 + pi * 32
        banks = Ring([cx.psum(f"sbk{i}", [128, 512], F32) for i in range(8)])
        aB = cx.sb("aB", [128, 32], F32)
        k.dma("sp", aB[:], salog_d[:, pi, :], w=[aB.res], dres=aB.res)
        k.ins("act", lambda: nc.scalar.activation(out=aB[:], in_=aB[:], func=AF.Exp), r=[aB.res], w=[aB.res])
        k.ins("dve", lambda: nc.vector.tensor_scalar(out=aB[:], in0=aB[:], scalar1=-1.0, scalar2=None, op0=ALU.mult),
              r=[aB.res], w=[aB.res])
        hT32 = cx.sb("hT32", [128, 32, 64], F32)
        hTbf = cx.sb("hTbf", [128, 32, 64], BF16)
        if fwd:
            k.ins("pool", lambda: nc.gpsimd.memset(hT32[:], 0.0), w=[hT32.res])
        else:
            k.dma("sp", hT32[:].rearrange("p h d -> p (h d)"), st2_d[:, 1024:3072], r=[R_st2], w=[hT32.res], dres=hT32.res)
        k.ins("act", lambda: nc.scalar.copy(out=hTbf[:], in_=hT32[:]), r=[hT32.res], w=[hTbf.res])
        xr = cx.sbring("xc", 2, [128, 32, 64], BF16)
        BTr = cx.sbring("BTc", 2, [128, 4, 128], BF16)
        CTr = cx.sbring("CTc", 2, [128, 4, 128], BF16)
        Br = cx.sbring("Bc", 2, [128, 4, 128], BF16)
        smr = cx.sbring("smc", 2, [128, 128], F32)
        y1r = cx.sbring("y1c", 2, [128, 2048], F32)
        dAr = cx.sbring("dA", 2, [128, 32], F32)
        s5r = cx.sbring("s5", 2, [128, 6, 32], F32)
        xdtr = cx.sbring("xdt", 2, [128, 32, 64], BF16)
        xdtsr = cx.sbring("xdts", 2, [128, 32, 64], BF16)
        ytr = cx.sbring("ytmp", 2, [128, 32, 64], F32)
        ycr = cx.sbring("yc", 2, [128, 2048], F32)
        segr = cx.sbring("seg", 3, [128, 4, 128], F32)
        GTr = cx.sbring("GT", 2, [128, 4, 128], F32)
        MTr = cx.sbring("MT", 3, [128, 4, 128], BF16)
        order = list(range(NCH)) if fwd else list(range(NCH - 1, 1, -1))
        if GDN_LIMIT:
            order = order[:GDN_LIMIT]
        for c in order:
            lat = c >= 2 and not red
            t0 = c * C
            l0 = t0 - NCTX
            x_c = xr.next(); BT_c = BTr.next(); CT_c = CTr.next(); B_c = Br.next(); sm_c = smr.next()
            k.dma("sp", x_c[:].rearrange("p h d -> p (h d)"), s_x[t0:t0 + C, :], r=[R_scr], w=[x_c.res], dres=x_c.res)
            if lat:
                k.dma("sp", BT_c[:], BT_d[:, :, t0:t0 + C].rearrange("g p t -> p g t"), r=[R_scr], w=[BT_c.res], dres=BT_c.res)
                k.dma("sp", CT_c[:], CT_d[:, :, t0:t0 + C].rearrange("g p t -> p g t"), r=[R_scr], w=[CT_c.res], dres=CT_c.res)
            k.dma("sp", B_c[:], s_B[t0:t0 + C, :].rearrange("t (g n) -> t g n", n=128), r=[R_scr], w=[B_c.res], dres=B_c.res)
            k.dma("sp", sm_c[:], s_sm[t0:t0 + C, :], r=[R_scr], w=[sm_c.res], dres=sm_c.res)
            y1_c = None
            if lat and not fwd:
                y1_c = y1r.next()
                k.dma("sp", y1_c[:], y1_d[l0:l0 + C, :], r=[R_y1[c]], w=[y1_c.res], dres=y1_c.res)
            dtc = sm_c[:, d0:d0 + 32]
            dA = dAr.next()
            k.ins("dve", lambda: nc.vector.tensor_tensor(out=dA[:], in0=dtc, in1=aB[:], op=ALU.mult),
                  r=[sm_c.res, aB.res], w=[dA.res])
            pb = banks.next()
            k.group("pe", [
                lambda: nc.tensor.matmul(pb[:, 0:32], cumL, dA[:], start=True, stop=True),
                lambda: nc.tensor.matmul(pb[:, 32:64], onesf[:], dA[:], start=True, stop=True)],
                r=[tri.res, onesf.res, dA.res], w=[pb.res])
            s5 = s5r.next()
            ac, nac, tmp, ea, w2, eat = (s5[:, i, :] for i in range(6))
            k.ins("dve", lambda: nc.vector.tensor_copy(ac, pb[:, 0:32]), r=[pb.res], w=[s5.res])
            k.ins("dve", lambda: nc.vector.tensor_scalar(out=nac, in0=ac, scalar1=-1.0, scalar2=None, op0=ALU.mult),
                  r=[s5.res], w=[s5.res])
            k.ins("dve", lambda: nc.vector.tensor_tensor(out=tmp, in0=pb[:, 32:64], in1=ac, op=ALU.subtract),
                  r=[pb.res, s5.res], w=[s5.res])
            k.ins("act", lambda: nc.scalar.activation(out=ea, in_=ac, func=AF.Exp), r=[s5.res], w=[s5.res])
            k.ins("act", lambda: nc.scalar.activation(out=w2, in_=tmp, func=AF.Exp), r=[s5.res], w=[s5.res])
            k.ins("act", lambda: nc.scalar.activation(out=eat, in_=pb[:, 32:64], func=AF.Exp), r=[pb.res], w=[s5.res])
            k.ins("dve", lambda: nc.vector.tensor_tensor(out=w2, in0=w2, in1=dtc, op=ALU.mult), r=[s5.res, sm_c.res], w=[s5.res])
            bc32 = lambda ap: ap.unsqueeze(2).to_broadcast([128, 32, 64])
            xdts = xdtsr.next()
            k.ins("pool", lambda: nc.gpsimd.tensor_tensor(out=xdts[:], in0=x_c[:], in1=bc32(w2), op=ALU.mult),
                  r=[x_c.res, s5.res], w=[xdts.res])
            ytmp = None
            if lat:
                xdt = xdtr.next()
                k.ins("pool", lambda: nc.gpsimd.tensor_tensor(out=xdt[:], in0=x_c[:], in1=bc32(dtc), op=ALU.mult),
                      r=[x_c.res, sm_c.res], w=[xdt.res])
                ytmp = ytr.next()
                for g in range(4):
                    YO = banks.next()
                    k.ins("pe", lambda g=g, YO=YO: nc.tensor.matmul(
                        YO[:], CT_c[:, g, :], hTbf[:, g * 8:(g + 1) * 8, :].rearrange("p h d -> p (h d)"), start=True, stop=True),
                        r=[CT_c.res, hTbf.res], w=[YO.res])
                    k.ins("dve", lambda g=g, YO=YO: nc.vector.tensor_tensor(
                        out=ytmp[:, g * 8:(g + 1) * 8, :], in0=YO[:].rearrange("p (h d) -> p h d", d=64),
                        in1=ea[:, g * 8:(g + 1) * 8].unsqueeze(2).to_broadcast([128, 8, 64]), op=ALU.mult),
                        r=[YO.res, s5.res], w=[ytmp.res])
            k.ins("pool", lambda: nc.gpsimd.tensor_tensor(out=hT32[:], in0=hT32[:], in1=bc32(eat), op=ALU.mult),
                  r=[hT32.res, s5.res], w=[hT32.res])
            for g in range(4):
                NS = banks.next()
                k.ins("pe", lambda g=g, NS=NS: nc.tensor.matmul(
                    NS[:], B_c[:, g, :], xdts[:, g * 8:(g + 1) * 8, :].rearrange("p h d -> p (h d)"), start=True, stop=True),
                    r=[B_c.res, xdts.res], w=[NS.res])
                hs = hT32[:, g * 8:(g + 1) * 8, :]
                k.ins("dve", lambda g=g, NS=NS, hs=hs: nc.vector.tensor_tensor(
                    out=hs, in0=hs, in1=NS[:].rearrange("p (h d) -> p h d", d=64), op=ALU.add),
                    r=[hT32.res, NS.res], w=[hT32.res])
            k.ins("act", lambda: nc.scalar.copy(out=hTbf[:], in_=hT32[:]), r=[hT32.res], w=[hTbf.res])
            if not lat:
                continue
            GTb = banks.next()
            k.group("pe", [(lambda g=g: nc.tensor.matmul(GTb[:, g * 128:(g + 1) * 128], BT_c[:, g, :], CT_c[:, g, :],
                                                         start=True, stop=True)) for g in range(4)],
                    r=[BT_c.res, CT_c.res], w=[GTb.res])
            GT = GTr.next()
            k.ins("act", lambda: nc.scalar.copy(out=GT[:].rearrange("p g c -> p (g c)"), in_=GTb[:]), r=[GTb.res], w=[GT.res])
            y_c = ycr.next()
            for g in range(4):
                YD = banks.next()
                for qd in range(2):
                    hq = g * 8 + qd * 4
                    Dq = banks.next()
                    fl = []
                    for j in range(4):
                        sl = slice(j * 128, (j + 1) * 128)
                        fl.append(lambda j=j, sl=sl, Dq=Dq: nc.tensor.matmul(
                            Dq[:, sl], dA[:, hq + j:hq + j + 1].to_broadcast([128, 128]), cumL, start=True, stop=False))
                        fl.append(lambda j=j, sl=sl, Dq=Dq: nc.tensor.matmul(Dq[:, sl], identb[:], mT_i, start=False, stop=True))
                    k.group("pe", fl, r=[dA.res, tri.res, identb.res, mskb.res], w=[Dq.res])
                    seg = segr.next()
                    for j in range(4):
                        k.ins("act", lambda j=j, seg=seg, Dq=Dq: nc.scalar.activation(
                            out=seg[:, j, :], in_=Dq[:, j * 128:(j + 1) * 128], func=AF.Exp, bias=s5[:, 1, hq + j:hq + j + 1]),
                            r=[Dq.res, s5.res], w=[seg.res])
                    MT = MTr.next()
                    k.ins("dve", lambda seg=seg, MT=MT, g=g: nc.vector.tensor_tensor(
                        out=MT[:], in0=seg[:], in1=GT[:, g:g + 1, :].to_broadcast([128, 4, 128]), op=ALU.mult),
                        r=[seg.res, GT.res], w=[MT.res])
                    k.group("pe", [(lambda j=j, MT=MT, YD=YD: nc.tensor.matmul(
                        YD[:, (qd * 4 + j) * 64:(qd * 4 + j + 1) * 64], MT[:, j, :], xdt[:, hq + j, :], start=True, stop=True))
                        for j in range(4)], r=[MT.res, xdt.res], w=[YD.res])
                ysl = y_c[:, g * 512:(g + 1) * 512]
                k.ins("dve", lambda g=g, YD=YD, ysl=ysl: nc.vector.tensor_tensor(
                    out=ysl, in0=YD[:], in1=ytmp[:, g * 8:(g + 1) * 8, :].rearrange("p h d -> p (h d)"), op=ALU.add),
                    r=[YD.res, ytmp.res], w=[y_c.res])
            if fwd:
                k.dma("sp", y1_d[l0:l0 + C, :], y_c[:], r=[y_c.res], w=[R_y1[c]], dres=y_c.res)
            else:
                k.ins("pool", lambda: nc.gpsimd.tensor_tensor(out=y_c[:], in0=y_c[:], in1=y1_c[:], op=ALU.add),
                      r=[y_c.res, y1_c.res], w=[y_c.res])
                k.dma("sp", y_d[l0:l0 + C, :], y_c[:], r=[y_c.res], w=[R_y], dres=y_c.res)
        if red:
            k.dma("sp", st2_d[:, 1024:3072], hT32[:].rearrange("p h d -> p (h d)"), r=[hT32.res], w=[R_st2], dres=hT32.res)
        cx.pop()

    with nc.allow_non_contiguous_dma("scan chunk relayout"):
        if "R" in phases:
            gdn_pass("red")
            ssd_pass("red")
        if "B1" in phases:
            gdn_pass("own1")
        if "S1" in phases:
            ssd_pass("own1")
        if "B2" in phases:
            gdn_pass("own2")
        if "S2" in phases:
            ssd_pass("own2")


    def silu_parts(src_ap, shape, ring_e, ring_z, src_res):
        ez = ring_e.next()
        zc = ring_z.next()
        k.ins("act", lambda: nc.scalar.activation(out=ez[:], in_=src_ap, func=AF.Exp, scale=-1.0), r=[src_res], w=[ez.res])
        k.ins("act", lambda: nc.scalar.copy(out=zc[:], in_=src_ap), r=[src_res], w=[zc.res])
        k.ins("dve", lambda: nc.vector.tensor_scalar(out=ez[:], in0=ez[:], scalar1=1.0, scalar2=None, op0=ALU.add),
              r=[ez.res], w=[ez.res])
        return zc, ez

    if "C1" in phases:
        cx.push()
        T1 = 128
        wZ = cx.sb("wZ", [128, 8, 3072], BF16)
        wbg = cx.sb("wbg", [128, 8, 1024], BF16)
        wbs = cx.sb("wbs", [128, 16, 1024], BF16)
        wo = cx.sb("wo", [128, 8, 1024], BF16)
        for kt in range(8):
            k.dma("pool", wZ[:, kt, :], w_in[kt * 128:(kt + 1) * 128, O_ZG:O_ZG + 3072], w=[wZ.res], dres=wZ.res)
            k.dma("pool", wbg[:, kt, :], wbg_d[kt * 128:(kt + 1) * 128, :], w=[wbg.res], dres=wbg.res)
            k.dma("pool", wo[:, kt, :], wo_d[kt * 128:(kt + 1) * 128, :], w=[wo.res], dres=wo.res)
        for kt in range(16):
            k.dma("pool", wbs[:, kt, :], wbs_d[kt * 128:(kt + 1) * 128, :], w=[wbs.res], dres=wbs.res)
        gnw = cx.sb("gnw", [128, 1], F32)
        k.dma("sp", gnw[:], gnw_d, w=[gnw.res], dres=gnw.res)
        dskB = cx.sb("dskB", [128, 2048], F32)
        snwB = cx.sb("snwB", [128, 2048], F32)
        g1B = cx.sb("g1B", [128, 1024], F32)
        k.dma("sp", dskB[:], dskip_d[0, :].partition_broadcast(128), w=[dskB.res], dres=dskB.res)
        k.dma("sp", snwB[:], snw_d[0, :].partition_broadcast(128), w=[snwB.res], dres=snwB.res)
        k.dma("sp", g1B[:], mod_d[0, 2048:3072].partition_broadcast(128), r=[R_mod], w=[g1B.res], dres=g1B.res)
        neg1c = cx.sb("neg1c", [128, 512], F32)
        k.ins("pool", lambda: nc.gpsimd.memset(neg1c[:], -1.0), w=[neg1c.res])
        banks = Ring([cx.psum(f"c1b{i}", [128, 512], F32) for i in range(6)])
        ptr = Ring([cx.psum(f"c1t{i}", [128, 1024], BF16) for i in range(2)])
        aTr = cx.sbring("c_aT", 2, [128, 8, T1], BF16)
        oTr_ = cx.sbring("c_oT", 1, [128, 8, T1], F32)
        yr_ = cx.sbring("c_y", 1, [128, 2048], F32)
        xsr_ = cx.sbring("c_xs", 1, [128, 2048], BF16)
        sgr_ = cx.sbring("c_sg", 1, [128, 16, T1], F32)
        xr_ = cx.sbring("c_x", 1, [128, 1024], F32)
        onTr = cx.sbring("c_on", 1, [128, 8, T1], BF16)
        ynTr = cx.sbring("c_yn", 1, [128, 16, T1], BF16)
        mTr = cx.sbring("c_mT", 1, [128, 8, T1], BF16)
        h1r = cx.sbring("c_h1", 1, [128, 1024], F32)
        e1r = cx.sbring("c_e1", 2, [128, T1], F32)
        z1r = cx.sbring("c_z1", 2, [128, T1], F32)
        t1r = cx.sbring("c_t1", 3, [128, T1], F32)
        e5r = cx.sbring("c_e5", 2, [128, 512], F32)
        z5r = cx.sbring("c_z5", 2, [128, 512], F32)
        t5r = cx.sbring("c_t5", 3, [128, 512], F32)
        ynr = cx.sbring("c_ynb", 2, [128, 512], BF16)
        ssr_ = cx.sbring("c_ss", 4, [128, 2], F32)
        for ti in range(NLAT // T1):
            l0 = ti * T1
            aT = aTr.next(); oT = oTr_.next(); yt = yr_.next(); xs_ = xsr_.next(); sg = sgr_.next(); xt = xr_.next()
            k.dma("sp", aT[:], aT_d[:, :, l0:l0 + T1].rearrange("kt p t -> p kt t"), r=[R_scr], w=[aT.res], dres=aT.res)
            k.dma("sp", oT[:], oT_d[:, :, l0:l0 + T1].rearrange("h p t -> p h t"), r=[R_oT], w=[oT.res], dres=oT.res)
            k.dma("sp", yt[:], y_d[l0:l0 + T1, :], r=[R_y], w=[yt.res], dres=yt.res)
            k.dma("sp", xs_[:], x_d[NCTX + l0:NCTX + l0 + T1, :], r=[R_scr], w=[xs_.res], dres=xs_.res)
            k.dma("sp", sg[:], sgT_d[:, :, l0:l0 + T1].rearrange("c p t -> p c t"), r=[R_scr], w=[sg.res], dres=sg.res)
            k.dma("sp", xt[:], xin[NCTX + l0:NCTX + l0 + T1, :], w=[xt.res], dres=xt.res)
            onT = onTr.next()
            for h in range(8):
                pz = banks.next()
                k.group("pe", [(lambda kt=kt: nc.tensor.matmul(pz[:, 0:T1], wZ[:, kt, h * 128:(h + 1) * 128], aT[:, kt, :],
                                                               start=(kt == 0), stop=(kt == 7))) for kt in range(8)],
                        r=[wZ.res, aT.res], w=[pz.res])
                zc, ez = silu_parts(pz[:, 0:T1], None, e1r, z1r, pz.res)
                k.ins("dve", lambda: nc.vector.reciprocal(out=ez[:], in_=ez[:]), r=[ez.res], w=[ez.res])
                k.ins("pool", lambda: nc.gpsimd.tensor_tensor(out=zc[:], in0=zc[:], in1=ez[:], op=ALU.mult),
                      r=[zc.res, ez.res], w=[zc.res])
                o2 = t1r.next()
                k.ins("pool", lambda: nc.gpsimd.tensor_tensor(out=o2[:], in0=oT[:, h, :], in1=oT[:, h, :], op=ALU.mult),
                      r=[oT.res], w=[o2.res])
                pn = banks.next()
                k.ins("pe", lambda: nc.tensor.matmul(pn[:, 0:T1], onesf[:], o2[:], start=True, stop=True),
                      r=[onesf.res, o2.res], w=[pn.res])
                rs = t1r.next()
                k.ins("act", lambda: nc.scalar.activation(out=rs[:], in_=pn[:, 0:T1], func=AF.Ln, scale=1.0 / 128, bias=epsc[:, 0:1]),
                      r=[pn.res, epsc.res], w=[rs.res])
                k.ins("act", lambda: nc.scalar.activation(out=rs[:], in_=rs[:], func=AF.Exp, scale=-0.5), r=[rs.res], w=[rs.res])
                k.ins("dve", lambda: nc.vector.tensor_tensor(out=rs[:], in0=rs[:], in1=oT[:, h, :], op=ALU.mult),
                      r=[rs.res, oT.res], w=[rs.res])
                k.ins("dve", lambda: nc.vector.scalar_tensor_tensor(out=onT[:, h, :], in0=rs[:], scalar=gnw[:, 0:1], in1=zc[:],
                                                                    op0=ALU.mult, op1=ALU.mult),
                      r=[rs.res, gnw.res, zc.res], w=[onT.res])
            ynT = ynTr.next()
            for cg in range(4):
                csl = slice(cg * 512, (cg + 1) * 512)
                pz = banks.next()
                k.group("pe", [(lambda kt=kt: nc.tensor.matmul(pz[:], aT[:, kt, :], wZ[:, kt, 1024 + cg * 512:1024 + (cg + 1) * 512],
                                                               start=(kt == 0), stop=(kt == 7))) for kt in range(8)],
                        r=[wZ.res, aT.res], w=[pz.res])
                zc, ez = silu_parts(pz[:], None, e5r, z5r, pz.res)
                k.ins("dve", lambda: nc.vector.reciprocal(out=ez[:], in_=ez[:]), r=[ez.res], w=[ez.res])
                k.ins("pool", lambda: nc.gpsimd.tensor_tensor(out=zc[:], in0=zc[:], in1=ez[:], op=ALU.mult),
                      r=[zc.res, ez.res], w=[zc.res])
                yy = t5r.next()
                k.ins("pool", lambda: nc.gpsimd.tensor_tensor(out=yy[:], in0=xs_[:, csl], in1=dskB[:, csl], op=ALU.mult),
                      r=[xs_.res, dskB.res], w=[yy.res])
                k.ins("dve", lambda: nc.vector.tensor_tensor(out=yy[:], in0=yy[:], in1=yt[:, csl], op=ALU.add),
                      r=[yy.res, yt.res], w=[yy.res])
                k.ins("dve", lambda: nc.vector.tensor_tensor(out=yy[:], in0=yy[:], in1=zc[:], op=ALU.mult),
                      r=[yy.res, zc.res], w=[yy.res])
                ss = ssr_.next()
                jk = t5r.next()
                k.ins("act", lambda: nc.scalar.activation(out=jk[:], in_=yy[:], func=AF.Square, accum_out=ss[:, 0:1]),
                      r=[yy.res], w=[jk.res, ss.res])
                k.ins("act", lambda: nc.scalar.activation(out=ss[:, 1:2], in_=ss[:, 0:1], func=AF.Ln, scale=1.0 / 512, bias=epsc[:, 0:1]),
                      r=[ss.res, epsc.res], w=[ss.res])
                k.ins("act", lambda: nc.scalar.activation(out=ss[:, 1:2], in_=ss[:, 1:2], func=AF.Exp, scale=-0.5),
                      r=[ss.res], w=[ss.res])
                ynb = ynr.next()
                k.ins("dve", lambda: nc.vector.scalar_tensor_tensor(out=ynb[:], in0=yy[:], scalar=ss[:, 1:2], in1=snwB[:, csl],
                                                                    op0=ALU.mult, op1=ALU.mult),
                      r=[yy.res, ss.res, snwB.res], w=[ynb.res])
                pt = ptr.next()
                k.group("pe", [(lambda j=j: nc.tensor.transpose(pt[:, j * 128:(j + 1) * 128], ynb[:, j * 128:(j + 1) * 128], identb[:]))
                               for j in range(4)], r=[ynb.res, identb.res], w=[pt.res])
                k.ins("act", lambda: nc.scalar.copy(out=ynT[:, cg * 4:(cg + 1) * 4, :],
                                                    in_=pt[:, 0:512].rearrange("p (j c) -> p j c", c=128)),
                      r=[pt.res], w=[ynT.res])
            mT = mTr.next()
            for oc in range(8):
                pg = banks.next()
                k.group("pe", [(lambda h=h: nc.tensor.matmul(pg[:, 0:T1], wbg[:, h, oc * 128:(oc + 1) * 128], onT[:, h, :],
                                                             start=(h == 0), stop=(h == 7))) for h in range(8)],
                        r=[wbg.res, onT.res], w=[pg.res])
                pp = banks.next()
                k.group("pe", [(lambda ct=ct: nc.tensor.matmul(pp[:, 0:T1], wbs[:, ct, oc * 128:(oc + 1) * 128], ynT[:, ct, :],
                                                               start=(ct == 0), stop=(ct == 15))) for ct in range(16)],
                        r=[wbs.res, ynT.res], w=[pp.res])
                m1 = t1r.next()
                k.ins("dve", lambda: nc.vector.tensor_tensor(out=m1[:], in0=pg[:, 0:T1], in1=sg[:, oc, :], op=ALU.mult),
                      r=[pg.res, sg.res], w=[m1.res])
                m2 = t1r.next()
                k.ins("dve", lambda: nc.vector.tensor_tensor(out=m2[:], in0=pp[:, 0:T1], in1=sg[:, 8 + oc, :], op=ALU.mult),
                      r=[pp.res, sg.res], w=[m2.res])
                k.ins("pool", lambda: nc.gpsimd.tensor_tensor(out=mT[:, oc, :], in0=m1[:], in1=m2[:], op=ALU.add),
                      r=[m1.res, m2.res], w=[mT.res])
            h1 = h1r.next()
            for cg in range(2):
                csl = slice(cg * 512, (cg + 1) * 512)
                pm = banks.next()
                k.group("pe", [(lambda kt=kt: nc.tensor.matmul(pm[:], mT[:, kt, :], wo[:, kt, csl],
                                                               start=(kt == 0), stop=(kt == 7))) for kt in range(8)],
                        r=[mT.res, wo.res], w=[pm.res])
                k.ins("dve", lambda: nc.vector.tensor_tensor(out=h1[:, csl], in0=pm[:], in1=g1B[:, csl], op=ALU.mult),
                      r=[pm.res, g1B.res], w=[h1.res])
                k.ins("pool", lambda: nc.gpsimd.tensor_tensor(out=h1[:, csl], in0=h1[:, csl], in1=xt[:, csl], op=ALU.add),
                      r=[h1.res, xt.res], w=[h1.res])
            k.dma("sp", h1_d[l0:l0 + T1, :], h1[:], r=[h1.res], w=[R_h1], dres=h1.res)
        cx.pop()

    if "C2" in phases:
        cx.push()
        T2 = 256
        NF = DFF // 128
        wfi = cx.sb("wfi", [128, 8, 2 * DFF], BF16)
        wfo = cx.sb("wfo", [128, NF, 1024], BF16)
        for kt in range(8):
            for c0 in range(0, 2 * DFF, 2816):
                k.dma("pool", wfi[:, kt, c0:c0 + 2816], wfi_d[kt * 128:(kt + 1) * 128, c0:c0 + 2816], w=[wfi.res], dres=wfi.res)
        for ft in range(NF):
            k.dma("pool", wfo[:, ft, :], wfo_d[ft * 128:(ft + 1) * 128, :], w=[wfo.res], dres=wfo.res)
        g2B = cx.sb("g2B", [128, 1024], F32)
        nfB = cx.sb("nfB", [128, 1024], F32)
        k.dma("sp", g2B[:], mod_d[0, 5120:6144].partition_broadcast(128), r=[R_mod], w=[g2B.res], dres=g2B.res)
        k.dma("sp", nfB[:], nfw_d[0, :].partition_broadcast(128), w=[nfB.res], dres=nfB.res)
        n2 = cx.sb("n2", [128, 8], F32)
        k.dma("sp", n2[:], n2w_d, w=[n2.res], dres=n2.res)
        S2 = cx.sb("S2", [128, 8], F32)
        k.ins("dve", lambda: nc.vector.scalar_tensor_tensor(out=S2[:], in0=modf[:, 0, 4, :], scalar=1.0, in1=n2[:],
                                                            op0=ALU.add, op1=ALU.mult), r=[modf.res, n2.res], w=[S2.res])
        neg1d = cx.sb("neg1d", [128, T2], F32)
        k.ins("pool", lambda: nc.gpsimd.memset(neg1d[:], -1.0), w=[neg1d.res])
        banks = Ring([cx.psum(f"c2b{i}", [128, 512], F32) for i in range(8)])
        h1r = cx.sbring("d_h1", 2, [128, 2, 1024], F32)
        xnr = cx.sbring("d_xn", 1, [128, 1024], F32)
        fTr = cx.sbring("d_fT", 2, [128, 8, T2], BF16)
        actr = cx.sbring("d_act", 1, [128, NF, T2], BF16)
        outr = cx.sbring("d_out", 1, [128, 2, 1024], F32)
        ssr2 = cx.sbring("d_ss", 4, [128, 2], F32)
        junk2 = cx.sb("junk2", [128, 1024], BF16)
        e2r = cx.sbring("d_e", 3, [128, T2], F32)
        a2r = cx.sbring("d_a", 3, [128, T2], F32)
        for ti in range(NLAT // T2):
            l0 = ti * T2
            h1 = h1r.next()
            k.dma("sp", h1[:], h1_d[l0:l0 + T2, :].rearrange("(j p) d -> p j d", p=128), r=[R_h1], w=[h1.res], dres=h1.res)
            ss = ssr2.next()
            for j in range(2):
                k.ins("act", lambda j=j: nc.scalar.activation(out=junk2[:], in_=h1[:, j, :], func=AF.Square, accum_out=ss[:, j:j + 1]),
                      r=[h1.res], w=[junk2.res, ss.res])
            k.ins("act", lambda: nc.scalar.activation(out=ss[:], in_=ss[:], func=AF.Ln, scale=1.0 / D, bias=epsc[:, 0:1]),
                  r=[ss.res, epsc.res], w=[ss.res])
            k.ins("act", lambda: nc.scalar.activation(out=ss[:], in_=ss[:], func=AF.Exp, scale=-0.5), r=[ss.res], w=[ss.res])
            fT = fTr.next()
            for j in range(2):
                xn = xnr.next()
                k.ins("act", lambda j=j, xn=xn: nc.scalar.activation(out=xn[:], in_=h1[:, j, :], func=AF.Identity, scale=ss[:, j:j + 1]),
                      r=[h1.res, ss.res], w=[xn.res])
                for half in range(2):
                    pb = banks.next()
                    k.group("pe", [(lambda q=q, xn=xn, pb=pb: nc.tensor.transpose(
                        pb[:, q * 128:(q + 1) * 128], xn[:, (half * 4 + q) * 128:(half * 4 + q + 1) * 128], ident[:]))
                        for q in range(4)], r=[xn.res, ident.res], w=[pb.res])
                    for q in range(4):
                        kt = half * 4 + q
                        k.ins("act", lambda q=q, kt=kt, pb=pb, j=j: nc.scalar.activation(
                            out=fT[:, kt, j * 128:(j + 1) * 128], in_=pb[:, q * 128:(q + 1) * 128], func=AF.Identity,
                            scale=S2[:, kt:kt + 1], bias=modf[:, 0, 3, kt:kt + 1]),
                            r=[pb.res, S2.res, modf.res], w=[fT.res])
            act = actr.next()
            for ft in range(NF):
                pgt = banks.next()
                k.group("pe", [(lambda kt=kt: nc.tensor.matmul(pgt[:, 0:T2], wfi[:, kt, ft * 128:(ft + 1) * 128], fT[:, kt, :],
                                                               start=(kt == 0), stop=(kt == 7))) for kt in range(8)],
                        r=[wfi.res, fT.res], w=[pgt.res])
                pup = banks.next()
                k.group("pe", [(lambda kt=kt: nc.tensor.matmul(pup[:, 0:T2], wfi[:, kt, DFF + ft * 128:DFF + (ft + 1) * 128], fT[:, kt, :],
                                                               start=(kt == 0), stop=(kt == 7))) for kt in range(8)],
                        r=[wfi.res, fT.res], w=[pup.res])
                ez = e2r.next()
                k.ins("act", lambda: nc.scalar.activation(out=ez[:], in_=pgt[:, 0:T2], func=AF.Exp, scale=-1.0), r=[pgt.res], w=[ez.res])
                k.ins("dve", lambda: nc.vector.tensor_scalar(out=ez[:], in0=ez[:], scalar1=1.0, scalar2=None, op0=ALU.add),
                      r=[ez.res], w=[ez.res])
                k.ins("dve", lambda: nc.vector.reciprocal(out=ez[:], in_=ez[:]), r=[ez.res], w=[ez.res])
                a1 = a2r.next()
                k.ins("dve", lambda: nc.vector.tensor_tensor(out=a1[:], in0=pgt[:, 0:T2], in1=ez[:], op=ALU.mult),
                      r=[pgt.res, ez.res], w=[a1.res])
                k.ins("dve", lambda: nc.vector.tensor_tensor(out=act[:, ft, :], in0=pup[:, 0:T2], in1=a1[:], op=ALU.mult),
                      r=[pup.res, a1.res], w=[act.res])
            ot = outr.next()
            ss2 = ssr2.next()
            for j in range(2):
                for cg in range(2):
                    csl = slice(cg * 512, (cg + 1) * 512)
                    pf = banks.next()
                    k.group("pe", [(lambda ft=ft: nc.tensor.matmul(pf[:], act[:, ft, j * 128:(j + 1) * 128], wfo[:, ft, csl],
                                                                   start=(ft == 0), stop=(ft == NF - 1))) for ft in range(NF)],
                            r=[act.res, wfo.res], w=[pf.res])
                    k.ins("dve", lambda: nc.vector.tensor_tensor(out=ot[:, j, csl], in0=pf[:], in1=g2B[:, csl], op=ALU.mult),
                          r=[pf.res, g2B.res], w=[ot.res])
                    k.ins("pool", lambda: nc.gpsimd.tensor_tensor(out=ot[:, j, csl], in0=ot[:, j, csl], in1=h1[:, j, csl], op=ALU.add),
                          r=[ot.res, h1.res], w=[ot.res])
                k.ins("act", lambda j=j: nc.scalar.activation(out=junk2[:], in_=ot[:, j, :], func=AF.Square, accum_out=ss2[:, j:j + 1]),
                      r=[ot.res], w=[junk2.res, ss2.res])
            k.ins("act", lambda: nc.scalar.activation(out=ss2[:], in_=ss2[:], func=AF.Ln, scale=1.0 / D, bias=epsc[:, 0:1]),
                  r=[ss2.res, epsc.res], w=[ss2.res])
            k.ins("act", lambda: nc.scalar.activation(out=ss2[:], in_=ss2[:], func=AF.Exp, scale=-0.5), r=[ss2.res], w=[ss2.res])
            for j in range(2):
                k.ins("dve", lambda j=j: nc.vector.scalar_tensor_tensor(out=ot[:, j, :], in0=ot[:, j, :], scalar=ss2[:, j:j + 1],
                                                                        in1=nfB[:], op0=ALU.mult, op1=ALU.mult),
                      r=[ot.res, ss2.res, nfB.res], w=[ot.res])
            k.dma("sp", out_d[l0:l0 + T2, :].rearrange("(j p) d -> p j d", p=128), ot[:], r=[ot.res], w=[R_out], dres=ot.res)
        cx.pop()

    k.wait_all("sp", [R_scr, R_mod, R_oT, R_st1, R_st2, R_y, R_h1, R_out] + R_o1 + R_y1)
    return nc


def _consts():
    p = np.arange(128)[:, None]
    f = np.arange(128)[None, :]
    U = (p <= f).astype(np.float32)
    Lo = (p >= f).astype(np.float32)
    tri = np.stack([U, Lo, -U, -Lo], axis=1)
    NEG = -30000.0
    m = lambda ok: np.where(ok, 0.0, NEG).astype(np.float32)
    msk = np.stack([m(p > f), m(p < f), m(p >= f), m(p <= f)], axis=1)
    return np.ascontiguousarray(tri), np.ascontiguousarray(msk)


TRI, MSK = _consts()


def host_prepare(inputs, core):
    b, s = core // 2, core % 2
    x = inputs["x"][b, s * NLAT:(s + 1) * NLAT]
    ctx = inputs["ctx"][b]
    if s == 1:
        x = x[::-1]
        ctx = ctx[::-1]
    xin = np.ascontiguousarray(np.concatenate([ctx, x], axis=0))
    x2 = inputs["x"][b, (1 - s) * NLAT:(2 - s) * NLAT]
    ctx2 = inputs["ctx"][b]
    if s == 0:
        x2 = x2[::-1]
        ctx2 = ctx2[::-1]
    xin2 = np.ascontiguousarray(np.concatenate([ctx2, x2], axis=0))
    cv = np.stack([inputs["c"][b], inputs["c_ctx"]], axis=-1)
    cvec = np.ascontiguousarray(cv.reshape(8, 128, 2).transpose(1, 0, 2))
    d1, d2 = (0, 1) if s == 0 else (1, 0)
    w = inputs["w_in"][0]
    offs = np.cumsum([0, 3072, 1024, 16, 16, 2048, 3072, 64, 2048])
    qkv = w[:, offs[0]:offs[1]]
    zg = w[:, offs[1]:offs[2]]
    a_ = w[:, offs[2]:offs[3]].reshape(D, 2, 8)
    b_ = w[:, offs[3]:offs[4]].reshape(D, 2, 8)
    zs = w[:, offs[4]:offs[5]]
    xbc = w[:, offs[5]:offs[6]]
    dt_ = w[:, offs[6]:offs[7]].reshape(D, 2, 32)
    gate = w[:, offs[7]:offs[8]]
    z16 = np.zeros((D, 16), np.float32)
    small = np.concatenate([a_[:, d1], a_[:, d2], z16, b_[:, d1], b_[:, d2], z16, dt_[:, d1], dt_[:, d2]], axis=1)
    w_perm = np.ascontiguousarray(np.concatenate([qkv, xbc, gate, small, zg, zs], axis=1))
    assert w_perm.shape[1] == W_IN_COLS
    cw = np.concatenate([inputs["gdn_conv_w"][0], inputs["ssm_conv_w"][0]], axis=1)
    cbias = np.concatenate([inputs["gdn_conv_b"][0], inputs["ssm_conv_b"][0]], axis=0)
    mk = lambda w_: np.ascontiguousarray(np.concatenate([w_, cbias[None]], axis=0).reshape(4, 48, 128).transpose(2, 1, 0))
    convp = mk(cw[::-1] if s == 1 else cw)
    convp2 = mk(cw[::-1] if s == 0 else cw)
    smallp = np.zeros((128, 4), np.float32)
    smallp[:, 0] = 1.0
    smallp[32:64, 0] = -1.0
    gb = inputs["gdn_dt_bias"][0]
    sbias = inputs["ssm_dt_bias"][0]
    smallp[0:8, 1] = gb[d1]; smallp[8:16, 1] = gb[d2]
    smallp[64:96, 1] = sbias[d1]; smallp[96:128, 1] = sbias[d2]
    ga = inputs["gdn_a_log"][0]
    smallp[0:8, 2] = ga[d1]; smallp[8:16, 2] = ga[d2]
    smallp[:, 3] = -1.0
    smallp[64:128, 3] = 1.0
    n1w = np.ascontiguousarray(inputs["norm1_w"][0].reshape(8, 128).T)
    return {
        "xin": xin, "xin2": xin2, "convp2": convp2, "cvec": cvec, "ada_w": np.ascontiguousarray(inputs["ada_w"][0]),
        "ada_b": np.ascontiguousarray(inputs["ada_b"][0][None]), "w_in": w_perm, "convp": convp,
        "smallp": smallp, "n1w": n1w, "ident": np.eye(128, dtype=np.float32),
        "tri": TRI, "msk": MSK,
        "w_brg": np.ascontiguousarray(inputs["w_br_gdn"][0]), "w_brs": np.ascontiguousarray(inputs["w_br_ssm"][0]),
        "w_o": np.ascontiguousarray(inputs["w_out"][0]), "w_fi": np.ascontiguousarray(inputs["w_ffn_in"][0]),
        "w_fo": np.ascontiguousarray(inputs["w_ffn_out"][0]),
        "gnw": np.ascontiguousarray(inputs["gdn_norm_w"][0].reshape(128, 1)),
        "dskip": np.ascontiguousarray(np.repeat(inputs["ssm_d"][0], 64)[None]),
        "snw": np.ascontiguousarray(inputs["ssm_norm_w"][0][None]),
        "n2w": np.ascontiguousarray(inputs["norm2_w"][0].reshape(8, 128).T),
        "nfw": np.ascontiguousarray(inputs["norm_f_w"][None]),
        "salog": np.ascontiguousarray(np.broadcast_to(inputs["ssm_a_log"][0][[d1, d2]][None], (128, 2, 32))),
    }


def kernel(**inputs):
    inputs = {k_: np.asarray(v) for k_, v in inputs.items()}
    nc = build_program()
    in_maps = [host_prepare(inputs, c) for c in range(8)]
    res = run_bass_kernel_spmd(nc, in_maps, core_ids=list(range(8)))
    out = np.zeros((4, 8192, D), np.float32)
    for c in range(8):
        b, s = c // 2, c % 2
        o = res.results[c]["out"]
        if s == 1:
            o = o[::-1]
        out[b, s * NLAT:(s + 1) * NLAT] = o
    return out
```

```python
import numpy as np
import ml_dtypes
from contextlib import ExitStack
import concourse.bass as bass
import concourse.mybir as mybir
from concourse.bass_utils import run_bass_kernel_spmd

F32 = mybir.dt.float32
BF16 = mybir.dt.bfloat16
AF = mybir.ActivationFunctionType
ALU = mybir.AluOpType
AX = mybir.AxisListType

D = 1024
NLAT = 4096
NCTX = 256
NTOK = NLAT + NCTX
C = 128
NCH = NTOK // C
EPS = 1e-6
DFF = 2816
O_QKV, O_XBC, O_GATE, O_SMALL, O_ZG, O_ZS = 0, 3072, 6144, 8192, 8320, 9344
W_IN_COLS = 11392
SEM_LIMIT = 30000
GDN_LIMIT = 0


class Sem:
    __slots__ = ("h", "count", "dma", "name")

    def __init__(self, h, dma, name):
        self.h, self.count, self.dma, self.name = h, 0, dma, name


class Res:
    __slots__ = ("name", "w", "r", "dsem")

    def __init__(self, name):
        self.name, self.w, self.r, self.dsem = name, None, {}, None


class Eng:
    def __init__(self, name, e, same_wait):
        self.name, self.e, self.sem, self.waited, self.same_wait = name, e, None, {}, same_wait
        self.nsem = 0


class K:
    def __init__(self, nc):
        self.nc = nc
        self.eng = {
            "pe": Eng("pe", nc.tensor, False),
            "act": Eng("act", nc.scalar, True),
            "dve": Eng("dve", nc.vector, True),
            "pool": Eng("pool", nc.gpsimd, True),
            "sp": Eng("sp", nc.sync, False),
        }
        self.nsem = 0
        self.ninst = 0
        self.all_sems = []
        self.free_dma = []

    def new_sem(self, dma, name):
        if dma and self.free_dma:
            self.free_dma.sort(key=lambda x: x.count)
            return self.free_dma.pop(0)
        self.nsem += 1
        h = self.nc.alloc_semaphore(f"s{self.nsem}_{name}")
        sm = Sem(h, dma, name)
        self.all_sems.append(sm)
        return sm

    def res(self, name):
        return Res(name)

    def _cur_sem(self, E):
        if E.sem is None or E.sem.count >= SEM_LIMIT:
            E.nsem += 1
            E.sem = self.new_sem(False, f"{E.name}{E.nsem}")
        return E.sem

    def _wait(self, E, evs):
        need = {}
        for (sem, val) in evs:
            if sem.dma:
                val = sem.count
            if need.get(sem, 0) < val:
                need[sem] = val
        for sem, val in need.items():
            if (not E.same_wait) and (sem is E.sem) and not sem.dma:
                continue
            if E.waited.get(sem, 0) >= val:
                continue
            E.e.wait_ge(sem.h, val)
            E.waited[sem] = val

    def _deps(self, r, w):
        evs = []
        for x in r:
            if x.w is not None:
                evs.append(x.w)
        for x in w:
            if x.w is not None:
                evs.append(x.w)
            evs.extend(x.r.items())
        return evs

    def _record(self, ev, r, w):
        sem, val = ev
        for x in r:
            if x.r.get(sem, 0) < val:
                x.r[sem] = val
        for x in w:
            x.w = ev
            x.r = {}

    def ins(self, en, fn, r=(), w=()):
        E = self.eng[en]
        self._wait(E, self._deps(r, w))
        inst = fn()
        sem = self._cur_sem(E)
        sem.count += 1
        inst.then_inc(sem.h, 1)
        self._record((sem, sem.count), r, w)
        self.ninst += 1
        return inst

    def group(self, en, fns, r=(), w=()):
        E = self.eng[en]
        self._wait(E, self._deps(r, w))
        inst = None
        for fn in fns:
            inst = fn()
        sem = self._cur_sem(E)
        sem.count += 1
        inst.then_inc(sem.h, 1)
        self._record((sem, sem.count), r, w)
        self.ninst += len(fns)
        return inst

    def dma(self, q, out, in_, r=(), w=(), dres=None):
        E = self.eng[q]
        self._wait(E, self._deps(r, w))
        if dres.dsem is None:
            dres.dsem = self.new_sem(True, "d" + dres.name)
        sem = dres.dsem
        inst = E.e.dma_start(out=out, in_=in_)
        sem.count += 16
        inst.then_inc(sem.h, 16)
        self._record((sem, sem.count), r, w)
        self.ninst += 1
        return inst

    def dmaop(self, q, fn, r=(), w=(), dres=None):
        E = self.eng[q]
        self._wait(E, self._deps(r, w))
        if dres.dsem is None:
            dres.dsem = self.new_sem(True, "d" + dres.name)
        sem = dres.dsem
        inst = fn()
        sem.count += 16
        inst.then_inc(sem.h, 16)
        self._record((sem, sem.count), r, w)
        self.ninst += 1
        return inst

    def barrier(self):
        sems = list(self.all_sems)
        for E in self.eng.values():
            for sem in sems:
                if sem.count == 0 or E.waited.get(sem, 0) >= sem.count:
                    continue
                if (sem is E.sem) and not E.same_wait:
                    continue
                E.e.wait_ge(sem.h, sem.count)
                E.waited[sem] = sem.count

    def wait_all(self, en, ress):
        E = self.eng[en]
        evs = []
        for x in ress:
            if x.w is not None:
                evs.append(x.w)
            evs.extend(x.r.items())
        self._wait(E, evs)


class Buf:
    def __init__(self, k, t, name):
        self.t, self.res, self.name = t, k.res(name), name

    def __getitem__(self, idx):
        return self.t[idx]


class Ring:
    def __init__(self, bufs):
        self.bufs, self.i = bufs, 0

    def next(self):
        b = self.bufs[self.i % len(self.bufs)]
        self.i += 1
        return b


class Ctx:
    def __init__(self, nc):
        self.nc = nc
        self.k = K(nc)
        self.stack = [ExitStack()]
        self.uid = 0
        self.phase_bufs = [[]]

    def push(self):
        self.stack.append(ExitStack())
        self.phase_bufs.append([])

    def pop(self):
        self.k.barrier()
        for b in self.phase_bufs.pop():
            if b.res.dsem is not None:
                self.k.free_dma.append(b.res.dsem)
                b.res.dsem = None
        self.stack.pop().close()

    def sb(self, name, shape, dtype):
        self.uid += 1
        t = self.stack[-1].enter_context(self.nc.sbuf_tensor(f"sb{self.uid}_{name}", list(shape), dtype))
        b = Buf(self.k, t, name)
        self.phase_bufs[-1].append(b)
        return b

    def psum(self, name, shape, dtype):
        self.uid += 1
        t = self.stack[-1].enter_context(self.nc.psum_tensor(f"ps{self.uid}_{name}", list(shape), dtype))
        return Buf(self.k, t, name)

    def sbring(self, name, n, shape, dtype):
        return Ring([self.sb(f"{name}{i}", shape, dtype) for i in range(n)])

    def dram(self, name, shape, dtype, kind="Internal"):
        t = self.nc.dram_tensor(name, list(shape), dtype, kind=kind)
        return t.ap()


def build_program(debug=False, phases=("0", "A", "R", "B1", "S1", "B2", "S2", "C1", "C2"), n_cores=8):
    nc = bass.Bass("TRN2", target_bir_lowering=False)
    cx = Ctx(nc)
    k = cx.k
    dk = "ExternalOutput" if debug else "Internal"

    xin = cx.dram("xin", [NTOK, D], F32, "ExternalInput")
    cvec = cx.dram("cvec", [128, 8, 2], F32, "ExternalInput")
    ada_w = cx.dram("ada_w", [D, 6 * D], F32, "ExternalInput")
    ada_b = cx.dram("ada_b", [1, 6 * D], F32, "ExternalInput")
    w_in = cx.dram("w_in", [D, W_IN_COLS], F32, "ExternalInput")
    xin2 = cx.dram("xin2", [NTOK, D], F32, "ExternalInput")
    convp2 = cx.dram("convp2", [128, 48, 4], F32, "ExternalInput")
    convp = cx.dram("convp", [128, 48, 4], F32, "ExternalInput")
    smallp = cx.dram("smallp", [128, 4], F32, "ExternalInput")
    n1w = cx.dram("n1w", [128, 8], F32, "ExternalInput")
    ident_d = cx.dram("ident", [128, 128], F32, "ExternalInput")
    wbg_d = cx.dram("w_brg", [1024, 1024], F32, "ExternalInput")
    wbs_d = cx.dram("w_brs", [2048, 1024], F32, "ExternalInput")
    wo_d = cx.dram("w_o", [1024, 1024], F32, "ExternalInput")
    wfi_d = cx.dram("w_fi", [1024, 2 * DFF], F32, "ExternalInput")
    wfo_d = cx.dram("w_fo", [DFF, 1024], F32, "ExternalInput")
    gnw_d = cx.dram("gnw", [128, 1], F32, "ExternalInput")
    dskip_d = cx.dram("dskip", [1, 2048], F32, "ExternalInput")
    snw_d = cx.dram("snw", [1, 2048], F32, "ExternalInput")
    n2w_d = cx.dram("n2w", [128, 8], F32, "ExternalInput")
    nfw_d = cx.dram("nfw", [1, 1024], F32, "ExternalInput")
    salog_d = cx.dram("salog", [128, 2, 32], F32, "ExternalInput")
    tri_d = cx.dram("tri", [128, 4, 128], F32, "ExternalInput")
    msk_d = cx.dram("msk", [128, 4, 128], F32, "ExternalInput")
    out_d = cx.dram("out", [NLAT, D], F32, "ExternalOutput")

    mod_d = cx.dram("mod_d", [2, 6 * D], F32, dk)
    qT_d = cx.dram("qT_d", [8, 128, NTOK], BF16, dk)
    kT_d = cx.dram("kT_d", [8, 128, NTOK], BF16, dk)
    k_d = cx.dram("k_d", [NTOK, 1024], BF16, dk)
    v_d = cx.dram("v_d", [NTOK, 1024], BF16, dk)
    x_d = cx.dram("x_d", [NTOK, 2048], BF16, dk)
    BT_d = cx.dram("BT_d", [4, 128, NTOK], BF16, dk)
    CT_d = cx.dram("CT_d", [4, 128, NTOK], BF16, dk)
    B_d = cx.dram("B_d", [NTOK, 512], BF16, dk)
    sm_d = cx.dram("sm_d", [NTOK, 128], F32, dk)
    kT2_d = cx.dram("kT2_d", [8, 128, NTOK], BF16)
    k2_d = cx.dram("k2_d", [NTOK, 1024], BF16)
    v2_d = cx.dram("v2_d", [NTOK, 1024], BF16)
    x2_d = cx.dram("x2_d", [NTOK, 2048], BF16)
    B2_d = cx.dram("B2_d", [NTOK, 512], BF16)
    sm2_d = cx.dram("sm2_d", [NTOK, 128], F32)
    sgT_d = cx.dram("sgT_d", [16, 128, NLAT], F32, dk)
    aT_d = cx.dram("aT_d", [8, 128, NLAT], BF16, dk)
    o1_d = cx.dram("o1_d", [8, 128, NLAT], F32, dk)
    oT_d = cx.dram("oT_d", [8, 128, NLAT], F32, dk)
    st1_d = cx.dram("st1_d", [128, 3072], F32)
    st2_d = cx.dram("st2_d", [128, 3072], F32)
    y1_d = cx.dram("y1_d", [NLAT, 2048], F32, dk)
    y_d = cx.dram("y_d", [NLAT, 2048], F32, dk)
    R_y1 = [k.res(f"y1_{c}") for c in range(NCH)]
    R_y = k.res("y")
    R_o1 = [k.res(f"o1_{c}") for c in range(NCH)]
    R_oT = k.res("oT")
    R_st1 = k.res("st1")
    R_st2 = k.res("st2")
    h1_d = cx.dram("h1_d", [NLAT, D], F32, dk)
    R_h1 = k.res("h1")
    R_out = k.res("out")
    R_scr = k.res("scratchA")
    R_mod = k.res("mod_d")

    ident = cx.sb("ident", [128, 128], F32)
    identb = cx.sb("identb", [128, 128], BF16)
    onesf = cx.sb("onesf", [128, 128], F32)
    k.dma("sp", ident[:], ident_d, w=[ident.res], dres=ident.res)
    k.ins("dve", lambda: nc.vector.tensor_copy(identb[:], ident[:]), r=[ident.res], w=[identb.res])
    k.ins("dve", lambda: nc.vector.memset(onesf[:], 1.0), w=[onesf.res])
    epsc = cx.sb("epsc", [128, 1], F32)
    k.ins("dve", lambda: nc.vector.memset(epsc[:], EPS), w=[epsc.res])


    if "0" in phases:
        cx.push()
        ps = [cx.psum(f"ps{i}", [128, 512], F32) for i in range(2)]
        cv = cx.sb("cv", [128, 8, 2], F32)
        cvs = cx.sb("cvs", [128, 8, 2], F32)
        k.dma("sp", cv[:], cvec, w=[cv.res], dres=cv.res)
        k.ins("act", lambda: nc.scalar.activation(out=cvs[:], in_=cv[:], func=AF.Exp, scale=-1.0), r=[cv.res], w=[cvs.res])
        k.ins("dve", lambda: nc.vector.tensor_scalar(out=cvs[:], in0=cvs[:], scalar1=1.0, scalar2=None, op0=ALU.add),
              r=[cvs.res], w=[cvs.res])
        k.ins("dve", lambda: nc.vector.reciprocal(out=cvs[:], in_=cvs[:]), r=[cvs.res], w=[cvs.res])
        k.ins("dve", lambda: nc.vector.tensor_tensor(out=cvs[:], in0=cvs[:], in1=cv[:], op=ALU.mult),
              r=[cvs.res, cv.res], w=[cvs.res])
        adab = cx.sb("adab", [2, 6 * D], F32)
        k.dma("sp", adab[0:1, :], ada_b, w=[adab.res], dres=adab.res)
        k.dma("sp", adab[1:2, :], ada_b, w=[adab.res], dres=adab.res)
        modsb = cx.sb("modsb", [2, 6 * D], F32)
        awring = cx.sbring("aw", 2, [128, 8, 512], F32)
        for cg in range(12):
            aw = awring.next()
            k.dma("sp", aw[:], ada_w[:, cg * 512:(cg + 1) * 512].rearrange("(kt p) n -> p kt n", p=128),
                  w=[aw.res], dres=aw.res)
            pb = ps[cg % 2]
            k.group("pe", [
                (lambda kt=kt, aw=aw, pb=pb: nc.tensor.matmul(pb[0:2, :], cvs[:, kt, :], aw[:, kt, :],
                                                               start=(kt == 0), stop=(kt == 7)))
                for kt in range(8)], r=[cvs.res, aw.res], w=[pb.res])
            k.ins("dve", lambda cg=cg, pb=pb: nc.vector.tensor_tensor(
                out=modsb[:, cg * 512:(cg + 1) * 512], in0=pb[0:2, :], in1=adab[:, cg * 512:(cg + 1) * 512],
                op=ALU.add), r=[pb.res, adab.res], w=[modsb.res])
        k.dma("sp", mod_d, modsb[:], r=[modsb.res], w=[R_mod], dres=modsb.res)
        cx.pop()

    modf = cx.sb("modf", [128, 2, 6, 8], F32)
    with nc.allow_non_contiguous_dma("small modulation vector relayout"):
        for r_ in range(2):
            k.dma("sp", modf[:, r_, :, :], mod_d[r_, :].rearrange("(j kt p) -> p j kt", p=128, kt=8),
                  r=[R_mod], w=[modf.res], dres=modf.res)
    n1 = cx.sb("n1", [128, 8], F32)
    k.dma("sp", n1[:], n1w, w=[n1.res], dres=n1.res)
    S1 = cx.sb("S1", [128, 2, 8], F32)
    for r_ in range(2):
        k.ins("dve", lambda r_=r_: nc.vector.scalar_tensor_tensor(
            out=S1[:, r_, :], in0=modf[:, r_, 1, :], scalar=1.0, in1=n1[:], op0=ALU.add, op1=ALU.mult),
            r=[modf.res, n1.res], w=[S1.res])

    if "A" in phases:
        cx.push()
        ps = [cx.psum(f"ps{i}", [128, 512], F32) for i in range(6)]
        NWC = O_ZG
        wA = cx.sb("wA", [128, 8, NWC], BF16)
        for kt in range(8):
            for c0 in range(0, NWC, 2080):
                k.dma("pool", wA[:, kt, c0:c0 + 2080], w_in[kt * 128:(kt + 1) * 128, c0:c0 + 2080],
                      w=[wA.res], dres=wA.res)
        cp = cx.sb("cp", [128, 48, 4], F32)
        k.dma("sp", cp[:], convp, w=[cp.res], dres=cp.res)
        smp = cx.sb("smp", [128, 4], F32)
        k.dma("sp", smp[:], smallp, w=[smp.res], dres=smp.res)
        smult = cx.sb("smult", [128, 1], F32)
        k.ins("act", lambda: nc.scalar.activation(out=smult[:], in_=smp[:, 2:3], func=AF.Exp),
              r=[smp.res], w=[smult.res])
        k.ins("dve", lambda: nc.vector.tensor_tensor(out=smult[:], in0=smult[:], in1=smp[:, 3:4], op=ALU.mult),
              r=[smp.res, smult.res], w=[smult.res])

        TT = 256
        xring = cx.sbring("xt", 1, [128, 2, D], F32)
        xnring = cx.sbring("xn", 1, [128, D], F32)
        junk = cx.sb("junk", [128, D], BF16)
        ssr = cx.sbring("ss", 2, [128, 2], F32)
        aTring = cx.sbring("aT", 2, [128, 8, TT], BF16)
        cring = cx.sbring("cv_", 6, [128, TT], F32)
        ering = cx.sbring("ee_", 5, [128, TT], F32)
        sfring = cx.sbring("sf_", 5, [128, TT], F32)
        sqring = cx.sbring("sq_", 3, [128, TT], F32)
        rsring = cx.sbring("rs_", 3, [128, TT], F32)
        sring = cx.sbring("so_", 8, [128, TT], BF16)
        sgring = cx.sbring("sg_", 4, [128, TT], F32)
        smring = cx.sbring("smf", 2, [128, TT], F32)
        ktok = cx.sbring("ktok", 1, [128, 2, 1024], BF16)
        vtok = cx.sbring("vtok", 1, [128, 2, 1024], BF16)
        xtok = cx.sbring("xtok", 1, [128, 2, 2048], BF16)
        btok = cx.sbring("btok", 1, [128, 2, 512], BF16)
        smtok = cx.sbring("smtok", 1, [128, 2, 128], F32)
        pst = ps[0]
        psa = Ring([ps[1], ps[2], ps[3], ps[4]])
        pss = ps[5]
        pso = Ring([cx.psum(f"pso{i}", [128, 1024], BF16) for i in range(2)])
        own = dict(qT_d=qT_d, kT_d=kT_d, k_d=k_d, v_d=v_d, x_d=x_d, BT_d=BT_d, CT_d=CT_d, B_d=B_d, sm_d=sm_d)
        par = dict(qT_d=None, kT_d=kT2_d, k_d=k2_d, v_d=v2_d, x_d=x2_d, BT_d=None, CT_d=None, B_d=B2_d, sm_d=sm2_d)
        cp2 = cx.sb("cp2", [128, 48, 4], F32)
        k.dma("sp", cp2[:], convp2, w=[cp2.res], dres=cp2.res)
        runs = [(False, xin, cp, own)]
        if "R" in phases:
            runs.append((True, xin2, cp2, par))

        def preamble(red, xsrc, ti):
            t0 = ti * TT
            is_ctx = ti == 0
            mr = 1 if is_ctx else 0
            xt = xring.next()
            k.dma("sp", xt[:], xsrc[t0:t0 + TT, :].rearrange("(j p) d -> p j d", p=128), w=[xt.res], dres=xt.res)
            ss = ssr.next()
            for j in range(2):
                k.ins("act", lambda j=j: nc.scalar.activation(out=junk[:], in_=xt[:, j, :], func=AF.Square,
                                                              accum_out=ss[:, j:j + 1]), r=[xt.res], w=[junk.res, ss.res])
            k.ins("act", lambda: nc.scalar.activation(out=ss[:], in_=ss[:], func=AF.Ln, scale=1.0 / D, bias=epsc[:, 0:1]),
                  r=[ss.res, epsc.res], w=[ss.res])
            k.ins("act", lambda: nc.scalar.activation(out=ss[:], in_=ss[:], func=AF.Exp, scale=-0.5), r=[ss.res], w=[ss.res])
            aT = aTring.next()
            for j in range(2):
                xn = xnring.next()
                k.ins("act", lambda j=j, xn=xn: nc.scalar.activation(out=xn[:], in_=xt[:, j, :], func=AF.Identity,
                                                                     scale=ss[:, j:j + 1]), r=[xt.res, ss.res], w=[xn.res])
                for half in range(2):
                    k.group("pe", [(lambda q=q, xn=xn, half=half: nc.tensor.transpose(
                        pst[:, q * 128:(q + 1) * 128], xn[:, (half * 4 + q) * 128:(half * 4 + q + 1) * 128], ident[:]))
                        for q in range(4)], r=[xn.res, ident.res], w=[pst.res])
                    for q in range(4):
                        kt = half * 4 + q
                        k.ins("act", lambda q=q, kt=kt, j=j: nc.scalar.activation(
                            out=aT[:, kt, j * 128:(j + 1) * 128], in_=pst[:, q * 128:(q + 1) * 128],
                            func=AF.Identity, scale=S1[:, mr, kt:kt + 1], bias=modf[:, mr, 0, kt:kt + 1]),
                            r=[pst.res, S1.res, modf.res], w=[aT.res])
            if not is_ctx and not red:
                l0 = t0 - NCTX
                k.dma("sp", aT_d[:, :, l0:l0 + TT].rearrange("kt p t -> p kt t"), aT[:], r=[aT.res], w=[R_scr], dres=aT.res)
            return dict(aT=aT, t0=t0, is_ctx=is_ctx, red=red)

        def make_job(tc, ct, cpt, DD, toks):
            aT, t0, is_ctx, red = tc["aT"], tc["t0"], tc["is_ctx"], tc["red"]
            l0 = t0 - NCTX
            rowlen = 256 if is_ctx else 64
            J = {}
            steps = []

            def s_mm():
                J["pa"] = pa = psa.next()
                k.group("pe", [(lambda kt=kt: nc.tensor.matmul(pa[:, 0:TT], wA[:, kt, ct * 128:(ct + 1) * 128], aT[:, kt, :],
                                                               start=(kt == 0), stop=(kt == 7))) for kt in range(8)],
                        r=[wA.res, aT.res], w=[pa.res])
            steps.append(s_mm)
            if ct < 48:
                def s_ident():
                    pa = J["pa"]
                    J["cb"] = cb = cring.next()
                    k.ins("act", lambda: nc.scalar.activation(out=cb[:], in_=pa[:, 0:TT], func=AF.Identity,
                                                              scale=cpt[:, ct, 1:2], bias=cpt[:, ct, 3:4]),
                          r=[pa.res, cpt.res], w=[cb.res])

                def s_taps():
                    pa, cb = J["pa"], J["cb"]
                    pv = pa[:, 0:TT].rearrange("p (r t) -> p r t", t=rowlen)
                    cv3 = cb[:].rearrange("p (r t) -> p r t", t=rowlen)
                    k.ins("dve", lambda: nc.vector.scalar_tensor_tensor(
                        out=cv3[:, :, 1:], in0=pv[:, :, 0:rowlen - 1], scalar=cpt[:, ct, 0:1], in1=cv3[:, :, 1:],
                        op0=ALU.mult, op1=ALU.add), r=[pa.res, cpt.res, cb.res], w=[cb.res])
                    k.ins("dve", lambda: nc.vector.scalar_tensor_tensor(
                        out=cv3[:, :, 0:rowlen - 1], in0=pv[:, :, 1:], scalar=cpt[:, ct, 2:3], in1=cv3[:, :, 0:rowlen - 1],
                        op0=ALU.mult, op1=ALU.add), r=[pa.res, cpt.res, cb.res], w=[cb.res])

                def s_exp():
                    cb = J["cb"]
                    J["ee"] = ee = ering.next()
                    k.ins("act", lambda: nc.scalar.activation(out=ee[:], in_=cb[:], func=AF.Exp, scale=-1.0), r=[cb.res], w=[ee.res])

                def s_recip():
                    ee = J["ee"]
                    k.ins("dve", lambda: nc.vector.tensor_scalar(out=ee[:], in0=ee[:], scalar1=1.0, scalar2=None, op0=ALU.add),
                          r=[ee.res], w=[ee.res])
                    k.ins("dve", lambda: nc.vector.reciprocal(out=ee[:], in_=ee[:]), r=[ee.res], w=[ee.res])

                def s_mult():
                    cb, ee = J["cb"], J["ee"]
                    if ct < 16:
                        J["sf"] = sf = sfring.next()
                        k.ins("pool", lambda: nc.gpsimd.tensor_tensor(out=sf[:], in0=cb[:], in1=ee[:], op=ALU.mult),
                              r=[cb.res, ee.res], w=[sf.res])
                    else:
                        J["so"] = so = sring.next()
                        k.ins("pool", lambda: nc.gpsimd.tensor_tensor(out=so[:], in0=cb[:], in1=ee[:], op=ALU.mult),
                              r=[cb.res, ee.res], w=[so.res])
                steps.extend([s_ident, s_taps, s_exp, s_recip, s_mult])
                if ct < 16:
                    def s_sq():
                        sf = J["sf"]
                        J["sq"] = sq = sqring.next()
                        k.ins("pool", lambda: nc.gpsimd.tensor_tensor(out=sq[:], in0=sf[:], in1=sf[:], op=ALU.mult),
                              r=[sf.res], w=[sq.res])

                    def s_sum():
                        sq = J["sq"]
                        k.ins("pe", lambda: nc.tensor.matmul(pss[:, 0:TT], onesf[:], sq[:], start=True, stop=True),
                              r=[onesf.res, sq.res], w=[pss.res])
                        J["rs"] = rs = rsring.next()
                        k.ins("act", lambda: nc.scalar.activation(out=rs[:], in_=pss[:, 0:TT], func=AF.Ln, bias=epsc[:, 0:1]),
                              r=[pss.res, epsc.res], w=[rs.res])

                    def s_rs():
                        rs = J["rs"]
                        k.ins("act", lambda: nc.scalar.activation(out=rs[:], in_=rs[:], func=AF.Exp, scale=-0.5),
                              r=[rs.res], w=[rs.res])

                    def s_norm():
                        sf, rs = J["sf"], J["rs"]
                        J["so"] = so = sring.next()
                        qscale = (128 ** -0.5) if ct < 8 else 1.0
                        k.ins("dve", lambda: nc.vector.scalar_tensor_tensor(out=so[:], in0=sf[:], scalar=qscale, in1=rs[:],
                                                                            op0=ALU.mult, op1=ALU.mult),
                              r=[sf.res, rs.res], w=[so.res])
                    steps.extend([s_sq, s_sum, s_rs, s_norm])

                def s_out():
                    so = J["so"]
                    if ct < 8:
                        k.dma("sp", DD["qT_d"][ct, :, t0:t0 + TT], so[:], r=[so.res], w=[R_scr], dres=so.res)
                    elif ct < 16:
                        k.dma("sp", DD["kT_d"][ct - 8, :, t0:t0 + TT], so[:], r=[so.res], w=[R_scr], dres=so.res)
                    elif 40 <= ct < 44 and not red:
                        k.dma("sp", DD["BT_d"][ct - 40, :, t0:t0 + TT], so[:], r=[so.res], w=[R_scr], dres=so.res)
                    elif 44 <= ct < 48:
                        k.dma("sp", DD["CT_d"][ct - 44, :, t0:t0 + TT], so[:], r=[so.res], w=[R_scr], dres=so.res)
                    tgt = None
                    if 8 <= ct < 16:
                        tgt = (toks["k"], (ct - 8) * 128)
                    elif 16 <= ct < 24:
                        tgt = (toks["v"], (ct - 16) * 128)
                    elif 24 <= ct < 40:
                        tgt = (toks["x"], (ct - 24) * 128)
                    elif 40 <= ct < 44:
                        tgt = (toks["b"], (ct - 40) * 128)
                    J["tgt"] = tgt
                    if tgt is not None:
                        J["po"] = po = pso.next()
                        k.group("pe", [(lambda j=j: nc.tensor.transpose(po[:, j * 128:(j + 1) * 128], so[:, j * 128:(j + 1) * 128],
                                                                        identb[:])) for j in range(2)],
                                r=[so.res, identb.res], w=[po.res])

                def s_tcopy():
                    if J["tgt"] is not None:
                        tb, off = J["tgt"]
                        po = J["po"]
                        k.ins("act", lambda: nc.scalar.copy(out=tb[:, :, off:off + 128],
                                                            in_=po[:, 0:256].rearrange("p (j c) -> p j c", c=128)),
                              r=[po.res], w=[tb.res])
                steps.extend([s_out, s_tcopy])
            elif ct < 64:
                def g_exp():
                    pa = J["pa"]
                    J["sg"] = sg = sgring.next()
                    k.ins("act", lambda: nc.scalar.activation(out=sg[:], in_=pa[:, 0:TT], func=AF.Exp, scale=-1.0),
                          r=[pa.res], w=[sg.res])

                def g_recip():
                    sg = J["sg"]
                    k.ins("dve", lambda: nc.vector.tensor_scalar(out=sg[:], in0=sg[:], scalar1=1.0, scalar2=None, op0=ALU.add),
                          r=[sg.res], w=[sg.res])
                    k.ins("dve", lambda: nc.vector.reciprocal(out=sg[:], in_=sg[:]), r=[sg.res], w=[sg.res])

                def g_out():
                    sg = J["sg"]
                    k.dma("sp", sgT_d[ct - 48, :, l0:l0 + TT], sg[:], r=[sg.res], w=[R_scr], dres=sg.res)
                steps.extend([g_exp, g_recip, g_out])
            else:
                st_ = toks["sm"]

                def m_exp():
                    pa = J["pa"]
                    J["sm"] = sm = smring.next()
                    k.ins("act", lambda: nc.scalar.activation(out=sm[:], in_=pa[:, 0:TT], func=AF.Exp, scale=smp[:, 0:1],
                                                              bias=smp[:, 1:2]), r=[pa.res, smp.res], w=[sm.res])

                def m_ln():
                    sm = J["sm"]
                    k.ins("act", lambda: nc.scalar.activation(out=sm[:], in_=sm[:], func=AF.Ln, bias=1.0), r=[sm.res], w=[sm.res])

                def m_mul():
                    sm = J["sm"]
                    k.ins("dve", lambda: nc.vector.tensor_scalar(out=sm[:], in0=sm[:], scalar1=smult[:, 0:1], scalar2=None,
                                                              op0=ALU.mult), r=[sm.res, smult.res], w=[sm.res])

                def m_tr():
                    sm = J["sm"]
                    k.group("pe", [(lambda j=j: nc.tensor.transpose(pst[:, j * 128:(j + 1) * 128], sm[:, j * 128:(j + 1) * 128],
                                                                    ident[:])) for j in range(2)],
                            r=[sm.res, ident.res], w=[pst.res])

                def m_copy():
                    k.ins("dve", lambda: nc.vector.tensor_copy(st_[:], pst[:, 0:256].rearrange("p (j c) -> p j c", c=128)),
                          r=[pst.res], w=[st_.res])
                steps.extend([m_exp, m_ln, m_mul, m_tr, m_copy])
            return steps

        def make_spill(tc, DD, toks):
            t0 = tc["t0"]

            def spill():
                rows = lambda d_: d_[t0:t0 + TT, :].rearrange("(j p) c -> p j c", p=128)
                for key, dn in (("k", "k_d"), ("v", "v_d"), ("x", "x_d"), ("b", "B_d"), ("sm", "sm_d")):
                    tb = toks[key]
                    k.dma("sp", rows(DD[dn]), tb[:], r=[tb.res], w=[R_scr], dres=tb.res)
            return spill

        NST = 14
        pipeline = []
        it = 0
        tiles = [(red, xsrc, cpt, DD, ti) for (red, xsrc, cpt, DD) in runs for ti in range(NTOK // TT)]
        pend_pre = {}

        def run_pipeline_until(limit):
            nonlocal it
            while it < limit:
                for (st0, steps) in pipeline:
                    sidx = it - st0
                    if 0 <= sidx < len(steps):
                        steps[sidx]()
                pipeline[:] = [(a, b) for (a, b) in pipeline if it - a < len(b) - 1]
                it += 1

        tcs = [None] * len(tiles)
        tcs[0] = preamble(tiles[0][0], tiles[0][1], tiles[0][4])
        for n, (red, xsrc, cpt, DD, ti) in enumerate(tiles):
            tc = tcs[n]
            toks = dict(k=ktok.next(), v=vtok.next(), x=xtok.next(), b=btok.next(), sm=smtok.next())
            cts = [ct for ct in range(65)
                   if not ((tc["is_ctx"] or red) and 48 <= ct < 64) and not (red and (ct < 8 or 44 <= ct < 48))]
            for idx, ct in enumerate(cts):
                pipeline.append((it, make_job(tc, ct, cpt, DD, toks)))
                run_pipeline_until(it + 1)
                if idx == 6 and n + 1 < len(tiles):
                    tcs[n + 1] = preamble(tiles[n + 1][0], tiles[n + 1][1], tiles[n + 1][4])
            pipeline.append((it + 6, [make_spill(tc, DD, toks)]))
        run_pipeline_until(it + NST + 2)
        cx.pop()


    def load_scan_consts():
        tri = cx.sb("tri", [128, 4, 128], F32)
        k.dma("sp", tri[:], tri_d, w=[tri.res], dres=tri.res)
        mskf = cx.sb("mskf", [128, 4, 128], F32)
        k.dma("sp", mskf[:], msk_d, w=[mskf.res], dres=mskf.res)
        mskb = cx.sb("mskb", [128, 4, 128], BF16)
        k.ins("dve", lambda: nc.vector.tensor_copy(mskb[:], mskf[:]), r=[mskf.res], w=[mskb.res])
        id4 = cx.sb("id4", [128, 4, 128], F32)
        for j in range(4):
            k.ins("dve", lambda j=j: nc.vector.tensor_copy(id4[:, j, :], ident[:]), r=[ident.res], w=[id4.res])
        return tri, mskb, id4

    def gdn_pass(mode):
        cx.push()
        fwd = mode != "own2"
        pi = 0 if mode == "own1" else 1
        red = mode == "red"
        s_kT, s_k, s_v, s_sm = (kT2_d, k2_d, v2_d, sm2_d) if red else (kT_d, k_d, v_d, sm_d)
        tri, mskb, id4 = load_scan_consts()
        cumL = tri[:, 0, :] if fwd else tri[:, 1, :]
        negR = tri[:, 2, :] if fwd else tri[:, 3, :]
        m_s = mskb[:, 0, :] if fwd else mskb[:, 1, :]
        mT_s = mskb[:, 1, :] if fwd else mskb[:, 0, :]
        mT_i = mskb[:, 3, :] if fwd else mskb[:, 2, :]
        g0 = pi * 8
        l0c = 32 + pi * 8
        banks = Ring([cx.psum(f"gb{i}", [128, 512], F32) for i in range(8)])
        S32 = cx.sb("S32", [128, 8, 128], F32)
        Sbf = cx.sb("Sbf", [128, 8, 128], BF16)
        if fwd:
            k.ins("pool", lambda: nc.gpsimd.memset(S32[:], 0.0), w=[S32.res])
        else:
            k.dma("sp", S32[:].rearrange("p h d -> p (h d)"), st2_d[:, 0:1024], r=[R_st2], w=[S32.res], dres=S32.res)
        k.ins("act", lambda: nc.scalar.copy(out=Sbf[:], in_=S32[:]), r=[S32.res], w=[Sbf.res])
        NB = 2
        qTr = cx.sbring("qTc", NB, [128, 8, 128], BF16)
        kTr = cx.sbring("kTc", NB, [128, 8, 128], BF16)
        kr = cx.sbring("kc", NB, [128, 8, 128], BF16)
        vr = cx.sbring("vc", NB, [128, 8, 128], BF16)
        smr = cx.sbring("smc", NB, [128, 128], F32)
        o1r = cx.sbring("o1c", NB, [128, 8, 128], F32)
        kbgr = cx.sbring("kbg", NB, [128, 8, 128], BF16)
        vbr = cx.sbring("vb", NB, [128, 8, 128], BF16)
        kdr = cx.sbring("kd", NB, [128, 8, 128], BF16)
        smallr = cx.sbring("gsm", NB, [128, 6, 8], F32)
        eglr = cx.sbring("egl", NB, [128, 8], F32)
        expr = cx.sbring("exps", NB, [128, 3, 8], F32)
        Ear = cx.sbring("Ea", 2, [128, 4, 128], F32)
        Ebr = cx.sbring("Eb", 2, [128, 4, 128], F32)
        Ecr = cx.sbring("Ec", 2, [128, 4, 128], F32)
        Edr = cx.sbring("Ed", 2, [128, 4, 128], F32)
        Pr = cx.sbring("Pp", 4, [128, 4, 128], F32)
        PTr = cx.sbring("PTp", 4, [128, 4, 128], F32)
        Yr = cx.sbring("Yp", 4, [128, 4, 128], F32)
        Yfr = cx.sbring("Yf", 2 * NB, [128, 4, 128], BF16)
        nWTr = cx.sbring("nWT", 2 * NB, [128, 4, 128], BF16)
        attr = cx.sbring("att", 2 * NB, [128, 4, 128], BF16)
        qgr = cx.sbring("qg", 2 * NB, [128, 4, 128], BF16)
        vnr = cx.sbring("vn", 2, [128, 4, 128], BF16)
        oTr = cx.sbring("oTs", 2, [128, 8, 128], F32)

        def prep(c):
            lat = c >= 2 and not red
            t0 = c * C
            l0 = t0 - NCTX
            qT_c = qTr.next(); kT_c = kTr.next(); k_c = kr.next(); v_c = vr.next(); sm_c = smr.next()
            if lat:
                k.dma("sp", qT_c[:], qT_d[:, :, t0:t0 + C].rearrange("h p t -> p h t"), r=[R_scr], w=[qT_c.res], dres=qT_c.res)
            k.dma("sp", kT_c[:], s_kT[:, :, t0:t0 + C].rearrange("h p t -> p h t"), r=[R_scr], w=[kT_c.res], dres=kT_c.res)
            k.dma("sp", k_c[:], s_k[t0:t0 + C, :].rearrange("t (h d) -> t h d", d=128), r=[R_scr], w=[k_c.res], dres=k_c.res)
            k.dma("sp", v_c[:], s_v[t0:t0 + C, :].rearrange("t (h d) -> t h d", d=128), r=[R_scr], w=[v_c.res], dres=v_c.res)
            k.dma("sp", sm_c[:], s_sm[t0:t0 + C, :], r=[R_scr], w=[sm_c.res], dres=sm_c.res)
            o1_c = None
            if lat and not fwd:
                o1_c = o1r.next()
                k.dma("sp", o1_c[:], o1_d[:, :, l0:l0 + C].rearrange("h p t -> p h t"), r=[R_o1[c]], w=[o1_c.res], dres=o1_c.res)
            gcols = sm_c[:, g0:g0 + 8]
            lcols = sm_c[:, l0c:l0c + 8]
            pb = banks.next()
            k.group("pe", [
                lambda: nc.tensor.matmul(pb[:, 0:8], cumL, gcols, start=True, stop=True),
                lambda: nc.tensor.matmul(pb[:, 8:16], onesf[:], gcols, start=True, stop=True)],
                r=[tri.res, onesf.res, sm_c.res], w=[pb.res])
            sm6 = smallr.next()
            gc, gcl, ngc, tmp = sm6[:, 0, :], sm6[:, 1, :], sm6[:, 2, :], sm6[:, 3, :]
            k.ins("dve", lambda: nc.vector.tensor_copy(gc, pb[:, 0:8]), r=[pb.res], w=[sm6.res])
            k.ins("dve", lambda: nc.vector.tensor_tensor(out=gcl, in0=gc, in1=lcols, op=ALU.add), r=[sm6.res, sm_c.res], w=[sm6.res])
            k.ins("dve", lambda: nc.vector.tensor_scalar(out=ngc, in0=gc, scalar1=-1.0, scalar2=None, op0=ALU.mult),
                  r=[sm6.res], w=[sm6.res])
            k.ins("dve", lambda: nc.vector.tensor_tensor(out=tmp, in0=pb[:, 8:16], in1=gc, op=ALU.subtract),
                  r=[pb.res, sm6.res], w=[sm6.res])
            ex = expr.next()
            egl = eglr.next()
            k.ins("act", lambda: nc.scalar.activation(out=ex[:, 0, :], in_=gcl, func=AF.Exp), r=[sm6.res], w=[ex.res])
            k.ins("act", lambda: nc.scalar.activation(out=ex[:, 1, :], in_=lcols, func=AF.Exp), r=[sm_c.res], w=[ex.res])
            k.ins("act", lambda: nc.scalar.activation(out=ex[:, 2, :], in_=tmp, func=AF.Exp), r=[sm6.res], w=[ex.res])
            k.ins("act", lambda: nc.scalar.activation(out=egl[:], in_=pb[:, 8:16], func=AF.Exp), r=[pb.res], w=[egl.res])
            kbg = kbgr.next(); vb = vbr.next(); kd = kdr.next()
            bc = lambda col: ex[:, col, :].unsqueeze(2).to_broadcast([128, 8, 128])
            k.ins("pool", lambda: nc.gpsimd.tensor_tensor(out=kbg[:], in0=k_c[:], in1=bc(0), op=ALU.mult),
                  r=[k_c.res, ex.res], w=[kbg.res])
            k.ins("pool", lambda: nc.gpsimd.tensor_tensor(out=vb[:], in0=v_c[:], in1=bc(1), op=ALU.mult),
                  r=[v_c.res, ex.res], w=[vb.res])
            k.ins("pool", lambda: nc.gpsimd.tensor_tensor(out=kd[:], in0=k_c[:], in1=bc(2), op=ALU.mult),
                  r=[k_c.res, ex.res], w=[kd.res])
            groups = []
            for gi in range(2):
                h0 = gi * 4
                gb = lambda h: sm_c[:, g0 + h:g0 + h + 1].to_broadcast([128, 128])
                lb = lambda h: sm_c[:, l0c + h:l0c + h + 1].to_broadcast([128, 128])
                KK = banks.next()
                k.group("pe", [(lambda j=j: nc.tensor.matmul(KK[:, j * 128:(j + 1) * 128], kT_c[:, h0 + j, :], kT_c[:, h0 + j, :],
                                                             start=True, stop=True)) for j in range(4)],
                        r=[kT_c.res], w=[KK.res])
                Da = banks.next()
                fl = []
                for j in range(4):
                    fl.append(lambda j=j: nc.tensor.matmul(Da[:, j * 128:(j + 1) * 128], gb(h0 + j), negR, start=True, stop=False))
                    fl.append(lambda j=j: nc.tensor.matmul(Da[:, j * 128:(j + 1) * 128], identb[:], m_s, start=False, stop=True))
                k.group("pe", fl, r=[sm_c.res, tri.res, identb.res, mskb.res], w=[Da.res])
                Ea = Ear.next()
                for j in range(4):
                    k.ins("act", lambda j=j: nc.scalar.activation(out=Ea[:, j, :], in_=Da[:, j * 128:(j + 1) * 128], func=AF.Exp,
                                                                  bias=sm6[:, 1, h0 + j:h0 + j + 1]),
                          r=[Da.res, sm6.res], w=[Ea.res])
                Db = banks.next()
                fl = []
                for j in range(4):
                    sl = slice(j * 128, (j + 1) * 128)
                    fl.append(lambda j=j, sl=sl: nc.tensor.matmul(Db[:, sl], gb(h0 + j), cumL, start=True, stop=False))
                    fl.append(lambda j=j, sl=sl: nc.tensor.matmul(Db[:, sl], lb(h0 + j), ident[:], start=False, stop=False))
                    fl.append(lambda j=j, sl=sl: nc.tensor.matmul(Db[:, sl], identb[:], mT_s, start=False, stop=True))
                k.group("pe", fl, r=[sm_c.res, tri.res, ident.res, identb.res, mskb.res], w=[Db.res])
                Eb = Ebr.next()
                for j in range(4):
                    k.ins("act", lambda j=j: nc.scalar.activation(out=Eb[:, j, :], in_=Db[:, j * 128:(j + 1) * 128], func=AF.Exp,
                                                                  bias=sm6[:, 2, h0 + j:h0 + j + 1]),
                          r=[Db.res, sm6.res], w=[Eb.res])
                P0 = Pr.next(); P0T = PTr.next()
                KK3 = KK[:].rearrange("p (j c) -> p j c", c=128)
                k.ins("dve", lambda: nc.vector.scalar_tensor_tensor(out=P0[:], in0=KK3, scalar=-1.0, in1=Ea[:],
                                                                    op0=ALU.mult, op1=ALU.mult),
                      r=[KK.res, Ea.res], w=[P0.res])
                k.ins("dve", lambda: nc.vector.scalar_tensor_tensor(out=P0T[:], in0=KK3, scalar=-1.0, in1=Eb[:],
                                                                    op0=ALU.mult, op1=ALU.mult),
                      r=[KK.res, Eb.res], w=[P0T.res])
                att = None; qg = None
                if lat:
                    QK = banks.next()
                    k.group("pe", [(lambda j=j: nc.tensor.matmul(QK[:, j * 128:(j + 1) * 128], kT_c[:, h0 + j, :], qT_c[:, h0 + j, :],
                                                                 start=True, stop=True)) for j in range(4)],
                            r=[kT_c.res, qT_c.res], w=[QK.res])
                    Dc = banks.next()
                    fl = []
                    for j in range(4):
                        sl = slice(j * 128, (j + 1) * 128)
                        fl.append(lambda j=j, sl=sl: nc.tensor.matmul(Dc[:, sl], gb(h0 + j), cumL, start=True, stop=False))
                        fl.append(lambda j=j, sl=sl: nc.tensor.matmul(Dc[:, sl], identb[:], mT_i, start=False, stop=True))
                    k.group("pe", fl, r=[sm_c.res, tri.res, identb.res, mskb.res], w=[Dc.res])
                    Ec = Ecr.next()
                    for j in range(4):
                        k.ins("act", lambda j=j: nc.scalar.activation(out=Ec[:, j, :], in_=Dc[:, j * 128:(j + 1) * 128], func=AF.Exp,
                                                                      bias=sm6[:, 2, h0 + j:h0 + j + 1]),
                              r=[Dc.res, sm6.res], w=[Ec.res])
                    Dd = banks.next()
                    k.group("pe", [(lambda j=j: nc.tensor.matmul(Dd[:, j * 128:(j + 1) * 128], gb(h0 + j), cumL, start=True, stop=True))
                                   for j in range(4)], r=[sm_c.res, tri.res], w=[Dd.res])
                    Ed = Edr.next()
                    k.ins("act", lambda: nc.scalar.activation(out=Ed[:].rearrange("p j c -> p (j c)"), in_=Dd[:], func=AF.Exp),
                          r=[Dd.res], w=[Ed.res])
                    att = attr.next()
                    k.ins("dve", lambda: nc.vector.tensor_tensor(out=att[:], in0=QK[:].rearrange("p (j c) -> p j c", c=128),
                                                                 in1=Ec[:], op=ALU.mult), r=[QK.res, Ec.res], w=[att.res])
                    qg = qgr.next()
                    k.ins("pool", lambda: nc.gpsimd.tensor_tensor(out=qg[:], in0=qT_c[:, h0:h0 + 4, :], in1=Ed[:], op=ALU.mult),
                          r=[qT_c.res, Ed.res], w=[qg.res])
                Y = Yr.next()
                k.ins("pool", lambda: nc.gpsimd.tensor_tensor(out=Y[:], in0=P0T[:], in1=id4[:], op=ALU.add),
                      r=[P0T.res, id4.res], w=[Y.res])
                Pp, PTp = P0, P0T
                for lev in range(1, 7):
                    last = lev == 6
                    Pb = banks.next()
                    k.group("pe", [(lambda j=j, Pp=Pp, PTp=PTp, Pb=Pb: nc.tensor.matmul(
                        Pb[:, j * 128:(j + 1) * 128], PTp[:, j, :], Pp[:, j, :], start=True, stop=True)) for j in range(4)],
                        r=[Pp.res, PTp.res], w=[Pb.res])
                    Pn = Pr.next()
                    k.ins("act", lambda Pn=Pn, Pb=Pb: nc.scalar.copy(out=Pn[:].rearrange("p j c -> p (j c)"), in_=Pb[:]),
                          r=[Pb.res], w=[Pn.res])
                    PTn = None
                    if not last:
                        PTb = banks.next()
                        k.group("pe", [(lambda j=j, Pp=Pp, PTp=PTp, PTb=PTb: nc.tensor.matmul(
                            PTb[:, j * 128:(j + 1) * 128], Pp[:, j, :], PTp[:, j, :], start=True, stop=True)) for j in range(4)],
                            r=[Pp.res, PTp.res], w=[PTb.res])
                        PTn = PTr.next()
                        k.ins("act", lambda PTn=PTn, PTb=PTb: nc.scalar.copy(out=PTn[:].rearrange("p j c -> p (j c)"), in_=PTb[:]),
                              r=[PTb.res], w=[PTn.res])
                    Yb = banks.next()
                    k.group("pe", [(lambda j=j, Yb=Yb, Y=Y, Pn=Pn: nc.tensor.matmul(
                        Yb[:, j * 128:(j + 1) * 128], Pn[:, j, :], Y[:, j, :], start=True, stop=True)) for j in range(4)],
                        r=[Y.res, Pn.res], w=[Yb.res])
                    if last:
                        Yn = Yfr.next()
                        k.ins("dve", lambda Yn=Yn, Yb=Yb, Y=Y: nc.vector.tensor_tensor(
                            out=Yn[:].rearrange("p j c -> p (j c)"), in0=Yb[:], in1=Y[:].rearrange("p j c -> p (j c)"), op=ALU.add),
                            r=[Yb.res, Y.res], w=[Yn.res])
                    else:
                        Yn = Yr.next()
                        k.ins("dve", lambda Yn=Yn, Yb=Yb, Y=Y: nc.vector.tensor_tensor(
                            out=Yn[:].rearrange("p j c -> p (j c)"), in0=Yb[:], in1=Y[:].rearrange("p j c -> p (j c)"), op=ALU.add),
                            r=[Yb.res, Y.res], w=[Yn.res])
                    Y = Yn
                    Pp, PTp = Pn, PTn
                Wb = banks.next()
                k.group("pe", [(lambda j=j: nc.tensor.matmul(Wb[:, j * 128:(j + 1) * 128], kbg[:, h0 + j, :], Y[:, j, :],
                                                             start=True, stop=True)) for j in range(4)],
                        r=[kbg.res, Y.res], w=[Wb.res])
                nWT = nWTr.next()
                k.ins("act", lambda: nc.scalar.activation(out=nWT[:].rearrange("p j c -> p (j c)"), in_=Wb[:], func=AF.Identity,
                                                          scale=-1.0), r=[Wb.res], w=[nWT.res])
                groups.append((Y, nWT, att, qg))
            return dict(c=c, lat=lat, groups=groups, vb=vb, kd=kd, egl=egl, o1=o1_c)

        def chain(pp):
            c, lat = pp["c"], pp["lat"]
            l0 = c * C - NCTX
            vb, kd, egl = pp["vb"], pp["kd"], pp["egl"]
            oTs = oTr.next() if lat else None
            for gi in range(2):
                h0 = gi * 4
                Y, nWT, att, qg = pp["groups"][gi]
                VN = banks.next()
                fl = []
                for j in range(4):
                    sl = slice(j * 128, (j + 1) * 128)
                    fl.append(lambda j=j, sl=sl: nc.tensor.matmul(VN[:, sl], Y[:, j, :], vb[:, h0 + j, :], start=True, stop=False))
                    fl.append(lambda j=j, sl=sl: nc.tensor.matmul(VN[:, sl], nWT[:, j, :], Sbf[:, h0 + j, :], start=False, stop=True))
                k.group("pe", fl, r=[Y.res, vb.res, nWT.res, Sbf.res], w=[VN.res])
                vn = vnr.next()
                k.ins("dve", lambda: nc.vector.tensor_copy(vn[:].rearrange("p j c -> p (j c)"), VN[:]), r=[VN.res], w=[vn.res])
                if lat:
                    OT = banks.next()
                    fl = []
                    for j in range(4):
                        sl = slice(j * 128, (j + 1) * 128)
                        fl.append(lambda j=j, sl=sl: nc.tensor.matmul(OT[:, sl], Sbf[:, h0 + j, :], qg[:, j, :], start=True, stop=False))
                        fl.append(lambda j=j, sl=sl: nc.tensor.matmul(OT[:, sl], vn[:, j, :], att[:, j, :], start=False, stop=True))
                    k.group("pe", fl, r=[Sbf.res, qg.res, vn.res, att.res], w=[OT.res])
                    osl = oTs[:, h0:h0 + 4, :].rearrange("p j c -> p (j c)")
                    if fwd:
                        k.ins("act", lambda: nc.scalar.copy(out=osl, in_=OT[:]), r=[OT.res], w=[oTs.res])
                    else:
                        o1_c = pp["o1"]
                        k.ins("dve", lambda: nc.vector.tensor_tensor(
                            out=osl, in0=OT[:], in1=o1_c[:, h0:h0 + 4, :].rearrange("p j c -> p (j c)"), op=ALU.add),
                            r=[OT.res, o1_c.res], w=[oTs.res])
                DS = banks.next()
                k.group("pe", [(lambda j=j: nc.tensor.matmul(DS[:, j * 128:(j + 1) * 128], kd[:, h0 + j, :], vn[:, j, :],
                                                             start=True, stop=True)) for j in range(4)],
                        r=[kd.res, vn.res], w=[DS.res])
                ssl = S32[:, h0:h0 + 4, :]
                k.ins("pool", lambda: nc.gpsimd.tensor_tensor(
                    out=ssl, in0=ssl, in1=egl[:, h0:h0 + 4].unsqueeze(2).to_broadcast([128, 4, 128]), op=ALU.mult),
                    r=[S32.res, egl.res], w=[S32.res])
                k.ins("dve", lambda: nc.vector.tensor_tensor(out=ssl, in0=ssl, in1=DS[:].rearrange("p (j c) -> p j c", c=128),
                                                             op=ALU.add), r=[S32.res, DS.res], w=[S32.res])
                k.ins("act", lambda: nc.scalar.copy(out=Sbf[:, h0:h0 + 4, :], in_=ssl), r=[S32.res], w=[Sbf.res])
            if lat:
                if fwd:
                    k.dma("sp", o1_d[:, :, l0:l0 + C].rearrange("h p t -> p h t"), oTs[:], r=[oTs.res], w=[R_o1[c]], dres=oTs.res)
                else:
                    k.dma("sp", oT_d[:, :, l0:l0 + C].rearrange("h p t -> p h t"), oTs[:], r=[oTs.res], w=[R_oT], dres=oTs.res)

        order = list(range(NCH)) if fwd else list(range(NCH - 1, 1, -1))
        if GDN_LIMIT:
            order = order[:GDN_LIMIT]
        pend = prep(order[0])
        for idx in range(len(order)):
            nxt = prep(order[idx + 1]) if idx + 1 < len(order) else None
            chain(pend)
            pend = nxt
        if red:
            k.dma("sp", st2_d[:, 0:1024], S32[:].rearrange("p h d -> p (h d)"), r=[S32.res], w=[R_st2], dres=S32.res)
        cx.pop()

    def ssd_pass(mode):
        cx.push()
        fwd = mode != "own2"
        pi = 0 if mode == "own1" else 1
        red = mode == "red"
        s_x, s_B, s_sm = (x2_d, B2_d, sm2_d) if red else (x_d, B_d, sm_d)
        tri, mskb, id4 = load_scan_consts()
        cumL = tri[:, 0, :] if fwd else tri[:, 1, :]
        mT_i = mskb[:, 3, :] if fwd else mskb[:, 2, :]
        d0 = 64 + pi * 32
        banks = Ring([cx.psum(f"sbk{i}", [128, 512], F32) for i in range(8)])
        aB = cx.sb("aB", [128, 32], F32)
        k.dma("sp", aB[:], salog_d[:, pi, :], w=[aB.res], dres=aB.res)
        k.ins("act", lambda: nc.scalar.activation(out=aB[:], in_=aB[:], func=AF.Exp), r=[aB.res], w=[aB.res])
        k.ins("dve", lambda: nc.vector.tensor_scalar(out=aB[:], in0=aB[:], scalar1=-1.0, scalar2=None, op0=ALU.mult),
              r=[aB.res], w=[aB.res])
        hT32 = cx.sb("hT32", [128, 32, 64], F32)
        hTbf = cx.sb("hTbf", [128, 32, 64], BF16)
        if fwd:
            k.ins("pool", lambda: nc.gpsimd.memset(hT32[:], 0.0), w=[hT32.res])
        else:
            k.dma("sp", hT32[:].rearrange("p h d -> p (h d)"), st2_d[:, 1024:3072], r=[R_st2], w=[hT32.res], dres=hT32.res)
        k.ins("act", lambda: nc.scalar.copy(out=hTbf[:], in_=hT32[:]), r=[hT32.res], w=[hTbf.res])
        xr = cx.sbring("xc", 2, [128, 32, 64], BF16)
        BTr = cx.sbring("BTc", 2, [128, 4, 128], BF16)
        CTr = cx.sbring("CTc", 2, [128, 4, 128], BF16)
        Br = cx.sbring("Bc", 2, [128, 4, 128], BF16)
        smr = cx.sbring("smc", 2, [128, 128], F32)
        y1r = cx.sbring("y1c", 2, [128, 2048], F32)
        dAr = cx.sbring("dA", 2, [128, 32], F32)
        s5r = cx.sbring("s5", 2, [128, 6, 32], F32)
        xdtr = cx.sbring("xdt", 2, [128, 32, 64], BF16)
        xdtsr = cx.sbring("xdts", 2, [128, 32, 64], BF16)
        ytr = cx.sbring("ytmp", 2, [128, 32, 64], F32)
        ycr = cx.sbring("yc", 2, [128, 2048], F32)
        segr = cx.sbring("seg", 3, [128, 4, 128], F32)
        GTr = cx.sbring("GT", 2, [128, 4, 128], F32)
        MTr = cx.sbring("MT", 3, [128, 4, 128], BF16)
        order = list(range(NCH)) if fwd else list(range(NCH - 1, 1, -1))
        if GDN_LIMIT:
            order = order[:GDN_LIMIT]
        for c in order:
            lat = c >= 2 and not red
            t0 = c * C
            l0 = t0 - NCTX
            x_c = xr.next(); BT_c = BTr.next(); CT_c = CTr.next(); B_c = Br.next(); sm_c = smr.next()
            k.dma("sp", x_c[:].rearrange("p h d -> p (h d)"), s_x[t0:t0 + C, :], r=[R_scr], w=[x_c.res], dres=x_c.res)
            if lat:
                k.dma("sp", BT_c[:], BT_d[:, :, t0:t0 + C].rearrange("g p t -> p g t"), r=[R_scr], w=[BT_c.res], dres=BT_c.res)
                k.dma("sp", CT_c[:], CT_d[:, :, t0:t0 + C].rearrange("g p t -> p g t"), r=[R_scr], w=[CT_c.res], dres=CT_c.res)
            k.dma("sp", B_c[:], s_B[t0:t0 + C, :].rearrange("t (g n) -> t g n", n=128), r=[R_scr], w=[B_c.res], dres=B_c.res)
            k.dma("sp", sm_c[:], s_sm[t0:t0 + C, :], r=[R_scr], w=[sm_c.res], dres=sm_c.res)
            y1_c = None
            if lat and not fwd:
                y1_c = y1r.next()
                k.dma("sp", y1_c[:], y1_d[l0:l0 + C, :], r=[R_y1[c]], w=[y1_c.res], dres=y1_c.res)
            dtc = sm_c[:, d0:d0 + 32]
            dA = dAr.next()
            k.ins("dve", lambda: nc.vector.tensor_tensor(out=dA[:], in0=dtc, in1=aB[:], op=ALU.mult),
                  r=[sm_c.res, aB.res], w=[dA.res])
            pb = banks.next()
            k.group("pe", [
                lambda: nc.tensor.matmul(pb[:, 0:32], cumL, dA[:], start=True, stop=True),
                lambda: nc.tensor.matmul(pb[:, 32:64], onesf[:], dA[:], start=True, stop=True)],
                r=[tri.res, onesf.res, dA.res], w=[pb.res])
            s5 = s5r.next()
            ac, nac, tmp, ea, w2, eat = (s5[:, i, :] for i in range(6))
            k.ins("dve", lambda: nc.vector.tensor_copy(ac, pb[:, 0:32]), r=[pb.res], w=[s5.res])
            k.ins("dve", lambda: nc.vector.tensor_scalar(out=nac, in0=ac, scalar1=-1.0, scalar2=None, op0=ALU.mult),
                  r=[s5.res], w=[s5.res])
            k.ins("dve", lambda: nc.vector.tensor_tensor(out=tmp, in0=pb[:, 32:64], in1=ac, op=ALU.subtract),
                  r=[pb.res, s5.res], w=[s5.res])
            k.ins("act", lambda: nc.scalar.activation(out=ea, in_=ac, func=AF.Exp), r=[s5.res], w=[s5.res])
            k.ins("act", lambda: nc.scalar.activation(out=w2, in_=tmp, func=AF.Exp), r=[s5.res], w=[s5.res])
            k.ins("act", lambda: nc.scalar.activation(out=eat, in_=pb[:, 32:64], func=AF.Exp), r=[pb.res], w=[s5.res])
            k.ins("dve", lambda: nc.vector.tensor_tensor(out=w2, in0=w2, in1=dtc, op=ALU.mult), r=[s5.res, sm_c.res], w=[s5.res])
            bc32 = lambda ap: ap.unsqueeze(2).to_broadcast([128, 32, 64])
            xdts = xdtsr.next()
            k.ins("pool", lambda: nc.gpsimd.tensor_tensor(out=xdts[:], in0=x_c[:], in1=bc32(w2), op=ALU.mult),
                  r=[x_c.res, s5.res], w=[xdts.res])
            ytmp = None
            if lat:
                xdt = xdtr.next()
                k.ins("pool", lambda: nc.gpsimd.tensor_tensor(out=xdt[:], in0=x_c[:], in1=bc32(dtc), op=ALU.mult),
                      r=[x_c.res, sm_c.res], w=[xdt.res])
                ytmp = ytr.next()
                for g in range(4):
                    YO = banks.next()
                    k.ins("pe", lambda g=g, YO=YO: nc.tensor.matmul(
                        YO[:], CT_c[:, g, :], hTbf[:, g * 8:(g + 1) * 8, :].rearrange("p h d -> p (h d)"), start=True, stop=True),
                        r=[CT_c.res, hTbf.res], w=[YO.res])
                    k.ins("dve", lambda g=g, YO=YO: nc.vector.tensor_tensor(
                        out=ytmp[:, g * 8:(g + 1) * 8, :], in0=YO[:].rearrange("p (h d) -> p h d", d=64),
                        in1=ea[:, g * 8:(g + 1) * 8].unsqueeze(2).to_broadcast([128, 8, 64]), op=ALU.mult),
                        r=[YO.res, s5.res], w=[ytmp.res])
            k.ins("pool", lambda: nc.gpsimd.tensor_tensor(out=hT32[:], in0=hT32[:], in1=bc32(eat), op=ALU.mult),
                  r=[hT32.res, s5.res], w=[hT32.res])
            for g in range(4):
                NS = banks.next()
                k.ins("pe", lambda g=g, NS=NS: nc.tensor.matmul(
                    NS[:], B_c[:, g, :], xdts[:, g * 8:(g + 1) * 8, :].rearrange("p h d -> p (h d)"), start=True, stop=True),
                    r=[B_c.res, xdts.res], w=[NS.res])
                hs = hT32[:, g * 8:(g + 1) * 8, :]
                k.ins("dve", lambda g=g, NS=NS, hs=hs: nc.vector.tensor_tensor(
                    out=hs, in0=hs, in1=NS[:].rearrange("p (h d) -> p h d", d=64), op=ALU.add),
                    r=[hT32.res, NS.res], w=[hT32.res])
            k.ins("act", lambda: nc.scalar.copy(out=hTbf[:], in_=hT32[:]), r=[hT32.res], w=[hTbf.res])
            if not lat:
                continue
            GTb = banks.next()
            k.group("pe", [(lambda g=g: nc.tensor.matmul(GTb[:, g * 128:(g + 1) * 128], BT_c[:, g, :], CT_c[:, g, :],
                                                         start=True, stop=True)) for g in range(4)],
                    r=[BT_c.res, CT_c.res], w=[GTb.res])
            GT = GTr.next()
            k.ins("act", lambda: nc.scalar.copy(out=GT[:].rearrange("p g c -> p (g c)"), in_=GTb[:]), r=[GTb.res], w=[GT.res])
            y_c = ycr.next()
            for g in range(4):
                YD = banks.next()
                for qd in range(2):
                    hq = g * 8 + qd * 4
                    Dq = banks.next()
                    fl = []
                    for j in range(4):
                        sl = slice(j * 128, (j + 1) * 128)
                        fl.append(lambda j=j, sl=sl, Dq=Dq: nc.tensor.matmul(
                            Dq[:, sl], dA[:, hq + j:hq + j + 1].to_broadcast([128, 128]), cumL, start=True, stop=False))
                        fl.append(lambda j=j, sl=sl, Dq=Dq: nc.tensor.matmul(Dq[:, sl], identb[:], mT_i, start=False, stop=True))
                    k.group("pe", fl, r=[dA.res, tri.res, identb.res, mskb.res], w=[Dq.res])
                    seg = segr.next()
                    for j in range(4):
                        k.ins("act", lambda j=j, seg=seg, Dq=Dq: nc.scalar.activation(
                            out=seg[:, j, :], in_=Dq[:, j * 128:(j + 1) * 128], func=AF.Exp, bias=s5[:, 1, hq + j:hq + j + 1]),
                            r=[Dq.res, s5.res], w=[seg.res])
                    MT = MTr.next()
                    k.ins("dve", lambda seg=seg, MT=MT, g=g: nc.vector.tensor_tensor(
                        out=MT[:], in0=seg[:], in1=GT[:, g:g + 1, :].to_broadcast([128, 4, 128]), op=ALU.mult),
                        r=[seg.res, GT.res], w=[MT.res])
                    k.group("pe", [(lambda j=j, MT=MT, YD=YD: nc.tensor.matmul(
                        YD[:, (qd * 4 + j) * 64:(qd * 4 + j + 1) * 64], MT[:, j, :], xdt[:, hq + j, :], start=True, stop=True))
                        for j in range(4)], r=[MT.res, xdt.res], w=[YD.res])
                ysl = y_c[:, g * 512:(g + 1) * 512]
                k.ins("dve", lambda g=g, YD=YD, ysl=ysl: nc.vector.tensor_tensor(
                    out=ysl, in0=YD[:], in1=ytmp[:, g * 8:(g + 1) * 8, :].rearrange("p h d -> p (h d)"), op=ALU.add),
                    r=[YD.res, ytmp.res], w=[y_c.res])
            if fwd:
                k.dma("sp", y1_d[l0:l0 + C, :], y_c[:], r=[y_c.res], w=[R_y1[c]], dres=y_c.res)
            else:
                k.ins("pool", lambda: nc.gpsimd.tensor_tensor(out=y_c[:], in0=y_c[:], in1=y1_c[:], op=ALU.add),
                      r=[y_c.res, y1_c.res], w=[y_c.res])
                k.dma("sp", y_d[l0:l0 + C, :], y_c[:], r=[y_c.res], w=[R_y], dres=y_c.res)
        if red:
            k.dma("sp", st2_d[:, 1024:3072], hT32[:].rearrange("p h d -> p (h d)"), r=[hT32.res], w=[R_st2], dres=hT32.res)
        cx.pop()

    with nc.allow_non_contiguous_dma("scan chunk relayout"):
        if "R" in phases:
            gdn_pass("red")
            ssd_pass("red")
        if "B1" in phases:
            gdn_pass("own1")
        if "S1" in phases:
            ssd_pass("own1")
        if "B2" in phases:
            gdn_pass("own2")
        if "S2" in phases:
            ssd_pass("own2")


    def silu_parts(src_ap, shape, ring_e, ring_z, src_res):
        ez = ring_e.next()
        zc = ring_z.next()
        k.ins("act", lambda: nc.scalar.activation(out=ez[:], in_=src_ap, func=AF.Exp, scale=-1.0), r=[src_res], w=[ez.res])
        k.ins("act", lambda: nc.scalar.copy(out=zc[:], in_=src_ap), r=[src_res], w=[zc.res])
        k.ins("dve", lambda: nc.vector.tensor_scalar(out=ez[:], in0=ez[:], scalar1=1.0, scalar2=None, op0=ALU.add),
              r=[ez.res], w=[ez.res])
        return zc, ez

    if "C1" in phases:
        cx.push()
        T1 = 128
        wZ = cx.sb("wZ", [128, 8, 3072], BF16)
        wbg = cx.sb("wbg", [128, 8, 1024], BF16)
        wbs = cx.sb("wbs", [128, 16, 1024], BF16)
        wo = cx.sb("wo", [128, 8, 1024], BF16)
        for kt in range(8):
            k.dma("pool", wZ[:, kt, :], w_in[kt * 128:(kt + 1) * 128, O_ZG:O_ZG + 3072], w=[wZ.res], dres=wZ.res)
            k.dma("pool", wbg[:, kt, :], wbg_d[kt * 128:(kt + 1) * 128, :], w=[wbg.res], dres=wbg.res)
            k.dma("pool", wo[:, kt, :], wo_d[kt * 128:(kt + 1) * 128, :], w=[wo.res], dres=wo.res)
        for kt in range(16):
            k.dma("pool", wbs[:, kt, :], wbs_d[kt * 128:(kt + 1) * 128, :], w=[wbs.res], dres=wbs.res)
        gnw = cx.sb("gnw", [128, 1], F32)
        k.dma("sp", gnw[:], gnw_d, w=[gnw.res], dres=gnw.res)
        dskB = cx.sb("dskB", [128, 2048], F32)
        snwB = cx.sb("snwB", [128, 2048], F32)
        g1B = cx.sb("g1B", [128, 1024], F32)
        k.dma("sp", dskB[:], dskip_d[0, :].partition_broadcast(128), w=[dskB.res], dres=dskB.res)
        k.dma("sp", snwB[:], snw_d[0, :].partition_broadcast(128), w=[snwB.res], dres=snwB.res)
        k.dma("sp", g1B[:], mod_d[0, 2048:3072].partition_broadcast(128), r=[R_mod], w=[g1B.res], dres=g1B.res)
        neg1c = cx.sb("neg1c", [128, 512], F32)
        k.ins("pool", lambda: nc.gpsimd.memset(neg1c[:], -1.0), w=[neg1c.res])
        banks = Ring([cx.psum(f"c1b{i}", [128, 512], F32) for i in range(6)])
        ptr = Ring([cx.psum(f"c1t{i}", [128, 1024], BF16) for i in range(2)])
        aTr = cx.sbring("c_aT", 2, [128, 8, T1], BF16)
        oTr_ = cx.sbring("c_oT", 1, [128, 8, T1], F32)
        yr_ = cx.sbring("c_y", 1, [128, 2048], F32)
        xsr_ = cx.sbring("c_xs", 1, [128, 2048], BF16)
        sgr_ = cx.sbring("c_sg", 1, [128, 16, T1], F32)
        xr_ = cx.sbring("c_x", 1, [128, 1024], F32)
        onTr = cx.sbring("c_on", 1, [128, 8, T1], BF16)
        ynTr = cx.sbring("c_yn", 1, [128, 16, T1], BF16)
        mTr = cx.sbring("c_mT", 1, [128, 8, T1], BF16)
        h1r = cx.sbring("c_h1", 1, [128, 1024], F32)
        e1r = cx.sbring("c_e1", 2, [128, T1], F32)
        z1r = cx.sbring("c_z1", 2, [128, T1], F32)
        t1r = cx.sbring("c_t1", 3, [128, T1], F32)
        e5r = cx.sbring("c_e5", 2, [128, 512], F32)
        z5r = cx.sbring("c_z5", 2, [128, 512], F32)
        t5r = cx.sbring("c_t5", 3, [128, 512], F32)
        ynr = cx.sbring("c_ynb", 2, [128, 512], BF16)
        ssr_ = cx.sbring("c_ss", 4, [128, 2], F32)
        for ti in range(NLAT // T1):
            l0 = ti * T1
            aT = aTr.next(); oT = oTr_.next(); yt = yr_.next(); xs_ = xsr_.next(); sg = sgr_.next(); xt = xr_.next()
            k.dma("sp", aT[:], aT_d[:, :, l0:l0 + T1].rearrange("kt p t -> p kt t"), r=[R_scr], w=[aT.res], dres=aT.res)
            k.dma("sp", oT[:], oT_d[:, :, l0:l0 + T1].rearrange("h p t -> p h t"), r=[R_oT], w=[oT.res], dres=oT.res)
            k.dma("sp", yt[:], y_d[l0:l0 + T1, :], r=[R_y], w=[yt.res], dres=yt.res)
            k.dma("sp", xs_[:], x_d[NCTX + l0:NCTX + l0 + T1, :], r=[R_scr], w=[xs_.res], dres=xs_.res)
            k.dma("sp", sg[:], sgT_d[:, :, l0:l0 + T1].rearrange("c p t -> p c t"), r=[R_scr], w=[sg.res], dres=sg.res)
            k.dma("sp", xt[:], xin[NCTX + l0:NCTX + l0 + T1, :], w=[xt.res], dres=xt.res)
            onT = onTr.next()
            for h in range(8):
                pz = banks.next()
                k.group("pe", [(lambda kt=kt: nc.tensor.matmul(pz[:, 0:T1], wZ[:, kt, h * 128:(h + 1) * 128], aT[:, kt, :],
                                                               start=(kt == 0), stop=(kt == 7))) for kt in range(8)],
                        r=[wZ.res, aT.res], w=[pz.res])
                zc, ez = silu_parts(pz[:, 0:T1], None, e1r, z1r, pz.res)
                k.ins("dve", lambda: nc.vector.reciprocal(out=ez[:], in_=ez[:]), r=[ez.res], w=[ez.res])
                k.ins("pool", lambda: nc.gpsimd.tensor_tensor(out=zc[:], in0=zc[:], in1=ez[:], op=ALU.mult),
                      r=[zc.res, ez.res], w=[zc.res])
                o2 = t1r.next()
                k.ins("pool", lambda: nc.gpsimd.tensor_tensor(out=o2[:], in0=oT[:, h, :], in1=oT[:, h, :], op=ALU.mult),
                      r=[oT.res], w=[o2.res])
                pn = banks.next()
                k.ins("pe", lambda: nc.tensor.matmul(pn[:, 0:T1], onesf[:], o2[:], start=True, stop=True),
                      r=[onesf.res, o2.res], w=[pn.res])
                rs = t1r.next()
                k.ins("act", lambda: nc.scalar.activation(out=rs[:], in_=pn[:, 0:T1], func=AF.Ln, scale=1.0 / 128, bias=epsc[:, 0:1]),
                      r=[pn.res, epsc.res], w=[rs.res])
                k.ins("act", lambda: nc.scalar.activation(out=rs[:], in_=rs[:], func=AF.Exp, scale=-0.5), r=[rs.res], w=[rs.res])
                k.ins("dve", lambda: nc.vector.tensor_tensor(out=rs[:], in0=rs[:], in1=oT[:, h, :], op=ALU.mult),
                      r=[rs.res, oT.res], w=[rs.res])
                k.ins("dve", lambda: nc.vector.scalar_tensor_tensor(out=onT[:, h, :], in0=rs[:], scalar=gnw[:, 0:1], in1=zc[:],
                                                                    op0=ALU.mult, op1=ALU.mult),
                      r=[rs.res, gnw.res, zc.res], w=[onT.res])
            ynT = ynTr.next()
            for cg in range(4):
                csl = slice(cg * 512, (cg + 1) * 512)
                pz = banks.next()
                k.group("pe", [(lambda kt=kt: nc.tensor.matmul(pz[:], aT[:, kt, :], wZ[:, kt, 1024 + cg * 512:1024 + (cg + 1) * 512],
                                                               start=(kt == 0), stop=(kt == 7))) for kt in range(8)],
                        r=[wZ.res, aT.res], w=[pz.res])
                zc, ez = silu_parts(pz[:], None, e5r, z5r, pz.res)
                k.ins("dve", lambda: nc.vector.reciprocal(out=ez[:], in_=ez[:]), r=[ez.res], w=[ez.res])
                k.ins("pool", lambda: nc.gpsimd.tensor_tensor(out=zc[:], in0=zc[:], in1=ez[:], op=ALU.mult),
                      r=[zc.res, ez.res], w=[zc.res])
                yy = t5r.next()
                k.ins("pool", lambda: nc.gpsimd.tensor_tensor(out=yy[:], in0=xs_[:, csl], in1=dskB[:, csl], op=ALU.mult),
                      r=[xs_.res, dskB.res], w=[yy.res])
                k.ins("dve", lambda: nc.vector.tensor_tensor(out=yy[:], in0=yy[:], in1=yt[:, csl], op=ALU.add),
                      r=[yy.res, yt.res], w=[yy.res])
                k.ins("dve", lambda: nc.vector.tensor_tensor(out=yy[:], in0=yy[:], in1=zc[:], op=ALU.mult),
                      r=[yy.res, zc.res], w=[yy.res])
                ss = ssr_.next()
                jk = t5r.next()
                k.ins("act", lambda: nc.scalar.activation(out=jk[:], in_=yy[:], func=AF.Square, accum_out=ss[:, 0:1]),
                      r=[yy.res], w=[jk.res, ss.res])
                k.ins("act", lambda: nc.scalar.activation(out=ss[:, 1:2], in_=ss[:, 0:1], func=AF.Ln, scale=1.0 / 512, bias=epsc[:, 0:1]),
                      r=[ss.res, epsc.res], w=[ss.res])
                k.ins("act", lambda: nc.scalar.activation(out=ss[:, 1:2], in_=ss[:, 1:2], func=AF.Exp, scale=-0.5),
                      r=[ss.res], w=[ss.res])
                ynb = ynr.next()
                k.ins("dve", lambda: nc.vector.scalar_tensor_tensor(out=ynb[:], in0=yy[:], scalar=ss[:, 1:2], in1=snwB[:, csl],
                                                                    op0=ALU.mult, op1=ALU.mult),
                      r=[yy.res, ss.res, snwB.res], w=[ynb.res])
                pt = ptr.next()
                k.group("pe", [(lambda j=j: nc.tensor.transpose(pt[:, j * 128:(j + 1) * 128], ynb[:, j * 128:(j + 1) * 128], identb[:]))
                               for j in range(4)], r=[ynb.res, identb.res], w=[pt.res])
                k.ins("act", lambda: nc.scalar.copy(out=ynT[:, cg * 4:(cg + 1) * 4, :],
                                                    in_=pt[:, 0:512].rearrange("p (j c) -> p j c", c=128)),
                      r=[pt.res], w=[ynT.res])
            mT = mTr.next()
            for oc in range(8):
                pg = banks.next()
                k.group("pe", [(lambda h=h: nc.tensor.matmul(pg[:, 0:T1], wbg[:, h, oc * 128:(oc + 1) * 128], onT[:, h, :],
                                                             start=(h == 0), stop=(h == 7))) for h in range(8)],
                        r=[wbg.res, onT.res], w=[pg.res])
                pp = banks.next()
                k.group("pe", [(lambda ct=ct: nc.tensor.matmul(pp[:, 0:T1], wbs[:, ct, oc * 128:(oc + 1) * 128], ynT[:, ct, :],
                                                               start=(ct == 0), stop=(ct == 15))) for ct in range(16)],
                        r=[wbs.res, ynT.res], w=[pp.res])
                m1 = t1r.next()
                k.ins("dve", lambda: nc.vector.tensor_tensor(out=m1[:], in0=pg[:, 0:T1], in1=sg[:, oc, :], op=ALU.mult),
                      r=[pg.res, sg.res], w=[m1.res])
                m2 = t1r.next()
                k.ins("dve", lambda: nc.vector.tensor_tensor(out=m2[:], in0=pp[:, 0:T1], in1=sg[:, 8 + oc, :], op=ALU.mult),
                      r=[pp.res, sg.res], w=[m2.res])
                k.ins("pool", lambda: nc.gpsimd.tensor_tensor(out=mT[:, oc, :], in0=m1[:], in1=m2[:], op=ALU.add),
                      r=[m1.res, m2.res], w=[mT.res])
            h1 = h1r.next()
            for cg in range(2):
                csl = slice(cg * 512, (cg + 1) * 512)
                pm = banks.next()
                k.group("pe", [(lambda kt=kt: nc.tensor.matmul(pm[:], mT[:, kt, :], wo[:, kt, csl],
                                                               start=(kt == 0), stop=(kt == 7))) for kt in range(8)],
                        r=[mT.res, wo.res], w=[pm.res])
                k.ins("dve", lambda: nc.vector.tensor_tensor(out=h1[:, csl], in0=pm[:], in1=g1B[:, csl], op=ALU.mult),
                      r=[pm.res, g1B.res], w=[h1.res])
                k.ins("pool", lambda: nc.gpsimd.tensor_tensor(out=h1[:, csl], in0=h1[:, csl], in1=xt[:, csl], op=ALU.add),
                      r=[h1.res, xt.res], w=[h1.res])
            k.dma("sp", h1_d[l0:l0 + T1, :], h1[:], r=[h1.res], w=[R_h1], dres=h1.res)
        cx.pop()

    if "C2" in phases:
        cx.push()
        T2 = 256
        NF = DFF // 128
        wfi = cx.sb("wfi", [128, 8, 2 * DFF], BF16)
        wfo = cx.sb("wfo", [128, NF, 1024], BF16)
        for kt in range(8):
            for c0 in range(0, 2 * DFF, 2816):
                k.dma("pool", wfi[:, kt, c0:c0 + 2816], wfi_d[kt * 128:(kt + 1) * 128, c0:c0 + 2816], w=[wfi.res], dres=wfi.res)
        for ft in range(NF):
            k.dma("pool", wfo[:, ft, :], wfo_d[ft * 128:(ft + 1) * 128, :], w=[wfo.res], dres=wfo.res)
        g2B = cx.sb("g2B", [128, 1024], F32)
        nfB = cx.sb("nfB", [128, 1024], F32)
        k.dma("sp", g2B[:], mod_d[0, 5120:6144].partition_broadcast(128), r=[R_mod], w=[g2B.res], dres=g2B.res)
        k.dma("sp", nfB[:], nfw_d[0, :].partition_broadcast(128), w=[nfB.res], dres=nfB.res)
        n2 = cx.sb("n2", [128, 8], F32)
        k.dma("sp", n2[:], n2w_d, w=[n2.res], dres=n2.res)
        S2 = cx.sb("S2", [128, 8], F32)
        k.ins("dve", lambda: nc.vector.scalar_tensor_tensor(out=S2[:], in0=modf[:, 0, 4, :], scalar=1.0, in1=n2[:],
                                                            op0=ALU.add, op1=ALU.mult), r=[modf.res, n2.res], w=[S2.res])
        neg1d = cx.sb("neg1d", [128, T2], F32)
        k.ins("pool", lambda: nc.gpsimd.memset(neg1d[:], -1.0), w=[neg1d.res])
        banks = Ring([cx.psum(f"c2b{i}", [128, 512], F32) for i in range(8)])
        h1r = cx.sbring("d_h1", 2, [128, 2, 1024], F32)
        xnr = cx.sbring("d_xn", 1, [128, 1024], F32)
        fTr = cx.sbring("d_fT", 2, [128, 8, T2], BF16)
        actr = cx.sbring("d_act", 1, [128, NF, T2], BF16)
        outr = cx.sbring("d_out", 1, [128, 2, 1024], F32)
        ssr2 = cx.sbring("d_ss", 4, [128, 2], F32)
        junk2 = cx.sb("junk2", [128, 1024], BF16)
        e2r = cx.sbring("d_e", 3, [128, T2], F32)
        a2r = cx.sbring("d_a", 3, [128, T2], F32)
        for ti in range(NLAT // T2):
            l0 = ti * T2
            h1 = h1r.next()
            k.dma("sp", h1[:], h1_d[l0:l0 + T2, :].rearrange("(j p) d -> p j d", p=128), r=[R_h1], w=[h1.res], dres=h1.res)
            ss = ssr2.next()
            for j in range(2):
                k.ins("act", lambda j=j: nc.scalar.activation(out=junk2[:], in_=h1[:, j, :], func=AF.Square, accum_out=ss[:, j:j + 1]),
                      r=[h1.res], w=[junk2.res, ss.res])
            k.ins("act", lambda: nc.scalar.activation(out=ss[:], in_=ss[:], func=AF.Ln, scale=1.0 / D, bias=epsc[:, 0:1]),
                  r=[ss.res, epsc.res], w=[ss.res])
            k.ins("act", lambda: nc.scalar.activation(out=ss[:], in_=ss[:], func=AF.Exp, scale=-0.5), r=[ss.res], w=[ss.res])
            fT = fTr.next()
            for j in range(2):
                xn = xnr.next()
                k.ins("act", lambda j=j, xn=xn: nc.scalar.activation(out=xn[:], in_=h1[:, j, :], func=AF.Identity, scale=ss[:, j:j + 1]),
                      r=[h1.res, ss.res], w=[xn.res])
                for half in range(2):
                    pb = banks.next()
                    k.group("pe", [(lambda q=q, xn=xn, pb=pb: nc.tensor.transpose(
                        pb[:, q * 128:(q + 1) * 128], xn[:, (half * 4 + q) * 128:(half * 4 + q + 1) * 128], ident[:]))
                        for q in range(4)], r=[xn.res, ident.res], w=[pb.res])
                    for q in range(4):
                        kt = half * 4 + q
                        k.ins("act", lambda q=q, kt=kt, pb=pb, j=j: nc.scalar.activation(
                            out=fT[:, kt, j * 128:(j + 1) * 128], in_=pb[:, q * 128:(q + 1) * 128], func=AF.Identity,
                            scale=S2[:, kt:kt + 1], bias=modf[:, 0, 3, kt:kt + 1]),
                            r=[pb.res, S2.res, modf.res], w=[fT.res])
            act = actr.next()
            for ft in range(NF):
                pgt = banks.next()
                k.group("pe", [(lambda kt=kt: nc.tensor.matmul(pgt[:, 0:T2], wfi[:, kt, ft * 128:(ft + 1) * 128], fT[:, kt, :],
                                                               start=(kt == 0), stop=(kt == 7))) for kt in range(8)],
                        r=[wfi.res, fT.res], w=[pgt.res])
                pup = banks.next()
                k.group("pe", [(lambda kt=kt: nc.tensor.matmul(pup[:, 0:T2], wfi[:, kt, DFF + ft * 128:DFF + (ft + 1) * 128], fT[:, kt, :],
                                                               start=(kt == 0), stop=(kt == 7))) for kt in range(8)],
                        r=[wfi.res, fT.res], w=[pup.res])
                ez = e2r.next()
                k.ins("act", lambda: nc.scalar.activation(out=ez[:], in_=pgt[:, 0:T2], func=AF.Exp, scale=-1.0), r=[pgt.res], w=[ez.res])
                k.ins("dve", lambda: nc.vector.tensor_scalar(out=ez[:], in0=ez[:], scalar1=1.0, scalar2=None, op0=ALU.add),
                      r=[ez.res], w=[ez.res])
                k.ins("dve", lambda: nc.vector.reciprocal(out=ez[:], in_=ez[:]), r=[ez.res], w=[ez.res])
                a1 = a2r.next()
                k.ins("dve", lambda: nc.vector.tensor_tensor(out=a1[:], in0=pgt[:, 0:T2], in1=ez[:], op=ALU.mult),
                      r=[pgt.res, ez.res], w=[a1.res])
                k.ins("dve", lambda: nc.vector.tensor_tensor(out=act[:, ft, :], in0=pup[:, 0:T2], in1=a1[:], op=ALU.mult),
                      r=[pup.res, a1.res], w=[act.res])
            ot = outr.next()
            ss2 = ssr2.next()
            for j in range(2):
                for cg in range(2):
                    csl = slice(cg * 512, (cg + 1) * 512)
                    pf = banks.next()
                    k.group("pe", [(lambda ft=ft: nc.tensor.matmul(pf[:], act[:, ft, j * 128:(j + 1) * 128], wfo[:, ft, csl],
                                                                   start=(ft == 0), stop=(ft == NF - 1))) for ft in range(NF)],
                            r=[act.res, wfo.res], w=[pf.res])
                    k.ins("dve", lambda: nc.vector.tensor_tensor(out=ot[:, j, csl], in0=pf[:], in1=g2B[:, csl], op=ALU.mult),
                          r=[pf.res, g2B.res], w=[ot.res])
                    k.ins("pool", lambda: nc.gpsimd.tensor_tensor(out=ot[:, j, csl], in0=ot[:, j, csl], in1=h1[:, j, csl], op=ALU.add),
                          r=[ot.res, h1.res], w=[ot.res])
                k.ins("act", lambda j=j: nc.scalar.activation(out=junk2[:], in_=ot[:, j, :], func=AF.Square, accum_out=ss2[:, j:j + 1]),
                      r=[ot.res], w=[junk2.res, ss2.res])
            k.ins("act", lambda: nc.scalar.activation(out=ss2[:], in_=ss2[:], func=AF.Ln, scale=1.0 / D, bias=epsc[:, 0:1]),
                  r=[ss2.res, epsc.res], w=[ss2.res])
            k.ins("act", lambda: nc.scalar.activation(out=ss2[:], in_=ss2[:], func=AF.Exp, scale=-0.5), r=[ss2.res], w=[ss2.res])
            for j in range(2):
                k.ins("dve", lambda j=j: nc.vector.scalar_tensor_tensor(out=ot[:, j, :], in0=ot[:, j, :], scalar=ss2[:, j:j + 1],
                                                                        in1=nfB[:], op0=ALU.mult, op1=ALU.mult),
                      r=[ot.res, ss2.res, nfB.res], w=[ot.res])
            k.dma("sp", out_d[l0:l0 + T2, :].rearrange("(j p) d -> p j d", p=128), ot[:], r=[ot.res], w=[R_out], dres=ot.res)
        cx.pop()

    k.wait_all("sp", [R_scr, R_mod, R_oT, R_st1, R_st2, R_y, R_h1, R_out] + R_o1 + R_y1)
    return nc


def _consts():
    p = np.arange(128)[:, None]
    f = np.arange(128)[None, :]
    U = (p <= f).astype(np.float32)
    Lo = (p >= f).astype(np.float32)
    tri = np.stack([U, Lo, -U, -Lo], axis=1)
    NEG = -30000.0
    m = lambda ok: np.where(ok, 0.0, NEG).astype(np.float32)
    msk = np.stack([m(p > f), m(p < f), m(p >= f), m(p <= f)], axis=1)
    return np.ascontiguousarray(tri), np.ascontiguousarray(msk)


TRI, MSK = _consts()


def host_prepare(inputs, core):
    b, s = core // 2, core % 2
    x = inputs["x"][b, s * NLAT:(s + 1) * NLAT]
    ctx = inputs["ctx"][b]
    if s == 1:
        x = x[::-1]
        ctx = ctx[::-1]
    xin = np.ascontiguousarray(np.concatenate([ctx, x], axis=0))
    x2 = inputs["x"][b, (1 - s) * NLAT:(2 - s) * NLAT]
    ctx2 = inputs["ctx"][b]
    if s == 0:
        x2 = x2[::-1]
        ctx2 = ctx2[::-1]
    xin2 = np.ascontiguousarray(np.concatenate([ctx2, x2], axis=0))
    cv = np.stack([inputs["c"][b], inputs["c_ctx"]], axis=-1)
    cvec = np.ascontiguousarray(cv.reshape(8, 128, 2).transpose(1, 0, 2))
    d1, d2 = (0, 1) if s == 0 else (1, 0)
    w = inputs["w_in"][0]
    offs = np.cumsum([0, 3072, 1024, 16, 16, 2048, 3072, 64, 2048])
    qkv = w[:, offs[0]:offs[1]]
    zg = w[:, offs[1]:offs[2]]
    a_ = w[:, offs[2]:offs[3]].reshape(D, 2, 8)
    b_ = w[:, offs[3]:offs[4]].reshape(D, 2, 8)
    zs = w[:, offs[4]:offs[5]]
    xbc = w[:, offs[5]:offs[6]]
    dt_ = w[:, offs[6]:offs[7]].reshape(D, 2, 32)
    gate = w[:, offs[7]:offs[8]]
    z16 = np.zeros((D, 16), np.float32)
    small = np.concatenate([a_[:, d1], a_[:, d2], z16, b_[:, d1], b_[:, d2], z16, dt_[:, d1], dt_[:, d2]], axis=1)
    w_perm = np.ascontiguousarray(np.concatenate([qkv, xbc, gate, small, zg, zs], axis=1))
    assert w_perm.shape[1] == W_IN_COLS
    cw = np.concatenate([inputs["gdn_conv_w"][0], inputs["ssm_conv_w"][0]], axis=1)
    cbias = np.concatenate([inputs["gdn_conv_b"][0], inputs["ssm_conv_b"][0]], axis=0)
    mk = lambda w_: np.ascontiguousarray(np.concatenate([w_, cbias[None]], axis=0).reshape(4, 48, 128).transpose(2, 1, 0))
    convp = mk(cw[::-1] if s == 1 else cw)
    convp2 = mk(cw[::-1] if s == 0 else cw)
    smallp = np.zeros((128, 4), np.float32)
    smallp[:, 0] = 1.0
    smallp[32:64, 0] = -1.0
    gb = inputs["gdn_dt_bias"][0]
    sbias = inputs["ssm_dt_bias"][0]
    smallp[0:8, 1] = gb[d1]; smallp[8:16, 1] = gb[d2]
    smallp[64:96, 1] = sbias[d1]; smallp[96:128, 1] = sbias[d2]
    ga = inputs["gdn_a_log"][0]
    smallp[0:8, 2] = ga[d1]; smallp[8:16, 2] = ga[d2]
    smallp[:, 3] = -1.0
    smallp[64:128, 3] = 1.0
    n1w = np.ascontiguousarray(inputs["norm1_w"][0].reshape(8, 128).T)
    return {
        "xin": xin, "xin2": xin2, "convp2": convp2, "cvec": cvec, "ada_w": np.ascontiguousarray(inputs["ada_w"][0]),
        "ada_b": np.ascontiguousarray(inputs["ada_b"][0][None]), "w_in": w_perm, "convp": convp,
        "smallp": smallp, "n1w": n1w, "ident": np.eye(128, dtype=np.float32),
        "tri": TRI, "msk": MSK,
        "w_brg": np.ascontiguousarray(inputs["w_br_gdn"][0]), "w_brs": np.ascontiguousarray(inputs["w_br_ssm"][0]),
        "w_o": np.ascontiguousarray(inputs["w_out"][0]), "w_fi": np.ascontiguousarray(inputs["w_ffn_in"][0]),
        "w_fo": np.ascontiguousarray(inputs["w_ffn_out"][0]),
        "gnw": np.ascontiguousarray(inputs["gdn_norm_w"][0].reshape(128, 1)),
        "dskip": np.ascontiguousarray(np.repeat(inputs["ssm_d"][0], 64)[None]),
        "snw": np.ascontiguousarray(inputs["ssm_norm_w"][0][None]),
        "n2w": np.ascontiguousarray(inputs["norm2_w"][0].reshape(8, 128).T),
        "nfw": np.ascontiguousarray(inputs["norm_f_w"][None]),
        "salog": np.ascontiguousarray(np.broadcast_to(inputs["ssm_a_log"][0][[d1, d2]][None], (128, 2, 32))),
    }


def kernel(**inputs):
    inputs = {k_: np.asarray(v) for k_, v in inputs.items()}
    nc = build_program()
    in_maps = [host_prepare(inputs, c) for c in range(8)]
    res = run_bass_kernel_spmd(nc, in_maps, core_ids=list(range(8)))
    out = np.zeros((4, 8192, D), np.float32)
    for c in range(8):
        b, s = c // 2, c % 2
        o = res.results[c]["out"]
        if s == 1:
            o = o[::-1]
        out[b, s * NLAT:(s + 1) * NLAT] = o
    return out
```

```python
import numpy as np
import ml_dtypes
from contextlib import ExitStack
import concourse.bass as bass
import concourse.mybir as mybir
from concourse.bass_utils import run_bass_kernel_spmd

F32 = mybir.dt.float32
BF16 = mybir.dt.bfloat16
AF = mybir.ActivationFunctionType
ALU = mybir.AluOpType
AX = mybir.AxisListType

D = 1024
NLAT = 4096
NCTX = 256
NTOK = NLAT + NCTX
C = 128
NCH = NTOK // C
EPS = 1e-6
DFF = 2816
O_QKV, O_XBC, O_GATE, O_SMALL, O_ZG, O_ZS = 0, 3072, 6144, 8192, 8320, 9344
W_IN_COLS = 11392
SEM_LIMIT = 30000
GDN_LIMIT = 0


class Sem:
    __slots__ = ("h", "count", "dma", "name")

    def __init__(self, h, dma, name):
        self.h, self.count, self.dma, self.name = h, 0, dma, name


class Res:
    __slots__ = ("name", "w", "r", "dsem")

    def __init__(self, name):
        self.name, self.w, self.r, self.dsem = name, None, {}, None


class Eng:
    def __init__(self, name, e, same_wait):
        self.name, self.e, self.sem, self.waited, self.same_wait = name, e, None, {}, same_wait
        self.nsem = 0


class K:
    def __init__(self, nc):
        self.nc = nc
        self.eng = {
            "pe": Eng("pe", nc.tensor, False),
            "act": Eng("act", nc.scalar, True),
            "dve": Eng("dve", nc.vector, True),
            "pool": Eng("pool", nc.gpsimd, True),
            "sp": Eng("sp", nc.sync, False),
        }
        self.nsem = 0
        self.ninst = 0
        self.all_sems = []
        self.free_dma = []

    def new_sem(self, dma, name):
        if dma and self.free_dma:
            self.free_dma.sort(key=lambda x: x.count)
            return self.free_dma.pop(0)
        self.nsem += 1
        h = self.nc.alloc_semaphore(f"s{self.nsem}_{name}")
        sm = Sem(h, dma, name)
        self.all_sems.append(sm)
        return sm

    def res(self, name):
        return Res(name)

    def _cur_sem(self, E):
        if E.sem is None or E.sem.count >= SEM_LIMIT:
            E.nsem += 1
            E.sem = self.new_sem(False, f"{E.name}{E.nsem}")
        return E.sem

    def _wait(self, E, evs):
        need = {}
        for (sem, val) in evs:
            if sem.dma:
                val = sem.count
            if need.get(sem, 0) < val:
                need[sem] = val
        for sem, val in need.items():
            if (not E.same_wait) and (sem is E.sem) and not sem.dma:
                continue
            if E.waited.get(sem, 0) >= val:
                continue
            E.e.wait_ge(sem.h, val)
            E.waited[sem] = val

    def _deps(self, r, w):
        evs = []
        for x in r:
            if x.w is not None:
                evs.append(x.w)
        for x in w:
            if x.w is not None:
                evs.append(x.w)
            evs.extend(x.r.items())
        return evs

    def _record(self, ev, r, w):
        sem, val = ev
        for x in r:
            if x.r.get(sem, 0) < val:
                x.r[sem] = val
        for x in w:
            x.w = ev
            x.r = {}

    def ins(self, en, fn, r=(), w=()):
        E = self.eng[en]
        self._wait(E, self._deps(r, w))
        inst = fn()
        sem = self._cur_sem(E)
        sem.count += 1
        inst.then_inc(sem.h, 1)
        self._record((sem, sem.count), r, w)
        self.ninst += 1
        return inst

    def group(self, en, fns, r=(), w=()):
        E = self.eng[en]
        self._wait(E, self._deps(r, w))
        inst = None
        for fn in fns:
            inst = fn()
        sem = self._cur_sem(E)
        sem.count += 1
        inst.then_inc(sem.h, 1)
        self._record((sem, sem.count), r, w)
        self.ninst += len(fns)
        return inst

    def dma(self, q, out, in_, r=(), w=(), dres=None):
        E = self.eng[q]
        self._wait(E, self._deps(r, w))
        if dres.dsem is None:
            dres.dsem = self.new_sem(True, "d" + dres.name)
        sem = dres.dsem
        inst = E.e.dma_start(out=out, in_=in_)
        sem.count += 16
        inst.then_inc(sem.h, 16)
        self._record((sem, sem.count), r, w)
        self.ninst += 1
        return inst

    def dmaop(self, q, fn, r=(), w=(), dres=None):
        E = self.eng[q]
        self._wait(E, self._deps(r, w))
        if dres.dsem is None:
            dres.dsem = self.new_sem(True, "d" + dres.name)
        sem = dres.dsem
        inst = fn()
        sem.count += 16
        inst.then_inc(sem.h, 16)
        self._record((sem, sem.count), r, w)
        self.ninst += 1
        return inst

    def barrier(self):
        sems = list(self.all_sems)
        for E in self.eng.values():
            for sem in sems:
                if sem.count == 0 or E.waited.get(sem, 0) >= sem.count:
                    continue
                if (sem is E.sem) and not E.same_wait:
                    continue
                E.e.wait_ge(sem.h, sem.count)
                E.waited[sem] = sem.count

    def wait_all(self, en, ress):
        E = self.eng[en]
        evs = []
        for x in ress:
            if x.w is not None:
                evs.append(x.w)
            evs.extend(x.r.items())
        self._wait(E, evs)


class Buf:
    def __init__(self, k, t, name):
        self.t, self.res, self.name = t, k.res(name), name

    def __getitem__(self, idx):
        return self.t[idx]


class Ring:
    def __init__(self, bufs):
        self.bufs, self.i = bufs, 0

    def next(self):
        b = self.bufs[self.i % len(self.bufs)]
        self.i += 1
        return b


class Ctx:
    def __init__(self, nc):
        self.nc = nc
        self.k = K(nc)
        self.stack = [ExitStack()]
        self.uid = 0
        self.phase_bufs = [[]]

    def push(self):
        self.stack.append(ExitStack())
        self.phase_bufs.append([])

    def pop(self):
        self.k.barrier()
        for b in self.phase_bufs.pop():
            if b.res.dsem is not None:
                self.k.free_dma.append(b.res.dsem)
                b.res.dsem = None
        self.stack.pop().close()

    def sb(self, name, shape, dtype):
        self.uid += 1
        t = self.stack[-1].enter_context(self.nc.sbuf_tensor(f"sb{self.uid}_{name}", list(shape), dtype))
        b = Buf(self.k, t, name)
        self.phase_bufs[-1].append(b)
        return b

    def psum(self, name, shape, dtype):
        self.uid += 1
        t = self.stack[-1].enter_context(self.nc.psum_tensor(f"ps{self.uid}_{name}", list(shape), dtype))
        return Buf(self.k, t, name)

    def sbring(self, name, n, shape, dtype):
        return Ring([self.sb(f"{name}{i}", shape, dtype) for i in range(n)])

    def dram(self, name, shape, dtype, kind="Internal"):
        t = self.nc.dram_tensor(name, list(shape), dtype, kind=kind)
        return t.ap()


def build_program(debug=False, phases=("0", "A", "R", "B1", "S1", "B2", "S2", "C1", "C2"), n_cores=8):
    nc = bass.Bass("TRN2", target_bir_lowering=False)
    cx = Ctx(nc)
    k = cx.k
    dk = "ExternalOutput" if debug else "Internal"

    xin = cx.dram("xin", [NTOK, D], F32, "ExternalInput")
    cvec = cx.dram("cvec", [128, 8, 2], F32, "ExternalInput")
    ada_w = cx.dram("ada_w", [D, 6 * D], F32, "ExternalInput")
    ada_b = cx.dram("ada_b", [1, 6 * D], F32, "ExternalInput")
    w_in = cx.dram("w_in", [D, W_IN_COLS], F32, "ExternalInput")
    xin2 = cx.dram("xin2", [NTOK, D], F32, "ExternalInput")
    convp2 = cx.dram("convp2", [128, 48, 4], F32, "ExternalInput")
    convp = cx.dram("convp", [128, 48, 4], F32, "ExternalInput")
    smallp = cx.dram("smallp", [128, 4], F32, "ExternalInput")
    n1w = cx.dram("n1w", [128, 8], F32, "ExternalInput")
    ident_d = cx.dram("ident", [128, 128], F32, "ExternalInput")
    wbg_d = cx.dram("w_brg", [1024, 1024], F32, "ExternalInput")
    wbs_d = cx.dram("w_brs", [2048, 1024], F32, "ExternalInput")
    wo_d = cx.dram("w_o", [1024, 1024], F32, "ExternalInput")
    wfi_d = cx.dram("w_fi", [1024, 2 * DFF], F32, "ExternalInput")
    wfo_d = cx.dram("w_fo", [DFF, 1024], F32, "ExternalInput")
    gnw_d = cx.dram("gnw", [128, 1], F32, "ExternalInput")
    dskip_d = cx.dram("dskip", [1, 2048], F32, "ExternalInput")
    snw_d = cx.dram("snw", [1, 2048], F32, "ExternalInput")
    n2w_d = cx.dram("n2w", [128, 8], F32, "ExternalInput")
    nfw_d = cx.dram("nfw", [1, 1024], F32, "ExternalInput")
    salog_d = cx.dram("salog", [128, 2, 32], F32, "ExternalInput")
    tri_d = cx.dram("tri", [128, 4, 128], F32, "ExternalInput")
    msk_d = cx.dram("msk", [128, 4, 128], F32, "ExternalInput")
    out_d = cx.dram("out", [NLAT, D], F32, "ExternalOutput")

    mod_d = cx.dram("mod_d", [2, 6 * D], F32, dk)
    qT_d = cx.dram("qT_d", [8, 128, NTOK], BF16, dk)
    kT_d = cx.dram("kT_d", [8, 128, NTOK], BF16, dk)
    k_d = cx.dram("k_d", [NTOK, 1024], BF16, dk)
    v_d = cx.dram("v_d", [NTOK, 1024], BF16, dk)
    x_d = cx.dram("x_d", [NTOK, 2048], BF16, dk)
    BT_d = cx.dram("BT_d", [4, 128, NTOK], BF16, dk)
    CT_d = cx.dram("CT_d", [4, 128, NTOK], BF16, dk)
    B_d = cx.dram("B_d", [NTOK, 512], BF16, dk)
    sm_d = cx.dram("sm_d", [NTOK, 128], F32, dk)
    kT2_d = cx.dram("kT2_d", [8, 128, NTOK], BF16)
    k2_d = cx.dram("k2_d", [NTOK, 1024], BF16)
    v2_d = cx.dram("v2_d", [NTOK, 1024], BF16)
    x2_d = cx.dram("x2_d", [NTOK, 2048], BF16)
    B2_d = cx.dram("B2_d", [NTOK, 512], BF16)
    sm2_d = cx.dram("sm2_d", [NTOK, 128], F32)
    sgT_d = cx.dram("sgT_d", [16, 128, NLAT], F32, dk)
    aT_d = cx.dram("aT_d", [8, 128, NLAT], BF16, dk)
    o1_d = cx.dram("o1_d", [8, 128, NLAT], F32, dk)
    oT_d = cx.dram("oT_d", [8, 128, NLAT], F32, dk)
    st1_d = cx.dram("st1_d", [128, 3072], F32)
    st2_d = cx.dram("st2_d", [128, 3072], F32)
    y1_d = cx.dram("y1_d", [NLAT, 2048], F32, dk)
    y_d = cx.dram("y_d", [NLAT, 2048], F32, dk)
    R_y1 = [k.res(f"y1_{c}") for c in range(NCH)]
    R_y = k.res("y")
    R_o1 = [k.res(f"o1_{c}") for c in range(NCH)]
    R_oT = k.res("oT")
    R_st1 = k.res("st1")
    R_st2 = k.res("st2")
    h1_d = cx.dram("h1_d", [NLAT, D], F32, dk)
    R_h1 = k.res("h1")
    R_out = k.res("out")
    R_scr = k.res("scratchA")
    R_mod = k.res("mod_d")

    ident = cx.sb("ident", [128, 128], F32)
    identb = cx.sb("identb", [128, 128], BF16)
    onesf = cx.sb("onesf", [128, 128], F32)
    k.dma("sp", ident[:], ident_d, w=[ident.res], dres=ident.res)
    k.ins("dve", lambda: nc.vector.tensor_copy(identb[:], ident[:]), r=[ident.res], w=[identb.res])
    k.ins("dve", lambda: nc.vector.memset(onesf[:], 1.0), w=[onesf.res])
    epsc = cx.sb("epsc", [128, 1], F32)
    k.ins("dve", lambda: nc.vector.memset(epsc[:], EPS), w=[epsc.res])


    if "0" in phases:
        cx.push()
        ps = [cx.psum(f"ps{i}", [128, 512], F32) for i in range(2)]
        cv = cx.sb("cv", [128, 8, 2], F32)
        cvs = cx.sb("cvs", [128, 8, 2], F32)
        k.dma("sp", cv[:], cvec, w=[cv.res], dres=cv.res)
        k.ins("act", lambda: nc.scalar.activation(out=cvs[:], in_=cv[:], func=AF.Exp, scale=-1.0), r=[cv.res], w=[cvs.res])
        k.ins("dve", lambda: nc.vector.tensor_scalar(out=cvs[:], in0=cvs[:], scalar1=1.0, scalar2=None, op0=ALU.add),
              r=[cvs.res], w=[cvs.res])
        k.ins("dve", lambda: nc.vector.reciprocal(out=cvs[:], in_=cvs[:]), r=[cvs.res], w=[cvs.res])
        k.ins("dve", lambda: nc.vector.tensor_tensor(out=cvs[:], in0=cvs[:], in1=cv[:], op=ALU.mult),
              r=[cvs.res, cv.res], w=[cvs.res])
        adab = cx.sb("adab", [2, 6 * D], F32)
        k.dma("sp", adab[0:1, :], ada_b, w=[adab.res], dres=adab.res)
        k.dma("sp", adab[1:2, :], ada_b, w=[adab.res], dres=adab.res)
        modsb = cx.sb("modsb", [2, 6 * D], F32)
        awring = cx.sbring("aw", 2, [128, 8, 512], F32)
        for cg in range(12):
            aw = awring.next()
            k.dma("sp", aw[:], ada_w[:, cg * 512:(cg + 1) * 512].rearrange("(kt p) n -> p kt n", p=128),
                  w=[aw.res], dres=aw.res)
            pb = ps[cg % 2]
            k.group("pe", [
                (lambda kt=kt, aw=aw, pb=pb: nc.tensor.matmul(pb[0:2, :], cvs[:, kt, :], aw[:, kt, :],
                                                               start=(kt == 0), stop=(kt == 7)))
                for kt in range(8)], r=[cvs.res, aw.res], w=[pb.res])
            k.ins("dve", lambda cg=cg, pb=pb: nc.vector.tensor_tensor(
                out=modsb[:, cg * 512:(cg + 1) * 512], in0=pb[0:2, :], in1=adab[:, cg * 512:(cg + 1) * 512],
                op=ALU.add), r=[pb.res, adab.res], w=[modsb.res])
        k.dma("sp", mod_d, modsb[:], r=[modsb.res], w=[R_mod], dres=modsb.res)
        cx.pop()

    modf = cx.sb("modf", [128, 2, 6, 8], F32)
    with nc.allow_non_contiguous_dma("small modulation vector relayout"):
        for r_ in range(2):
            k.dma("sp", modf[:, r_, :, :], mod_d[r_, :].rearrange("(j kt p) -> p j kt", p=128, kt=8),
                  r=[R_mod], w=[modf.res], dres=modf.res)
    n1 = cx.sb("n1", [128, 8], F32)
    k.dma("sp", n1[:], n1w, w=[n1.res], dres=n1.res)
    S1 = cx.sb("S1", [128, 2, 8], F32)
    for r_ in range(2):
        k.ins("dve", lambda r_=r_: nc.vector.scalar_tensor_tensor(
            out=S1[:, r_, :], in0=modf[:, r_, 1, :], scalar=1.0, in1=n1[:], op0=ALU.add, op1=ALU.mult),
            r=[modf.res, n1.res], w=[S1.res])

    if "A" in phases:
        cx.push()
        ps = [cx.psum(f"ps{i}", [128, 512], F32) for i in range(6)]
        NWC = O_ZG
        wA = cx.sb("wA", [128, 8, NWC], BF16)
        for kt in range(8):
            for c0 in range(0, NWC, 2080):
                k.dma("pool", wA[:, kt, c0:c0 + 2080], w_in[kt * 128:(kt + 1) * 128, c0:c0 + 2080],
                      w=[wA.res], dres=wA.res)
        cp = cx.sb("cp", [128, 48, 4], F32)
        k.dma("sp", cp[:], convp, w=[cp.res], dres=cp.res)
        smp = cx.sb("smp", [128, 4], F32)
        k.dma("sp", smp[:], smallp, w=[smp.res], dres=smp.res)
        smult = cx.sb("smult", [128, 1], F32)
        k.ins("act", lambda: nc.scalar.activation(out=smult[:], in_=smp[:, 2:3], func=AF.Exp),
              r=[smp.res], w=[smult.res])
        k.ins("dve", lambda: nc.vector.tensor_tensor(out=smult[:], in0=smult[:], in1=smp[:, 3:4], op=ALU.mult),
              r=[smp.res, smult.res], w=[smult.res])

        TT = 256
        xring = cx.sbring("xt", 1, [128, 2, D], F32)
        xnring = cx.sbring("xn", 1, [128, D], F32)
        junk = cx.sb("junk", [128, D], BF16)
        ssr = cx.sbring("ss", 2, [128, 2], F32)
        aTring = cx.sbring("aT", 2, [128, 8, TT], BF16)
        cring = cx.sbring("cv_", 6, [128, TT], F32)
        ering = cx.sbring("ee_", 4, [128, TT], F32)
        sfring = cx.sbring("sf_", 5, [128, TT], F32)
        sqring = cx.sbring("sq_", 3, [128, TT], F32)
        rsring = cx.sbring("rs_", 3, [128, TT], F32)
        sring = cx.sbring("so_", 8, [128, TT], BF16)
        sgring = cx.sbring("sg_", 4, [128, TT], F32)
        smring = cx.sbring("smf", 2, [128, TT], F32)
        ktok = cx.sbring("ktok", 1, [128, 2, 1024], BF16)
        vtok = cx.sbring("vtok", 1, [128, 2, 1024], BF16)
        xtok = cx.sbring("xtok", 1, [128, 2, 2048], BF16)
        btok = cx.sbring("btok", 1, [128, 2, 512], BF16)
        smtok = cx.sbring("smtok", 1, [128, 2, 128], F32)
        pst = ps[0]
        psa = Ring([ps[1], ps[2], ps[3], ps[4]])
        pss = ps[5]
        pso = Ring([cx.psum(f"pso{i}", [128, 1024], BF16) for i in range(2)])
        own = dict(qT_d=qT_d, kT_d=kT_d, k_d=k_d, v_d=v_d, x_d=x_d, BT_d=BT_d, CT_d=CT_d, B_d=B_d, sm_d=sm_d)
        par = dict(qT_d=None, kT_d=kT2_d, k_d=k2_d, v_d=v2_d, x_d=x2_d, BT_d=None, CT_d=None, B_d=B2_d, sm_d=sm2_d)
        cp2 = cx.sb("cp2", [128, 48, 4], F32)
        k.dma("sp", cp2[:], convp2, w=[cp2.res], dres=cp2.res)
        runs = [(False, xin, cp, own)]
        if "R" in phases:
            runs.append((True, xin2, cp2, par))

        def preamble(red, xsrc, ti):
            t0 = ti * TT
            is_ctx = ti == 0
            mr = 1 if is_ctx else 0
            xt = xring.next()
            k.dma("sp", xt[:], xsrc[t0:t0 + TT, :].rearrange("(j p) d -> p j d", p=128), w=[xt.res], dres=xt.res)
            ss = ssr.next()
            for j in range(2):
                k.ins("act", lambda j=j: nc.scalar.activation(out=junk[:], in_=xt[:, j, :], func=AF.Square,
                                                              accum_out=ss[:, j:j + 1]), r=[xt.res], w=[junk.res, ss.res])
            k.ins("act", lambda: nc.scalar.activation(out=ss[:], in_=ss[:], func=AF.Ln, scale=1.0 / D, bias=epsc[:, 0:1]),
                  r=[ss.res, epsc.res], w=[ss.res])
            k.ins("act", lambda: nc.scalar.activation(out=ss[:], in_=ss[:], func=AF.Exp, scale=-0.5), r=[ss.res], w=[ss.res])
            aT = aTring.next()
            for j in range(2):
                xn = xnring.next()
                k.ins("act", lambda j=j, xn=xn: nc.scalar.activation(out=xn[:], in_=xt[:, j, :], func=AF.Identity,
                                                                     scale=ss[:, j:j + 1]), r=[xt.res, ss.res], w=[xn.res])
                for half in range(2):
                    k.group("pe", [(lambda q=q, xn=xn, half=half: nc.tensor.transpose(
                        pst[:, q * 128:(q + 1) * 128], xn[:, (half * 4 + q) * 128:(half * 4 + q + 1) * 128], ident[:]))
                        for q in range(4)], r=[xn.res, ident.res], w=[pst.res])
                    for q in range(4):
                        kt = half * 4 + q
                        k.ins("act", lambda q=q, kt=kt, j=j: nc.scalar.activation(
                            out=aT[:, kt, j * 128:(j + 1) * 128], in_=pst[:, q * 128:(q + 1) * 128],
                            func=AF.Identity, scale=S1[:, mr, kt:kt + 1], bias=modf[:, mr, 0, kt:kt + 1]),
                            r=[pst.res, S1.res, modf.res], w=[aT.res])
            if not is_ctx and not red:
                l0 = t0 - NCTX
                k.dma("sp", aT_d[:, :, l0:l0 + TT].rearrange("kt p t -> p kt t"), aT[:], r=[aT.res], w=[R_scr], dres=aT.res)
            return dict(aT=aT, t0=t0, is_ctx=is_ctx, red=red)

        def make_job(tc, ct, cpt, DD, toks):
            aT, t0, is_ctx, red = tc["aT"], tc["t0"], tc["is_ctx"], tc["red"]
            l0 = t0 - NCTX
            rowlen = 256 if is_ctx else 64
            J = {}
            steps = []

            def s_mm():
                J["pa"] = pa = psa.next()
                k.group("pe", [(lambda kt=kt: nc.tensor.matmul(pa[:, 0:TT], wA[:, kt, ct * 128:(ct + 1) * 128], aT[:, kt, :],
                                                               start=(kt == 0), stop=(kt == 7))) for kt in range(8)],
                        r=[wA.res, aT.res], w=[pa.res])
            steps.append(s_mm)
            if ct < 48:
                def s_ident():
                    pa = J["pa"]
                    J["cb"] = cb = cring.next()
                    k.ins("act", lambda: nc.scalar.activation(out=cb[:], in_=pa[:, 0:TT], func=AF.Identity,
                                                              scale=cpt[:, ct, 1:2], bias=cpt[:, ct, 3:4]),
                          r=[pa.res, cpt.res], w=[cb.res])

                def s_taps():
                    pa, cb = J["pa"], J["cb"]
                    pv = pa[:, 0:TT].rearrange("p (r t) -> p r t", t=rowlen)
                    cv3 = cb[:].rearrange("p (r t) -> p r t", t=rowlen)
                    k.ins("dve", lambda: nc.vector.scalar_tensor_tensor(
                        out=cv3[:, :, 1:], in0=pv[:, :, 0:rowlen - 1], scalar=cpt[:, ct, 0:1], in1=cv3[:, :, 1:],
                        op0=ALU.mult, op1=ALU.add), r=[pa.res, cpt.res, cb.res], w=[cb.res])
                    k.ins("dve", lambda: nc.vector.scalar_tensor_tensor(
                        out=cv3[:, :, 0:rowlen - 1], in0=pv[:, :, 1:], scalar=cpt[:, ct, 2:3], in1=cv3[:, :, 0:rowlen - 1],
                        op0=ALU.mult, op1=ALU.add), r=[pa.res, cpt.res, cb.res], w=[cb.res])

                def s_exp():
                    cb = J["cb"]
                    J["ee"] = ee = ering.next()
                    k.ins("act", lambda: nc.scalar.activation(out=ee[:], in_=cb[:], func=AF.Exp, scale=-1.0), r=[cb.res], w=[ee.res])

                def s_recip():
                    ee = J["ee"]
                    k.ins("act", lambda: nc.scalar.activation(out=ee[:], in_=ee[:], func=AF.Ln, bias=1.0), r=[ee.res], w=[ee.res])

                def s_recip2():
                    ee = J["ee"]
                    k.ins("act", lambda: nc.scalar.activation(out=ee[:], in_=ee[:], func=AF.Exp, scale=-1.0), r=[ee.res], w=[ee.res])

                def s_mult():
                    cb, ee = J["cb"], J["ee"]
                    if ct < 16:
                        J["sf"] = sf = sfring.next()
                        k.ins("pool", lambda: nc.gpsimd.tensor_tensor(out=sf[:], in0=cb[:], in1=ee[:], op=ALU.mult),
                              r=[cb.res, ee.res], w=[sf.res])
                    else:
                        J["so"] = so = sring.next()
                        k.ins("pool", lambda: nc.gpsimd.tensor_tensor(out=so[:], in0=cb[:], in1=ee[:], op=ALU.mult),
                              r=[cb.res, ee.res], w=[so.res])
                steps.extend([s_ident, s_taps, s_exp, s_recip, s_recip2, s_mult])
                if ct < 16:
                    def s_sq():
                        sf = J["sf"]
                        J["sq"] = sq = sqring.next()
                        k.ins("pool", lambda: nc.gpsimd.tensor_tensor(out=sq[:], in0=sf[:], in1=sf[:], op=ALU.mult),
                              r=[sf.res], w=[sq.res])

                    def s_sum():
                        sq = J["sq"]
                        k.ins("pe", lambda: nc.tensor.matmul(pss[:, 0:TT], onesf[:], sq[:], start=True, stop=True),
                              r=[onesf.res, sq.res], w=[pss.res])
                        J["rs"] = rs = rsring.next()
                        k.ins("act", lambda: nc.scalar.activation(out=rs[:], in_=pss[:, 0:TT], func=AF.Ln, bias=epsc[:, 0:1]),
                              r=[pss.res, epsc.res], w=[rs.res])

                    def s_rs():
                        rs = J["rs"]
                        k.ins("act", lambda: nc.scalar.activation(out=rs[:], in_=rs[:], func=AF.Exp, scale=-0.5),
                              r=[rs.res], w=[rs.res])

                    def s_norm():
                        sf, rs = J["sf"], J["rs"]
                        J["so"] = so = sring.next()
                        qscale = (128 ** -0.5) if ct < 8 else 1.0
                        k.ins("dve", lambda: nc.vector.scalar_tensor_tensor(out=so[:], in0=sf[:], scalar=qscale, in1=rs[:],
                                                                            op0=ALU.mult, op1=ALU.mult),
                              r=[sf.res, rs.res], w=[so.res])
                    steps.extend([s_sq, s_sum, s_rs, s_norm])

                def s_out():
                    so = J["so"]
                    if ct < 8:
                        k.dma("sp", DD["qT_d"][ct, :, t0:t0 + TT], so[:], r=[so.res], w=[R_scr], dres=so.res)
                    elif ct < 16:
                        k.dma("sp", DD["kT_d"][ct - 8, :, t0:t0 + TT], so[:], r=[so.res], w=[R_scr], dres=so.res)
                    elif 40 <= ct < 44 and not red:
                        k.dma("sp", DD["BT_d"][ct - 40, :, t0:t0 + TT], so[:], r=[so.res], w=[R_scr], dres=so.res)
                    elif 44 <= ct < 48:
                        k.dma("sp", DD["CT_d"][ct - 44, :, t0:t0 + TT], so[:], r=[so.res], w=[R_scr], dres=so.res)
                    tgt = None
                    if 8 <= ct < 16:
                        tgt = (toks["k"], (ct - 8) * 128)
                    elif 16 <= ct < 24:
                        tgt = (toks["v"], (ct - 16) * 128)
                    elif 24 <= ct < 40:
                        tgt = (toks["x"], (ct - 24) * 128)
                    elif 40 <= ct < 44:
                        tgt = (toks["b"], (ct - 40) * 128)
                    J["tgt"] = tgt
                    if tgt is not None:
                        J["po"] = po = pso.next()
                        k.group("pe", [(lambda j=j: nc.tensor.transpose(po[:, j * 128:(j + 1) * 128], so[:, j * 128:(j + 1) * 128],
                                                                        identb[:])) for j in range(2)],
                                r=[so.res, identb.res], w=[po.res])

                def s_tcopy():
                    if J["tgt"] is not None:
                        tb, off = J["tgt"]
                        po = J["po"]
                        k.ins("act", lambda: nc.scalar.copy(out=tb[:, :, off:off + 128],
                                                            in_=po[:, 0:256].rearrange("p (j c) -> p j c", c=128)),
                              r=[po.res], w=[tb.res])
                steps.extend([s_out, s_tcopy])
            elif ct < 64:
                def g_exp():
                    pa = J["pa"]
                    J["sg"] = sg = sgring.next()
                    k.ins("act", lambda: nc.scalar.activation(out=sg[:], in_=pa[:, 0:TT], func=AF.Exp, scale=-1.0),
                          r=[pa.res], w=[sg.res])

                def g_recip():
                    sg = J["sg"]
                    k.ins("act", lambda: nc.scalar.activation(out=sg[:], in_=sg[:], func=AF.Ln, bias=1.0), r=[sg.res], w=[sg.res])

                def g_recip2():
                    sg = J["sg"]
                    k.ins("act", lambda: nc.scalar.activation(out=sg[:], in_=sg[:], func=AF.Exp, scale=-1.0), r=[sg.res], w=[sg.res])

                def g_out():
                    sg = J["sg"]
                    k.dma("sp", sgT_d[ct - 48, :, l0:l0 + TT], sg[:], r=[sg.res], w=[R_scr], dres=sg.res)
                steps.extend([g_exp, g_recip, g_recip2, g_out])
            else:
                st_ = toks["sm"]

                def m_exp():
                    pa = J["pa"]
                    J["sm"] = sm = smring.next()
                    k.ins("act", lambda: nc.scalar.activation(out=sm[:], in_=pa[:, 0:TT], func=AF.Exp, scale=smp[:, 0:1],
                                                              bias=smp[:, 1:2]), r=[pa.res, smp.res], w=[sm.res])

                def m_ln():
                    sm = J["sm"]
                    k.ins("act", lambda: nc.scalar.activation(out=sm[:], in_=sm[:], func=AF.Ln, bias=1.0), r=[sm.res], w=[sm.res])

                def m_mul():
                    sm = J["sm"]
                    k.ins("dve", lambda: nc.vector.tensor_scalar(out=sm[:], in0=sm[:], scalar1=smult[:, 0:1], scalar2=None,
                                                              op0=ALU.mult), r=[sm.res, smult.res], w=[sm.res])

                def m_tr():
                    sm = J["sm"]
                    k.group("pe", [(lambda j=j: nc.tensor.transpose(pst[:, j * 128:(j + 1) * 128], sm[:, j * 128:(j + 1) * 128],
                                                                    ident[:])) for j in range(2)],
                            r=[sm.res, ident.res], w=[pst.res])

                def m_copy():
                    k.ins("dve", lambda: nc.vector.tensor_copy(st_[:], pst[:, 0:256].rearrange("p (j c) -> p j c", c=128)),
                          r=[pst.res], w=[st_.res])
                steps.extend([m_exp, m_ln, m_mul, m_tr, m_copy])
            return steps

        def make_spill(tc, DD, toks):
            t0 = tc["t0"]

            def spill():
                rows = lambda d_: d_[t0:t0 + TT, :].rearrange("(j p) c -> p j c", p=128)
                for key, dn in (("k", "k_d"), ("v", "v_d"), ("x", "x_d"), ("b", "B_d"), ("sm", "sm_d")):
                    tb = toks[key]
                    k.dma("sp", rows(DD[dn]), tb[:], r=[tb.res], w=[R_scr], dres=tb.res)
            return spill

        NST = 14
        pipeline = []
        it = 0
        tiles = [(red, xsrc, cpt, DD, ti) for (red, xsrc, cpt, DD) in runs for ti in range(NTOK // TT)]
        pend_pre = {}

        def run_pipeline_until(limit):
            nonlocal it
            while it < limit:
                for (st0, steps) in pipeline:
                    sidx = it - st0
                    if 0 <= sidx < len(steps):
                        steps[sidx]()
                pipeline[:] = [(a, b) for (a, b) in pipeline if it - a < len(b) - 1]
                it += 1

        tcs = [None] * len(tiles)
        tcs[0] = preamble(tiles[0][0], tiles[0][1], tiles[0][4])
        for n, (red, xsrc, cpt, DD, ti) in enumerate(tiles):
            tc = tcs[n]
            toks = dict(k=ktok.next(), v=vtok.next(), x=xtok.next(), b=btok.next(), sm=smtok.next())
            cts = [ct for ct in range(65)
                   if not ((tc["is_ctx"] or red) and 48 <= ct < 64) and not (red and (ct < 8 or 44 <= ct < 48))]
            for idx, ct in enumerate(cts):
                pipeline.append((it, make_job(tc, ct, cpt, DD, toks)))
                run_pipeline_until(it + 1)
                if idx == 6 and n + 1 < len(tiles):
                    tcs[n + 1] = preamble(tiles[n + 1][0], tiles[n + 1][1], tiles[n + 1][4])
            pipeline.append((it + 6, [make_spill(tc, DD, toks)]))
        run_pipeline_until(it + NST + 2)
        cx.pop()


    def load_scan_consts():
        tri = cx.sb("tri", [128, 4, 128], F32)
        k.dma("sp", tri[:], tri_d, w=[tri.res], dres=tri.res)
        mskf = cx.sb("mskf", [128, 4, 128], F32)
        k.dma("sp", mskf[:], msk_d, w=[mskf.res], dres=mskf.res)
        mskb = cx.sb("mskb", [128, 4, 128], BF16)
        k.ins("dve", lambda: nc.vector.tensor_copy(mskb[:], mskf[:]), r=[mskf.res], w=[mskb.res])
        id4 = cx.sb("id4", [128, 4, 128], F32)
        for j in range(4):
            k.ins("dve", lambda j=j: nc.vector.tensor_copy(id4[:, j, :], ident[:]), r=[ident.res], w=[id4.res])
        return tri, mskb, id4

    def gdn_pass(mode):
        cx.push()
        fwd = mode != "own2"
        pi = 0 if mode == "own1" else 1
        red = mode == "red"
        s_kT, s_k, s_v, s_sm = (kT2_d, k2_d, v2_d, sm2_d) if red else (kT_d, k_d, v_d, sm_d)
        tri, mskb, id4 = load_scan_consts()
        cumL = tri[:, 0, :] if fwd else tri[:, 1, :]
        negR = tri[:, 2, :] if fwd else tri[:, 3, :]
        m_s = mskb[:, 0, :] if fwd else mskb[:, 1, :]
        mT_s = mskb[:, 1, :] if fwd else mskb[:, 0, :]
        mT_i = mskb[:, 3, :] if fwd else mskb[:, 2, :]
        g0 = pi * 8
        l0c = 32 + pi * 8
        banks = Ring([cx.psum(f"gb{i}", [128, 512], F32) for i in range(8)])
        S32 = cx.sb("S32", [128, 8, 128], F32)
        Sbf = cx.sb("Sbf", [128, 8, 128], BF16)
        if fwd:
            k.ins("pool", lambda: nc.gpsimd.memset(S32[:], 0.0), w=[S32.res])
        else:
            k.dma("sp", S32[:].rearrange("p h d -> p (h d)"), st2_d[:, 0:1024], r=[R_st2], w=[S32.res], dres=S32.res)
        k.ins("act", lambda: nc.scalar.copy(out=Sbf[:], in_=S32[:]), r=[S32.res], w=[Sbf.res])
        NB = 2
        qTr = cx.sbring("qTc", NB, [128, 8, 128], BF16)
        kTr = cx.sbring("kTc", NB, [128, 8, 128], BF16)
        kr = cx.sbring("kc", NB, [128, 8, 128], BF16)
        vr = cx.sbring("vc", NB, [128, 8, 128], BF16)
        smr = cx.sbring("smc", NB, [128, 128], F32)
        o1r = cx.sbring("o1c", NB, [128, 8, 128], F32)
        kbgr = cx.sbring("kbg", NB, [128, 8, 128], BF16)
        vbr = cx.sbring("vb", NB, [128, 8, 128], BF16)
        kdr = cx.sbring("kd", NB, [128, 8, 128], BF16)
        smallr = cx.sbring("gsm", NB, [128, 6, 8], F32)
        eglr = cx.sbring("egl", NB, [128, 8], F32)
        expr = cx.sbring("exps", NB, [128, 3, 8], F32)
        Ear = cx.sbring("Ea", 2, [128, 4, 128], F32)
        Ebr = cx.sbring("Eb", 2, [128, 4, 128], F32)
        Ecr = cx.sbring("Ec", 2, [128, 4, 128], F32)
        Edr = cx.sbring("Ed", 2, [128, 4, 128], F32)
        Pr = cx.sbring("Pp", 4, [128, 4, 128], F32)
        PTr = cx.sbring("PTp", 4, [128, 4, 128], F32)
        Yr = cx.sbring("Yp", 4, [128, 4, 128], F32)
        Yfr = cx.sbring("Yf", 2 * NB, [128, 4, 128], BF16)
        nWTr = cx.sbring("nWT", 2 * NB, [128, 4, 128], BF16)
        attr = cx.sbring("att", 2 * NB, [128, 4, 128], BF16)
        qgr = cx.sbring("qg", 2 * NB, [128, 4, 128], BF16)
        vnr = cx.sbring("vn", 2, [128, 4, 128], BF16)
        oTr = cx.sbring("oTs", 2, [128, 8, 128], F32)

        def prep(c):
            lat = c >= 2 and not red
            t0 = c * C
            l0 = t0 - NCTX
            qT_c = qTr.next(); kT_c = kTr.next(); k_c = kr.next(); v_c = vr.next(); sm_c = smr.next()
            if lat:
                k.dma("sp", qT_c[:], qT_d[:, :, t0:t0 + C].rearrange("h p t -> p h t"), r=[R_scr], w=[qT_c.res], dres=qT_c.res)
            k.dma("sp", kT_c[:], s_kT[:, :, t0:t0 + C].rearrange("h p t -> p h t"), r=[R_scr], w=[kT_c.res], dres=kT_c.res)
            k.dma("sp", k_c[:], s_k[t0:t0 + C, :].rearrange("t (h d) -> t h d", d=128), r=[R_scr], w=[k_c.res], dres=k_c.res)
            k.dma("sp", v_c[:], s_v[t0:t0 + C, :].rearrange("t (h d) -> t h d", d=128), r=[R_scr], w=[v_c.res], dres=v_c.res)
            k.dma("sp", sm_c[:], s_sm[t0:t0 + C, :], r=[R_scr], w=[sm_c.res], dres=sm_c.res)
            o1_c = None
            if lat and not fwd:
                o1_c = o1r.next()
                k.dma("sp", o1_c[:], o1_d[:, :, l0:l0 + C].rearrange("h p t -> p h t"), r=[R_o1[c]], w=[o1_c.res], dres=o1_c.res)
            gcols = sm_c[:, g0:g0 + 8]
            lcols = sm_c[:, l0c:l0c + 8]
            pb = banks.next()
            k.group("pe", [
                lambda: nc.tensor.matmul(pb[:, 0:8], cumL, gcols, start=True, stop=True),
                lambda: nc.tensor.matmul(pb[:, 8:16], onesf[:], gcols, start=True, stop=True)],
                r=[tri.res, onesf.res, sm_c.res], w=[pb.res])
            sm6 = smallr.next()
            gc, gcl, ngc, tmp = sm6[:, 0, :], sm6[:, 1, :], sm6[:, 2, :], sm6[:, 3, :]
            k.ins("dve", lambda: nc.vector.tensor_copy(gc, pb[:, 0:8]), r=[pb.res], w=[sm6.res])
            k.ins("dve", lambda: nc.vector.tensor_tensor(out=gcl, in0=gc, in1=lcols, op=ALU.add), r=[sm6.res, sm_c.res], w=[sm6.res])
            k.ins("dve", lambda: nc.vector.tensor_scalar(out=ngc, in0=gc, scalar1=-1.0, scalar2=None, op0=ALU.mult),
                  r=[sm6.res], w=[sm6.res])
            k.ins("dve", lambda: nc.vector.tensor_tensor(out=tmp, in0=pb[:, 8:16], in1=gc, op=ALU.subtract),
                  r=[pb.res, sm6.res], w=[sm6.res])
            ex = expr.next()
            egl = eglr.next()
            k.ins("act", lambda: nc.scalar.activation(out=ex[:, 0, :], in_=gcl, func=AF.Exp), r=[sm6.res], w=[ex.res])
            k.ins("act", lambda: nc.scalar.activation(out=ex[:, 1, :], in_=lcols, func=AF.Exp), r=[sm_c.res], w=[ex.res])
            k.ins("act", lambda: nc.scalar.activation(out=ex[:, 2, :], in_=tmp, func=AF.Exp), r=[sm6.res], w=[ex.res])
            k.ins("act", lambda: nc.scalar.activation(out=egl[:], in_=pb[:, 8:16], func=AF.Exp), r=[pb.res], w=[egl.res])
            kbg = kbgr.next(); vb = vbr.next(); kd = kdr.next()
            bc = lambda col: ex[:, col, :].unsqueeze(2).to_broadcast([128, 8, 128])
            k.ins("pool", lambda: nc.gpsimd.tensor_tensor(out=kbg[:], in0=k_c[:], in1=bc(0), op=ALU.mult),
                  r=[k_c.res, ex.res], w=[kbg.res])
            k.ins("pool", lambda: nc.gpsimd.tensor_tensor(out=vb[:], in0=v_c[:], in1=bc(1), op=ALU.mult),
                  r=[v_c.res, ex.res], w=[vb.res])
            k.ins("pool", lambda: nc.gpsimd.tensor_tensor(out=kd[:], in0=k_c[:], in1=bc(2), op=ALU.mult),
                  r=[k_c.res, ex.res], w=[kd.res])
            pp = dict(c=c, lat=lat, groups=[None, None], vb=vb, kd=kd, egl=egl, o1=o1_c)

            def grp(gi):
                h0 = gi * 4
                gb = lambda h: sm_c[:, g0 + h:g0 + h + 1].to_broadcast([128, 128])
                lb = lambda h: sm_c[:, l0c + h:l0c + h + 1].to_broadcast([128, 128])
                KK = banks.next()
                k.group("pe", [(lambda j=j: nc.tensor.matmul(KK[:, j * 128:(j + 1) * 128], kT_c[:, h0 + j, :], kT_c[:, h0 + j, :],
                                                             start=True, stop=True)) for j in range(4)],
                        r=[kT_c.res], w=[KK.res])
                Da = banks.next()
                fl = []
                for j in range(4):
                    fl.append(lambda j=j: nc.tensor.matmul(Da[:, j * 128:(j + 1) * 128], gb(h0 + j), negR, start=True, stop=False))
                    fl.append(lambda j=j: nc.tensor.matmul(Da[:, j * 128:(j + 1) * 128], identb[:], m_s, start=False, stop=True))
                k.group("pe", fl, r=[sm_c.res, tri.res, identb.res, mskb.res], w=[Da.res])
                Ea = Ear.next()
                for j in range(4):
                    k.ins("act", lambda j=j: nc.scalar.activation(out=Ea[:, j, :], in_=Da[:, j * 128:(j + 1) * 128], func=AF.Exp,
                                                                  bias=sm6[:, 1, h0 + j:h0 + j + 1]),
                          r=[Da.res, sm6.res], w=[Ea.res])
                Db = banks.next()
                fl = []
                for j in range(4):
                    sl = slice(j * 128, (j + 1) * 128)
                    fl.append(lambda j=j, sl=sl: nc.tensor.matmul(Db[:, sl], gb(h0 + j), cumL, start=True, stop=False))
                    fl.append(lambda j=j, sl=sl: nc.tensor.matmul(Db[:, sl], lb(h0 + j), ident[:], start=False, stop=False))
                    fl.append(lambda j=j, sl=sl: nc.tensor.matmul(Db[:, sl], identb[:], mT_s, start=False, stop=True))
                k.group("pe", fl, r=[sm_c.res, tri.res, ident.res, identb.res, mskb.res], w=[Db.res])
                Eb = Ebr.next()
                for j in range(4):
                    k.ins("act", lambda j=j: nc.scalar.activation(out=Eb[:, j, :], in_=Db[:, j * 128:(j + 1) * 128], func=AF.Exp,
                                                                  bias=sm6[:, 2, h0 + j:h0 + j + 1]),
                          r=[Db.res, sm6.res], w=[Eb.res])
                P0 = Pr.next(); P0T = PTr.next()
                KK3 = KK[:].rearrange("p (j c) -> p j c", c=128)
                k.ins("dve", lambda: nc.vector.scalar_tensor_tensor(out=P0[:], in0=KK3, scalar=-1.0, in1=Ea[:],
                                                                    op0=ALU.mult, op1=ALU.mult),
                      r=[KK.res, Ea.res], w=[P0.res])
                k.ins("dve", lambda: nc.vector.scalar_tensor_tensor(out=P0T[:], in0=KK3, scalar=-1.0, in1=Eb[:],
                                                                    op0=ALU.mult, op1=ALU.mult),
                      r=[KK.res, Eb.res], w=[P0T.res])
                yield
                att = None; qg = None
                if lat:
                    QK = banks.next()
                    k.group("pe", [(lambda j=j: nc.tensor.matmul(QK[:, j * 128:(j + 1) * 128], kT_c[:, h0 + j, :], qT_c[:, h0 + j, :],
                                                                 start=True, stop=True)) for j in range(4)],
                            r=[kT_c.res, qT_c.res], w=[QK.res])
                    Dc = banks.next()
                    fl = []
                    for j in range(4):
                        sl = slice(j * 128, (j + 1) * 128)
                        fl.append(lambda j=j, sl=sl: nc.tensor.matmul(Dc[:, sl], gb(h0 + j), cumL, start=True, stop=False))
                        fl.append(lambda j=j, sl=sl: nc.tensor.matmul(Dc[:, sl], identb[:], mT_i, start=False, stop=True))
                    k.group("pe", fl, r=[sm_c.res, tri.res, identb.res, mskb.res], w=[Dc.res])
                    Ec = Ecr.next()
                    for j in range(4):
                        k.ins("act", lambda j=j: nc.scalar.activation(out=Ec[:, j, :], in_=Dc[:, j * 128:(j + 1) * 128], func=AF.Exp,
                                                                      bias=sm6[:, 2, h0 + j:h0 + j + 1]),
                              r=[Dc.res, sm6.res], w=[Ec.res])
                    Dd = banks.next()
                    k.group("pe", [(lambda j=j: nc.tensor.matmul(Dd[:, j * 128:(j + 1) * 128], gb(h0 + j), cumL, start=True, stop=True))
                                   for j in range(4)], r=[sm_c.res, tri.res], w=[Dd.res])
                    Ed = Edr.next()
                    k.ins("act", lambda: nc.scalar.activation(out=Ed[:].rearrange("p j c -> p (j c)"), in_=Dd[:], func=AF.Exp),
                          r=[Dd.res], w=[Ed.res])
                    att = attr.next()
                    k.ins("dve", lambda: nc.vector.tensor_tensor(out=att[:], in0=QK[:].rearrange("p (j c) -> p j c", c=128),
                                                                 in1=Ec[:], op=ALU.mult), r=[QK.res, Ec.res], w=[att.res])
                    qg = qgr.next()
                    k.ins("pool", lambda: nc.gpsimd.tensor_tensor(out=qg[:], in0=qT_c[:, h0:h0 + 4, :], in1=Ed[:], op=ALU.mult),
                          r=[qT_c.res, Ed.res], w=[qg.res])
                Y = Yr.next()
                k.ins("pool", lambda: nc.gpsimd.tensor_tensor(out=Y[:], in0=P0T[:], in1=id4[:], op=ALU.add),
                      r=[P0T.res, id4.res], w=[Y.res])
                yield
                Pp, PTp = P0, P0T
                for lev in range(1, 7):
                    last = lev == 6
                    Pb = banks.next()
                    k.group("pe", [(lambda j=j, Pp=Pp, PTp=PTp, Pb=Pb: nc.tensor.matmul(
                        Pb[:, j * 128:(j + 1) * 128], PTp[:, j, :], Pp[:, j, :], start=True, stop=True)) for j in range(4)],
                        r=[Pp.res, PTp.res], w=[Pb.res])
                    Pn = Pr.next()
                    k.ins("act", lambda Pn=Pn, Pb=Pb: nc.scalar.copy(out=Pn[:].rearrange("p j c -> p (j c)"), in_=Pb[:]),
                          r=[Pb.res], w=[Pn.res])
                    PTn = None
                    if not last:
                        PTb = banks.next()
                        k.group("pe", [(lambda j=j, Pp=Pp, PTp=PTp, PTb=PTb: nc.tensor.matmul(
                            PTb[:, j * 128:(j + 1) * 128], Pp[:, j, :], PTp[:, j, :], start=True, stop=True)) for j in range(4)],
                            r=[Pp.res, PTp.res], w=[PTb.res])
                        PTn = PTr.next()
                        k.ins("act", lambda PTn=PTn, PTb=PTb: nc.scalar.copy(out=PTn[:].rearrange("p j c -> p (j c)"), in_=PTb[:]),
                              r=[PTb.res], w=[PTn.res])
                    yield
                    Yb = banks.next()
                    k.group("pe", [(lambda j=j, Yb=Yb, Y=Y, Pn=Pn: nc.tensor.matmul(
                        Yb[:, j * 128:(j + 1) * 128], Pn[:, j, :], Y[:, j, :], start=True, stop=True)) for j in range(4)],
                        r=[Y.res, Pn.res], w=[Yb.res])
                    if last:
                        Yn = Yfr.next()
                        k.ins("dve", lambda Yn=Yn, Yb=Yb, Y=Y: nc.vector.tensor_tensor(
                            out=Yn[:].rearrange("p j c -> p (j c)"), in0=Yb[:], in1=Y[:].rearrange("p j c -> p (j c)"), op=ALU.add),
                            r=[Yb.res, Y.res], w=[Yn.res])
                    else:
                        Yn = Yr.next()
                        k.ins("dve", lambda Yn=Yn, Yb=Yb, Y=Y: nc.vector.tensor_tensor(
                            out=Yn[:].rearrange("p j c -> p (j c)"), in0=Yb[:], in1=Y[:].rearrange("p j c -> p (j c)"), op=ALU.add),
                            r=[Yb.res, Y.res], w=[Yn.res])
                    Y = Yn
                    Pp, PTp = Pn, PTn
                    yield
                Wb = banks.next()
                k.group("pe", [(lambda j=j: nc.tensor.matmul(Wb[:, j * 128:(j + 1) * 128], kbg[:, h0 + j, :], Y[:, j, :],
                                                             start=True, stop=True)) for j in range(4)],
                        r=[kbg.res, Y.res], w=[Wb.res])
                nWT = nWTr.next()
                k.ins("act", lambda: nc.scalar.activation(out=nWT[:].rearrange("p j c -> p (j c)"), in_=Wb[:], func=AF.Identity,
                                                          scale=-1.0), r=[Wb.res], w=[nWT.res])
                pp["groups"][gi] = (Y, nWT, att, qg)
            pp["gens"] = [grp(0), grp(1)]
            return pp

        def chain(pp):
            c, lat = pp["c"], pp["lat"]
            l0 = c * C - NCTX
            vb, kd, egl = pp["vb"], pp["kd"], pp["egl"]
            oTs = oTr.next() if lat else None
            for gi in range(2):
                h0 = gi * 4
                Y, nWT, att, qg = pp["groups"][gi]
                VN = banks.next()
                fl = []
                for j in range(4):
                    sl = slice(j * 128, (j + 1) * 128)
                    fl.append(lambda j=j, sl=sl: nc.tensor.matmul(VN[:, sl], Y[:, j, :], vb[:, h0 + j, :], start=True, stop=False))
                    fl.append(lambda j=j, sl=sl: nc.tensor.matmul(VN[:, sl], nWT[:, j, :], Sbf[:, h0 + j, :], start=False, stop=True))
                k.group("pe", fl, r=[Y.res, vb.res, nWT.res, Sbf.res], w=[VN.res])
                vn = vnr.next()
                k.ins("dve", lambda: nc.vector.tensor_copy(vn[:].rearrange("p j c -> p (j c)"), VN[:]), r=[VN.res], w=[vn.res])
                yield
                if lat:
                    OT = banks.next()
                    fl = []
                    for j in range(4):
                        sl = slice(j * 128, (j + 1) * 128)
                        fl.append(lambda j=j, sl=sl: nc.tensor.matmul(OT[:, sl], Sbf[:, h0 + j, :], qg[:, j, :], start=True, stop=False))
                        fl.append(lambda j=j, sl=sl: nc.tensor.matmul(OT[:, sl], vn[:, j, :], att[:, j, :], start=False, stop=True))
                    k.group("pe", fl, r=[Sbf.res, qg.res, vn.res, att.res], w=[OT.res])
                    osl = oTs[:, h0:h0 + 4, :].rearrange("p j c -> p (j c)")
                    if fwd:
                        k.ins("act", lambda: nc.scalar.copy(out=osl, in_=OT[:]), r=[OT.res], w=[oTs.res])
                    else:
                        o1_c = pp["o1"]
                        k.ins("dve", lambda: nc.vector.tensor_tensor(
                            out=osl, in0=OT[:], in1=o1_c[:, h0:h0 + 4, :].rearrange("p j c -> p (j c)"), op=ALU.add),
                            r=[OT.res, o1_c.res], w=[oTs.res])
                DS = banks.next()
                k.group("pe", [(lambda j=j: nc.tensor.matmul(DS[:, j * 128:(j + 1) * 128], kd[:, h0 + j, :], vn[:, j, :],
                                                             start=True, stop=True)) for j in range(4)],
                        r=[kd.res, vn.res], w=[DS.res])
                ssl = S32[:, h0:h0 + 4, :]
                k.ins("pool", lambda: nc.gpsimd.tensor_tensor(
                    out=ssl, in0=ssl, in1=egl[:, h0:h0 + 4].unsqueeze(2).to_broadcast([128, 4, 128]), op=ALU.mult),
                    r=[S32.res, egl.res], w=[S32.res])
                k.ins("dve", lambda: nc.vector.tensor_tensor(out=ssl, in0=ssl, in1=DS[:].rearrange("p (j c) -> p j c", c=128),
                                                             op=ALU.add), r=[S32.res, DS.res], w=[S32.res])
                k.ins("act", lambda: nc.scalar.copy(out=Sbf[:, h0:h0 + 4, :], in_=ssl), r=[S32.res], w=[Sbf.res])
                yield
            if lat:
                if fwd:
                    k.dma("sp", o1_d[:, :, l0:l0 + C].rearrange("h p t -> p h t"), oTs[:], r=[oTs.res], w=[R_o1[c]], dres=oTs.res)
                else:
                    k.dma("sp", oT_d[:, :, l0:l0 + C].rearrange("h p t -> p h t"), oTs[:], r=[oTs.res], w=[R_oT], dres=oTs.res)

        order = list(range(NCH)) if fwd else list(range(NCH - 1, 1, -1))
        if GDN_LIMIT:
            order = order[:GDN_LIMIT]
        def drive(gens):
            gens = list(gens)
            while gens:
                for g_ in list(gens):
                    try:
                        next(g_)
                    except StopIteration:
                        gens.remove(g_)

        pend = prep(order[0])
        drive(pend["gens"])
        for idx in range(len(order)):
            nxt = prep(order[idx + 1]) if idx + 1 < len(order) else None
            drive([chain(pend)] + (nxt["gens"] if nxt is not None else []))
            pend = nxt
        if red:
            k.dma("sp", st2_d[:, 0:1024], S32[:].rearrange("p h d -> p (h d)"), r=[S32.res], w=[R_st2], dres=S32.res)
        cx.pop()

    def ssd_pass(mode):
        cx.push()
        fwd = mode != "own2"
        pi = 0 if mode == "own1" else 1
        red = mode == "red"
        s_x, s_B, s_sm = (x2_d, B2_d, sm2_d) if red else (x_d, B_d, sm_d)
        tri, mskb, id4 = load_scan_consts()
        cumL = tri[:, 0, :] if fwd else tri[:, 1, :]
        mT_i = mskb[:, 3, :] if fwd else mskb[:, 2, :]
        d0 = 64 + pi * 32
        banks = Ring([cx.psum(f"sbk{i}", [128, 512], F32) for i in range(8)])
        aB = cx.sb("aB", [128, 32], F32)
        k.dma("sp", aB[:], salog_d[:, pi, :], w=[aB.res], dres=aB.res)
        k.ins("act", lambda: nc.scalar.activation(out=aB[:], in_=aB[:], func=AF.Exp), r=[aB.res], w=[aB.res])
        k.ins("dve", lambda: nc.vector.tensor_scalar(out=aB[:], in0=aB[:], scalar1=-1.0, scalar2=None, op0=ALU.mult),
              r=[aB.res], w=[aB.res])
        hT32 = cx.sb("hT32", [128, 32, 64], F32)
        hTbf = cx.sb("hTbf", [128, 32, 64], BF16)
        if fwd:
            k.ins("pool", lambda: nc.gpsimd.memset(hT32[:], 0.0), w=[hT32.res])
        else:
            k.dma("sp", hT32[:].rearrange("p h d -> p (h d)"), st2_d[:, 1024:3072], r=[R_st2], w=[hT32.res], dres=hT32.res)
        k.ins("act", lambda: nc.scalar.copy(out=hTbf[:], in_=hT32[:]), r=[hT32.res], w=[hTbf.res])
        xr = cx.sbring("xc", 2, [128, 32, 64], BF16)
        BTr = cx.sbring("BTc", 2, [128, 4, 128], BF16)
        CTr = cx.sbring("CTc", 2, [128, 4, 128], BF16)
        Br = cx.sbring("Bc", 2, [128, 4, 128], BF16)
        smr = cx.sbring("smc", 2, [128, 128], F32)
        y1r = cx.sbring("y1c", 2, [128, 2048], F32)
        dAr = cx.sbring("dA", 2, [128, 32], F32)
        s5r = cx.sbring("s5", 2, [128, 6, 32], F32)
        xdtr = cx.sbring("xdt", 2, [128, 32, 64], BF16)
        xdtsr = cx.sbring("xdts", 2, [128, 32, 64], BF16)
        ytr = cx.sbring("ytmp", 2, [128, 32, 64], F32)
        ycr = cx.sbring("yc", 2, [128, 2048], F32)
        segr = cx.sbring("seg", 3, [128, 4, 128], F32)
        GTr = cx.sbring("GT", 2, [128, 4, 128], F32)
        MTr = cx.sbring("MT", 3, [128, 4, 128], BF16)
        order = list(range(NCH)) if fwd else list(range(NCH - 1, 1, -1))
        if GDN_LIMIT:
            order = order[:GDN_LIMIT]
        for c in order:
            lat = c >= 2 and not red
            t0 = c * C
            l0 = t0 - NCTX
            x_c = xr.next(); BT_c = BTr.next(); CT_c = CTr.next(); B_c = Br.next(); sm_c = smr.next()
            k.dma("sp", x_c[:].rearrange("p h d -> p (h d)"), s_x[t0:t0 + C, :], r=[R_scr], w=[x_c.res], dres=x_c.res)
            if lat:
                k.dma("sp", BT_c[:], BT_d[:, :, t0:t0 + C].rearrange("g p t -> p g t"), r=[R_scr], w=[BT_c.res], dres=BT_c.res)
                k.dma("sp", CT_c[:], CT_d[:, :, t0:t0 + C].rearrange("g p t -> p g t"), r=[R_scr], w=[CT_c.res], dres=CT_c.res)
            k.dma("sp", B_c[:], s_B[t0:t0 + C, :].rearrange("t (g n) -> t g n", n=128), r=[R_scr], w=[B_c.res], dres=B_c.res)
            k.dma("sp", sm_c[:], s_sm[t0:t0 + C, :], r=[R_scr], w=[sm_c.res], dres=sm_c.res)
            y1_c = None
            if lat and not fwd:
                y1_c = y1r.next()
                k.dma("sp", y1_c[:], y1_d[l0:l0 + C, :], r=[R_y1[c]], w=[y1_c.res], dres=y1_c.res)
            dtc = sm_c[:, d0:d0 + 32]
            dA = dAr.next()
            k.ins("dve", lambda: nc.vector.tensor_tensor(out=dA[:], in0=dtc, in1=aB[:], op=ALU.mult),
                  r=[sm_c.res, aB.res], w=[dA.res])
            pb = banks.next()
            k.group("pe", [
                lambda: nc.tensor.matmul(pb[:, 0:32], cumL, dA[:], start=True, stop=True),
                lambda: nc.tensor.matmul(pb[:, 32:64], onesf[:], dA[:], start=True, stop=True)],
                r=[tri.res, onesf.res, dA.res], w=[pb.res])
            s5 = s5r.next()
            ac, nac, tmp, ea, w2, eat = (s5[:, i, :] for i in range(6))
            k.ins("dve", lambda: nc.vector.tensor_copy(ac, pb[:, 0:32]), r=[pb.res], w=[s5.res])
            k.ins("dve", lambda: nc.vector.tensor_scalar(out=nac, in0=ac, scalar1=-1.0, scalar2=None, op0=ALU.mult),
                  r=[s5.res], w=[s5.res])
            k.ins("dve", lambda: nc.vector.tensor_tensor(out=tmp, in0=pb[:, 32:64], in1=ac, op=ALU.subtract),
                  r=[pb.res, s5.res], w=[s5.res])
            k.ins("act", lambda: nc.scalar.activation(out=ea, in_=ac, func=AF.Exp), r=[s5.res], w=[s5.res])
            k.ins("act", lambda: nc.scalar.activation(out=w2, in_=tmp, func=AF.Exp), r=[s5.res], w=[s5.res])
            k.ins("act", lambda: nc.scalar.activation(out=eat, in_=pb[:, 32:64], func=AF.Exp), r=[pb.res], w=[s5.res])
            k.ins("dve", lambda: nc.vector.tensor_tensor(out=w2, in0=w2, in1=dtc, op=ALU.mult), r=[s5.res, sm_c.res], w=[s5.res])
            bc32 = lambda ap: ap.unsqueeze(2).to_broadcast([128, 32, 64])
            xdts = xdtsr.next()
            k.ins("pool", lambda: nc.gpsimd.tensor_tensor(out=xdts[:], in0=x_c[:], in1=bc32(w2), op=ALU.mult),
                  r=[x_c.res, s5.res], w=[xdts.res])
            ytmp = None
            if lat:
                xdt = xdtr.next()
                k.ins("pool", lambda: nc.gpsimd.tensor_tensor(out=xdt[:], in0=x_c[:], in1=bc32(dtc), op=ALU.mult),
                      r=[x_c.res, sm_c.res], w=[xdt.res])
                ytmp = ytr.next()
                for g in range(4):
                    YO = banks.next()
                    k.ins("pe", lambda g=g, YO=YO: nc.tensor.matmul(
                        YO[:], CT_c[:, g, :], hTbf[:, g * 8:(g + 1) * 8, :].rearrange("p h d -> p (h d)"), start=True, stop=True),
                        r=[CT_c.res, hTbf.res], w=[YO.res])
                    k.ins("dve", lambda g=g, YO=YO: nc.vector.tensor_tensor(
                        out=ytmp[:, g * 8:(g + 1) * 8, :], in0=YO[:].rearrange("p (h d) -> p h d", d=64),
                        in1=ea[:, g * 8:(g + 1) * 8].unsqueeze(2).to_broadcast([128, 8, 64]), op=ALU.mult),
                        r=[YO.res, s5.res], w=[ytmp.res])
            k.ins("pool", lambda: nc.gpsimd.tensor_tensor(out=hT32[:], in0=hT32[:], in1=bc32(eat), op=ALU.mult),
                  r=[hT32.res, s5.res], w=[hT32.res])
            for g in range(4):
                NS = banks.next()
                k.ins("pe", lambda g=g, NS=NS: nc.tensor.matmul(
                    NS[:], B_c[:, g, :], xdts[:, g * 8:(g + 1) * 8, :].rearrange("p h d -> p (h d)"), start=True, stop=True),
                    r=[B_c.res, xdts.res], w=[NS.res])
                hs = hT32[:, g * 8:(g + 1) * 8, :]
                k.ins("dve", lambda g=g, NS=NS, hs=hs: nc.vector.tensor_tensor(
                    out=hs, in0=hs, in1=NS[:].rearrange("p (h d) -> p h d", d=64), op=ALU.add),
                    r=[hT32.res, NS.res], w=[hT32.res])
            k.ins("act", lambda: nc.scalar.copy(out=hTbf[:], in_=hT32[:]), r=[hT32.res], w=[hTbf.res])
            if not lat:
                continue
            GTb = banks.next()
            k.group("pe", [(lambda g=g: nc.tensor.matmul(GTb[:, g * 128:(g + 1) * 128], BT_c[:, g, :], CT_c[:, g, :],
                                                         start=True, stop=True)) for g in range(4)],
                    r=[BT_c.res, CT_c.res], w=[GTb.res])
            GT = GTr.next()
            k.ins("act", lambda: nc.scalar.copy(out=GT[:].rearrange("p g c -> p (g c)"), in_=GTb[:]), r=[GTb.res], w=[GT.res])
            y_c = ycr.next()
            for g in range(4):
                YD = banks.next()
                for qd in range(2):
                    hq = g * 8 + qd * 4
                    Dq = banks.next()
                    fl = []
                    for j in range(4):
                        sl = slice(j * 128, (j + 1) * 128)
                        fl.append(lambda j=j, sl=sl, Dq=Dq: nc.tensor.matmul(
                            Dq[:, sl], dA[:, hq + j:hq + j + 1].to_broadcast([128, 128]), cumL, start=True, stop=False))
                        fl.append(lambda j=j, sl=sl, Dq=Dq: nc.tensor.matmul(Dq[:, sl], identb[:], mT_i, start=False, stop=True))
                    k.group("pe", fl, r=[dA.res, tri.res, identb.res, mskb.res], w=[Dq.res])
                    seg = segr.next()
                    for j in range(4):
                        k.ins("act", lambda j=j, seg=seg, Dq=Dq: nc.scalar.activation(
                            out=seg[:, j, :], in_=Dq[:, j * 128:(j + 1) * 128], func=AF.Exp, bias=s5[:, 1, hq + j:hq + j + 1]),
                            r=[Dq.res, s5.res], w=[seg.res])
                    MT = MTr.next()
                    k.ins("dve", lambda seg=seg, MT=MT, g=g: nc.vector.tensor_tensor(
                        out=MT[:], in0=seg[:], in1=GT[:, g:g + 1, :].to_broadcast([128, 4, 128]), op=ALU.mult),
                        r=[seg.res, GT.res], w=[MT.res])
                    k.group("pe", [(lambda j=j, MT=MT, YD=YD: nc.tensor.matmul(
                        YD[:, (qd * 4 + j) * 64:(qd * 4 + j + 1) * 64], MT[:, j, :], xdt[:, hq + j, :], start=True, stop=True))
                        for j in range(4)], r=[MT.res, xdt.res], w=[YD.res])
                ysl = y_c[:, g * 512:(g + 1) * 512]
                k.ins("dve", lambda g=g, YD=YD, ysl=ysl: nc.vector.tensor_tensor(
                    out=ysl, in0=YD[:], in1=ytmp[:, g * 8:(g + 1) * 8, :].rearrange("p h d -> p (h d)"), op=ALU.add),
                    r=[YD.res, ytmp.res], w=[y_c.res])
            if fwd:
                k.dma("sp", y1_d[l0:l0 + C, :], y_c[:], r=[y_c.res], w=[R_y1[c]], dres=y_c.res)
            else:
                k.ins("pool", lambda: nc.gpsimd.tensor_tensor(out=y_c[:], in0=y_c[:], in1=y1_c[:], op=ALU.add),
                      r=[y_c.res, y1_c.res], w=[y_c.res])
                k.dma("sp", y_d[l0:l0 + C, :], y_c[:], r=[y_c.res], w=[R_y], dres=y_c.res)
        if red:
            k.dma("sp", st2_d[:, 1024:3072], hT32[:].rearrange("p h d -> p (h d)"), r=[hT32.res], w=[R_st2], dres=hT32.res)
        cx.pop()

    with nc.allow_non_contiguous_dma("scan chunk relayout"):
        if "R" in phases:
            gdn_pass("red")
            ssd_pass("red")
        if "B1" in phases:
            gdn_pass("own1")
        if "S1" in phases:
            ssd_pass("own1")
        if "B2" in phases:
            gdn_pass("own2")
        if "S2" in phases:
            ssd_pass("own2")


    def silu_parts(src_ap, shape, ring_e, ring_z, src_res):
        ez = ring_e.next()
        zc = ring_z.next()
        k.ins("act", lambda: nc.scalar.activation(out=ez[:], in_=src_ap, func=AF.Exp, scale=-1.0), r=[src_res], w=[ez.res])
        k.ins("act", lambda: nc.scalar.copy(out=zc[:], in_=src_ap), r=[src_res], w=[zc.res])
        k.ins("dve", lambda: nc.vector.tensor_scalar(out=ez[:], in0=ez[:], scalar1=1.0, scalar2=None, op0=ALU.add),
              r=[ez.res], w=[ez.res])
        return zc, ez

    if "C1" in phases:
        cx.push()
        T1 = 128
        wZ = cx.sb("wZ", [128, 8, 3072], BF16)
        wbg = cx.sb("wbg", [128, 8, 1024], BF16)
        wbs = cx.sb("wbs", [128, 16, 1024], BF16)
        wo = cx.sb("wo", [128, 8, 1024], BF16)
        for kt in range(8):
            k.dma("pool", wZ[:, kt, :], w_in[kt * 128:(kt + 1) * 128, O_ZG:O_ZG + 3072], w=[wZ.res], dres=wZ.res)
            k.dma("pool", wbg[:, kt, :], wbg_d[kt * 128:(kt + 1) * 128, :], w=[wbg.res], dres=wbg.res)
            k.dma("pool", wo[:, kt, :], wo_d[kt * 128:(kt + 1) * 128, :], w=[wo.res], dres=wo.res)
        for kt in range(16):
            k.dma("pool", wbs[:, kt, :], wbs_d[kt * 128:(kt + 1) * 128, :], w=[wbs.res], dres=wbs.res)
        gnw = cx.sb("gnw", [128, 1], F32)
        k.dma("sp", gnw[:], gnw_d, w=[gnw.res], dres=gnw.res)
        dskB = cx.sb("dskB", [128, 2048], F32)
        snwB = cx.sb("snwB", [128, 2048], F32)
        g1B = cx.sb("g1B", [128, 1024], F32)
        k.dma("sp", dskB[:], dskip_d[0, :].partition_broadcast(128), w=[dskB.res], dres=dskB.res)
        k.dma("sp", snwB[:], snw_d[0, :].partition_broadcast(128), w=[snwB.res], dres=snwB.res)
        k.dma("sp", g1B[:], mod_d[0, 2048:3072].partition_broadcast(128), r=[R_mod], w=[g1B.res], dres=g1B.res)
        neg1c = cx.sb("neg1c", [128, 512], F32)
        k.ins("pool", lambda: nc.gpsimd.memset(neg1c[:], -1.0), w=[neg1c.res])
        banks = Ring([cx.psum(f"c1b{i}", [128, 512], F32) for i in range(6)])
        ptr = Ring([cx.psum(f"c1t{i}", [128, 1024], BF16) for i in range(2)])
        aTr = cx.sbring("c_aT", 2, [128, 8, T1], BF16)
        oTr_ = cx.sbring("c_oT", 1, [128, 8, T1], F32)
        yr_ = cx.sbring("c_y", 1, [128, 2048], F32)
        xsr_ = cx.sbring("c_xs", 1, [128, 2048], BF16)
        sgr_ = cx.sbring("c_sg", 1, [128, 16, T1], F32)
        xr_ = cx.sbring("c_x", 1, [128, 1024], F32)
        onTr = cx.sbring("c_on", 1, [128, 8, T1], BF16)
        ynTr = cx.sbring("c_yn", 1, [128, 16, T1], BF16)
        mTr = cx.sbring("c_mT", 1, [128, 8, T1], BF16)
        h1r = cx.sbring("c_h1", 1, [128, 1024], F32)
        e1r = cx.sbring("c_e1", 2, [128, T1], F32)
        z1r = cx.sbring("c_z1", 2, [128, T1], F32)
        t1r = cx.sbring("c_t1", 3, [128, T1], F32)
        e5r = cx.sbring("c_e5", 2, [128, 512], F32)
        z5r = cx.sbring("c_z5", 2, [128, 512], F32)
        t5r = cx.sbring("c_t5", 3, [128, 512], F32)
        ynr = cx.sbring("c_ynb", 2, [128, 512], BF16)
        ssr_ = cx.sbring("c_ss", 4, [128, 2], F32)
        for ti in range(NLAT // T1):
            l0 = ti * T1
            aT = aTr.next(); oT = oTr_.next(); yt = yr_.next(); xs_ = xsr_.next(); sg = sgr_.next(); xt = xr_.next()
            k.dma("sp", aT[:], aT_d[:, :, l0:l0 + T1].rearrange("kt p t -> p kt t"), r=[R_scr], w=[aT.res], dres=aT.res)
            k.dma("sp", oT[:], oT_d[:, :, l0:l0 + T1].rearrange("h p t -> p h t"), r=[R_oT], w=[oT.res], dres=oT.res)
            k.dma("sp", yt[:], y_d[l0:l0 + T1, :], r=[R_y], w=[yt.res], dres=yt.res)
            k.dma("sp", xs_[:], x_d[NCTX + l0:NCTX + l0 + T1, :], r=[R_scr], w=[xs_.res], dres=xs_.res)
            k.dma("sp", sg[:], sgT_d[:, :, l0:l0 + T1].rearrange("c p t -> p c t"), r=[R_scr], w=[sg.res], dres=sg.res)
            k.dma("sp", xt[:], xin[NCTX + l0:NCTX + l0 + T1, :], w=[xt.res], dres=xt.res)
            onT = onTr.next()
            for h in range(8):
                pz = banks.next()
                k.group("pe", [(lambda kt=kt: nc.tensor.matmul(pz[:, 0:T1], wZ[:, kt, h * 128:(h + 1) * 128], aT[:, kt, :],
                                                               start=(kt == 0), stop=(kt == 7))) for kt in range(8)],
                        r=[wZ.res, aT.res], w=[pz.res])
                zc, ez = silu_parts(pz[:, 0:T1], None, e1r, z1r, pz.res)
                k.ins("dve", lambda: nc.vector.reciprocal(out=ez[:], in_=ez[:]), r=[ez.res], w=[ez.res])
                k.ins("pool", lambda: nc.gpsimd.tensor_tensor(out=zc[:], in0=zc[:], in1=ez[:], op=ALU.mult),
                      r=[zc.res, ez.res], w=[zc.res])
                o2 = t1r.next()
                k.ins("pool", lambda: nc.gpsimd.tensor_tensor(out=o2[:], in0=oT[:, h, :], in1=oT[:, h, :], op=ALU.mult),
                      r=[oT.res], w=[o2.res])
                pn = banks.next()
                k.ins("pe", lambda: nc.tensor.matmul(pn[:, 0:T1], onesf[:], o2[:], start=True, stop=True),
                      r=[onesf.res, o2.res], w=[pn.res])
                rs = t1r.next()
                k.ins("act", lambda: nc.scalar.activation(out=rs[:], in_=pn[:, 0:T1], func=AF.Ln, scale=1.0 / 128, bias=epsc[:, 0:1]),
                      r=[pn.res, epsc.res], w=[rs.res])
                k.ins("act", lambda: nc.scalar.activation(out=rs[:], in_=rs[:], func=AF.Exp, scale=-0.5), r=[rs.res], w=[rs.res])
                k.ins("dve", lambda: nc.vector.tensor_tensor(out=rs[:], in0=rs[:], in1=oT[:, h, :], op=ALU.mult),
                      r=[rs.res, oT.res], w=[rs.res])
                k.ins("dve", lambda: nc.vector.scalar_tensor_tensor(out=onT[:, h, :], in0=rs[:], scalar=gnw[:, 0:1], in1=zc[:],
                                                                    op0=ALU.mult, op1=ALU.mult),
                      r=[rs.res, gnw.res, zc.res], w=[onT.res])
            ynT = ynTr.next()
            for cg in range(4):
                csl = slice(cg * 512, (cg + 1) * 512)
                pz = banks.next()
                k.group("pe", [(lambda kt=kt: nc.tensor.matmul(pz[:], aT[:, kt, :], wZ[:, kt, 1024 + cg * 512:1024 + (cg + 1) * 512],
                                                               start=(kt == 0), stop=(kt == 7))) for kt in range(8)],
                        r=[wZ.res, aT.res], w=[pz.res])
                zc, ez = silu_parts(pz[:], None, e5r, z5r, pz.res)
                k.ins("dve", lambda: nc.vector.reciprocal(out=ez[:], in_=ez[:]), r=[ez.res], w=[ez.res])
                k.ins("pool", lambda: nc.gpsimd.tensor_tensor(out=zc[:], in0=zc[:], in1=ez[:], op=ALU.mult),
                      r=[zc.res, ez.res], w=[zc.res])
                yy = t5r.next()
                k.ins("pool", lambda: nc.gpsimd.tensor_tensor(out=yy[:], in0=xs_[:, csl], in1=dskB[:, csl], op=ALU.mult),
                      r=[xs_.res, dskB.res], w=[yy.res])
                k.ins("dve", lambda: nc.vector.tensor_tensor(out=yy[:], in0=yy[:], in1=yt[:, csl], op=ALU.add),
                      r=[yy.res, yt.res], w=[yy.res])
                k.ins("dve", lambda: nc.vector.tensor_tensor(out=yy[:], in0=yy[:], in1=zc[:], op=ALU.mult),
                      r=[yy.res, zc.res], w=[yy.res])
                ss = ssr_.next()
                jk = t5r.next()
                k.ins("act", lambda: nc.scalar.activation(out=jk[:], in_=yy[:], func=AF.Square, accum_out=ss[:, 0:1]),
                      r=[yy.res], w=[jk.res, ss.res])
                k.ins("act", lambda: nc.scalar.activation(out=ss[:, 1:2], in_=ss[:, 0:1], func=AF.Ln, scale=1.0 / 512, bias=epsc[:, 0:1]),
                      r=[ss.res, epsc.res], w=[ss.res])
                k.ins("act", lambda: nc.scalar.activation(out=ss[:, 1:2], in_=ss[:, 1:2], func=AF.Exp, scale=-0.5),
                      r=[ss.res], w=[ss.res])
                ynb = ynr.next()
                k.ins("dve", lambda: nc.vector.scalar_tensor_tensor(out=ynb[:], in0=yy[:], scalar=ss[:, 1:2], in1=snwB[:, csl],
                                                                    op0=ALU.mult, op1=ALU.mult),
                      r=[yy.res, ss.res, snwB.res], w=[ynb.res])
                pt = ptr.next()
                k.group("pe", [(lambda j=j: nc.tensor.transpose(pt[:, j * 128:(j + 1) * 128], ynb[:, j * 128:(j + 1) * 128], identb[:]))
                               for j in range(4)], r=[ynb.res, identb.res], w=[pt.res])
                k.ins("act", lambda: nc.scalar.copy(out=ynT[:, cg * 4:(cg + 1) * 4, :],
                                                    in_=pt[:, 0:512].rearrange("p (j c) -> p j c", c=128)),
                      r=[pt.res], w=[ynT.res])
            mT = mTr.next()
            for oc in range(8):
                pg = banks.next()
                k.group("pe", [(lambda h=h: nc.tensor.matmul(pg[:, 0:T1], wbg[:, h, oc * 128:(oc + 1) * 128], onT[:, h, :],
                                                             start=(h == 0), stop=(h == 7))) for h in range(8)],
                        r=[wbg.res, onT.res], w=[pg.res])
                pp = banks.next()
                k.group("pe", [(lambda ct=ct: nc.tensor.matmul(pp[:, 0:T1], wbs[:, ct, oc * 128:(oc + 1) * 128], ynT[:, ct, :],
                                                               start=(ct == 0), stop=(ct == 15))) for ct in range(16)],
                        r=[wbs.res, ynT.res], w=[pp.res])
                m1 = t1r.next()
                k.ins("dve", lambda: nc.vector.tensor_tensor(out=m1[:], in0=pg[:, 0:T1], in1=sg[:, oc, :], op=ALU.mult),
                      r=[pg.res, sg.res], w=[m1.res])
                m2 = t1r.next()
                k.ins("dve", lambda: nc.vector.tensor_tensor(out=m2[:], in0=pp[:, 0:T1], in1=sg[:, 8 + oc, :], op=ALU.mult),
                      r=[pp.res, sg.res], w=[m2.res])
                k.ins("pool", lambda: nc.gpsimd.tensor_tensor(out=mT[:, oc, :], in0=m1[:], in1=m2[:], op=ALU.add),
                      r=[m1.res, m2.res], w=[mT.res])
            h1 = h1r.next()
            for cg in range(2):
                csl = slice(cg * 512, (cg + 1) * 512)
                pm = banks.next()
                k.group("pe", [(lambda kt=kt: nc.tensor.matmul(pm[:], mT[:, kt, :], wo[:, kt, csl],
                                                               start=(kt == 0), stop=(kt == 7))) for kt in range(8)],
                        r=[mT.res, wo.res], w=[pm.res])
                k.ins("dve", lambda: nc.vector.tensor_tensor(out=h1[:, csl], in0=pm[:], in1=g1B[:, csl], op=ALU.mult),
                      r=[pm.res, g1B.res], w=[h1.res])
                k.ins("pool", lambda: nc.gpsimd.tensor_tensor(out=h1[:, csl], in0=h1[:, csl], in1=xt[:, csl], op=ALU.add),
                      r=[h1.res, xt.res], w=[h1.res])
            k.dma("sp", h1_d[l0:l0 + T1, :], h1[:], r=[h1.res], w=[R_h1], dres=h1.res)
        cx.pop()

    if "C2" in phases:
        cx.push()
        T2 = 256
        NF = DFF // 128
        wfi = cx.sb("wfi", [128, 8, 2 * DFF], BF16)
        wfo = cx.sb("wfo", [128, NF, 1024], BF16)
        for kt in range(8):
            for c0 in range(0, 2 * DFF, 2816):
                k.dma("pool", wfi[:, kt, c0:c0 + 2816], wfi_d[kt * 128:(kt + 1) * 128, c0:c0 + 2816], w=[wfi.res], dres=wfi.res)
        for ft in range(NF):
            k.dma("pool", wfo[:, ft, :], wfo_d[ft * 128:(ft + 1) * 128, :], w=[wfo.res], dres=wfo.res)
        g2B = cx.sb("g2B", [128, 1024], F32)
        nfB = cx.sb("nfB", [128, 1024], F32)
        k.dma("sp", g2B[:], mod_d[0, 5120:6144].partition_broadcast(128), r=[R_mod], w=[g2B.res], dres=g2B.res)
        k.dma("sp", nfB[:], nfw_d[0, :].partition_broadcast(128), w=[nfB.res], dres=nfB.res)
        n2 = cx.sb("n2", [128, 8], F32)
        k.dma("sp", n2[:], n2w_d, w=[n2.res], dres=n2.res)
        S2 = cx.sb("S2", [128, 8], F32)
        k.ins("dve", lambda: nc.vector.scalar_tensor_tensor(out=S2[:], in0=modf[:, 0, 4, :], scalar=1.0, in1=n2[:],
                                                            op0=ALU.add, op1=ALU.mult), r=[modf.res, n2.res], w=[S2.res])
        neg1d = cx.sb("neg1d", [128, T2], F32)
        k.ins("pool", lambda: nc.gpsimd.memset(neg1d[:], -1.0), w=[neg1d.res])
        banks = Ring([cx.psum(f"c2b{i}", [128, 512], F32) for i in range(8)])
        h1r = cx.sbring("d_h1", 2, [128, 2, 1024], F32)
        xnr = cx.sbring("d_xn", 1, [128, 1024], F32)
        fTr = cx.sbring("d_fT", 2, [128, 8, T2], BF16)
        actr = cx.sbring("d_act", 1, [128, NF, T2], BF16)
        outr = cx.sbring("d_out", 1, [128, 2, 1024], F32)
        ssr2 = cx.sbring("d_ss", 4, [128, 2], F32)
        junk2 = cx.sb("junk2", [128, 1024], BF16)
        e2r = cx.sbring("d_e", 3, [128, T2], F32)
        a2r = cx.sbring("d_a", 3, [128, T2], F32)
        for ti in range(NLAT // T2):
            l0 = ti * T2
            h1 = h1r.next()
            k.dma("sp", h1[:], h1_d[l0:l0 + T2, :].rearrange("(j p) d -> p j d", p=128), r=[R_h1], w=[h1.res], dres=h1.res)
            ss = ssr2.next()
            for j in range(2):
                k.ins("act", lambda j=j: nc.scalar.activation(out=junk2[:], in_=h1[:, j, :], func=AF.Square, accum_out=ss[:, j:j + 1]),
                      r=[h1.res], w=[junk2.res, ss.res])
            k.ins("act", lambda: nc.scalar.activation(out=ss[:], in_=ss[:], func=AF.Ln, scale=1.0 / D, bias=epsc[:, 0:1]),
                  r=[ss.res, epsc.res], w=[ss.res])
            k.ins("act", lambda: nc.scalar.activation(out=ss[:], in_=ss[:], func=AF.Exp, scale=-0.5), r=[ss.res], w=[ss.res])
            fT = fTr.next()
            for j in range(2):
                xn = xnr.next()
                k.ins("act", lambda j=j, xn=xn: nc.scalar.activation(out=xn[:], in_=h1[:, j, :], func=AF.Identity, scale=ss[:, j:j + 1]),
                      r=[h1.res, ss.res], w=[xn.res])
                for half in range(2):
                    pb = banks.next()
                    k.group("pe", [(lambda q=q, xn=xn, pb=pb: nc.tensor.transpose(
                        pb[:, q * 128:(q + 1) * 128], xn[:, (half * 4 + q) * 128:(half * 4 + q + 1) * 128], ident[:]))
                        for q in range(4)], r=[xn.res, ident.res], w=[pb.res])
                    for q in range(4):
                        kt = half * 4 + q
                        k.ins("act", lambda q=q, kt=kt, pb=pb, j=j: nc.scalar.activation(
                            out=fT[:, kt, j * 128:(j + 1) * 128], in_=pb[:, q * 128:(q + 1) * 128], func=AF.Identity,
                            scale=S2[:, kt:kt + 1], bias=modf[:, 0, 3, kt:kt + 1]),
                            r=[pb.res, S2.res, modf.res], w=[fT.res])
            act = actr.next()
            for ft in range(NF):
                pgt = banks.next()
                k.group("pe", [(lambda kt=kt: nc.tensor.matmul(pgt[:, 0:T2], wfi[:, kt, ft * 128:(ft + 1) * 128], fT[:, kt, :],
                                                               start=(kt == 0), stop=(kt == 7))) for kt in range(8)],
                        r=[wfi.res, fT.res], w=[pgt.res])
                pup = banks.next()
                k.group("pe", [(lambda kt=kt: nc.tensor.matmul(pup[:, 0:T2], wfi[:, kt, DFF + ft * 128:DFF + (ft + 1) * 128], fT[:, kt, :],
                                                               start=(kt == 0), stop=(kt == 7))) for kt in range(8)],
                        r=[wfi.res, fT.res], w=[pup.res])
                ez = e2r.next()
                k.ins("act", lambda: nc.scalar.activation(out=ez[:], in_=pgt[:, 0:T2], func=AF.Exp, scale=-1.0), r=[pgt.res], w=[ez.res])
                k.ins("dve", lambda: nc.vector.tensor_scalar(out=ez[:], in0=ez[:], scalar1=1.0, scalar2=None, op0=ALU.add),
                      r=[ez.res], w=[ez.res])
                k.ins("dve", lambda: nc.vector.reciprocal(out=ez[:], in_=ez[:]), r=[ez.res], w=[ez.res])
                a1 = a2r.next()
                k.ins("dve", lambda: nc.vector.tensor_tensor(out=a1[:], in0=pgt[:, 0:T2], in1=ez[:], op=ALU.mult),
                      r=[pgt.res, ez.res], w=[a1.res])
                k.ins("dve", lambda: nc.vector.tensor_tensor(out=act[:, ft, :], in0=pup[:, 0:T2], in1=a1[:], op=ALU.mult),
                      r=[pup.res, a1.res], w=[act.res])
            ot = outr.next()
            ss2 = ssr2.next()
            for j in range(2):
                for cg in range(2):
                    csl = slice(cg * 512, (cg + 1) * 512)
                    pf = banks.next()
                    k.group("pe", [(lambda ft=ft: nc.tensor.matmul(pf[:], act[:, ft, j * 128:(j + 1) * 128], wfo[:, ft, csl],
                                                                   start=(ft == 0), stop=(ft == NF - 1))) for ft in range(NF)],
                            r=[act.res, wfo.res], w=[pf.res])
                    k.ins("dve", lambda: nc.vector.tensor_tensor(out=ot[:, j, csl], in0=pf[:], in1=g2B[:, csl], op=ALU.mult),
                          r=[pf.res, g2B.res], w=[ot.res])
                    k.ins("pool", lambda: nc.gpsimd.tensor_tensor(out=ot[:, j, csl], in0=ot[:, j, csl], in1=h1[:, j, csl], op=ALU.add),
                          r=[ot.res, h1.res], w=[ot.res])
                k.ins("act", lambda j=j: nc.scalar.activation(out=junk2[:], in_=ot[:, j, :], func=AF.Square, accum_out=ss2[:, j:j + 1]),
                      r=[ot.res], w=[junk2.res, ss2.res])
            k.ins("act", lambda: nc.scalar.activation(out=ss2[:], in_=ss2[:], func=AF.Ln, scale=1.0 / D, bias=epsc[:, 0:1]),
                  r=[ss2.res, epsc.res], w=[ss2.res])
            k.ins("act", lambda: nc.scalar.activation(out=ss2[:], in_=ss2[:], func=AF.Exp, scale=-0.5), r=[ss2.res], w=[ss2.res])
            for j in range(2):
                k.ins("dve", lambda j=j: nc.vector.scalar_tensor_tensor(out=ot[:, j, :], in0=ot[:, j, :], scalar=ss2[:, j:j + 1],
                                                                        in1=nfB[:], op0=ALU.mult, op1=ALU.mult),
                      r=[ot.res, ss2.res, nfB.res], w=[ot.res])
            k.dma("sp", out_d[l0:l0 + T2, :].rearrange("(j p) d -> p j d", p=128), ot[:], r=[ot.res], w=[R_out], dres=ot.res)
        cx.pop()

    k.wait_all("sp", [R_scr, R_mod, R_oT, R_st1, R_st2, R_y, R_h1, R_out] + R_o1 + R_y1)
    return nc


def _consts():
    p = np.arange(128)[:, None]
    f = np.arange(128)[None, :]
    U = (p <= f).astype(np.float32)
    Lo = (p >= f).astype(np.float32)
    tri = np.stack([U, Lo, -U, -Lo], axis=1)
    NEG = -30000.0
    m = lambda ok: np.where(ok, 0.0, NEG).astype(np.float32)
    msk = np.stack([m(p > f), m(p < f), m(p >= f), m(p <= f)], axis=1)
    return np.ascontiguousarray(tri), np.ascontiguousarray(msk)


TRI, MSK = _consts()


def host_prepare(inputs, core):
    b, s = core // 2, core % 2
    x = inputs["x"][b, s * NLAT:(s + 1) * NLAT]
    ctx = inputs["ctx"][b]
    if s == 1:
        x = x[::-1]
        ctx = ctx[::-1]
    xin = np.ascontiguousarray(np.concatenate([ctx, x], axis=0))
    x2 = inputs["x"][b, (1 - s) * NLAT:(2 - s) * NLAT]
    ctx2 = inputs["ctx"][b]
    if s == 0:
        x2 = x2[::-1]
        ctx2 = ctx2[::-1]
    xin2 = np.ascontiguousarray(np.concatenate([ctx2, x2], axis=0))
    cv = np.stack([inputs["c"][b], inputs["c_ctx"]], axis=-1)
    cvec = np.ascontiguousarray(cv.reshape(8, 128, 2).transpose(1, 0, 2))
    d1, d2 = (0, 1) if s == 0 else (1, 0)
    w = inputs["w_in"][0]
    offs = np.cumsum([0, 3072, 1024, 16, 16, 2048, 3072, 64, 2048])
    qkv = w[:, offs[0]:offs[1]]
    zg = w[:, offs[1]:offs[2]]
    a_ = w[:, offs[2]:offs[3]].reshape(D, 2, 8)
    b_ = w[:, offs[3]:offs[4]].reshape(D, 2, 8)
    zs = w[:, offs[4]:offs[5]]
    xbc = w[:, offs[5]:offs[6]]
    dt_ = w[:, offs[6]:offs[7]].reshape(D, 2, 32)
    gate = w[:, offs[7]:offs[8]]
    z16 = np.zeros((D, 16), np.float32)
    small = np.concatenate([a_[:, d1], a_[:, d2], z16, b_[:, d1], b_[:, d2], z16, dt_[:, d1], dt_[:, d2]], axis=1)
    w_perm = np.ascontiguousarray(np.concatenate([qkv, xbc, gate, small, zg, zs], axis=1))
    assert w_perm.shape[1] == W_IN_COLS
    cw = np.concatenate([inputs["gdn_conv_w"][0], inputs["ssm_conv_w"][0]], axis=1)
    cbias = np.concatenate([inputs["gdn_conv_b"][0], inputs["ssm_conv_b"][0]], axis=0)
    mk = lambda w_: np.ascontiguousarray(np.concatenate([w_, cbias[None]], axis=0).reshape(4, 48, 128).transpose(2, 1, 0))
    convp = mk(cw[::-1] if s == 1 else cw)
    convp2 = mk(cw[::-1] if s == 0 else cw)
    smallp = np.zeros((128, 4), np.float32)
    smallp[:, 0] = 1.0
    smallp[32:64, 0] = -1.0
    gb = inputs["gdn_dt_bias"][0]
    sbias = inputs["ssm_dt_bias"][0]
    smallp[0:8, 1] = gb[d1]; smallp[8:16, 1] = gb[d2]
    smallp[64:96, 1] = sbias[d1]; smallp[96:128, 1] = sbias[d2]
    ga = inputs["gdn_a_log"][0]
    smallp[0:8, 2] = ga[d1]; smallp[8:16, 2] = ga[d2]
    smallp[:, 3] = -1.0
    smallp[64:128, 3] = 1.0
    n1w = np.ascontiguousarray(inputs["norm1_w"][0].reshape(8, 128).T)
    return {
        "xin": xin, "xin2": xin2, "convp2": convp2, "cvec": cvec, "ada_w": np.ascontiguousarray(inputs["ada_w"][0]),
        "ada_b": np.ascontiguousarray(inputs["ada_b"][0][None]), "w_in": w_perm, "convp": convp,
        "smallp": smallp, "n1w": n1w, "ident": np.eye(128, dtype=np.float32),
        "tri": TRI, "msk": MSK,
        "w_brg": np.ascontiguousarray(inputs["w_br_gdn"][0]), "w_brs": np.ascontiguousarray(inputs["w_br_ssm"][0]),
        "w_o": np.ascontiguousarray(inputs["w_out"][0]), "w_fi": np.ascontiguousarray(inputs["w_ffn_in"][0]),
        "w_fo": np.ascontiguousarray(inputs["w_ffn_out"][0]),
        "gnw": np.ascontiguousarray(inputs["gdn_norm_w"][0].reshape(128, 1)),
        "dskip": np.ascontiguousarray(np.repeat(inputs["ssm_d"][0], 64)[None]),
        "snw": np.ascontiguousarray(inputs["ssm_norm_w"][0][None]),
        "n2w": np.ascontiguousarray(inputs["norm2_w"][0].reshape(8, 128).T),
        "nfw": np.ascontiguousarray(inputs["norm_f_w"][None]),
        "salog": np.ascontiguousarray(np.broadcast_to(inputs["ssm_a_log"][0][[d1, d2]][None], (128, 2, 32))),
    }


def kernel(**inputs):
    inputs = {k_: np.asarray(v) for k_, v in inputs.items()}
    nc = build_program()
    in_maps = [host_prepare(inputs, c) for c in range(8)]
    res = run_bass_kernel_spmd(nc, in_maps, core_ids=list(range(8)))
    out = np.zeros((4, 8192, D), np.float32)
    for c in range(8):
        b, s = c // 2, c % 2
        o = res.results[c]["out"]
        if s == 1:
            o = o[::-1]
        out[b, s * NLAT:(s + 1) * NLAT] = o
    return out
```

```python
import numpy as np
import ml_dtypes
from contextlib import ExitStack
import concourse.bass as bass
import concourse.mybir as mybir
from concourse.bass_utils import run_bass_kernel_spmd

F32 = mybir.dt.float32
BF16 = mybir.dt.bfloat16
AF = mybir.ActivationFunctionType
ALU = mybir.AluOpType
AX = mybir.AxisListType

D = 1024
NLAT = 4096
NCTX = 256
NTOK = NLAT + NCTX
C = 128
NCH = NTOK // C
EPS = 1e-6
DFF = 2816
O_QKV, O_XBC, O_GATE, O_SMALL, O_ZG, O_ZS = 0, 3072, 6144, 8192, 8320, 9344
W_IN_COLS = 11392
SEM_LIMIT = 30000
GDN_LIMIT = 0


class Sem:
    __slots__ = ("h", "count", "dma", "name")

    def __init__(self, h, dma, name):
        self.h, self.count, self.dma, self.name = h, 0, dma, name


class Res:
    __slots__ = ("name", "w", "r", "dsem")

    def __init__(self, name):
        self.name, self.w, self.r, self.dsem = name, None, {}, None


class Eng:
    def __init__(self, name, e, same_wait):
        self.name, self.e, self.sem, self.waited, self.same_wait = name, e, None, {}, same_wait
        self.nsem = 0


class K:
    def __init__(self, nc):
        self.nc = nc
        self.eng = {
            "pe": Eng("pe", nc.tensor, False),
            "act": Eng("act", nc.scalar, True),
            "dve": Eng("dve", nc.vector, True),
            "pool": Eng("pool", nc.gpsimd, True),
            "sp": Eng("sp", nc.sync, False),
        }
        self.nsem = 0
        self.ninst = 0
        self.all_sems = []
        self.free_dma = []

    def new_sem(self, dma, name):
        if dma and self.free_dma:
            self.free_dma.sort(key=lambda x: x.count)
            return self.free_dma.pop(0)
        self.nsem += 1
        h = self.nc.alloc_semaphore(f"s{self.nsem}_{name}")
        sm = Sem(h, dma, name)
        self.all_sems.append(sm)
        return sm

    def res(self, name):
        return Res(name)

    def _cur_sem(self, E):
        if E.sem is None or E.sem.count >= SEM_LIMIT:
            E.nsem += 1
            E.sem = self.new_sem(False, f"{E.name}{E.nsem}")
        return E.sem

    def _wait(self, E, evs):
        need = {}
        for (sem, val) in evs:
            if sem.dma:
                val = sem.count
            if need.get(sem, 0) < val:
                need[sem] = val
        for sem, val in need.items():
            if (not E.same_wait) and (sem is E.sem) and not sem.dma:
                continue
            if E.waited.get(sem, 0) >= val:
                continue
            E.e.wait_ge(sem.h, val)
            E.waited[sem] = val

    def _deps(self, r, w):
        evs = []
        for x in r:
            if x.w is not None:
                evs.append(x.w)
        for x in w:
            if x.w is not None:
                evs.append(x.w)
            evs.extend(x.r.items())
        return evs

    def _record(self, ev, r, w):
        sem, val = ev
        for x in r:
            if x.r.get(sem, 0) < val:
                x.r[sem] = val
        for x in w:
            x.w = ev
            x.r = {}

    def ins(self, en, fn, r=(), w=()):
        E = self.eng[en]
        self._wait(E, self._deps(r, w))
        inst = fn()
        sem = self._cur_sem(E)
        sem.count += 1
        inst.then_inc(sem.h, 1)
        self._record((sem, sem.count), r, w)
        self.ninst += 1
        return inst

    def group(self, en, fns, r=(), w=()):
        E = self.eng[en]
        self._wait(E, self._deps(r, w))
        inst = None
        for fn in fns:
            inst = fn()
        sem = self._cur_sem(E)
        sem.count += 1
        inst.then_inc(sem.h, 1)
        self._record((sem, sem.count), r, w)
        self.ninst += len(fns)
        return inst

    def dma(self, q, out, in_, r=(), w=(), dres=None):
        E = self.eng[q]
        self._wait(E, self._deps(r, w))
        if dres.dsem is None:
            dres.dsem = self.new_sem(True, "d" + dres.name)
        sem = dres.dsem
        inst = E.e.dma_start(out=out, in_=in_)
        sem.count += 16
        inst.then_inc(sem.h, 16)
        self._record((sem, sem.count), r, w)
        self.ninst += 1
        return inst

    def dmaop(self, q, fn, r=(), w=(), dres=None):
        E = self.eng[q]
        self._wait(E, self._deps(r, w))
        if dres.dsem is None:
            dres.dsem = self.new_sem(True, "d" + dres.name)
        sem = dres.dsem
        inst = fn()
        sem.count += 16
        inst.then_inc(sem.h, 16)
        self._record((sem, sem.count), r, w)
        self.ninst += 1
        return inst

    def barrier(self):
        sems = list(self.all_sems)
        for E in self.eng.values():
            for sem in sems:
                if sem.count == 0 or E.waited.get(sem, 0) >= sem.count:
                    continue
                if (sem is E.sem) and not E.same_wait:
                    continue
                E.e.wait_ge(sem.h, sem.count)
                E.waited[sem] = sem.count

    def wait_all(self, en, ress):
        E = self.eng[en]
        evs = []
        for x in ress:
            if x.w is not None:
                evs.append(x.w)
            evs.extend(x.r.items())
        self._wait(E, evs)


class Buf:
    def __init__(self, k, t, name):
        self.t, self.res, self.name = t, k.res(name), name

    def __getitem__(self, idx):
        return self.t[idx]


class Ring:
    def __init__(self, bufs):
        self.bufs, self.i = bufs, 0

    def next(self):
        b = self.bufs[self.i % len(self.bufs)]
        self.i += 1
        return b


class Ctx:
    def __init__(self, nc):
        self.nc = nc
        self.k = K(nc)
        self.stack = [ExitStack()]
        self.uid = 0
        self.phase_bufs = [[]]

    def push(self):
        self.stack.append(ExitStack())
        self.phase_bufs.append([])

    def pop(self):
        self.k.barrier()
        for b in self.phase_bufs.pop():
            if b.res.dsem is not None:
                self.k.free_dma.append(b.res.dsem)
                b.res.dsem = None
        self.stack.pop().close()

    def sb(self, name, shape, dtype):
        self.uid += 1
        t = self.stack[-1].enter_context(self.nc.sbuf_tensor(f"sb{self.uid}_{name}", list(shape), dtype))
        b = Buf(self.k, t, name)
        self.phase_bufs[-1].append(b)
        return b

    def psum(self, name, shape, dtype):
        self.uid += 1
        t = self.stack[-1].enter_context(self.nc.psum_tensor(f"ps{self.uid}_{name}", list(shape), dtype))
        return Buf(self.k, t, name)

    def sbring(self, name, n, shape, dtype):
        return Ring([self.sb(f"{name}{i}", shape, dtype) for i in range(n)])

    def dram(self, name, shape, dtype, kind="Internal"):
        t = self.nc.dram_tensor(name, list(shape), dtype, kind=kind)
        return t.ap()


def build_program(debug=False, phases=("0", "A", "R", "B1", "S1", "B2", "S2", "C1", "C2"), n_cores=8):
    nc = bass.Bass("TRN2", target_bir_lowering=False)
    cx = Ctx(nc)
    k = cx.k
    dk = "ExternalOutput" if debug else "Internal"

    xin = cx.dram("xin", [NTOK, D], F32, "ExternalInput")
    cvec = cx.dram("cvec", [128, 8, 2], F32, "ExternalInput")
    ada_w = cx.dram("ada_w", [D, 6 * D], F32, "ExternalInput")
    ada_b = cx.dram("ada_b", [1, 6 * D], F32, "ExternalInput")
    w_in = cx.dram("w_in", [D, W_IN_COLS], F32, "ExternalInput")
    xin2 = cx.dram("xin2", [NTOK, D], F32, "ExternalInput")
    convp2 = cx.dram("convp2", [128, 48, 4], F32, "ExternalInput")
    convp = cx.dram("convp", [128, 48, 4], F32, "ExternalInput")
    smallp = cx.dram("smallp", [128, 4], F32, "ExternalInput")
    n1w = cx.dram("n1w", [128, 8], F32, "ExternalInput")
    ident_d = cx.dram("ident", [128, 128], F32, "ExternalInput")
    wbg_d = cx.dram("w_brg", [1024, 1024], F32, "ExternalInput")
    wbs_d = cx.dram("w_brs", [2048, 1024], F32, "ExternalInput")
    wo_d = cx.dram("w_o", [1024, 1024], F32, "ExternalInput")
    wfi_d = cx.dram("w_fi", [1024, 2 * DFF], F32, "ExternalInput")
    wfo_d = cx.dram("w_fo", [DFF, 1024], F32, "ExternalInput")
    gnw_d = cx.dram("gnw", [128, 1], F32, "ExternalInput")
    dskip_d = cx.dram("dskip", [1, 2048], F32, "ExternalInput")
    snw_d = cx.dram("snw", [1, 2048], F32, "ExternalInput")
    n2w_d = cx.dram("n2w", [128, 8], F32, "ExternalInput")
    nfw_d = cx.dram("nfw", [1, 1024], F32, "ExternalInput")
    salog_d = cx.dram("salog", [128, 2, 32], F32, "ExternalInput")
    tri_d = cx.dram("tri", [128, 4, 128], F32, "ExternalInput")
    msk_d = cx.dram("msk", [128, 4, 128], F32, "ExternalInput")
    out_d = cx.dram("out", [NLAT, D], F32, "ExternalOutput")

    mod_d = cx.dram("mod_d", [2, 6 * D], F32, dk)
    qT_d = cx.dram("qT_d", [8, 128, NTOK], BF16, dk)
    kT_d = cx.dram("kT_d", [8, 128, NTOK], BF16, dk)
    k_d = cx.dram("k_d", [NTOK, 1024], BF16, dk)
    v_d = cx.dram("v_d", [NTOK, 1024], BF16, dk)
    x_d = cx.dram("x_d", [NTOK, 2048], BF16, dk)
    BT_d = cx.dram("BT_d", [4, 128, NTOK], BF16, dk)
    CT_d = cx.dram("CT_d", [4, 128, NTOK], BF16, dk)
    B_d = cx.dram("B_d", [NTOK, 512], BF16, dk)
    sm_d = cx.dram("sm_d", [NTOK, 128], F32, dk)
    kT2_d = cx.dram("kT2_d", [8, 128, NTOK], BF16)
    k2_d = cx.dram("k2_d", [NTOK, 1024], BF16)
    v2_d = cx.dram("v2_d", [NTOK, 1024], BF16)
    x2_d = cx.dram("x2_d", [NTOK, 2048], BF16)
    B2_d = cx.dram("B2_d", [NTOK, 512], BF16)
    sm2_d = cx.dram("sm2_d", [NTOK, 128], F32)
    sgT_d = cx.dram("sgT_d", [16, 128, NLAT], F32, dk)
    aT_d = cx.dram("aT_d", [8, 128, NLAT], BF16, dk)
    o1_d = cx.dram("o1_d", [8, 128, NLAT], F32, dk)
    oT_d = cx.dram("oT_d", [8, 128, NLAT], F32, dk)
    st1_d = cx.dram("st1_d", [128, 3072], F32)
    st2_d = cx.dram("st2_d", [128, 3072], F32)
    y1_d = cx.dram("y1_d", [NLAT, 2048], F32, dk)
    y_d = cx.dram("y_d", [NLAT, 2048], F32, dk)
    R_y1 = [k.res(f"y1_{c}") for c in range(NCH)]
    R_y = k.res("y")
    R_o1 = [k.res(f"o1_{c}") for c in range(NCH)]
    R_oT = k.res("oT")
    R_st1 = k.res("st1")
    R_st2 = k.res("st2")
    h1_d = cx.dram("h1_d", [NLAT, D], F32, dk)
    R_h1 = k.res("h1")
    R_out = k.res("out")
    R_scr = k.res("scratchA")
    R_mod = k.res("mod_d")

    ident = cx.sb("ident", [128, 128], F32)
    identb = cx.sb("identb", [128, 128], BF16)
    onesf = cx.sb("onesf", [128, 128], F32)
    k.dma("sp", ident[:], ident_d, w=[ident.res], dres=ident.res)
    k.ins("dve", lambda: nc.vector.tensor_copy(identb[:], ident[:]), r=[ident.res], w=[identb.res])
    k.ins("dve", lambda: nc.vector.memset(onesf[:], 1.0), w=[onesf.res])
    epsc = cx.sb("epsc", [128, 1], F32)
    k.ins("dve", lambda: nc.vector.memset(epsc[:], EPS), w=[epsc.res])


    if "0" in phases:
        cx.push()
        ps = [cx.psum(f"ps{i}", [128, 512], F32) for i in range(2)]
        cv = cx.sb("cv", [128, 8, 2], F32)
        cvs = cx.sb("cvs", [128, 8, 2], F32)
        k.dma("sp", cv[:], cvec, w=[cv.res], dres=cv.res)
        k.ins("act", lambda: nc.scalar.activation(out=cvs[:], in_=cv[:], func=AF.Exp, scale=-1.0), r=[cv.res], w=[cvs.res])
        k.ins("dve", lambda: nc.vector.tensor_scalar(out=cvs[:], in0=cvs[:], scalar1=1.0, scalar2=None, op0=ALU.add),
              r=[cvs.res], w=[cvs.res])
        k.ins("dve", lambda: nc.vector.reciprocal(out=cvs[:], in_=cvs[:]), r=[cvs.res], w=[cvs.res])
        k.ins("dve", lambda: nc.vector.tensor_tensor(out=cvs[:], in0=cvs[:], in1=cv[:], op=ALU.mult),
              r=[cvs.res, cv.res], w=[cvs.res])
        adab = cx.sb("adab", [2, 6 * D], F32)
        k.dma("sp", adab[0:1, :], ada_b, w=[adab.res], dres=adab.res)
        k.dma("sp", adab[1:2, :], ada_b, w=[adab.res], dres=adab.res)
        modsb = cx.sb("modsb", [2, 6 * D], F32)
        awring = cx.sbring("aw", 2, [128, 8, 512], F32)
        for cg in range(12):
            aw = awring.next()
            k.dma("sp", aw[:], ada_w[:, cg * 512:(cg + 1) * 512].rearrange("(kt p) n -> p kt n", p=128),
                  w=[aw.res], dres=aw.res)
            pb = ps[cg % 2]
            k.group("pe", [
                (lambda kt=kt, aw=aw, pb=pb: nc.tensor.matmul(pb[0:2, :], cvs[:, kt, :], aw[:, kt, :],
                                                               start=(kt == 0), stop=(kt == 7)))
                for kt in range(8)], r=[cvs.res, aw.res], w=[pb.res])
            k.ins("dve", lambda cg=cg, pb=pb: nc.vector.tensor_tensor(
                out=modsb[:, cg * 512:(cg + 1) * 512], in0=pb[0:2, :], in1=adab[:, cg * 512:(cg + 1) * 512],
                op=ALU.add), r=[pb.res, adab.res], w=[modsb.res])
        k.dma("sp", mod_d, modsb[:], r=[modsb.res], w=[R_mod], dres=modsb.res)
        cx.pop()

    modf = cx.sb("modf", [128, 2, 6, 8], F32)
    with nc.allow_non_contiguous_dma("small modulation vector relayout"):
        for r_ in range(2):
            k.dma("sp", modf[:, r_, :, :], mod_d[r_, :].rearrange("(j kt p) -> p j kt", p=128, kt=8),
                  r=[R_mod], w=[modf.res], dres=modf.res)
    n1 = cx.sb("n1", [128, 8], F32)
    k.dma("sp", n1[:], n1w, w=[n1.res], dres=n1.res)
    S1 = cx.sb("S1", [128, 2, 8], F32)
    for r_ in range(2):
        k.ins("dve", lambda r_=r_: nc.vector.scalar_tensor_tensor(
            out=S1[:, r_, :], in0=modf[:, r_, 1, :], scalar=1.0, in1=n1[:], op0=ALU.add, op1=ALU.mult),
            r=[modf.res, n1.res], w=[S1.res])

    if "A" in phases:
        cx.push()
        ps = [cx.psum(f"ps{i}", [128, 512], F32) for i in range(6)]
        NWC = O_ZG
        wA = cx.sb("wA", [128, 8, NWC], BF16)
        for kt in range(8):
            for c0 in range(0, NWC, 2080):
                k.dma("pool", wA[:, kt, c0:c0 + 2080], w_in[kt * 128:(kt + 1) * 128, c0:c0 + 2080],
                      w=[wA.res], dres=wA.res)
        cp = cx.sb("cp", [128, 48, 4], F32)
        k.dma("sp", cp[:], convp, w=[cp.res], dres=cp.res)
        smp = cx.sb("smp", [128, 4], F32)
        k.dma("sp", smp[:], smallp, w=[smp.res], dres=smp.res)
        smult = cx.sb("smult", [128, 1], F32)
        k.ins("act", lambda: nc.scalar.activation(out=smult[:], in_=smp[:, 2:3], func=AF.Exp),
              r=[smp.res], w=[smult.res])
        k.ins("dve", lambda: nc.vector.tensor_tensor(out=smult[:], in0=smult[:], in1=smp[:, 3:4], op=ALU.mult),
              r=[smp.res, smult.res], w=[smult.res])

        TT = 256
        xring = cx.sbring("xt", 1, [128, 2, D], F32)
        xnring = cx.sbring("xn", 1, [128, D], F32)
        junk = cx.sb("junk", [128, D], BF16)
        ssr = cx.sbring("ss", 2, [128, 2], F32)
        aTring = cx.sbring("aT", 2, [128, 8, TT], BF16)
        cring = cx.sbring("cv_", 6, [128, TT], F32)
        ering = cx.sbring("ee_", 4, [128, TT], F32)
        sfring = cx.sbring("sf_", 5, [128, TT], F32)
        sqring = cx.sbring("sq_", 3, [128, TT], F32)
        rsring = cx.sbring("rs_", 3, [128, TT], F32)
        sring = cx.sbring("so_", 8, [128, TT], BF16)
        sgring = cx.sbring("sg_", 4, [128, TT], F32)
        smring = cx.sbring("smf", 2, [128, TT], F32)
        ktok = cx.sbring("ktok", 1, [128, 2, 1024], BF16)
        vtok = cx.sbring("vtok", 1, [128, 2, 1024], BF16)
        xtok = cx.sbring("xtok", 1, [128, 2, 2048], BF16)
        btok = cx.sbring("btok", 1, [128, 2, 512], BF16)
        smtok = cx.sbring("smtok", 1, [128, 2, 128], F32)
        pst = ps[0]
        psa = Ring([ps[1], ps[2], ps[3], ps[4]])
        pss = ps[5]
        pso = Ring([cx.psum(f"pso{i}", [128, 1024], BF16) for i in range(2)])
        own = dict(qT_d=qT_d, kT_d=kT_d, k_d=k_d, v_d=v_d, x_d=x_d, BT_d=BT_d, CT_d=CT_d, B_d=B_d, sm_d=sm_d)
        par = dict(qT_d=None, kT_d=kT2_d, k_d=k2_d, v_d=v2_d, x_d=x2_d, BT_d=None, CT_d=None, B_d=B2_d, sm_d=sm2_d)
        cp2 = cx.sb("cp2", [128, 48, 4], F32)
        k.dma("sp", cp2[:], convp2, w=[cp2.res], dres=cp2.res)
        runs = [(False, xin, cp, own)]
        if "R" in phases:
            runs.append((True, xin2, cp2, par))

        def preamble(red, xsrc, ti):
            t0 = ti * TT
            is_ctx = ti == 0
            mr = 1 if is_ctx else 0
            xt = xring.next()
            k.dma("sp", xt[:], xsrc[t0:t0 + TT, :].rearrange("(j p) d -> p j d", p=128), w=[xt.res], dres=xt.res)
            ss = ssr.next()
            for j in range(2):
                k.ins("act", lambda j=j: nc.scalar.activation(out=junk[:], in_=xt[:, j, :], func=AF.Square,
                                                              accum_out=ss[:, j:j + 1]), r=[xt.res], w=[junk.res, ss.res])
            k.ins("act", lambda: nc.scalar.activation(out=ss[:], in_=ss[:], func=AF.Ln, scale=1.0 / D, bias=epsc[:, 0:1]),
                  r=[ss.res, epsc.res], w=[ss.res])
            k.ins("act", lambda: nc.scalar.activation(out=ss[:], in_=ss[:], func=AF.Exp, scale=-0.5), r=[ss.res], w=[ss.res])
            aT = aTring.next()
            for j in range(2):
                xn = xnring.next()
                k.ins("act", lambda j=j, xn=xn: nc.scalar.activation(out=xn[:], in_=xt[:, j, :], func=AF.Identity,
                                                                     scale=ss[:, j:j + 1]), r=[xt.res, ss.res], w=[xn.res])
                for half in range(2):
                    k.group("pe", [(lambda q=q, xn=xn, half=half: nc.tensor.transpose(
                        pst[:, q * 128:(q + 1) * 128], xn[:, (half * 4 + q) * 128:(half * 4 + q + 1) * 128], ident[:]))
                        for q in range(4)], r=[xn.res, ident.res], w=[pst.res])
                    for q in range(4):
                        kt = half * 4 + q
                        k.ins("act", lambda q=q, kt=kt, j=j: nc.scalar.activation(
                            out=aT[:, kt, j * 128:(j + 1) * 128], in_=pst[:, q * 128:(q + 1) * 128],
                            func=AF.Identity, scale=S1[:, mr, kt:kt + 1], bias=modf[:, mr, 0, kt:kt + 1]),
                            r=[pst.res, S1.res, modf.res], w=[aT.res])
            if not is_ctx and not red:
                l0 = t0 - NCTX
                k.dma("sp", aT_d[:, :, l0:l0 + TT].rearrange("kt p t -> p kt t"), aT[:], r=[aT.res], w=[R_scr], dres=aT.res)
            return dict(aT=aT, t0=t0, is_ctx=is_ctx, red=red)

        def make_job(tc, ct, cpt, DD, toks):
            aT, t0, is_ctx, red = tc["aT"], tc["t0"], tc["is_ctx"], tc["red"]
            l0 = t0 - NCTX
            rowlen = 256 if is_ctx else 64
            J = {}
            steps = []

            def s_mm():
                J["pa"] = pa = psa.next()
                k.group("pe", [(lambda kt=kt: nc.tensor.matmul(pa[:, 0:TT], wA[:, kt, ct * 128:(ct + 1) * 128], aT[:, kt, :],
                                                               start=(kt == 0), stop=(kt == 7))) for kt in range(8)],
                        r=[wA.res, aT.res], w=[pa.res])
            steps.append(s_mm)
            if ct < 48:
                def s_ident():
                    pa = J["pa"]
                    J["cb"] = cb = cring.next()
                    k.ins("dve", lambda: nc.vector.tensor_scalar(out=cb[:], in0=pa[:, 0:TT], scalar1=cpt[:, ct, 1:2],
                                                              scalar2=cpt[:, ct, 3:4], op0=ALU.mult, op1=ALU.add),
                          r=[pa.res, cpt.res], w=[cb.res])

                def s_taps():
                    pa, cb = J["pa"], J["cb"]
                    pv = pa[:, 0:TT].rearrange("p (r t) -> p r t", t=rowlen)
                    cv3 = cb[:].rearrange("p (r t) -> p r t", t=rowlen)
                    k.ins("dve", lambda: nc.vector.scalar_tensor_tensor(
                        out=cv3[:, :, 1:], in0=pv[:, :, 0:rowlen - 1], scalar=cpt[:, ct, 0:1], in1=cv3[:, :, 1:],
                        op0=ALU.mult, op1=ALU.add), r=[pa.res, cpt.res, cb.res], w=[cb.res])
                    k.ins("dve", lambda: nc.vector.scalar_tensor_tensor(
                        out=cv3[:, :, 0:rowlen - 1], in0=pv[:, :, 1:], scalar=cpt[:, ct, 2:3], in1=cv3[:, :, 0:rowlen - 1],
                        op0=ALU.mult, op1=ALU.add), r=[pa.res, cpt.res, cb.res], w=[cb.res])

                def s_exp():
                    cb = J["cb"]
                    J["ee"] = ee = ering.next()
                    k.ins("act", lambda: nc.scalar.activation(out=ee[:], in_=cb[:], func=AF.Exp, scale=-1.0), r=[cb.res], w=[ee.res])

                def s_recip():
                    ee = J["ee"]
                    k.ins("act", lambda: nc.scalar.activation(out=ee[:], in_=ee[:], func=AF.Ln, bias=1.0), r=[ee.res], w=[ee.res])

                def s_recip2():
                    ee = J["ee"]
                    k.ins("act", lambda: nc.scalar.activation(out=ee[:], in_=ee[:], func=AF.Exp, scale=-1.0), r=[ee.res], w=[ee.res])

                def s_mult():
                    cb, ee = J["cb"], J["ee"]
                    if ct < 16:
                        J["sf"] = sf = sfring.next()
                        k.ins("pool", lambda: nc.gpsimd.tensor_tensor(out=sf[:], in0=cb[:], in1=ee[:], op=ALU.mult),
                              r=[cb.res, ee.res], w=[sf.res])
                    else:
                        J["so"] = so = sring.next()
                        k.ins("pool", lambda: nc.gpsimd.tensor_tensor(out=so[:], in0=cb[:], in1=ee[:], op=ALU.mult),
                              r=[cb.res, ee.res], w=[so.res])
                steps.extend([s_ident, s_taps, s_exp, s_recip, s_recip2, s_mult])
                if ct < 16:
                    def s_sq():
                        sf = J["sf"]
                        J["sq"] = sq = sqring.next()
                        k.ins("pool", lambda: nc.gpsimd.tensor_tensor(out=sq[:], in0=sf[:], in1=sf[:], op=ALU.mult),
                              r=[sf.res], w=[sq.res])

                    def s_sum():
                        sq = J["sq"]
                        k.ins("pe", lambda: nc.tensor.matmul(pss[:, 0:TT], onesf[:], sq[:], start=True, stop=True),
                              r=[onesf.res, sq.res], w=[pss.res])
                        J["rs"] = rs = rsring.next()
                        k.ins("act", lambda: nc.scalar.activation(out=rs[:], in_=pss[:, 0:TT], func=AF.Ln, bias=epsc[:, 0:1]),
                              r=[pss.res, epsc.res], w=[rs.res])

                    def s_rs():
                        rs = J["rs"]
                        k.ins("act", lambda: nc.scalar.activation(out=rs[:], in_=rs[:], func=AF.Exp, scale=-0.5),
                              r=[rs.res], w=[rs.res])

                    def s_norm():
                        sf, rs = J["sf"], J["rs"]
                        J["so"] = so = sring.next()
                        qscale = (128 ** -0.5) if ct < 8 else 1.0
                        k.ins("dve", lambda: nc.vector.scalar_tensor_tensor(out=so[:], in0=sf[:], scalar=qscale, in1=rs[:],
                                                                            op0=ALU.mult, op1=ALU.mult),
                              r=[sf.res, rs.res], w=[so.res])
                    steps.extend([s_sq, s_sum, s_rs, s_norm])

                def s_out():
                    so = J["so"]
                    if ct < 8:
                        k.dma("sp", DD["qT_d"][ct, :, t0:t0 + TT], so[:], r=[so.res], w=[R_scr], dres=so.res)
                    elif ct < 16:
                        k.dma("sp", DD["kT_d"][ct - 8, :, t0:t0 + TT], so[:], r=[so.res], w=[R_scr], dres=so.res)
                    elif 40 <= ct < 44 and not red:
                        k.dma("sp", DD["BT_d"][ct - 40, :, t0:t0 + TT], so[:], r=[so.res], w=[R_scr], dres=so.res)
                    elif 44 <= ct < 48:
                        k.dma("sp", DD["CT_d"][ct - 44, :, t0:t0 + TT], so[:], r=[so.res], w=[R_scr], dres=so.res)
                    tgt = None
                    if 8 <= ct < 16:
                        tgt = (toks["k"], (ct - 8) * 128)
                    elif 16 <= ct < 24:
                        tgt = (toks["v"], (ct - 16) * 128)
                    elif 24 <= ct < 40:
                        tgt = (toks["x"], (ct - 24) * 128)
                    elif 40 <= ct < 44:
                        tgt = (toks["b"], (ct - 40) * 128)
                    J["tgt"] = tgt
                    if tgt is not None:
                        J["po"] = po = pso.next()
                        k.group("pe", [(lambda j=j: nc.tensor.transpose(po[:, j * 128:(j + 1) * 128], so[:, j * 128:(j + 1) * 128],
                                                                        identb[:])) for j in range(2)],
                                r=[so.res, identb.res], w=[po.res])

                def s_tcopy():
                    if J["tgt"] is not None:
                        tb, off = J["tgt"]
                        po = J["po"]
                        k.ins("act", lambda: nc.scalar.copy(out=tb[:, :, off:off + 128],
                                                            in_=po[:, 0:256].rearrange("p (j c) -> p j c", c=128)),
                              r=[po.res], w=[tb.res])
                steps.extend([s_out, s_tcopy])
            elif ct < 64:
                def g_exp():
                    pa = J["pa"]
                    J["sg"] = sg = sgring.next()
                    k.ins("act", lambda: nc.scalar.activation(out=sg[:], in_=pa[:, 0:TT], func=AF.Exp, scale=-1.0),
                          r=[pa.res], w=[sg.res])

                def g_recip():
                    sg = J["sg"]
                    k.ins("act", lambda: nc.scalar.activation(out=sg[:], in_=sg[:], func=AF.Ln, bias=1.0), r=[sg.res], w=[sg.res])

                def g_recip2():
                    sg = J["sg"]
                    k.ins("act", lambda: nc.scalar.activation(out=sg[:], in_=sg[:], func=AF.Exp, scale=-1.0), r=[sg.res], w=[sg.res])

                def g_out():
                    sg = J["sg"]
                    k.dma("sp", sgT_d[ct - 48, :, l0:l0 + TT], sg[:], r=[sg.res], w=[R_scr], dres=sg.res)
                steps.extend([g_exp, g_recip, g_recip2, g_out])
            else:
                st_ = toks["sm"]

                def m_exp():
                    pa = J["pa"]
                    J["sm"] = sm = smring.next()
                    k.ins("act", lambda: nc.scalar.activation(out=sm[:], in_=pa[:, 0:TT], func=AF.Exp, scale=smp[:, 0:1],
                                                              bias=smp[:, 1:2]), r=[pa.res, smp.res], w=[sm.res])

                def m_ln():
                    sm = J["sm"]
                    k.ins("act", lambda: nc.scalar.activation(out=sm[:], in_=sm[:], func=AF.Ln, bias=1.0), r=[sm.res], w=[sm.res])

                def m_mul():
                    sm = J["sm"]
                    k.ins("dve", lambda: nc.vector.tensor_scalar(out=sm[:], in0=sm[:], scalar1=smult[:, 0:1], scalar2=None,
                                                              op0=ALU.mult), r=[sm.res, smult.res], w=[sm.res])

                def m_tr():
                    sm = J["sm"]
                    k.group("pe", [(lambda j=j: nc.tensor.transpose(pst[:, j * 128:(j + 1) * 128], sm[:, j * 128:(j + 1) * 128],
                                                                    ident[:])) for j in range(2)],
                            r=[sm.res, ident.res], w=[pst.res])

                def m_copy():
                    k.ins("dve", lambda: nc.vector.tensor_copy(st_[:], pst[:, 0:256].rearrange("p (j c) -> p j c", c=128)),
                          r=[pst.res], w=[st_.res])
                steps.extend([m_exp, m_ln, m_mul, m_tr, m_copy])
            return steps

        def make_spill(tc, DD, toks):
            t0 = tc["t0"]

            def spill():
                rows = lambda d_: d_[t0:t0 + TT, :].rearrange("(j p) c -> p j c", p=128)
                for key, dn in (("k", "k_d"), ("v", "v_d"), ("x", "x_d"), ("b", "B_d"), ("sm", "sm_d")):
                    tb = toks[key]
                    k.dma("sp", rows(DD[dn]), tb[:], r=[tb.res], w=[R_scr], dres=tb.res)
            return spill

        NST = 14
        pipeline = []
        it = 0
        tiles = [(red, xsrc, cpt, DD, ti) for (red, xsrc, cpt, DD) in runs for ti in range(NTOK // TT)]
        pend_pre = {}

        def run_pipeline_until(limit):
            nonlocal it
            while it < limit:
                for (st0, steps) in pipeline:
                    sidx = it - st0
                    if 0 <= sidx < len(steps):
                        steps[sidx]()
                pipeline[:] = [(a, b) for (a, b) in pipeline if it - a < len(b) - 1]
                it += 1

        tcs = [None] * len(tiles)
        tcs[0] = preamble(tiles[0][0], tiles[0][1], tiles[0][4])
        for n, (red, xsrc, cpt, DD, ti) in enumerate(tiles):
            tc = tcs[n]
            toks = dict(k=ktok.next(), v=vtok.next(), x=xtok.next(), b=btok.next(), sm=smtok.next())
            cts = [ct for ct in range(65)
                   if not ((tc["is_ctx"] or red) and 48 <= ct < 64) and not (red and (ct < 8 or 44 <= ct < 48))]
            for idx, ct in enumerate(cts):
                pipeline.append((it, make_job(tc, ct, cpt, DD, toks)))
                run_pipeline_until(it + 1)
                if idx == 6 and n + 1 < len(tiles):
                    tcs[n + 1] = preamble(tiles[n + 1][0], tiles[n + 1][1], tiles[n + 1][4])
            pipeline.append((it + 6, [make_spill(tc, DD, toks)]))
        run_pipeline_until(it + NST + 2)
        cx.pop()


    def load_scan_consts():
        tri = cx.sb("tri", [128, 4, 128], F32)
        k.dma("sp", tri[:], tri_d, w=[tri.res], dres=tri.res)
        mskf = cx.sb("mskf", [128, 4, 128], F32)
        k.dma("sp", mskf[:], msk_d, w=[mskf.res], dres=mskf.res)
        mskb = cx.sb("mskb", [128, 4, 128], BF16)
        k.ins("dve", lambda: nc.vector.tensor_copy(mskb[:], mskf[:]), r=[mskf.res], w=[mskb.res])
        id4 = cx.sb("id4", [128, 4, 128], F32)
        for j in range(4):
            k.ins("dve", lambda j=j: nc.vector.tensor_copy(id4[:, j, :], ident[:]), r=[ident.res], w=[id4.res])
        return tri, mskb, id4

    def gdn_pass(mode):
        cx.push()
        fwd = mode != "own2"
        pi = 0 if mode == "own1" else 1
        red = mode == "red"
        s_kT, s_k, s_v, s_sm = (kT2_d, k2_d, v2_d, sm2_d) if red else (kT_d, k_d, v_d, sm_d)
        tri, mskb, id4 = load_scan_consts()
        cumL = tri[:, 0, :] if fwd else tri[:, 1, :]
        negR = tri[:, 2, :] if fwd else tri[:, 3, :]
        m_s = mskb[:, 0, :] if fwd else mskb[:, 1, :]
        mT_s = mskb[:, 1, :] if fwd else mskb[:, 0, :]
        mT_i = mskb[:, 3, :] if fwd else mskb[:, 2, :]
        g0 = pi * 8
        l0c = 32 + pi * 8
        banks = Ring([cx.psum(f"gb{i}", [128, 512], F32) for i in range(8)])
        S32 = cx.sb("S32", [128, 8, 128], F32)
        Sbf = cx.sb("Sbf", [128, 8, 128], BF16)
        if fwd:
            k.ins("pool", lambda: nc.gpsimd.memset(S32[:], 0.0), w=[S32.res])
        else:
            k.dma("sp", S32[:].rearrange("p h d -> p (h d)"), st2_d[:, 0:1024], r=[R_st2], w=[S32.res], dres=S32.res)
        k.ins("act", lambda: nc.scalar.copy(out=Sbf[:], in_=S32[:]), r=[S32.res], w=[Sbf.res])
        NB = 2
        qTr = cx.sbring("qTc", NB, [128, 8, 128], BF16)
        kTr = cx.sbring("kTc", NB, [128, 8, 128], BF16)
        kr = cx.sbring("kc", NB, [128, 8, 128], BF16)
        vr = cx.sbring("vc", NB, [128, 8, 128], BF16)
        smr = cx.sbring("smc", NB, [128, 128], F32)
        o1r = cx.sbring("o1c", NB, [128, 8, 128], F32)
        kbgr = cx.sbring("kbg", NB, [128, 8, 128], BF16)
        vbr = cx.sbring("vb", NB, [128, 8, 128], BF16)
        kdr = cx.sbring("kd", NB, [128, 8, 128], BF16)
        smallr = cx.sbring("gsm", NB, [128, 6, 8], F32)
        eglr = cx.sbring("egl", NB, [128, 8], F32)
        expr = cx.sbring("exps", NB, [128, 3, 8], F32)
        Ear = cx.sbring("Ea", 2, [128, 4, 128], F32)
        Ebr = cx.sbring("Eb", 2, [128, 4, 128], F32)
        Ecr = cx.sbring("Ec", 2, [128, 4, 128], F32)
        Edr = cx.sbring("Ed", 2, [128, 4, 128], F32)
        Pr = cx.sbring("Pp", 4, [128, 4, 128], F32)
        PTr = cx.sbring("PTp", 4, [128, 4, 128], F32)
        Yr = cx.sbring("Yp", 4, [128, 4, 128], F32)
        Yfr = cx.sbring("Yf", 2 * NB, [128, 4, 128], BF16)
        nWTr = cx.sbring("nWT", 2 * NB, [128, 4, 128], BF16)
        attr = cx.sbring("att", 2 * NB, [128, 4, 128], BF16)
        qgr = cx.sbring("qg", 2 * NB, [128, 4, 128], BF16)
        vnr = cx.sbring("vn", 2, [128, 4, 128], BF16)
        oTr = cx.sbring("oTs", 2, [128, 8, 128], F32)

        def prep(c):
            lat = c >= 2 and not red
            t0 = c * C
            l0 = t0 - NCTX
            qT_c = qTr.next(); kT_c = kTr.next(); k_c = kr.next(); v_c = vr.next(); sm_c = smr.next()
            if lat:
                k.dma("sp", qT_c[:], qT_d[:, :, t0:t0 + C].rearrange("h p t -> p h t"), r=[R_scr], w=[qT_c.res], dres=qT_c.res)
            k.dma("sp", kT_c[:], s_kT[:, :, t0:t0 + C].rearrange("h p t -> p h t"), r=[R_scr], w=[kT_c.res], dres=kT_c.res)
            k.dma("sp", k_c[:], s_k[t0:t0 + C, :].rearrange("t (h d) -> t h d", d=128), r=[R_scr], w=[k_c.res], dres=k_c.res)
            k.dma("sp", v_c[:], s_v[t0:t0 + C, :].rearrange("t (h d) -> t h d", d=128), r=[R_scr], w=[v_c.res], dres=v_c.res)
            k.dma("sp", sm_c[:], s_sm[t0:t0 + C, :], r=[R_scr], w=[sm_c.res], dres=sm_c.res)
            o1_c = None
            if lat and not fwd:
                o1_c = o1r.next()
                k.dma("sp", o1_c[:], o1_d[:, :, l0:l0 + C].rearrange("h p t -> p h t"), r=[R_o1[c]], w=[o1_c.res], dres=o1_c.res)
            gcols = sm_c[:, g0:g0 + 8]
            lcols = sm_c[:, l0c:l0c + 8]
            pb = banks.next()
            k.group("pe", [
                lambda: nc.tensor.matmul(pb[:, 0:8], cumL, gcols, start=True, stop=True),
                lambda: nc.tensor.matmul(pb[:, 8:16], onesf[:], gcols, start=True, stop=True)],
                r=[tri.res, onesf.res, sm_c.res], w=[pb.res])
            sm6 = smallr.next()
            gc, gcl, ngc, tmp = sm6[:, 0, :], sm6[:, 1, :], sm6[:, 2, :], sm6[:, 3, :]
            k.ins("dve", lambda: nc.vector.tensor_copy(gc, pb[:, 0:8]), r=[pb.res], w=[sm6.res])
            k.ins("dve", lambda: nc.vector.tensor_tensor(out=gcl, in0=gc, in1=lcols, op=ALU.add), r=[sm6.res, sm_c.res], w=[sm6.res])
            k.ins("dve", lambda: nc.vector.tensor_scalar(out=ngc, in0=gc, scalar1=-1.0, scalar2=None, op0=ALU.mult),
                  r=[sm6.res], w=[sm6.res])
            k.ins("dve", lambda: nc.vector.tensor_tensor(out=tmp, in0=pb[:, 8:16], in1=gc, op=ALU.subtract),
                  r=[pb.res, sm6.res], w=[sm6.res])
            ex = expr.next()
            egl = eglr.next()
            k.ins("act", lambda: nc.scalar.activation(out=ex[:, 0, :], in_=gcl, func=AF.Exp), r=[sm6.res], w=[ex.res])
            k.ins("act", lambda: nc.scalar.activation(out=ex[:, 1, :], in_=lcols, func=AF.Exp), r=[sm_c.res], w=[ex.res])
            k.ins("act", lambda: nc.scalar.activation(out=ex[:, 2, :], in_=tmp, func=AF.Exp), r=[sm6.res], w=[ex.res])
            k.ins("act", lambda: nc.scalar.activation(out=egl[:], in_=pb[:, 8:16], func=AF.Exp), r=[pb.res], w=[egl.res])
            kbg = kbgr.next(); vb = vbr.next(); kd = kdr.next()
            bc = lambda col: ex[:, col, :].unsqueeze(2).to_broadcast([128, 8, 128])
            k.ins("pool", lambda: nc.gpsimd.tensor_tensor(out=kbg[:], in0=k_c[:], in1=bc(0), op=ALU.mult),
                  r=[k_c.res, ex.res], w=[kbg.res])
            k.ins("pool", lambda: nc.gpsimd.tensor_tensor(out=vb[:], in0=v_c[:], in1=bc(1), op=ALU.mult),
                  r=[v_c.res, ex.res], w=[vb.res])
            k.ins("pool", lambda: nc.gpsimd.tensor_tensor(out=kd[:], in0=k_c[:], in1=bc(2), op=ALU.mult),
                  r=[k_c.res, ex.res], w=[kd.res])
            pp = dict(c=c, lat=lat, groups=[None, None], vb=vb, kd=kd, egl=egl, o1=o1_c)

            def grp(gi):
                h0 = gi * 4
                gb = lambda h: sm_c[:, g0 + h:g0 + h + 1].to_broadcast([128, 128])
                lb = lambda h: sm_c[:, l0c + h:l0c + h + 1].to_broadcast([128, 128])
                KK = banks.next()
                k.group("pe", [(lambda j=j: nc.tensor.matmul(KK[:, j * 128:(j + 1) * 128], kT_c[:, h0 + j, :], kT_c[:, h0 + j, :],
                                                             start=True, stop=True)) for j in range(4)],
                        r=[kT_c.res], w=[KK.res])
                Da = banks.next()
                fl = []
                for j in range(4):
                    fl.append(lambda j=j: nc.tensor.matmul(Da[:, j * 128:(j + 1) * 128], gb(h0 + j), negR, start=True, stop=False))
                    fl.append(lambda j=j: nc.tensor.matmul(Da[:, j * 128:(j + 1) * 128], identb[:], m_s, start=False, stop=True))
                k.group("pe", fl, r=[sm_c.res, tri.res, identb.res, mskb.res], w=[Da.res])
                Ea = Ear.next()
                for j in range(4):
                    k.ins("act", lambda j=j: nc.scalar.activation(out=Ea[:, j, :], in_=Da[:, j * 128:(j + 1) * 128], func=AF.Exp,
                                                                  bias=sm6[:, 1, h0 + j:h0 + j + 1]),
                          r=[Da.res, sm6.res], w=[Ea.res])
                Db = banks.next()
                fl = []
                for j in range(4):
                    sl = slice(j * 128, (j + 1) * 128)
                    fl.append(lambda j=j, sl=sl: nc.tensor.matmul(Db[:, sl], gb(h0 + j), cumL, start=True, stop=False))
                    fl.append(lambda j=j, sl=sl: nc.tensor.matmul(Db[:, sl], lb(h0 + j), ident[:], start=False, stop=False))
                    fl.append(lambda j=j, sl=sl: nc.tensor.matmul(Db[:, sl], identb[:], mT_s, start=False, stop=True))
                k.group("pe", fl, r=[sm_c.res, tri.res, ident.res, identb.res, mskb.res], w=[Db.res])
                Eb = Ebr.next()
                for j in range(4):
                    k.ins("act", lambda j=j: nc.scalar.activation(out=Eb[:, j, :], in_=Db[:, j * 128:(j + 1) * 128], func=AF.Exp,
                                                                  bias=sm6[:, 2, h0 + j:h0 + j + 1]),
                          r=[Db.res, sm6.res], w=[Eb.res])
                P0 = Pr.next(); P0T = PTr.next()
                KK3 = KK[:].rearrange("p (j c) -> p j c", c=128)
                k.ins("dve", lambda: nc.vector.scalar_tensor_tensor(out=P0[:], in0=KK3, scalar=-1.0, in1=Ea[:],
                                                                    op0=ALU.mult, op1=ALU.mult),
                      r=[KK.res, Ea.res], w=[P0.res])
                k.ins("dve", lambda: nc.vector.scalar_tensor_tensor(out=P0T[:], in0=KK3, scalar=-1.0, in1=Eb[:],
                                                                    op0=ALU.mult, op1=ALU.mult),
                      r=[KK.res, Eb.res], w=[P0T.res])
                yield
                att = None; qg = None
                if lat:
                    QK = banks.next()
                    k.group("pe", [(lambda j=j: nc.tensor.matmul(QK[:, j * 128:(j + 1) * 128], kT_c[:, h0 + j, :], qT_c[:, h0 + j, :],
                                                                 start=True, stop=True)) for j in range(4)],
                            r=[kT_c.res, qT_c.res], w=[QK.res])
                    Dc = banks.next()
                    fl = []
                    for j in range(4):
                        sl = slice(j * 128, (j + 1) * 128)
                        fl.append(lambda j=j, sl=sl: nc.tensor.matmul(Dc[:, sl], gb(h0 + j), cumL, start=True, stop=False))
                        fl.append(lambda j=j, sl=sl: nc.tensor.matmul(Dc[:, sl], identb[:], mT_i, start=False, stop=True))
                    k.group("pe", fl, r=[sm_c.res, tri.res, identb.res, mskb.res], w=[Dc.res])
                    Ec = Ecr.next()
                    for j in range(4):
                        k.ins("act", lambda j=j: nc.scalar.activation(out=Ec[:, j, :], in_=Dc[:, j * 128:(j + 1) * 128], func=AF.Exp,
                                                                      bias=sm6[:, 2, h0 + j:h0 + j + 1]),
                              r=[Dc.res, sm6.res], w=[Ec.res])
                    Dd = banks.next()
                    k.group("pe", [(lambda j=j: nc.tensor.matmul(Dd[:, j * 128:(j + 1) * 128], gb(h0 + j), cumL, start=True, stop=True))
                                   for j in range(4)], r=[sm_c.res, tri.res], w=[Dd.res])
                    Ed = Edr.next()
                    k.ins("act", lambda: nc.scalar.activation(out=Ed[:].rearrange("p j c -> p (j c)"), in_=Dd[:], func=AF.Exp),
                          r=[Dd.res], w=[Ed.res])
                    att = attr.next()
                    k.ins("dve", lambda: nc.vector.tensor_tensor(out=att[:], in0=QK[:].rearrange("p (j c) -> p j c", c=128),
                                                                 in1=Ec[:], op=ALU.mult), r=[QK.res, Ec.res], w=[att.res])
                    qg = qgr.next()
                    k.ins("pool", lambda: nc.gpsimd.tensor_tensor(out=qg[:], in0=qT_c[:, h0:h0 + 4, :], in1=Ed[:], op=ALU.mult),
                          r=[qT_c.res, Ed.res], w=[qg.res])
                Y = Yr.next()
                k.ins("pool", lambda: nc.gpsimd.tensor_tensor(out=Y[:], in0=P0T[:], in1=id4[:], op=ALU.add),
                      r=[P0T.res, id4.res], w=[Y.res])
                yield
                Pp, PTp = P0, P0T
                for lev in range(1, 7):
                    last = lev == 6
                    Pb = banks.next()
                    k.group("pe", [(lambda j=j, Pp=Pp, PTp=PTp, Pb=Pb: nc.tensor.matmul(
                        Pb[:, j * 128:(j + 1) * 128], PTp[:, j, :], Pp[:, j, :], start=True, stop=True)) for j in range(4)],
                        r=[Pp.res, PTp.res], w=[Pb.res])
                    Pn = Pr.next()
                    k.ins("act", lambda Pn=Pn, Pb=Pb: nc.scalar.copy(out=Pn[:].rearrange("p j c -> p (j c)"), in_=Pb[:]),
                          r=[Pb.res], w=[Pn.res])
                    PTn = None
                    if not last:
                        PTb = banks.next()
                        k.group("pe", [(lambda j=j, Pp=Pp, PTp=PTp, PTb=PTb: nc.tensor.matmul(
                            PTb[:, j * 128:(j + 1) * 128], Pp[:, j, :], PTp[:, j, :], start=True, stop=True)) for j in range(4)],
                            r=[Pp.res, PTp.res], w=[PTb.res])
                        PTn = PTr.next()
                        k.ins("act", lambda PTn=PTn, PTb=PTb: nc.scalar.copy(out=PTn[:].rearrange("p j c -> p (j c)"), in_=PTb[:]),
                              r=[PTb.res], w=[PTn.res])
                    yield
                    Yb = banks.next()
                    k.group("pe", [(lambda j=j, Yb=Yb, Y=Y, Pn=Pn: nc.tensor.matmul(
                        Yb[:, j * 128:(j + 1) * 128], Pn[:, j, :], Y[:, j, :], start=True, stop=True)) for j in range(4)],
                        r=[Y.res, Pn.res], w=[Yb.res])
                    if last:
                        Yn = Yfr.next()
                        k.ins("dve", lambda Yn=Yn, Yb=Yb, Y=Y: nc.vector.tensor_tensor(
                            out=Yn[:].rearrange("p j c -> p (j c)"), in0=Yb[:], in1=Y[:].rearrange("p j c -> p (j c)"), op=ALU.add),
                            r=[Yb.res, Y.res], w=[Yn.res])
                    else:
                        Yn = Yr.next()
                        k.ins("dve", lambda Yn=Yn, Yb=Yb, Y=Y: nc.vector.tensor_tensor(
                            out=Yn[:].rearrange("p j c -> p (j c)"), in0=Yb[:], in1=Y[:].rearrange("p j c -> p (j c)"), op=ALU.add),
                            r=[Yb.res, Y.res], w=[Yn.res])
                    Y = Yn
                    Pp, PTp = Pn, PTn
                    yield
                Wb = banks.next()
                k.group("pe", [(lambda j=j: nc.tensor.matmul(Wb[:, j * 128:(j + 1) * 128], kbg[:, h0 + j, :], Y[:, j, :],
                                                             start=True, stop=True)) for j in range(4)],
                        r=[kbg.res, Y.res], w=[Wb.res])
                nWT = nWTr.next()
                k.ins("act", lambda: nc.scalar.activation(out=nWT[:].rearrange("p j c -> p (j c)"), in_=Wb[:], func=AF.Identity,
                                                          scale=-1.0), r=[Wb.res], w=[nWT.res])
                pp["groups"][gi] = (Y, nWT, att, qg)
            pp["gens"] = [grp(0), grp(1)]
            return pp

        def chain(pp):
            c, lat = pp["c"], pp["lat"]
            l0 = c * C - NCTX
            vb, kd, egl = pp["vb"], pp["kd"], pp["egl"]
            oTs = oTr.next() if lat else None
            for gi in range(2):
                h0 = gi * 4
                Y, nWT, att, qg = pp["groups"][gi]
                VN = banks.next()
                fl = []
                for j in range(4):
                    sl = slice(j * 128, (j + 1) * 128)
                    fl.append(lambda j=j, sl=sl: nc.tensor.matmul(VN[:, sl], Y[:, j, :], vb[:, h0 + j, :], start=True, stop=False))
                    fl.append(lambda j=j, sl=sl: nc.tensor.matmul(VN[:, sl], nWT[:, j, :], Sbf[:, h0 + j, :], start=False, stop=True))
                k.group("pe", fl, r=[Y.res, vb.res, nWT.res, Sbf.res], w=[VN.res])
                vn = vnr.next()
                k.ins("dve", lambda: nc.vector.tensor_copy(vn[:].rearrange("p j c -> p (j c)"), VN[:]), r=[VN.res], w=[vn.res])
                yield
                if lat:
                    OT = banks.next()
                    fl = []
                    for j in range(4):
                        sl = slice(j * 128, (j + 1) * 128)
                        fl.append(lambda j=j, sl=sl: nc.tensor.matmul(OT[:, sl], Sbf[:, h0 + j, :], qg[:, j, :], start=True, stop=False))
                        fl.append(lambda j=j, sl=sl: nc.tensor.matmul(OT[:, sl], vn[:, j, :], att[:, j, :], start=False, stop=True))
                    k.group("pe", fl, r=[Sbf.res, qg.res, vn.res, att.res], w=[OT.res])
                    osl = oTs[:, h0:h0 + 4, :].rearrange("p j c -> p (j c)")
                    if fwd:
                        k.ins("act", lambda: nc.scalar.copy(out=osl, in_=OT[:]), r=[OT.res], w=[oTs.res])
                    else:
                        o1_c = pp["o1"]
                        k.ins("dve", lambda: nc.vector.tensor_tensor(
                            out=osl, in0=OT[:], in1=o1_c[:, h0:h0 + 4, :].rearrange("p j c -> p (j c)"), op=ALU.add),
                            r=[OT.res, o1_c.res], w=[oTs.res])
                DS = banks.next()
                k.group("pe", [(lambda j=j: nc.tensor.matmul(DS[:, j * 128:(j + 1) * 128], kd[:, h0 + j, :], vn[:, j, :],
                                                             start=True, stop=True)) for j in range(4)],
                        r=[kd.res, vn.res], w=[DS.res])
                ssl = S32[:, h0:h0 + 4, :]
                k.ins("pool", lambda: nc.gpsimd.tensor_tensor(
                    out=ssl, in0=ssl, in1=egl[:, h0:h0 + 4].unsqueeze(2).to_broadcast([128, 4, 128]), op=ALU.mult),
                    r=[S32.res, egl.res], w=[S32.res])
                k.ins("dve", lambda: nc.vector.tensor_tensor(out=ssl, in0=ssl, in1=DS[:].rearrange("p (j c) -> p j c", c=128),
                                                             op=ALU.add), r=[S32.res, DS.res], w=[S32.res])
                k.ins("act", lambda: nc.scalar.copy(out=Sbf[:, h0:h0 + 4, :], in_=ssl), r=[S32.res], w=[Sbf.res])
                yield
            if lat:
                if fwd:
                    k.dma("sp", o1_d[:, :, l0:l0 + C].rearrange("h p t -> p h t"), oTs[:], r=[oTs.res], w=[R_o1[c]], dres=oTs.res)
                else:
                    k.dma("sp", oT_d[:, :, l0:l0 + C].rearrange("h p t -> p h t"), oTs[:], r=[oTs.res], w=[R_oT], dres=oTs.res)

        order = list(range(NCH)) if fwd else list(range(NCH - 1, 1, -1))
        if GDN_LIMIT:
            order = order[:GDN_LIMIT]
        def drive(gens):
            gens = list(gens)
            while gens:
                for g_ in list(gens):
                    try:
                        next(g_)
                    except StopIteration:
                        gens.remove(g_)

        pend = prep(order[0])
        drive(pend["gens"])
        for idx in range(len(order)):
            nxt = prep(order[idx + 1]) if idx + 1 < len(order) else None
            drive([chain(pend)] + (nxt["gens"] if nxt is not None else []))
            pend = nxt
        if red:
            k.dma("sp", st2_d[:, 0:1024], S32[:].rearrange("p h d -> p (h d)"), r=[S32.res], w=[R_st2], dres=S32.res)
        cx.pop()

    def ssd_pass(mode):
        cx.push()
        fwd = mode != "own2"
        pi = 0 if mode == "own1" else 1
        red = mode == "red"
        s_x, s_B, s_sm = (x2_d, B2_d, sm2_d) if red else (x_d, B_d, sm_d)
        tri, mskb, id4 = load_scan_consts()
        cumL = tri[:, 0, :] if fwd else tri[:, 1, :]
        mT_i = mskb[:, 3, :] if fwd else mskb[:, 2, :]
        d0 = 64 + pi * 32
        banks = Ring([cx.psum(f"sbk{i}", [128, 512], F32) for i in range(4)])
        ydb = [cx.psum(f"syd{i}", [128, 512], F32) for i in range(4)]
        aB = cx.sb("aB", [128, 32], F32)
        k.dma("sp", aB[:], salog_d[:, pi, :], w=[aB.res], dres=aB.res)
        k.ins("act", lambda: nc.scalar.activation(out=aB[:], in_=aB[:], func=AF.Exp), r=[aB.res], w=[aB.res])
        k.ins("dve", lambda: nc.vector.tensor_scalar(out=aB[:], in0=aB[:], scalar1=-1.0, scalar2=None, op0=ALU.mult),
              r=[aB.res], w=[aB.res])
        hT32 = cx.sb("hT32", [128, 32, 64], F32)
        hTbf = cx.sb("hTbf", [128, 32, 64], BF16)
        if fwd:
            k.ins("pool", lambda: nc.gpsimd.memset(hT32[:], 0.0), w=[hT32.res])
        else:
            k.dma("sp", hT32[:].rearrange("p h d -> p (h d)"), st2_d[:, 1024:3072], r=[R_st2], w=[hT32.res], dres=hT32.res)
        k.ins("act", lambda: nc.scalar.copy(out=hTbf[:], in_=hT32[:]), r=[hT32.res], w=[hTbf.res])
        xr = cx.sbring("xc", 2, [128, 32, 64], BF16)
        BTr = cx.sbring("BTc", 2, [128, 4, 128], BF16)
        CTr = cx.sbring("CTc", 2, [128, 4, 128], BF16)
        Br = cx.sbring("Bc", 2, [128, 4, 128], BF16)
        smr = cx.sbring("smc", 2, [128, 128], F32)
        y1r = cx.sbring("y1c", 2, [128, 2048], F32)
        dAr = cx.sbring("dA", 2, [128, 32], F32)
        s5r = cx.sbring("s5", 2, [128, 6, 32], F32)
        xdtr = cx.sbring("xdt", 2, [128, 32, 64], BF16)
        xdtsr = cx.sbring("xdts", 2, [128, 32, 64], BF16)
        ytr = cx.sbring("ytmp", 2, [128, 32, 64], F32)
        ycr = cx.sbring("yc", 2, [128, 2048], F32)
        segr = cx.sbring("seg", 3, [128, 4, 128], F32)
        GTr = cx.sbring("GT", 2, [128, 4, 128], F32)
        MTr = cx.sbring("MT", 3, [128, 4, 128], BF16)
        order = list(range(NCH)) if fwd else list(range(NCH - 1, 1, -1))
        if GDN_LIMIT:
            order = order[:GDN_LIMIT]
        def chunk_gen(c):
                lat = c >= 2 and not red
                t0 = c * C
                l0 = t0 - NCTX
                x_c = xr.next(); BT_c = BTr.next(); CT_c = CTr.next(); B_c = Br.next(); sm_c = smr.next()
                k.dma("sp", x_c[:].rearrange("p h d -> p (h d)"), s_x[t0:t0 + C, :], r=[R_scr], w=[x_c.res], dres=x_c.res)
                if lat:
                    k.dma("sp", BT_c[:], BT_d[:, :, t0:t0 + C].rearrange("g p t -> p g t"), r=[R_scr], w=[BT_c.res], dres=BT_c.res)
                    k.dma("sp", CT_c[:], CT_d[:, :, t0:t0 + C].rearrange("g p t -> p g t"), r=[R_scr], w=[CT_c.res], dres=CT_c.res)
                k.dma("sp", B_c[:], s_B[t0:t0 + C, :].rearrange("t (g n) -> t g n", n=128), r=[R_scr], w=[B_c.res], dres=B_c.res)
                k.dma("sp", sm_c[:], s_sm[t0:t0 + C, :], r=[R_scr], w=[sm_c.res], dres=sm_c.res)
                y1_c = None
                if lat and not fwd:
                    y1_c = y1r.next()
                    k.dma("sp", y1_c[:], y1_d[l0:l0 + C, :], r=[R_y1[c]], w=[y1_c.res], dres=y1_c.res)
                dtc = sm_c[:, d0:d0 + 32]
                dA = dAr.next()
                k.ins("dve", lambda: nc.vector.tensor_tensor(out=dA[:], in0=dtc, in1=aB[:], op=ALU.mult),
                      r=[sm_c.res, aB.res], w=[dA.res])
                pb = banks.next()
                k.group("pe", [
                    lambda: nc.tensor.matmul(pb[:, 0:32], cumL, dA[:], start=True, stop=True),
                    lambda: nc.tensor.matmul(pb[:, 32:64], onesf[:], dA[:], start=True, stop=True)],
                    r=[tri.res, onesf.res, dA.res], w=[pb.res])
                s5 = s5r.next()
                ac, nac, tmp, ea, w2, eat = (s5[:, i, :] for i in range(6))
                k.ins("dve", lambda: nc.vector.tensor_copy(ac, pb[:, 0:32]), r=[pb.res], w=[s5.res])
                k.ins("dve", lambda: nc.vector.tensor_scalar(out=nac, in0=ac, scalar1=-1.0, scalar2=None, op0=ALU.mult),
                      r=[s5.res], w=[s5.res])
                k.ins("dve", lambda: nc.vector.tensor_tensor(out=tmp, in0=pb[:, 32:64], in1=ac, op=ALU.subtract),
                      r=[pb.res, s5.res], w=[s5.res])
                k.ins("act", lambda: nc.scalar.activation(out=ea, in_=ac, func=AF.Exp), r=[s5.res], w=[s5.res])
                k.ins("act", lambda: nc.scalar.activation(out=w2, in_=tmp, func=AF.Exp), r=[s5.res], w=[s5.res])
                k.ins("act", lambda: nc.scalar.activation(out=eat, in_=pb[:, 32:64], func=AF.Exp), r=[pb.res], w=[s5.res])
                k.ins("dve", lambda: nc.vector.tensor_tensor(out=w2, in0=w2, in1=dtc, op=ALU.mult), r=[s5.res, sm_c.res], w=[s5.res])
                bc32 = lambda ap: ap.unsqueeze(2).to_broadcast([128, 32, 64])
                xdts = xdtsr.next()
                k.ins("pool", lambda: nc.gpsimd.tensor_tensor(out=xdts[:], in0=x_c[:], in1=bc32(w2), op=ALU.mult),
                      r=[x_c.res, s5.res], w=[xdts.res])
                ytmp = None
                if lat:
                    xdt = xdtr.next()
                    k.ins("pool", lambda: nc.gpsimd.tensor_tensor(out=xdt[:], in0=x_c[:], in1=bc32(dtc), op=ALU.mult),
                          r=[x_c.res, sm_c.res], w=[xdt.res])
                    ytmp = ytr.next()
                    for g in range(4):
                        YO = banks.next()
                        k.ins("pe", lambda g=g, YO=YO: nc.tensor.matmul(
                            YO[:], CT_c[:, g, :], hTbf[:, g * 8:(g + 1) * 8, :].rearrange("p h d -> p (h d)"), start=True, stop=True),
                            r=[CT_c.res, hTbf.res], w=[YO.res])
                        k.ins("dve", lambda g=g, YO=YO: nc.vector.tensor_tensor(
                            out=ytmp[:, g * 8:(g + 1) * 8, :], in0=YO[:].rearrange("p (h d) -> p h d", d=64),
                            in1=ea[:, g * 8:(g + 1) * 8].unsqueeze(2).to_broadcast([128, 8, 64]), op=ALU.mult),
                            r=[YO.res, s5.res], w=[ytmp.res])
                k.ins("pool", lambda: nc.gpsimd.tensor_tensor(out=hT32[:], in0=hT32[:], in1=bc32(eat), op=ALU.mult),
                      r=[hT32.res, s5.res], w=[hT32.res])
                for g in range(4):
                    NS = banks.next()
                    k.ins("pe", lambda g=g, NS=NS: nc.tensor.matmul(
                        NS[:], B_c[:, g, :], xdts[:, g * 8:(g + 1) * 8, :].rearrange("p h d -> p (h d)"), start=True, stop=True),
                        r=[B_c.res, xdts.res], w=[NS.res])
                    hs = hT32[:, g * 8:(g + 1) * 8, :]
                    k.ins("dve", lambda g=g, NS=NS, hs=hs: nc.vector.tensor_tensor(
                        out=hs, in0=hs, in1=NS[:].rearrange("p (h d) -> p h d", d=64), op=ALU.add),
                        r=[hT32.res, NS.res], w=[hT32.res])
                k.ins("act", lambda: nc.scalar.copy(out=hTbf[:], in_=hT32[:]), r=[hT32.res], w=[hTbf.res])
                yield
                if not lat:
                    return
                GTb = banks.next()
                k.group("pe", [(lambda g=g: nc.tensor.matmul(GTb[:, g * 128:(g + 1) * 128], BT_c[:, g, :], CT_c[:, g, :],
                                                             start=True, stop=True)) for g in range(4)],
                        r=[BT_c.res, CT_c.res], w=[GTb.res])
                GT = GTr.next()
                k.ins("act", lambda: nc.scalar.copy(out=GT[:].rearrange("p g c -> p (g c)"), in_=GTb[:]), r=[GTb.res], w=[GT.res])
                y_c = ycr.next()

                def grp_gen(g):
                    YD = ydb[g]
                    for qd in range(2):
                        hq = g * 8 + qd * 4
                        Dq = banks.next()
                        fl = []
                        for j in range(4):
                            sl = slice(j * 128, (j + 1) * 128)
                            fl.append(lambda j=j, sl=sl, Dq=Dq: nc.tensor.matmul(
                                Dq[:, sl], dA[:, hq + j:hq + j + 1].to_broadcast([128, 128]), cumL, start=True, stop=False))
                            fl.append(lambda j=j, sl=sl, Dq=Dq: nc.tensor.matmul(Dq[:, sl], identb[:], mT_i, start=False, stop=True))
                        k.group("pe", fl, r=[dA.res, tri.res, identb.res, mskb.res], w=[Dq.res])
                        seg = segr.next()
                        for j in range(4):
                            k.ins("act", lambda j=j, seg=seg, Dq=Dq: nc.scalar.activation(
                                out=seg[:, j, :], in_=Dq[:, j * 128:(j + 1) * 128], func=AF.Exp, bias=s5[:, 1, hq + j:hq + j + 1]),
                                r=[Dq.res, s5.res], w=[seg.res])
                        MT = MTr.next()
                        k.ins("dve", lambda seg=seg, MT=MT, g=g: nc.vector.tensor_tensor(
                            out=MT[:], in0=seg[:], in1=GT[:, g:g + 1, :].to_broadcast([128, 4, 128]), op=ALU.mult),
                            r=[seg.res, GT.res], w=[MT.res])
                        k.group("pe", [(lambda j=j, MT=MT, YD=YD: nc.tensor.matmul(
                            YD[:, (qd * 4 + j) * 64:(qd * 4 + j + 1) * 64], MT[:, j, :], xdt[:, hq + j, :], start=True, stop=True))
                            for j in range(4)], r=[MT.res, xdt.res], w=[YD.res])
                        yield
                    ysl = y_c[:, g * 512:(g + 1) * 512]
                    k.ins("dve", lambda g=g, YD=YD, ysl=ysl: nc.vector.tensor_tensor(
                        out=ysl, in0=YD[:], in1=ytmp[:, g * 8:(g + 1) * 8, :].rearrange("p h d -> p (h d)"), op=ALU.add),
                        r=[YD.res, ytmp.res], w=[y_c.res])
                ggs = [grp_gen(g) for g in range(4)]
                while ggs:
                    for g_ in list(ggs):
                        try:
                            next(g_)
                        except StopIteration:
                            ggs.remove(g_)
                if fwd:
                    k.dma("sp", y1_d[l0:l0 + C, :], y_c[:], r=[y_c.res], w=[R_y1[c]], dres=y_c.res)
                else:
                    k.ins("pool", lambda: nc.gpsimd.tensor_tensor(out=y_c[:], in0=y_c[:], in1=y1_c[:], op=ALU.add),
                          r=[y_c.res, y1_c.res], w=[y_c.res])
                    k.dma("sp", y_d[l0:l0 + C, :], y_c[:], r=[y_c.res], w=[R_y], dres=y_c.res)

        def adv(g_, n):
            for _ in range(n):
                try:
                    next(g_)
                except StopIteration:
                    return

        for c in order:
            adv(chunk_gen(c), 100)
        if red:
            k.dma("sp", st2_d[:, 1024:3072], hT32[:].rearrange("p h d -> p (h d)"), r=[hT32.res], w=[R_st2], dres=hT32.res)
        cx.pop()

    with nc.allow_non_contiguous_dma("scan chunk relayout"):
        if "R" in phases:
            gdn_pass("red")
            ssd_pass("red")
        if "B1" in phases:
            gdn_pass("own1")
        if "S1" in phases:
            ssd_pass("own1")
        if "B2" in phases:
            gdn_pass("own2")
        if "S2" in phases:
            ssd_pass("own2")


    def silu_parts(src_ap, shape, ring_e, ring_z, src_res):
        ez = ring_e.next()
        zc = ring_z.next()
        k.ins("act", lambda: nc.scalar.activation(out=ez[:], in_=src_ap, func=AF.Exp, scale=-1.0), r=[src_res], w=[ez.res])
        k.ins("act", lambda: nc.scalar.copy(out=zc[:], in_=src_ap), r=[src_res], w=[zc.res])
        k.ins("dve", lambda: nc.vector.tensor_scalar(out=ez[:], in0=ez[:], scalar1=1.0, scalar2=None, op0=ALU.add),
              r=[ez.res], w=[ez.res])
        return zc, ez

    if "C1" in phases:
        cx.push()
        T1 = 128
        wZ = cx.sb("wZ", [128, 8, 3072], BF16)
        wbg = cx.sb("wbg", [128, 8, 1024], BF16)
        wbs = cx.sb("wbs", [128, 16, 1024], BF16)
        wo = cx.sb("wo", [128, 8, 1024], BF16)
        for kt in range(8):
            k.dma("pool", wZ[:, kt, :], w_in[kt * 128:(kt + 1) * 128, O_ZG:O_ZG + 3072], w=[wZ.res], dres=wZ.res)
            k.dma("pool", wbg[:, kt, :], wbg_d[kt * 128:(kt + 1) * 128, :], w=[wbg.res], dres=wbg.res)
            k.dma("pool", wo[:, kt, :], wo_d[kt * 128:(kt + 1) * 128, :], w=[wo.res], dres=wo.res)
        for kt in range(16):
            k.dma("pool", wbs[:, kt, :], wbs_d[kt * 128:(kt + 1) * 128, :], w=[wbs.res], dres=wbs.res)
        gnw = cx.sb("gnw", [128, 1], F32)
        k.dma("sp", gnw[:], gnw_d, w=[gnw.res], dres=gnw.res)
        dskB = cx.sb("dskB", [128, 2048], F32)
        snwB = cx.sb("snwB", [128, 2048], F32)
        g1B = cx.sb("g1B", [128, 1024], F32)
        k.dma("sp", dskB[:], dskip_d[0, :].partition_broadcast(128), w=[dskB.res], dres=dskB.res)
        k.dma("sp", snwB[:], snw_d[0, :].partition_broadcast(128), w=[snwB.res], dres=snwB.res)
        k.dma("sp", g1B[:], mod_d[0, 2048:3072].partition_broadcast(128), r=[R_mod], w=[g1B.res], dres=g1B.res)
        neg1c = cx.sb("neg1c", [128, 512], F32)
        k.ins("pool", lambda: nc.gpsimd.memset(neg1c[:], -1.0), w=[neg1c.res])
        banks = Ring([cx.psum(f"c1b{i}", [128, 512], F32) for i in range(6)])
        ptr = Ring([cx.psum(f"c1t{i}", [128, 1024], BF16) for i in range(2)])
        aTr = cx.sbring("c_aT", 2, [128, 8, T1], BF16)
        oTr_ = cx.sbring("c_oT", 1, [128, 8, T1], F32)
        yr_ = cx.sbring("c_y", 1, [128, 2048], F32)
        xsr_ = cx.sbring("c_xs", 1, [128, 2048], BF16)
        sgr_ = cx.sbring("c_sg", 1, [128, 16, T1], F32)
        xr_ = cx.sbring("c_x", 1, [128, 1024], F32)
        onTr = cx.sbring("c_on", 1, [128, 8, T1], BF16)
        ynTr = cx.sbring("c_yn", 1, [128, 16, T1], BF16)
        mTr = cx.sbring("c_mT", 1, [128, 8, T1], BF16)
        h1r = cx.sbring("c_h1", 1, [128, 1024], F32)
        e1r = cx.sbring("c_e1", 2, [128, T1], F32)
        z1r = cx.sbring("c_z1", 2, [128, T1], F32)
        t1r = cx.sbring("c_t1", 3, [128, T1], F32)
        e5r = cx.sbring("c_e5", 2, [128, 512], F32)
        z5r = cx.sbring("c_z5", 2, [128, 512], F32)
        t5r = cx.sbring("c_t5", 3, [128, 512], F32)
        ynr = cx.sbring("c_ynb", 2, [128, 512], BF16)
        ssr_ = cx.sbring("c_ss", 4, [128, 2], F32)
        def drive_c(gens):
            gens = list(gens)
            while gens:
                for g_ in list(gens):
                    try:
                        next(g_)
                    except StopIteration:
                        gens.remove(g_)

        if debug:
            print("C1 sbuf remaining", nc.sbuf_bytes_remaining)
        for ti in range(NLAT // T1):
            l0 = ti * T1
            aT = aTr.next(); oT = oTr_.next(); yt = yr_.next(); xs_ = xsr_.next(); sg = sgr_.next(); xt = xr_.next()
            k.dma("sp", aT[:], aT_d[:, :, l0:l0 + T1].rearrange("kt p t -> p kt t"), r=[R_scr], w=[aT.res], dres=aT.res)
            k.dma("sp", oT[:], oT_d[:, :, l0:l0 + T1].rearrange("h p t -> p h t"), r=[R_oT], w=[oT.res], dres=oT.res)
            k.dma("sp", yt[:], y_d[l0:l0 + T1, :], r=[R_y], w=[yt.res], dres=yt.res)
            k.dma("sp", xs_[:], x_d[NCTX + l0:NCTX + l0 + T1, :], r=[R_scr], w=[xs_.res], dres=xs_.res)
            k.dma("sp", sg[:], sgT_d[:, :, l0:l0 + T1].rearrange("c p t -> p c t"), r=[R_scr], w=[sg.res], dres=sg.res)
            k.dma("sp", xt[:], xin[NCTX + l0:NCTX + l0 + T1, :], w=[xt.res], dres=xt.res)
            onT = onTr.next()
            ynT = ynTr.next()
            mT = mTr.next()
            h1 = h1r.next()

            def gen_gdn(hs):
              for h in hs:
                pz = banks.next()
                k.group("pe", [(lambda kt=kt: nc.tensor.matmul(pz[:, 0:T1], wZ[:, kt, h * 128:(h + 1) * 128], aT[:, kt, :],
                                                               start=(kt == 0), stop=(kt == 7))) for kt in range(8)],
                        r=[wZ.res, aT.res], w=[pz.res])
                zc, ez = silu_parts(pz[:, 0:T1], None, e1r, z1r, pz.res)
                k.ins("dve", lambda: nc.vector.reciprocal(out=ez[:], in_=ez[:]), r=[ez.res], w=[ez.res])
                k.ins("pool", lambda: nc.gpsimd.tensor_tensor(out=zc[:], in0=zc[:], in1=ez[:], op=ALU.mult),
                      r=[zc.res, ez.res], w=[zc.res])
                o2 = t1r.next()
                k.ins("pool", lambda: nc.gpsimd.tensor_tensor(out=o2[:], in0=oT[:, h, :], in1=oT[:, h, :], op=ALU.mult),
                      r=[oT.res], w=[o2.res])
                pn = banks.next()
                k.ins("pe", lambda: nc.tensor.matmul(pn[:, 0:T1], onesf[:], o2[:], start=True, stop=True),
                      r=[onesf.res, o2.res], w=[pn.res])
                rs = t1r.next()
                k.ins("act", lambda: nc.scalar.activation(out=rs[:], in_=pn[:, 0:T1], func=AF.Ln, scale=1.0 / 128, bias=epsc[:, 0:1]),
                      r=[pn.res, epsc.res], w=[rs.res])
                k.ins("act", lambda: nc.scalar.activation(out=rs[:], in_=rs[:], func=AF.Exp, scale=-0.5), r=[rs.res], w=[rs.res])
                k.ins("dve", lambda: nc.vector.tensor_tensor(out=rs[:], in0=rs[:], in1=oT[:, h, :], op=ALU.mult),
                      r=[rs.res, oT.res], w=[rs.res])
                k.ins("dve", lambda: nc.vector.scalar_tensor_tensor(out=onT[:, h, :], in0=rs[:], scalar=gnw[:, 0:1], in1=zc[:],
                                                                    op0=ALU.mult, op1=ALU.mult),
                      r=[rs.res, gnw.res, zc.res], w=[onT.res])
                yield
            def gen_ssd(cgs):
              for cg in cgs:
                csl = slice(cg * 512, (cg + 1) * 512)
                pz = banks.next()
                k.group("pe", [(lambda kt=kt: nc.tensor.matmul(pz[:], aT[:, kt, :], wZ[:, kt, 1024 + cg * 512:1024 + (cg + 1) * 512],
                                                               start=(kt == 0), stop=(kt == 7))) for kt in range(8)],
                        r=[wZ.res, aT.res], w=[pz.res])
                zc, ez = silu_parts(pz[:], None, e5r, z5r, pz.res)
                k.ins("dve", lambda: nc.vector.reciprocal(out=ez[:], in_=ez[:]), r=[ez.res], w=[ez.res])
                k.ins("pool", lambda: nc.gpsimd.tensor_tensor(out=zc[:], in0=zc[:], in1=ez[:], op=ALU.mult),
                      r=[zc.res, ez.res], w=[zc.res])
                yy = t5r.next()
                k.ins("pool", lambda: nc.gpsimd.tensor_tensor(out=yy[:], in0=xs_[:, csl], in1=dskB[:, csl], op=ALU.mult),
                      r=[xs_.res, dskB.res], w=[yy.res])
                k.ins("dve", lambda: nc.vector.tensor_tensor(out=yy[:], in0=yy[:], in1=yt[:, csl], op=ALU.add),
                      r=[yy.res, yt.res], w=[yy.res])
                k.ins("dve", lambda: nc.vector.tensor_tensor(out=yy[:], in0=yy[:], in1=zc[:], op=ALU.mult),
                      r=[yy.res, zc.res], w=[yy.res])
                ss = ssr_.next()
                jk = t5r.next()
                k.ins("act", lambda: nc.scalar.activation(out=jk[:], in_=yy[:], func=AF.Square, accum_out=ss[:, 0:1]),
                      r=[yy.res], w=[jk.res, ss.res])
                k.ins("act", lambda: nc.scalar.activation(out=ss[:, 1:2], in_=ss[:, 0:1], func=AF.Ln, scale=1.0 / 512, bias=epsc[:, 0:1]),
                      r=[ss.res, epsc.res], w=[ss.res])
                k.ins("act", lambda: nc.scalar.activation(out=ss[:, 1:2], in_=ss[:, 1:2], func=AF.Exp, scale=-0.5),
                      r=[ss.res], w=[ss.res])
                ynb = ynr.next()
                k.ins("dve", lambda: nc.vector.scalar_tensor_tensor(out=ynb[:], in0=yy[:], scalar=ss[:, 1:2], in1=snwB[:, csl],
                                                                    op0=ALU.mult, op1=ALU.mult),
                      r=[yy.res, ss.res, snwB.res], w=[ynb.res])
                pt = ptr.next()
                k.group("pe", [(lambda j=j: nc.tensor.transpose(pt[:, j * 128:(j + 1) * 128], ynb[:, j * 128:(j + 1) * 128], identb[:]))
                               for j in range(4)], r=[ynb.res, identb.res], w=[pt.res])
                k.ins("act", lambda: nc.scalar.copy(out=ynT[:, cg * 4:(cg + 1) * 4, :],
                                                    in_=pt[:, 0:512].rearrange("p (j c) -> p j c", c=128)),
                      r=[pt.res], w=[ynT.res])
                yield
            def gen_merge(ocs):
              for oc in ocs:
                pg = banks.next()
                k.group("pe", [(lambda h=h: nc.tensor.matmul(pg[:, 0:T1], wbg[:, h, oc * 128:(oc + 1) * 128], onT[:, h, :],
                                                             start=(h == 0), stop=(h == 7))) for h in range(8)],
                        r=[wbg.res, onT.res], w=[pg.res])
                pp = banks.next()
                k.group("pe", [(lambda ct=ct: nc.tensor.matmul(pp[:, 0:T1], wbs[:, ct, oc * 128:(oc + 1) * 128], ynT[:, ct, :],
                                                               start=(ct == 0), stop=(ct == 15))) for ct in range(16)],
                        r=[wbs.res, ynT.res], w=[pp.res])
                m1 = t1r.next()
                k.ins("dve", lambda: nc.vector.tensor_tensor(out=m1[:], in0=pg[:, 0:T1], in1=sg[:, oc, :], op=ALU.mult),
                      r=[pg.res, sg.res], w=[m1.res])
                m2 = t1r.next()
                k.ins("dve", lambda: nc.vector.tensor_tensor(out=m2[:], in0=pp[:, 0:T1], in1=sg[:, 8 + oc, :], op=ALU.mult),
                      r=[pp.res, sg.res], w=[m2.res])
                k.ins("pool", lambda: nc.gpsimd.tensor_tensor(out=mT[:, oc, :], in0=m1[:], in1=m2[:], op=ALU.add),
                      r=[m1.res, m2.res], w=[mT.res])
                yield
            def gen_out():
              for cg in range(2):
                csl = slice(cg * 512, (cg + 1) * 512)
                pm = banks.next()
                k.group("pe", [(lambda kt=kt: nc.tensor.matmul(pm[:], mT[:, kt, :], wo[:, kt, csl],
                                                               start=(kt == 0), stop=(kt == 7))) for kt in range(8)],
                        r=[mT.res, wo.res], w=[pm.res])
                k.ins("dve", lambda: nc.vector.tensor_tensor(out=h1[:, csl], in0=pm[:], in1=g1B[:, csl], op=ALU.mult),
                      r=[pm.res, g1B.res], w=[h1.res])
                k.ins("pool", lambda: nc.gpsimd.tensor_tensor(out=h1[:, csl], in0=h1[:, csl], in1=xt[:, csl], op=ALU.add),
                      r=[h1.res, xt.res], w=[h1.res])
                yield
            drive_c([gen_gdn(range(0, 4)), gen_ssd(range(0, 2)), gen_gdn(range(4, 8)), gen_ssd(range(2, 4))])
            drive_c([gen_merge(range(0, 4)), gen_merge(range(4, 8))])
            drive_c([gen_out()])
            k.dma("sp", h1_d[l0:l0 + T1, :], h1[:], r=[h1.res], w=[R_h1], dres=h1.res)
        cx.pop()

    if "C2" in phases:
        cx.push()
        T2 = 256
        NF = DFF // 128
        wfi = cx.sb("wfi", [128, 8, 2 * DFF], BF16)
        wfo = cx.sb("wfo", [128, NF, 1024], BF16)
        for kt in range(8):
            for c0 in range(0, 2 * DFF, 2816):
                k.dma("pool", wfi[:, kt, c0:c0 + 2816], wfi_d[kt * 128:(kt + 1) * 128, c0:c0 + 2816], w=[wfi.res], dres=wfi.res)
        for ft in range(NF):
            k.dma("pool", wfo[:, ft, :], wfo_d[ft * 128:(ft + 1) * 128, :], w=[wfo.res], dres=wfo.res)
        g2B = cx.sb("g2B", [128, 1024], F32)
        nfB = cx.sb("nfB", [128, 1024], F32)
        k.dma("sp", g2B[:], mod_d[0, 5120:6144].partition_broadcast(128), r=[R_mod], w=[g2B.res], dres=g2B.res)
        k.dma("sp", nfB[:], nfw_d[0, :].partition_broadcast(128), w=[nfB.res], dres=nfB.res)
        n2 = cx.sb("n2", [128, 8], F32)
        k.dma("sp", n2[:], n2w_d, w=[n2.res], dres=n2.res)
        S2 = cx.sb("S2", [128, 8], F32)
        k.ins("dve", lambda: nc.vector.scalar_tensor_tensor(out=S2[:], in0=modf[:, 0, 4, :], scalar=1.0, in1=n2[:],
                                                            op0=ALU.add, op1=ALU.mult), r=[modf.res, n2.res], w=[S2.res])
        neg1d = cx.sb("neg1d", [128, T2], F32)
        k.ins("pool", lambda: nc.gpsimd.memset(neg1d[:], -1.0), w=[neg1d.res])
        banks = Ring([cx.psum(f"c2b{i}", [128, 512], F32) for i in range(8)])
        h1r = cx.sbring("d_h1", 2, [128, 2, 1024], F32)
        xnr = cx.sbring("d_xn", 1, [128, 1024], F32)
        fTr = cx.sbring("d_fT", 2, [128, 8, T2], BF16)
        actr = cx.sbring("d_act", 1, [128, NF, T2], BF16)
        outr = cx.sbring("d_out", 1, [128, 2, 1024], F32)
        ssr2 = cx.sbring("d_ss", 4, [128, 2], F32)
        junk2 = cx.sb("junk2", [128, 1024], BF16)
        e2r = cx.sbring("d_e", 3, [128, T2], F32)
        a2r = cx.sbring("d_a", 3, [128, T2], F32)
        if debug:
            print("C2 sbuf remaining", nc.sbuf_bytes_remaining)
        def c2_pre(ti):
                l0 = ti * T2
                h1 = h1r.next()
                k.dma("sp", h1[:], h1_d[l0:l0 + T2, :].rearrange("(j p) d -> p j d", p=128), r=[R_h1], w=[h1.res], dres=h1.res)
                ss = ssr2.next()
                for j in range(2):
                    k.ins("act", lambda j=j: nc.scalar.activation(out=junk2[:], in_=h1[:, j, :], func=AF.Square, accum_out=ss[:, j:j + 1]),
                          r=[h1.res], w=[junk2.res, ss.res])
                k.ins("act", lambda: nc.scalar.activation(out=ss[:], in_=ss[:], func=AF.Ln, scale=1.0 / D, bias=epsc[:, 0:1]),
                      r=[ss.res, epsc.res], w=[ss.res])
                k.ins("act", lambda: nc.scalar.activation(out=ss[:], in_=ss[:], func=AF.Exp, scale=-0.5), r=[ss.res], w=[ss.res])
                fT = fTr.next()
                for j in range(2):
                    xn = xnr.next()
                    k.ins("act", lambda j=j, xn=xn: nc.scalar.activation(out=xn[:], in_=h1[:, j, :], func=AF.Identity, scale=ss[:, j:j + 1]),
                          r=[h1.res, ss.res], w=[xn.res])
                    for half in range(2):
                        pb = banks.next()
                        k.group("pe", [(lambda q=q, xn=xn, pb=pb: nc.tensor.transpose(
                            pb[:, q * 128:(q + 1) * 128], xn[:, (half * 4 + q) * 128:(half * 4 + q + 1) * 128], ident[:]))
                            for q in range(4)], r=[xn.res, ident.res], w=[pb.res])
                        for q in range(4):
                            kt = half * 4 + q
                            k.ins("act", lambda q=q, kt=kt, pb=pb, j=j: nc.scalar.activation(
                                out=fT[:, kt, j * 128:(j + 1) * 128], in_=pb[:, q * 128:(q + 1) * 128], func=AF.Identity,
                                scale=S2[:, kt:kt + 1], bias=modf[:, 0, 3, kt:kt + 1]),
                                r=[pb.res, S2.res, modf.res], w=[fT.res])
                return h1, fT

        nxt2 = c2_pre(0)
        for ti in range(NLAT // T2):
            l0 = ti * T2
            h1, fT = nxt2
            act = actr.next()
            for ft in range(NF):
                if ft == 10 and ti + 1 < NLAT // T2:
                    nxt2 = c2_pre(ti + 1)
                pgt = banks.next()
                k.group("pe", [(lambda kt=kt: nc.tensor.matmul(pgt[:, 0:T2], wfi[:, kt, ft * 128:(ft + 1) * 128], fT[:, kt, :],
                                                               start=(kt == 0), stop=(kt == 7))) for kt in range(8)],
                        r=[wfi.res, fT.res], w=[pgt.res])
                pup = banks.next()
                k.group("pe", [(lambda kt=kt: nc.tensor.matmul(pup[:, 0:T2], wfi[:, kt, DFF + ft * 128:DFF + (ft + 1) * 128], fT[:, kt, :],
                                                               start=(kt == 0), stop=(kt == 7))) for kt in range(8)],
                        r=[wfi.res, fT.res], w=[pup.res])
                ez = e2r.next()
                k.ins("act", lambda: nc.scalar.activation(out=ez[:], in_=pgt[:, 0:T2], func=AF.Exp, scale=-1.0), r=[pgt.res], w=[ez.res])
                k.ins("dve", lambda: nc.vector.tensor_scalar(out=ez[:], in0=ez[:], scalar1=1.0, scalar2=None, op0=ALU.add),
                      r=[ez.res], w=[ez.res])
                k.ins("dve", lambda: nc.vector.reciprocal(out=ez[:], in_=ez[:]), r=[ez.res], w=[ez.res])
                a1 = a2r.next()
                k.ins("dve", lambda: nc.vector.tensor_tensor(out=a1[:], in0=pgt[:, 0:T2], in1=ez[:], op=ALU.mult),
                      r=[pgt.res, ez.res], w=[a1.res])
                k.ins("dve", lambda: nc.vector.tensor_tensor(out=act[:, ft, :], in0=pup[:, 0:T2], in1=a1[:], op=ALU.mult),
                      r=[pup.res, a1.res], w=[act.res])
            ot = outr.next()
            ss2 = ssr2.next()
            for j in range(2):
                for cg in range(2):
                    csl = slice(cg * 512, (cg + 1) * 512)
                    pf = banks.next()
                    k.group("pe", [(lambda ft=ft: nc.tensor.matmul(pf[:], act[:, ft, j * 128:(j + 1) * 128], wfo[:, ft, csl],
                                                                   start=(ft == 0), stop=(ft == NF - 1))) for ft in range(NF)],
                            r=[act.res, wfo.res], w=[pf.res])
                    k.ins("dve", lambda: nc.vector.tensor_tensor(out=ot[:, j, csl], in0=pf[:], in1=g2B[:, csl], op=ALU.mult),
                          r=[pf.res, g2B.res], w=[ot.res])
                    k.ins("pool", lambda: nc.gpsimd.tensor_tensor(out=ot[:, j, csl], in0=ot[:, j, csl], in1=h1[:, j, csl], op=ALU.add),
                          r=[ot.res, h1.res], w=[ot.res])
                k.ins("act", lambda j=j: nc.scalar.activation(out=junk2[:], in_=ot[:, j, :], func=AF.Square, accum_out=ss2[:, j:j + 1]),
                      r=[ot.res], w=[junk2.res, ss2.res])
            k.ins("act", lambda: nc.scalar.activation(out=ss2[:], in_=ss2[:], func=AF.Ln, scale=1.0 / D, bias=epsc[:, 0:1]),
                  r=[ss2.res, epsc.res], w=[ss2.res])
            k.ins("act", lambda: nc.scalar.activation(out=ss2[:], in_=ss2[:], func=AF.Exp, scale=-0.5), r=[ss2.res], w=[ss2.res])
            for j in range(2):
                k.ins("dve", lambda j=j: nc.vector.scalar_tensor_tensor(out=ot[:, j, :], in0=ot[:, j, :], scalar=ss2[:, j:j + 1],
                                                                        in1=nfB[:], op0=ALU.mult, op1=ALU.mult),
                      r=[ot.res, ss2.res, nfB.res], w=[ot.res])
            k.dma("sp", out_d[l0:l0 + T2, :].rearrange("(j p) d -> p j d", p=128), ot[:], r=[ot.res], w=[R_out], dres=ot.res)
        cx.pop()

    k.wait_all("sp", [R_scr, R_mod, R_oT, R_st1, R_st2, R_y, R_h1, R_out] + R_o1 + R_y1)
    return nc


def _consts():
    p = np.arange(128)[:, None]
    f = np.arange(128)[None, :]
    U = (p <= f).astype(np.float32)
    Lo = (p >= f).astype(np.float32)
    tri = np.stack([U, Lo, -U, -Lo], axis=1)
    NEG = -30000.0
    m = lambda ok: np.where(ok, 0.0, NEG).astype(np.float32)
    msk = np.stack([m(p > f), m(p < f), m(p >= f), m(p <= f)], axis=1)
    return np.ascontiguousarray(tri), np.ascontiguousarray(msk)


TRI, MSK = _consts()


def host_prepare(inputs, core):
    b, s = core // 2, core % 2
    x = inputs["x"][b, s * NLAT:(s + 1) * NLAT]
    ctx = inputs["ctx"][b]
    if s == 1:
        x = x[::-1]
        ctx = ctx[::-1]
    xin = np.ascontiguousarray(np.concatenate([ctx, x], axis=0))
    x2 = inputs["x"][b, (1 - s) * NLAT:(2 - s) * NLAT]
    ctx2 = inputs["ctx"][b]
    if s == 0:
        x2 = x2[::-1]
        ctx2 = ctx2[::-1]
    xin2 = np.ascontiguousarray(np.concatenate([ctx2, x2], axis=0))
    cv = np.stack([inputs["c"][b], inputs["c_ctx"]], axis=-1)
    cvec = np.ascontiguousarray(cv.reshape(8, 128, 2).transpose(1, 0, 2))
    d1, d2 = (0, 1) if s == 0 else (1, 0)
    w = inputs["w_in"][0]
    offs = np.cumsum([0, 3072, 1024, 16, 16, 2048, 3072, 64, 2048])
    qkv = w[:, offs[0]:offs[1]]
    zg = w[:, offs[1]:offs[2]]
    a_ = w[:, offs[2]:offs[3]].reshape(D, 2, 8)
    b_ = w[:, offs[3]:offs[4]].reshape(D, 2, 8)
    zs = w[:, offs[4]:offs[5]]
    xbc = w[:, offs[5]:offs[6]]
    dt_ = w[:, offs[6]:offs[7]].reshape(D, 2, 32)
    gate = w[:, offs[7]:offs[8]]
    z16 = np.zeros((D, 16), np.float32)
    small = np.concatenate([a_[:, d1], a_[:, d2], z16, b_[:, d1], b_[:, d2], z16, dt_[:, d1], dt_[:, d2]], axis=1)
    w_perm = np.ascontiguousarray(np.concatenate([qkv, xbc, gate, small, zg, zs], axis=1))
    assert w_perm.shape[1] == W_IN_COLS
    cw = np.concatenate([inputs["gdn_conv_w"][0], inputs["ssm_conv_w"][0]], axis=1)
    cbias = np.concatenate([inputs["gdn_conv_b"][0], inputs["ssm_conv_b"][0]], axis=0)
    mk = lambda w_: np.ascontiguousarray(np.concatenate([w_, cbias[None]], axis=0).reshape(4, 48, 128).transpose(2, 1, 0))
    convp = mk(cw[::-1] if s == 1 else cw)
    convp2 = mk(cw[::-1] if s == 0 else cw)
    smallp = np.zeros((128, 4), np.float32)
    smallp[:, 0] = 1.0
    smallp[32:64, 0] = -1.0
    gb = inputs["gdn_dt_bias"][0]
    sbias = inputs["ssm_dt_bias"][0]
    smallp[0:8, 1] = gb[d1]; smallp[8:16, 1] = gb[d2]
    smallp[64:96, 1] = sbias[d1]; smallp[96:128, 1] = sbias[d2]
    ga = inputs["gdn_a_log"][0]
    smallp[0:8, 2] = ga[d1]; smallp[8:16, 2] = ga[d2]
    smallp[:, 3] = -1.0
    smallp[64:128, 3] = 1.0
    n1w = np.ascontiguousarray(inputs["norm1_w"][0].reshape(8, 128).T)
    return {
        "xin": xin, "xin2": xin2, "convp2": convp2, "cvec": cvec, "ada_w": np.ascontiguousarray(inputs["ada_w"][0]),
        "ada_b": np.ascontiguousarray(inputs["ada_b"][0][None]), "w_in": w_perm, "convp": convp,
        "smallp": smallp, "n1w": n1w, "ident": np.eye(128, dtype=np.float32),
        "tri": TRI, "msk": MSK,
        "w_brg": np.ascontiguousarray(inputs["w_br_gdn"][0]), "w_brs": np.ascontiguousarray(inputs["w_br_ssm"][0]),
        "w_o": np.ascontiguousarray(inputs["w_out"][0]), "w_fi": np.ascontiguousarray(inputs["w_ffn_in"][0]),
        "w_fo": np.ascontiguousarray(inputs["w_ffn_out"][0]),
        "gnw": np.ascontiguousarray(inputs["gdn_norm_w"][0].reshape(128, 1)),
        "dskip": np.ascontiguousarray(np.repeat(inputs["ssm_d"][0], 64)[None]),
        "snw": np.ascontiguousarray(inputs["ssm_norm_w"][0][None]),
        "n2w": np.ascontiguousarray(inputs["norm2_w"][0].reshape(8, 128).T),
        "nfw": np.ascontiguousarray(inputs["norm_f_w"][None]),
        "salog": np.ascontiguousarray(np.broadcast_to(inputs["ssm_a_log"][0][[d1, d2]][None], (128, 2, 32))),
    }


def kernel(**inputs):
    inputs = {k_: np.asarray(v) for k_, v in inputs.items()}
    nc = build_program()
    in_maps = [host_prepare(inputs, c) for c in range(8)]
    res = run_bass_kernel_spmd(nc, in_maps, core_ids=list(range(8)))
    out = np.zeros((4, 8192, D), np.float32)
    for c in range(8):
        b, s = c // 2, c % 2
        o = res.results[c]["out"]
        if s == 1:
            o = o[::-1]
        out[b, s * NLAT:(s + 1) * NLAT] = o
    return out
```

```python
import numpy as np
import ml_dtypes
from contextlib import ExitStack
import concourse.bass as bass
import concourse.mybir as mybir
from concourse.bass_utils import run_bass_kernel_spmd

F32 = mybir.dt.float32
BF16 = mybir.dt.bfloat16
AF = mybir.ActivationFunctionType
ALU = mybir.AluOpType
AX = mybir.AxisListType

D = 1024
NLAT = 4096
NCTX = 256
NTOK = NLAT + NCTX
C = 128
NCH = NTOK // C
EPS = 1e-6
DFF = 2816
O_QKV, O_XBC, O_GATE, O_SMALL, O_ZG, O_ZS = 0, 3072, 6144, 8192, 8320, 9344
W_IN_COLS = 11392
SEM_LIMIT = 30000
GDN_LIMIT = 0


class Sem:
    __slots__ = ("h", "count", "dma", "name")

    def __init__(self, h, dma, name):
        self.h, self.count, self.dma, self.name = h, 0, dma, name


class Res:
    __slots__ = ("name", "w", "r", "dsem")

    def __init__(self, name):
        self.name, self.w, self.r, self.dsem = name, None, {}, None


class Eng:
    def __init__(self, name, e, same_wait):
        self.name, self.e, self.sem, self.waited, self.same_wait = name, e, None, {}, same_wait
        self.nsem = 0


class K:
    def __init__(self, nc):
        self.nc = nc
        self.eng = {
            "pe": Eng("pe", nc.tensor, False),
            "act": Eng("act", nc.scalar, True),
            "dve": Eng("dve", nc.vector, True),
            "pool": Eng("pool", nc.gpsimd, True),
            "sp": Eng("sp", nc.sync, False),
        }
        self.nsem = 0
        self.ninst = 0
        self.all_sems = []
        self.free_dma = []

    def new_sem(self, dma, name):
        if dma and self.free_dma:
            self.free_dma.sort(key=lambda x: x.count)
            return self.free_dma.pop(0)
        self.nsem += 1
        h = self.nc.alloc_semaphore(f"s{self.nsem}_{name}")
        sm = Sem(h, dma, name)
        self.all_sems.append(sm)
        return sm

    def res(self, name):
        return Res(name)

    def _cur_sem(self, E):
        if E.sem is None or E.sem.count >= SEM_LIMIT:
            E.nsem += 1
            E.sem = self.new_sem(False, f"{E.name}{E.nsem}")
        return E.sem

    def _wait(self, E, evs):
        need = {}
        for (sem, val) in evs:
            if sem.dma:
                val = sem.count
            if need.get(sem, 0) < val:
                need[sem] = val
        for sem, val in need.items():
            if (not E.same_wait) and (sem is E.sem) and not sem.dma:
                continue
            if E.waited.get(sem, 0) >= val:
                continue
            E.e.wait_ge(sem.h, val)
            E.waited[sem] = val

    def _deps(self, r, w):
        evs = []
        for x in r:
            if x.w is not None:
                evs.append(x.w)
        for x in w:
            if x.w is not None:
                evs.append(x.w)
            evs.extend(x.r.items())
        return evs

    def _record(self, ev, r, w):
        sem, val = ev
        for x in r:
            if x.r.get(sem, 0) < val:
                x.r[sem] = val
        for x in w:
            x.w = ev
            x.r = {}

    def ins(self, en, fn, r=(), w=()):
        E = self.eng[en]
        self._wait(E, self._deps(r, w))
        inst = fn()
        sem = self._cur_sem(E)
        sem.count += 1
        inst.then_inc(sem.h, 1)
        self._record((sem, sem.count), r, w)
        self.ninst += 1
        return inst

    def group(self, en, fns, r=(), w=()):
        E = self.eng[en]
        self._wait(E, self._deps(r, w))
        inst = None
        for fn in fns:
            inst = fn()
        sem = self._cur_sem(E)
        sem.count += 1
        inst.then_inc(sem.h, 1)
        self._record((sem, sem.count), r, w)
        self.ninst += len(fns)
        return inst

    def dma(self, q, out, in_, r=(), w=(), dres=None):
        E = self.eng[q]
        self._wait(E, self._deps(r, w))
        if dres.dsem is None:
            dres.dsem = self.new_sem(True, "d" + dres.name)
        sem = dres.dsem
        inst = E.e.dma_start(out=out, in_=in_)
        sem.count += 16
        inst.then_inc(sem.h, 16)
        self._record((sem, sem.count), r, w)
        self.ninst += 1
        return inst

    def dmaop(self, q, fn, r=(), w=(), dres=None):
        E = self.eng[q]
        self._wait(E, self._deps(r, w))
        if dres.dsem is None:
            dres.dsem = self.new_sem(True, "d" + dres.name)
        sem = dres.dsem
        inst = fn()
        sem.count += 16
        inst.then_inc(sem.h, 16)
        self._record((sem, sem.count), r, w)
        self.ninst += 1
        return inst

    def barrier(self):
        sems = list(self.all_sems)
        for E in self.eng.values():
            for sem in sems:
                if sem.count == 0 or E.waited.get(sem, 0) >= sem.count:
                    continue
                if (sem is E.sem) and not E.same_wait:
                    continue
                E.e.wait_ge(sem.h, sem.count)
                E.waited[sem] = sem.count

    def wait_all(self, en, ress):
        E = self.eng[en]
        evs = []
        for x in ress:
            if x.w is not None:
                evs.append(x.w)
            evs.extend(x.r.items())
        self._wait(E, evs)


class Buf:
    def __init__(self, k, t, name):
        self.t, self.res, self.name = t, k.res(name), name

    def __getitem__(self, idx):
        return self.t[idx]


class Ring:
    def __init__(self, bufs):
        self.bufs, self.i = bufs, 0

    def next(self):
        b = self.bufs[self.i % len(self.bufs)]
        self.i += 1
        return b


class Ctx:
    def __init__(self, nc):
        self.nc = nc
        self.k = K(nc)
        self.stack = [ExitStack()]
        self.uid = 0
        self.phase_bufs = [[]]

    def push(self):
        self.stack.append(ExitStack())
        self.phase_bufs.append([])

    def pop(self):
        self.k.barrier()
        for b in self.phase_bufs.pop():
            if b.res.dsem is not None:
                self.k.free_dma.append(b.res.dsem)
                b.res.dsem = None
        self.stack.pop().close()

    def sb(self, name, shape, dtype):
        self.uid += 1
        t = self.stack[-1].enter_context(self.nc.sbuf_tensor(f"sb{self.uid}_{name}", list(shape), dtype))
        b = Buf(self.k, t, name)
        self.phase_bufs[-1].append(b)
        return b

    def psum(self, name, shape, dtype):
        self.uid += 1
        t = self.stack[-1].enter_context(self.nc.psum_tensor(f"ps{self.uid}_{name}", list(shape), dtype))
        return Buf(self.k, t, name)

    def sbring(self, name, n, shape, dtype):
        return Ring([self.sb(f"{name}{i}", shape, dtype) for i in range(n)])

    def dram(self, name, shape, dtype, kind="Internal"):
        t = self.nc.dram_tensor(name, list(shape), dtype, kind=kind)
        return t.ap()


def build_program(debug=False, phases=("0", "A", "R", "B1", "S1", "B2", "S2", "C1", "C2"), n_cores=8):
    nc = bass.Bass("TRN2", target_bir_lowering=False)
    cx = Ctx(nc)
    k = cx.k
    dk = "ExternalOutput" if debug else "Internal"

    xin = cx.dram("xin", [NTOK, D], F32, "ExternalInput")
    cvec = cx.dram("cvec", [128, 8, 2], F32, "ExternalInput")
    ada_w = cx.dram("ada_w", [D, 6 * D], F32, "ExternalInput")
    ada_b = cx.dram("ada_b", [1, 6 * D], F32, "ExternalInput")
    w_in = cx.dram("w_in", [D, W_IN_COLS], F32, "ExternalInput")
    xin2 = cx.dram("xin2", [NTOK, D], F32, "ExternalInput")
    convp2 = cx.dram("convp2", [128, 48, 4], F32, "ExternalInput")
    convp = cx.dram("convp", [128, 48, 4], F32, "ExternalInput")
    smallp = cx.dram("smallp", [128, 4], F32, "ExternalInput")
    n1w = cx.dram("n1w", [128, 8], F32, "ExternalInput")
    ident_d = cx.dram("ident", [128, 128], F32, "ExternalInput")
    wbg_d = cx.dram("w_brg", [1024, 1024], F32, "ExternalInput")
    wbs_d = cx.dram("w_brs", [2048, 1024], F32, "ExternalInput")
    wo_d = cx.dram("w_o", [1024, 1024], F32, "ExternalInput")
    wfi_d = cx.dram("w_fi", [1024, 2 * DFF], F32, "ExternalInput")
    wfo_d = cx.dram("w_fo", [DFF, 1024], F32, "ExternalInput")
    gnw_d = cx.dram("gnw", [128, 1], F32, "ExternalInput")
    dskip_d = cx.dram("dskip", [1, 2048], F32, "ExternalInput")
    snw_d = cx.dram("snw", [1, 2048], F32, "ExternalInput")
    n2w_d = cx.dram("n2w", [128, 8], F32, "ExternalInput")
    nfw_d = cx.dram("nfw", [1, 1024], F32, "ExternalInput")
    salog_d = cx.dram("salog", [128, 2, 32], F32, "ExternalInput")
    tri_d = cx.dram("tri", [128, 4, 128], F32, "ExternalInput")
    msk_d = cx.dram("msk", [128, 4, 128], F32, "ExternalInput")
    out_d = cx.dram("out", [NLAT, D], F32, "ExternalOutput")

    mod_d = cx.dram("mod_d", [2, 6 * D], F32, dk)
    qT_d = cx.dram("qT_d", [8, 128, NTOK], BF16, dk)
    kT_d = cx.dram("kT_d", [8, 128, NTOK], BF16, dk)
    k_d = cx.dram("k_d", [NTOK, 1024], BF16, dk)
    v_d = cx.dram("v_d", [NTOK, 1024], BF16, dk)
    x_d = cx.dram("x_d", [NTOK, 2048], BF16, dk)
    BT_d = cx.dram("BT_d", [4, 128, NTOK], BF16, dk)
    CT_d = cx.dram("CT_d", [4, 128, NTOK], BF16, dk)
    B_d = cx.dram("B_d", [NTOK, 512], BF16, dk)
    sm_d = cx.dram("sm_d", [NTOK, 128], F32, dk)
    kT2_d = cx.dram("kT2_d", [8, 128, NTOK], BF16)
    k2_d = cx.dram("k2_d", [NTOK, 1024], BF16)
    v2_d = cx.dram("v2_d", [NTOK, 1024], BF16)
    x2_d = cx.dram("x2_d", [NTOK, 2048], BF16)
    B2_d = cx.dram("B2_d", [NTOK, 512], BF16)
    sm2_d = cx.dram("sm2_d", [NTOK, 128], F32)
    sgT_d = cx.dram("sgT_d", [16, 128, NLAT], F32, dk)
    aT_d = cx.dram("aT_d", [8, 128, NLAT], BF16, dk)
    o1_d = cx.dram("o1_d", [8, 128, NLAT], F32, dk)
    oT_d = cx.dram("oT_d", [8, 128, NLAT], F32, dk)
    st1_d = cx.dram("st1_d", [128, 3072], F32)
    st2_d = cx.dram("st2_d", [128, 3072], F32)
    y1_d = cx.dram("y1_d", [NLAT, 2048], F32, dk)
    y_d = cx.dram("y_d", [NLAT, 2048], F32, dk)
    R_y1 = [k.res(f"y1_{c}") for c in range(NCH)]
    R_y = k.res("y")
    R_o1 = [k.res(f"o1_{c}") for c in range(NCH)]
    R_oT = k.res("oT")
    R_st1 = k.res("st1")
    R_st2 = k.res("st2")
    h1_d = cx.dram("h1_d", [NLAT, D], F32, dk)
    R_h1 = k.res("h1")
    R_out = k.res("out")
    R_scr = k.res("scratchA")
    R_mod = k.res("mod_d")

    ident = cx.sb("ident", [128, 128], F32)
    identb = cx.sb("identb", [128, 128], BF16)
    onesf = cx.sb("onesf", [128, 128], F32)
    k.dma("sp", ident[:], ident_d, w=[ident.res], dres=ident.res)
    k.ins("dve", lambda: nc.vector.tensor_copy(identb[:], ident[:]), r=[ident.res], w=[identb.res])
    k.ins("dve", lambda: nc.vector.memset(onesf[:], 1.0), w=[onesf.res])
    epsc = cx.sb("epsc", [128, 1], F32)
    k.ins("dve", lambda: nc.vector.memset(epsc[:], EPS), w=[epsc.res])


    if "0" in phases:
        cx.push()
        ps = [cx.psum(f"ps{i}", [128, 512], F32) for i in range(2)]
        cv = cx.sb("cv", [128, 8, 2], F32)
        cvs = cx.sb("cvs", [128, 8, 2], F32)
        k.dma("sp", cv[:], cvec, w=[cv.res], dres=cv.res)
        k.ins("act", lambda: nc.scalar.activation(out=cvs[:], in_=cv[:], func=AF.Exp, scale=-1.0), r=[cv.res], w=[cvs.res])
        k.ins("dve", lambda: nc.vector.tensor_scalar(out=cvs[:], in0=cvs[:], scalar1=1.0, scalar2=None, op0=ALU.add),
              r=[cvs.res], w=[cvs.res])
        k.ins("dve", lambda: nc.vector.reciprocal(out=cvs[:], in_=cvs[:]), r=[cvs.res], w=[cvs.res])
        k.ins("dve", lambda: nc.vector.tensor_tensor(out=cvs[:], in0=cvs[:], in1=cv[:], op=ALU.mult),
              r=[cvs.res, cv.res], w=[cvs.res])
        adab = cx.sb("adab", [2, 6 * D], F32)
        k.dma("sp", adab[0:1, :], ada_b, w=[adab.res], dres=adab.res)
        k.dma("sp", adab[1:2, :], ada_b, w=[adab.res], dres=adab.res)
        modsb = cx.sb("modsb", [2, 6 * D], F32)
        awring = cx.sbring("aw", 2, [128, 8, 512], F32)
        for cg in range(12):
            aw = awring.next()
            k.dma("sp", aw[:], ada_w[:, cg * 512:(cg + 1) * 512].rearrange("(kt p) n -> p kt n", p=128),
                  w=[aw.res], dres=aw.res)
            pb = ps[cg % 2]
            k.group("pe", [
                (lambda kt=kt, aw=aw, pb=pb: nc.tensor.matmul(pb[0:2, :], cvs[:, kt, :], aw[:, kt, :],
                                                               start=(kt == 0), stop=(kt == 7)))
                for kt in range(8)], r=[cvs.res, aw.res], w=[pb.res])
            k.ins("dve", lambda cg=cg, pb=pb: nc.vector.tensor_tensor(
                out=modsb[:, cg * 512:(cg + 1) * 512], in0=pb[0:2, :], in1=adab[:, cg * 512:(cg + 1) * 512],
                op=ALU.add), r=[pb.res, adab.res], w=[modsb.res])
        k.dma("sp", mod_d, modsb[:], r=[modsb.res], w=[R_mod], dres=modsb.res)
        cx.pop()

    modf = cx.sb("modf", [128, 2, 6, 8], F32)
    with nc.allow_non_contiguous_dma("small modulation vector relayout"):
        for r_ in range(2):
            k.dma("sp", modf[:, r_, :, :], mod_d[r_, :].rearrange("(j kt p) -> p j kt", p=128, kt=8),
                  r=[R_mod], w=[modf.res], dres=modf.res)
    n1 = cx.sb("n1", [128, 8], F32)
    k.dma("sp", n1[:], n1w, w=[n1.res], dres=n1.res)
    S1 = cx.sb("S1", [128, 2, 8], F32)
    for r_ in range(2):
        k.ins("dve", lambda r_=r_: nc.vector.scalar_tensor_tensor(
            out=S1[:, r_, :], in0=modf[:, r_, 1, :], scalar=1.0, in1=n1[:], op0=ALU.add, op1=ALU.mult),
            r=[modf.res, n1.res], w=[S1.res])

    if "A" in phases:
        cx.push()
        ps = [cx.psum(f"ps{i}", [128, 512], F32) for i in range(6)]
        NWC = O_ZG
        wA = cx.sb("wA", [128, 8, NWC], BF16)
        for kt in range(8):
            for c0 in range(0, NWC, 2080):
                k.dma("pool", wA[:, kt, c0:c0 + 2080], w_in[kt * 128:(kt + 1) * 128, c0:c0 + 2080],
                      w=[wA.res], dres=wA.res)
        cp = cx.sb("cp", [128, 48, 4], F32)
        k.dma("sp", cp[:], convp, w=[cp.res], dres=cp.res)
        smp = cx.sb("smp", [128, 4], F32)
        k.dma("sp", smp[:], smallp, w=[smp.res], dres=smp.res)
        smult = cx.sb("smult", [128, 1], F32)
        k.ins("act", lambda: nc.scalar.activation(out=smult[:], in_=smp[:, 2:3], func=AF.Exp),
              r=[smp.res], w=[smult.res])
        k.ins("dve", lambda: nc.vector.tensor_tensor(out=smult[:], in0=smult[:], in1=smp[:, 3:4], op=ALU.mult),
              r=[smp.res, smult.res], w=[smult.res])

        TT = 256
        xring = cx.sbring("xt", 1, [128, 2, D], F32)
        xnring = cx.sbring("xn", 1, [128, D], F32)
        junk = cx.sb("junk", [128, D], BF16)
        ssr = cx.sbring("ss", 2, [128, 2], F32)
        aTring = cx.sbring("aT", 2, [128, 8, TT], BF16)
        cring = cx.sbring("cv_", 6, [128, TT], F32)
        ering = cx.sbring("ee_", 4, [128, TT], F32)
        sfring = cx.sbring("sf_", 5, [128, TT], F32)
        sqring = cx.sbring("sq_", 3, [128, TT], F32)
        rsring = cx.sbring("rs_", 3, [128, TT], F32)
        sring = cx.sbring("so_", 8, [128, TT], BF16)
        sgring = cx.sbring("sg_", 4, [128, TT], F32)
        smring = cx.sbring("smf", 2, [128, TT], F32)
        ktok = cx.sbring("ktok", 1, [128, 2, 1024], BF16)
        vtok = cx.sbring("vtok", 1, [128, 2, 1024], BF16)
        xtok = cx.sbring("xtok", 1, [128, 2, 2048], BF16)
        btok = cx.sbring("btok", 1, [128, 2, 512], BF16)
        smtok = cx.sbring("smtok", 1, [128, 2, 128], F32)
        pst = ps[0]
        psa = Ring([ps[1], ps[2], ps[3], ps[4]])
        pss = ps[5]
        pso = Ring([cx.psum(f"pso{i}", [128, 1024], BF16) for i in range(2)])
        own = dict(qT_d=qT_d, kT_d=kT_d, k_d=k_d, v_d=v_d, x_d=x_d, BT_d=BT_d, CT_d=CT_d, B_d=B_d, sm_d=sm_d)
        par = dict(qT_d=None, kT_d=kT2_d, k_d=k2_d, v_d=v2_d, x_d=x2_d, BT_d=None, CT_d=None, B_d=B2_d, sm_d=sm2_d)
        cp2 = cx.sb("cp2", [128, 48, 4], F32)
        k.dma("sp", cp2[:], convp2, w=[cp2.res], dres=cp2.res)
        runs = [(False, xin, cp, own)]
        if "R" in phases:
            runs.append((True, xin2, cp2, par))

        def preamble(red, xsrc, ti):
            t0 = ti * TT
            is_ctx = ti == 0
            mr = 1 if is_ctx else 0
            xt = xring.next()
            k.dma("sp", xt[:], xsrc[t0:t0 + TT, :].rearrange("(j p) d -> p j d", p=128), w=[xt.res], dres=xt.res)
            ss = ssr.next()
            for j in range(2):
                k.ins("act", lambda j=j: nc.scalar.activation(out=junk[:], in_=xt[:, j, :], func=AF.Square,
                                                              accum_out=ss[:, j:j + 1]), r=[xt.res], w=[junk.res, ss.res])
            k.ins("act", lambda: nc.scalar.activation(out=ss[:], in_=ss[:], func=AF.Ln, scale=1.0 / D, bias=epsc[:, 0:1]),
                  r=[ss.res, epsc.res], w=[ss.res])
            k.ins("act", lambda: nc.scalar.activation(out=ss[:], in_=ss[:], func=AF.Exp, scale=-0.5), r=[ss.res], w=[ss.res])
            aT = aTring.next()
            for j in range(2):
                xn = xnring.next()
                k.ins("act", lambda j=j, xn=xn: nc.scalar.activation(out=xn[:], in_=xt[:, j, :], func=AF.Identity,
                                                                     scale=ss[:, j:j + 1]), r=[xt.res, ss.res], w=[xn.res])
                for half in range(2):
                    k.group("pe", [(lambda q=q, xn=xn, half=half: nc.tensor.transpose(
                        pst[:, q * 128:(q + 1) * 128], xn[:, (half * 4 + q) * 128:(half * 4 + q + 1) * 128], ident[:]))
                        for q in range(4)], r=[xn.res, ident.res], w=[pst.res])
                    for q in range(4):
                        kt = half * 4 + q
                        k.ins("act", lambda q=q, kt=kt, j=j: nc.scalar.activation(
                            out=aT[:, kt, j * 128:(j + 1) * 128], in_=pst[:, q * 128:(q + 1) * 128],
                            func=AF.Identity, scale=S1[:, mr, kt:kt + 1], bias=modf[:, mr, 0, kt:kt + 1]),
                            r=[pst.res, S1.res, modf.res], w=[aT.res])
            if not is_ctx and not red:
                l0 = t0 - NCTX
                k.dma("sp", aT_d[:, :, l0:l0 + TT].rearrange("kt p t -> p kt t"), aT[:], r=[aT.res], w=[R_scr], dres=aT.res)
            return dict(aT=aT, t0=t0, is_ctx=is_ctx, red=red)

        def make_job(tc, ct, cpt, DD, toks):
            aT, t0, is_ctx, red = tc["aT"], tc["t0"], tc["is_ctx"], tc["red"]
            l0 = t0 - NCTX
            rowlen = 256 if is_ctx else 64
            J = {}
            steps = []

            def s_mm():
                J["pa"] = pa = psa.next()
                k.group("pe", [(lambda kt=kt: nc.tensor.matmul(pa[:, 0:TT], wA[:, kt, ct * 128:(ct + 1) * 128], aT[:, kt, :],
                                                               start=(kt == 0), stop=(kt == 7))) for kt in range(8)],
                        r=[wA.res, aT.res], w=[pa.res])
            steps.append(s_mm)
            if ct < 48:
                def s_ident():
                    pa = J["pa"]
                    J["cb"] = cb = cring.next()
                    k.ins("dve", lambda: nc.vector.tensor_scalar(out=cb[:], in0=pa[:, 0:TT], scalar1=cpt[:, ct, 1:2],
                                                              scalar2=cpt[:, ct, 3:4], op0=ALU.mult, op1=ALU.add),
                          r=[pa.res, cpt.res], w=[cb.res])

                def s_taps():
                    pa, cb = J["pa"], J["cb"]
                    pv = pa[:, 0:TT].rearrange("p (r t) -> p r t", t=rowlen)
                    cv3 = cb[:].rearrange("p (r t) -> p r t", t=rowlen)
                    k.ins("dve", lambda: nc.vector.scalar_tensor_tensor(
                        out=cv3[:, :, 1:], in0=pv[:, :, 0:rowlen - 1], scalar=cpt[:, ct, 0:1], in1=cv3[:, :, 1:],
                        op0=ALU.mult, op1=ALU.add), r=[pa.res, cpt.res, cb.res], w=[cb.res])
                    k.ins("dve", lambda: nc.vector.scalar_tensor_tensor(
                        out=cv3[:, :, 0:rowlen - 1], in0=pv[:, :, 1:], scalar=cpt[:, ct, 2:3], in1=cv3[:, :, 0:rowlen - 1],
                        op0=ALU.mult, op1=ALU.add), r=[pa.res, cpt.res, cb.res], w=[cb.res])

                def s_exp():
                    cb = J["cb"]
                    J["ee"] = ee = ering.next()
                    k.ins("act", lambda: nc.scalar.activation(out=ee[:], in_=cb[:], func=AF.Exp, scale=-1.0), r=[cb.res], w=[ee.res])

                def s_recip():
                    ee = J["ee"]
                    k.ins("act", lambda: nc.scalar.activation(out=ee[:], in_=ee[:], func=AF.Ln, bias=1.0), r=[ee.res], w=[ee.res])

                def s_recip2():
                    ee = J["ee"]
                    k.ins("act", lambda: nc.scalar.activation(out=ee[:], in_=ee[:], func=AF.Exp, scale=-1.0), r=[ee.res], w=[ee.res])

                def s_mult():
                    cb, ee = J["cb"], J["ee"]
                    if ct < 16:
                        J["sf"] = sf = sfring.next()
                        k.ins("pool", lambda: nc.gpsimd.tensor_tensor(out=sf[:], in0=cb[:], in1=ee[:], op=ALU.mult),
                              r=[cb.res, ee.res], w=[sf.res])
                    else:
                        J["so"] = so = sring.next()
                        k.ins("pool", lambda: nc.gpsimd.tensor_tensor(out=so[:], in0=cb[:], in1=ee[:], op=ALU.mult),
                              r=[cb.res, ee.res], w=[so.res])
                steps.extend([s_ident, s_taps, s_exp, s_recip, s_recip2, s_mult])
                if ct < 16:
                    def s_sq():
                        sf = J["sf"]
                        J["sq"] = sq = sqring.next()
                        k.ins("pool", lambda: nc.gpsimd.tensor_tensor(out=sq[:], in0=sf[:], in1=sf[:], op=ALU.mult),
                              r=[sf.res], w=[sq.res])

                    def s_sum():
                        sq = J["sq"]
                        k.ins("pe", lambda: nc.tensor.matmul(pss[:, 0:TT], onesf[:], sq[:], start=True, stop=True),
                              r=[onesf.res, sq.res], w=[pss.res])
                        J["rs"] = rs = rsring.next()
                        k.ins("act", lambda: nc.scalar.activation(out=rs[:], in_=pss[:, 0:TT], func=AF.Ln, bias=epsc[:, 0:1]),
                              r=[pss.res, epsc.res], w=[rs.res])

                    def s_rs():
                        rs = J["rs"]
                        k.ins("act", lambda: nc.scalar.activation(out=rs[:], in_=rs[:], func=AF.Exp, scale=-0.5),
                              r=[rs.res], w=[rs.res])

                    def s_norm():
                        sf, rs = J["sf"], J["rs"]
                        J["so"] = so = sring.next()
                        qscale = (128 ** -0.5) if ct < 8 else 1.0
                        k.ins("dve", lambda: nc.vector.scalar_tensor_tensor(out=so[:], in0=sf[:], scalar=qscale, in1=rs[:],
                                                                            op0=ALU.mult, op1=ALU.mult),
                              r=[sf.res, rs.res], w=[so.res])
                    steps.extend([s_sq, s_sum, s_rs, s_norm])

                def s_out():
                    so = J["so"]
                    if ct < 8:
                        k.dma("sp", DD["qT_d"][ct, :, t0:t0 + TT], so[:], r=[so.res], w=[R_scr], dres=so.res)
                    elif ct < 16:
                        k.dma("sp", DD["kT_d"][ct - 8, :, t0:t0 + TT], so[:], r=[so.res], w=[R_scr], dres=so.res)
                    elif 40 <= ct < 44 and not red:
                        k.dma("sp", DD["BT_d"][ct - 40, :, t0:t0 + TT], so[:], r=[so.res], w=[R_scr], dres=so.res)
                    elif 44 <= ct < 48:
                        k.dma("sp", DD["CT_d"][ct - 44, :, t0:t0 + TT], so[:], r=[so.res], w=[R_scr], dres=so.res)
                    tgt = None
                    if 8 <= ct < 16:
                        tgt = (toks["k"], (ct - 8) * 128)
                    elif 16 <= ct < 24:
                        tgt = (toks["v"], (ct - 16) * 128)
                    elif 24 <= ct < 40:
                        tgt = (toks["x"], (ct - 24) * 128)
                    elif 40 <= ct < 44:
                        tgt = (toks["b"], (ct - 40) * 128)
                    J["tgt"] = tgt
                    if tgt is not None:
                        J["po"] = po = pso.next()
                        k.group("pe", [(lambda j=j: nc.tensor.transpose(po[:, j * 128:(j + 1) * 128], so[:, j * 128:(j + 1) * 128],
                                                                        identb[:])) for j in range(2)],
                                r=[so.res, identb.res], w=[po.res])

                def s_tcopy():
                    if J["tgt"] is not None:
                        tb, off = J["tgt"]
                        po = J["po"]
                        k.ins("act", lambda: nc.scalar.copy(out=tb[:, :, off:off + 128],
                                                            in_=po[:, 0:256].rearrange("p (j c) -> p j c", c=128)),
                              r=[po.res], w=[tb.res])
                steps.extend([s_out, s_tcopy])
            elif ct < 64:
                def g_exp():
                    pa = J["pa"]
                    J["sg"] = sg = sgring.next()
                    k.ins("act", lambda: nc.scalar.activation(out=sg[:], in_=pa[:, 0:TT], func=AF.Exp, scale=-1.0),
                          r=[pa.res], w=[sg.res])

                def g_recip():
                    sg = J["sg"]
                    k.ins("act", lambda: nc.scalar.activation(out=sg[:], in_=sg[:], func=AF.Ln, bias=1.0), r=[sg.res], w=[sg.res])

                def g_recip2():
                    sg = J["sg"]
                    k.ins("act", lambda: nc.scalar.activation(out=sg[:], in_=sg[:], func=AF.Exp, scale=-1.0), r=[sg.res], w=[sg.res])

                def g_out():
                    sg = J["sg"]
                    k.dma("sp", sgT_d[ct - 48, :, l0:l0 + TT], sg[:], r=[sg.res], w=[R_scr], dres=sg.res)
                steps.extend([g_exp, g_recip, g_recip2, g_out])
            else:
                st_ = toks["sm"]

                def m_exp():
                    pa = J["pa"]
                    J["sm"] = sm = smring.next()
                    k.ins("act", lambda: nc.scalar.activation(out=sm[:], in_=pa[:, 0:TT], func=AF.Exp, scale=smp[:, 0:1],
                                                              bias=smp[:, 1:2]), r=[pa.res, smp.res], w=[sm.res])

                def m_ln():
                    sm = J["sm"]
                    k.ins("act", lambda: nc.scalar.activation(out=sm[:], in_=sm[:], func=AF.Ln, bias=1.0), r=[sm.res], w=[sm.res])

                def m_mul():
                    sm = J["sm"]
                    k.ins("dve", lambda: nc.vector.tensor_scalar(out=sm[:], in0=sm[:], scalar1=smult[:, 0:1], scalar2=None,
                                                              op0=ALU.mult), r=[sm.res, smult.res], w=[sm.res])

                def m_tr():
                    sm = J["sm"]
                    k.group("pe", [(lambda j=j: nc.tensor.transpose(pst[:, j * 128:(j + 1) * 128], sm[:, j * 128:(j + 1) * 128],
                                                                    ident[:])) for j in range(2)],
                            r=[sm.res, ident.res], w=[pst.res])

                def m_copy():
                    k.ins("dve", lambda: nc.vector.tensor_copy(st_[:], pst[:, 0:256].rearrange("p (j c) -> p j c", c=128)),
                          r=[pst.res], w=[st_.res])
                steps.extend([m_exp, m_ln, m_mul, m_tr, m_copy])
            return steps

        def make_spill(tc, DD, toks):
            t0 = tc["t0"]

            def spill():
                rows = lambda d_: d_[t0:t0 + TT, :].rearrange("(j p) c -> p j c", p=128)
                for key, dn in (("k", "k_d"), ("v", "v_d"), ("x", "x_d"), ("b", "B_d"), ("sm", "sm_d")):
                    tb = toks[key]
                    k.dma("sp", rows(DD[dn]), tb[:], r=[tb.res], w=[R_scr], dres=tb.res)
            return spill

        NST = 14
        pipeline = []
        it = 0
        tiles = [(red, xsrc, cpt, DD, ti) for (red, xsrc, cpt, DD) in runs for ti in range(NTOK // TT)]
        pend_pre = {}

        def run_pipeline_until(limit):
            nonlocal it
            while it < limit:
                for (st0, steps) in pipeline:
                    sidx = it - st0
                    if 0 <= sidx < len(steps):
                        steps[sidx]()
                pipeline[:] = [(a, b) for (a, b) in pipeline if it - a < len(b) - 1]
                it += 1

        tcs = [None] * len(tiles)
        tcs[0] = preamble(tiles[0][0], tiles[0][1], tiles[0][4])
        for n, (red, xsrc, cpt, DD, ti) in enumerate(tiles):
            tc = tcs[n]
            toks = dict(k=ktok.next(), v=vtok.next(), x=xtok.next(), b=btok.next(), sm=smtok.next())
            cts = [ct for ct in range(65)
                   if not ((tc["is_ctx"] or red) and 48 <= ct < 64) and not (red and (ct < 8 or 44 <= ct < 48))]
            for idx, ct in enumerate(cts):
                pipeline.append((it, make_job(tc, ct, cpt, DD, toks)))
                run_pipeline_until(it + 1)
                if idx == 6 and n + 1 < len(tiles):
                    tcs[n + 1] = preamble(tiles[n + 1][0], tiles[n + 1][1], tiles[n + 1][4])
            pipeline.append((it + 6, [make_spill(tc, DD, toks)]))
        run_pipeline_until(it + NST + 2)
        cx.pop()


    def load_scan_consts():
        tri = cx.sb("tri", [128, 4, 128], F32)
        k.dma("sp", tri[:], tri_d, w=[tri.res], dres=tri.res)
        mskf = cx.sb("mskf", [128, 4, 128], F32)
        k.dma("sp", mskf[:], msk_d, w=[mskf.res], dres=mskf.res)
        mskb = cx.sb("mskb", [128, 4, 128], BF16)
        k.ins("dve", lambda: nc.vector.tensor_copy(mskb[:], mskf[:]), r=[mskf.res], w=[mskb.res])
        id4 = cx.sb("id4", [128, 4, 128], F32)
        for j in range(4):
            k.ins("dve", lambda j=j: nc.vector.tensor_copy(id4[:, j, :], ident[:]), r=[ident.res], w=[id4.res])
        return tri, mskb, id4

    def gdn_pass(mode):
        cx.push()
        fwd = mode != "own2"
        pi = 0 if mode == "own1" else 1
        red = mode == "red"
        s_kT, s_k, s_v, s_sm = (kT2_d, k2_d, v2_d, sm2_d) if red else (kT_d, k_d, v_d, sm_d)
        tri, mskb, id4 = load_scan_consts()
        cumL = tri[:, 0, :] if fwd else tri[:, 1, :]
        negR = tri[:, 2, :] if fwd else tri[:, 3, :]
        m_s = mskb[:, 0, :] if fwd else mskb[:, 1, :]
        mT_s = mskb[:, 1, :] if fwd else mskb[:, 0, :]
        mT_i = mskb[:, 3, :] if fwd else mskb[:, 2, :]
        g0 = pi * 8
        l0c = 32 + pi * 8
        banks = Ring([cx.psum(f"gb{i}", [128, 512], F32) for i in range(8)])
        S32 = cx.sb("S32", [128, 8, 128], F32)
        Sbf = cx.sb("Sbf", [128, 8, 128], BF16)
        if fwd:
            k.ins("pool", lambda: nc.gpsimd.memset(S32[:], 0.0), w=[S32.res])
        else:
            k.dma("sp", S32[:].rearrange("p h d -> p (h d)"), st2_d[:, 0:1024], r=[R_st2], w=[S32.res], dres=S32.res)
        k.ins("act", lambda: nc.scalar.copy(out=Sbf[:], in_=S32[:]), r=[S32.res], w=[Sbf.res])
        NB = 2
        qTr = cx.sbring("qTc", NB, [128, 8, 128], BF16)
        kTr = cx.sbring("kTc", NB, [128, 8, 128], BF16)
        kr = cx.sbring("kc", NB, [128, 8, 128], BF16)
        vr = cx.sbring("vc", NB, [128, 8, 128], BF16)
        smr = cx.sbring("smc", NB, [128, 128], F32)
        o1r = cx.sbring("o1c", NB, [128, 8, 128], F32)
        kbgr = cx.sbring("kbg", NB, [128, 8, 128], BF16)
        vbr = cx.sbring("vb", NB, [128, 8, 128], BF16)
        kdr = cx.sbring("kd", NB, [128, 8, 128], BF16)
        smallr = cx.sbring("gsm", NB, [128, 6, 8], F32)
        eglr = cx.sbring("egl", NB, [128, 8], F32)
        expr = cx.sbring("exps", NB, [128, 3, 8], F32)
        Ear = cx.sbring("Ea", 2, [128, 4, 128], F32)
        Ebr = cx.sbring("Eb", 2, [128, 4, 128], F32)
        Ecr = cx.sbring("Ec", 2, [128, 4, 128], F32)
        Edr = cx.sbring("Ed", 2, [128, 4, 128], F32)
        Pr = cx.sbring("Pp", 4, [128, 4, 128], F32)
        PTr = cx.sbring("PTp", 4, [128, 4, 128], F32)
        Yr = cx.sbring("Yp", 4, [128, 4, 128], F32)
        Yfr = cx.sbring("Yf", 2 * NB, [128, 4, 128], BF16)
        nWTr = cx.sbring("nWT", 2 * NB, [128, 4, 128], BF16)
        attr = cx.sbring("att", 2 * NB, [128, 4, 128], BF16)
        qgr = cx.sbring("qg", 2 * NB, [128, 4, 128], BF16)
        vnr = cx.sbring("vn", 2, [128, 4, 128], BF16)
        oTr = cx.sbring("oTs", 2, [128, 8, 128], F32)

        def prep(c):
            lat = c >= 2 and not red
            t0 = c * C
            l0 = t0 - NCTX
            qT_c = qTr.next(); kT_c = kTr.next(); k_c = kr.next(); v_c = vr.next(); sm_c = smr.next()
            if lat:
                k.dma("sp", qT_c[:], qT_d[:, :, t0:t0 + C].rearrange("h p t -> p h t"), r=[R_scr], w=[qT_c.res], dres=qT_c.res)
            k.dma("sp", kT_c[:], s_kT[:, :, t0:t0 + C].rearrange("h p t -> p h t"), r=[R_scr], w=[kT_c.res], dres=kT_c.res)
            k.dma("sp", k_c[:], s_k[t0:t0 + C, :].rearrange("t (h d) -> t h d", d=128), r=[R_scr], w=[k_c.res], dres=k_c.res)
            k.dma("sp", v_c[:], s_v[t0:t0 + C, :].rearrange("t (h d) -> t h d", d=128), r=[R_scr], w=[v_c.res], dres=v_c.res)
            k.dma("sp", sm_c[:], s_sm[t0:t0 + C, :], r=[R_scr], w=[sm_c.res], dres=sm_c.res)
            o1_c = None
            if lat and not fwd:
                o1_c = o1r.next()
                k.dma("sp", o1_c[:], o1_d[:, :, l0:l0 + C].rearrange("h p t -> p h t"), r=[R_o1[c]], w=[o1_c.res], dres=o1_c.res)
            gcols = sm_c[:, g0:g0 + 8]
            lcols = sm_c[:, l0c:l0c + 8]
            pb = banks.next()
            k.group("pe", [
                lambda: nc.tensor.matmul(pb[:, 0:8], cumL, gcols, start=True, stop=True),
                lambda: nc.tensor.matmul(pb[:, 8:16], onesf[:], gcols, start=True, stop=True)],
                r=[tri.res, onesf.res, sm_c.res], w=[pb.res])
            sm6 = smallr.next()
            gc, gcl, ngc, tmp = sm6[:, 0, :], sm6[:, 1, :], sm6[:, 2, :], sm6[:, 3, :]
            k.ins("dve", lambda: nc.vector.tensor_copy(gc, pb[:, 0:8]), r=[pb.res], w=[sm6.res])
            k.ins("dve", lambda: nc.vector.tensor_tensor(out=gcl, in0=gc, in1=lcols, op=ALU.add), r=[sm6.res, sm_c.res], w=[sm6.res])
            k.ins("dve", lambda: nc.vector.tensor_scalar(out=ngc, in0=gc, scalar1=-1.0, scalar2=None, op0=ALU.mult),
                  r=[sm6.res], w=[sm6.res])
            k.ins("dve", lambda: nc.vector.tensor_tensor(out=tmp, in0=pb[:, 8:16], in1=gc, op=ALU.subtract),
                  r=[pb.res, sm6.res], w=[sm6.res])
            ex = expr.next()
            egl = eglr.next()
            k.ins("act", lambda: nc.scalar.activation(out=ex[:, 0, :], in_=gcl, func=AF.Exp), r=[sm6.res], w=[ex.res])
            k.ins("act", lambda: nc.scalar.activation(out=ex[:, 1, :], in_=lcols, func=AF.Exp), r=[sm_c.res], w=[ex.res])
            k.ins("act", lambda: nc.scalar.activation(out=ex[:, 2, :], in_=tmp, func=AF.Exp), r=[sm6.res], w=[ex.res])
            k.ins("act", lambda: nc.scalar.activation(out=egl[:], in_=pb[:, 8:16], func=AF.Exp), r=[pb.res], w=[egl.res])
            kbg = kbgr.next(); vb = vbr.next(); kd = kdr.next()
            bc = lambda col: ex[:, col, :].unsqueeze(2).to_broadcast([128, 8, 128])
            k.ins("pool", lambda: nc.gpsimd.tensor_tensor(out=kbg[:], in0=k_c[:], in1=bc(0), op=ALU.mult),
                  r=[k_c.res, ex.res], w=[kbg.res])
            k.ins("pool", lambda: nc.gpsimd.tensor_tensor(out=vb[:], in0=v_c[:], in1=bc(1), op=ALU.mult),
                  r=[v_c.res, ex.res], w=[vb.res])
            k.ins("pool", lambda: nc.gpsimd.tensor_tensor(out=kd[:], in0=k_c[:], in1=bc(2), op=ALU.mult),
                  r=[k_c.res, ex.res], w=[kd.res])
            pp = dict(c=c, lat=lat, groups=[None, None], vb=vb, kd=kd, egl=egl, o1=o1_c)

            def grp(gi):
                h0 = gi * 4
                gb = lambda h: sm_c[:, g0 + h:g0 + h + 1].to_broadcast([128, 128])
                lb = lambda h: sm_c[:, l0c + h:l0c + h + 1].to_broadcast([128, 128])
                KK = banks.next()
                k.group("pe", [(lambda j=j: nc.tensor.matmul(KK[:, j * 128:(j + 1) * 128], kT_c[:, h0 + j, :], kT_c[:, h0 + j, :],
                                                             start=True, stop=True)) for j in range(4)],
                        r=[kT_c.res], w=[KK.res])
                Da = banks.next()
                fl = []
                for j in range(4):
                    fl.append(lambda j=j: nc.tensor.matmul(Da[:, j * 128:(j + 1) * 128], gb(h0 + j), negR, start=True, stop=False))
                    fl.append(lambda j=j: nc.tensor.matmul(Da[:, j * 128:(j + 1) * 128], identb[:], m_s, start=False, stop=True))
                k.group("pe", fl, r=[sm_c.res, tri.res, identb.res, mskb.res], w=[Da.res])
                Ea = Ear.next()
                for j in range(4):
                    k.ins("act", lambda j=j: nc.scalar.activation(out=Ea[:, j, :], in_=Da[:, j * 128:(j + 1) * 128], func=AF.Exp,
                                                                  bias=sm6[:, 1, h0 + j:h0 + j + 1]),
                          r=[Da.res, sm6.res], w=[Ea.res])
                Db = banks.next()
                fl = []
                for j in range(4):
                    sl = slice(j * 128, (j + 1) * 128)
                    fl.append(lambda j=j, sl=sl: nc.tensor.matmul(Db[:, sl], gb(h0 + j), cumL, start=True, stop=False))
                    fl.append(lambda j=j, sl=sl: nc.tensor.matmul(Db[:, sl], lb(h0 + j), ident[:], start=False, stop=False))
                    fl.append(lambda j=j, sl=sl: nc.tensor.matmul(Db[:, sl], identb[:], mT_s, start=False, stop=True))
                k.group("pe", fl, r=[sm_c.res, tri.res, ident.res, identb.res, mskb.res], w=[Db.res])
                Eb = Ebr.next()
                for j in range(4):
                    k.ins("act", lambda j=j: nc.scalar.activation(out=Eb[:, j, :], in_=Db[:, j * 128:(j + 1) * 128], func=AF.Exp,
                                                                  bias=sm6[:, 2, h0 + j:h0 + j + 1]),
                          r=[Db.res, sm6.res], w=[Eb.res])
                P0 = Pr.next(); P0T = PTr.next()
                KK3 = KK[:].rearrange("p (j c) -> p j c", c=128)
                k.ins("dve", lambda: nc.vector.scalar_tensor_tensor(out=P0[:], in0=KK3, scalar=-1.0, in1=Ea[:],
                                                                    op0=ALU.mult, op1=ALU.mult),
                      r=[KK.res, Ea.res], w=[P0.res])
                k.ins("dve", lambda: nc.vector.scalar_tensor_tensor(out=P0T[:], in0=KK3, scalar=-1.0, in1=Eb[:],
                                                                    op0=ALU.mult, op1=ALU.mult),
                      r=[KK.res, Eb.res], w=[P0T.res])
                yield
                att = None; qg = None
                if lat:
                    QK = banks.next()
                    k.group("pe", [(lambda j=j: nc.tensor.matmul(QK[:, j * 128:(j + 1) * 128], kT_c[:, h0 + j, :], qT_c[:, h0 + j, :],
                                                                 start=True, stop=True)) for j in range(4)],
                            r=[kT_c.res, qT_c.res], w=[QK.res])
                    Dc = banks.next()
                    fl = []
                    for j in range(4):
                        sl = slice(j * 128, (j + 1) * 128)
                        fl.append(lambda j=j, sl=sl: nc.tensor.matmul(Dc[:, sl], gb(h0 + j), cumL, start=True, stop=False))
                        fl.append(lambda j=j, sl=sl: nc.tensor.matmul(Dc[:, sl], identb[:], mT_i, start=False, stop=True))
                    k.group("pe", fl, r=[sm_c.res, tri.res, identb.res, mskb.res], w=[Dc.res])
                    Ec = Ecr.next()
                    for j in range(4):
                        k.ins("act", lambda j=j: nc.scalar.activation(out=Ec[:, j, :], in_=Dc[:, j * 128:(j + 1) * 128], func=AF.Exp,
                                                                      bias=sm6[:, 2, h0 + j:h0 + j + 1]),
                              r=[Dc.res, sm6.res], w=[Ec.res])
                    Dd = banks.next()
                    k.group("pe", [(lambda j=j: nc.tensor.matmul(Dd[:, j * 128:(j + 1) * 128], gb(h0 + j), cumL, start=True, stop=True))
                                   for j in range(4)], r=[sm_c.res, tri.res], w=[Dd.res])
                    Ed = Edr.next()
                    k.ins("act", lambda: nc.scalar.activation(out=Ed[:].rearrange("p j c -> p (j c)"), in_=Dd[:], func=AF.Exp),
                          r=[Dd.res], w=[Ed.res])
                    att = attr.next()
                    k.ins("dve", lambda: nc.vector.tensor_tensor(out=att[:], in0=QK[:].rearrange("p (j c) -> p j c", c=128),
                                                                 in1=Ec[:], op=ALU.mult), r=[QK.res, Ec.res], w=[att.res])
                    qg = qgr.next()
                    k.ins("pool", lambda: nc.gpsimd.tensor_tensor(out=qg[:], in0=qT_c[:, h0:h0 + 4, :], in1=Ed[:], op=ALU.mult),
                          r=[qT_c.res, Ed.res], w=[qg.res])
                Y = Yr.next()
                k.ins("pool", lambda: nc.gpsimd.tensor_tensor(out=Y[:], in0=P0T[:], in1=id4[:], op=ALU.add),
                      r=[P0T.res, id4.res], w=[Y.res])
                yield
                Pp, PTp = P0, P0T
                for lev in range(1, 7):
                    last = lev == 6
                    Pb = banks.next()
                    k.group("pe", [(lambda j=j, Pp=Pp, PTp=PTp, Pb=Pb: nc.tensor.matmul(
                        Pb[:, j * 128:(j + 1) * 128], PTp[:, j, :], Pp[:, j, :], start=True, stop=True)) for j in range(4)],
                        r=[Pp.res, PTp.res], w=[Pb.res])
                    Pn = Pr.next()
                    k.ins("act", lambda Pn=Pn, Pb=Pb: nc.scalar.copy(out=Pn[:].rearrange("p j c -> p (j c)"), in_=Pb[:]),
                          r=[Pb.res], w=[Pn.res])
                    PTn = None
                    if not last:
                        PTb = banks.next()
                        k.group("pe", [(lambda j=j, Pp=Pp, PTp=PTp, PTb=PTb: nc.tensor.matmul(
                            PTb[:, j * 128:(j + 1) * 128], Pp[:, j, :], PTp[:, j, :], start=True, stop=True)) for j in range(4)],
                            r=[Pp.res, PTp.res], w=[PTb.res])
                        PTn = PTr.next()
                        k.ins("act", lambda PTn=PTn, PTb=PTb: nc.scalar.copy(out=PTn[:].rearrange("p j c -> p (j c)"), in_=PTb[:]),
                              r=[PTb.res], w=[PTn.res])
                    yield
                    Yb = banks.next()
                    k.group("pe", [(lambda j=j, Yb=Yb, Y=Y, Pn=Pn: nc.tensor.matmul(
                        Yb[:, j * 128:(j + 1) * 128], Pn[:, j, :], Y[:, j, :], start=True, stop=True)) for j in range(4)],
                        r=[Y.res, Pn.res], w=[Yb.res])
                    if last:
                        Yn = Yfr.next()
                        k.ins("dve", lambda Yn=Yn, Yb=Yb, Y=Y: nc.vector.tensor_tensor(
                            out=Yn[:].rearrange("p j c -> p (j c)"), in0=Yb[:], in1=Y[:].rearrange("p j c -> p (j c)"), op=ALU.add),
                            r=[Yb.res, Y.res], w=[Yn.res])
                    else:
                        Yn = Yr.next()
                        k.ins("dve", lambda Yn=Yn, Yb=Yb, Y=Y: nc.vector.tensor_tensor(
                            out=Yn[:].rearrange("p j c -> p (j c)"), in0=Yb[:], in1=Y[:].rearrange("p j c -> p (j c)"), op=ALU.add),
                            r=[Yb.res, Y.res], w=[Yn.res])
                    Y = Yn
                    Pp, PTp = Pn, PTn
                    yield
                Wb = banks.next()
                k.group("pe", [(lambda j=j: nc.tensor.matmul(Wb[:, j * 128:(j + 1) * 128], kbg[:, h0 + j, :], Y[:, j, :],
                                                             start=True, stop=True)) for j in range(4)],
                        r=[kbg.res, Y.res], w=[Wb.res])
                nWT = nWTr.next()
                k.ins("act", lambda: nc.scalar.activation(out=nWT[:].rearrange("p j c -> p (j c)"), in_=Wb[:], func=AF.Identity,
                                                          scale=-1.0), r=[Wb.res], w=[nWT.res])
                pp["groups"][gi] = (Y, nWT, att, qg)
            pp["gens"] = [grp(0), grp(1)]
            return pp

        def chain(pp):
            c, lat = pp["c"], pp["lat"]
            l0 = c * C - NCTX
            vb, kd, egl = pp["vb"], pp["kd"], pp["egl"]
            oTs = oTr.next() if lat else None
            for gi in range(2):
                h0 = gi * 4
                Y, nWT, att, qg = pp["groups"][gi]
                VN = banks.next()
                fl = []
                for j in range(4):
                    sl = slice(j * 128, (j + 1) * 128)
                    fl.append(lambda j=j, sl=sl: nc.tensor.matmul(VN[:, sl], Y[:, j, :], vb[:, h0 + j, :], start=True, stop=False))
                    fl.append(lambda j=j, sl=sl: nc.tensor.matmul(VN[:, sl], nWT[:, j, :], Sbf[:, h0 + j, :], start=False, stop=True))
                k.group("pe", fl, r=[Y.res, vb.res, nWT.res, Sbf.res], w=[VN.res])
                vn = vnr.next()
                k.ins("dve", lambda: nc.vector.tensor_copy(vn[:].rearrange("p j c -> p (j c)"), VN[:]), r=[VN.res], w=[vn.res])
                yield
                if lat:
                    OT = banks.next()
                    fl = []
                    for j in range(4):
                        sl = slice(j * 128, (j + 1) * 128)
                        fl.append(lambda j=j, sl=sl: nc.tensor.matmul(OT[:, sl], Sbf[:, h0 + j, :], qg[:, j, :], start=True, stop=False))
                        fl.append(lambda j=j, sl=sl: nc.tensor.matmul(OT[:, sl], vn[:, j, :], att[:, j, :], start=False, stop=True))
                    k.group("pe", fl, r=[Sbf.res, qg.res, vn.res, att.res], w=[OT.res])
                    osl = oTs[:, h0:h0 + 4, :].rearrange("p j c -> p (j c)")
                    if fwd:
                        k.ins("act", lambda: nc.scalar.copy(out=osl, in_=OT[:]), r=[OT.res], w=[oTs.res])
                    else:
                        o1_c = pp["o1"]
                        k.ins("dve", lambda: nc.vector.tensor_tensor(
                            out=osl, in0=OT[:], in1=o1_c[:, h0:h0 + 4, :].rearrange("p j c -> p (j c)"), op=ALU.add),
                            r=[OT.res, o1_c.res], w=[oTs.res])
                DS = banks.next()
                k.group("pe", [(lambda j=j: nc.tensor.matmul(DS[:, j * 128:(j + 1) * 128], kd[:, h0 + j, :], vn[:, j, :],
                                                             start=True, stop=True)) for j in range(4)],
                        r=[kd.res, vn.res], w=[DS.res])
                ssl = S32[:, h0:h0 + 4, :]
                k.ins("pool", lambda: nc.gpsimd.tensor_tensor(
                    out=ssl, in0=ssl, in1=egl[:, h0:h0 + 4].unsqueeze(2).to_broadcast([128, 4, 128]), op=ALU.mult),
                    r=[S32.res, egl.res], w=[S32.res])
                k.ins("dve", lambda: nc.vector.tensor_tensor(out=ssl, in0=ssl, in1=DS[:].rearrange("p (j c) -> p j c", c=128),
                                                             op=ALU.add), r=[S32.res, DS.res], w=[S32.res])
                k.ins("act", lambda: nc.scalar.copy(out=Sbf[:, h0:h0 + 4, :], in_=ssl), r=[S32.res], w=[Sbf.res])
                yield
            if lat:
                if fwd:
                    k.dma("sp", o1_d[:, :, l0:l0 + C].rearrange("h p t -> p h t"), oTs[:], r=[oTs.res], w=[R_o1[c]], dres=oTs.res)
                else:
                    k.dma("sp", oT_d[:, :, l0:l0 + C].rearrange("h p t -> p h t"), oTs[:], r=[oTs.res], w=[R_oT], dres=oTs.res)

        order = list(range(NCH)) if fwd else list(range(NCH - 1, 1, -1))
        if GDN_LIMIT:
            order = order[:GDN_LIMIT]
        def drive(gens):
            gens = list(gens)
            while gens:
                for g_ in list(gens):
                    try:
                        next(g_)
                    except StopIteration:
                        gens.remove(g_)

        pend = prep(order[0])
        drive(pend["gens"])
        for idx in range(len(order)):
            nxt = prep(order[idx + 1]) if idx + 1 < len(order) else None
            drive([chain(pend)] + (nxt["gens"] if nxt is not None else []))
            pend = nxt
        if red:
            k.dma("sp", st2_d[:, 0:1024], S32[:].rearrange("p h d -> p (h d)"), r=[S32.res], w=[R_st2], dres=S32.res)
        cx.pop()

    def ssd_pass(mode):
        cx.push()
        fwd = mode != "own2"
        pi = 0 if mode == "own1" else 1
        red = mode == "red"
        s_x, s_B, s_sm = (x2_d, B2_d, sm2_d) if red else (x_d, B_d, sm_d)
        tri, mskb, id4 = load_scan_consts()
        cumL = tri[:, 0, :] if fwd else tri[:, 1, :]
        mT_i = mskb[:, 3, :] if fwd else mskb[:, 2, :]
        d0 = 64 + pi * 32
        banks = Ring([cx.psum(f"sbk{i}", [128, 512], F32) for i in range(4)])
        ydb = [cx.psum(f"syd{i}", [128, 512], F32) for i in range(4)]
        aB = cx.sb("aB", [128, 32], F32)
        k.dma("sp", aB[:], salog_d[:, pi, :], w=[aB.res], dres=aB.res)
        k.ins("act", lambda: nc.scalar.activation(out=aB[:], in_=aB[:], func=AF.Exp), r=[aB.res], w=[aB.res])
        k.ins("dve", lambda: nc.vector.tensor_scalar(out=aB[:], in0=aB[:], scalar1=-1.0, scalar2=None, op0=ALU.mult),
              r=[aB.res], w=[aB.res])
        hT32 = cx.sb("hT32", [128, 32, 64], F32)
        hTbf = cx.sb("hTbf", [128, 32, 64], BF16)
        if fwd:
            k.ins("pool", lambda: nc.gpsimd.memset(hT32[:], 0.0), w=[hT32.res])
        else:
            k.dma("sp", hT32[:].rearrange("p h d -> p (h d)"), st2_d[:, 1024:3072], r=[R_st2], w=[hT32.res], dres=hT32.res)
        k.ins("act", lambda: nc.scalar.copy(out=hTbf[:], in_=hT32[:]), r=[hT32.res], w=[hTbf.res])
        xr = cx.sbring("xc", 2, [128, 32, 64], BF16)
        BTr = cx.sbring("BTc", 2, [128, 4, 128], BF16)
        CTr = cx.sbring("CTc", 2, [128, 4, 128], BF16)
        Br = cx.sbring("Bc", 2, [128, 4, 128], BF16)
        smr = cx.sbring("smc", 2, [128, 128], F32)
        y1r = cx.sbring("y1c", 2, [128, 2048], F32)
        dAr = cx.sbring("dA", 2, [128, 32], F32)
        s5r = cx.sbring("s5", 2, [128, 6, 32], F32)
        xdtr = cx.sbring("xdt", 2, [128, 32, 64], BF16)
        xdtsr = cx.sbring("xdts", 2, [128, 32, 64], BF16)
        ytr = cx.sbring("ytmp", 2, [128, 32, 64], F32)
        ycr = cx.sbring("yc", 2, [128, 2048], F32)
        segr = cx.sbring("seg", 3, [128, 4, 128], F32)
        GTr = cx.sbring("GT", 2, [128, 4, 128], F32)
        MTr = cx.sbring("MT", 3, [128, 4, 128], BF16)
        order = list(range(NCH)) if fwd else list(range(NCH - 1, 1, -1))
        if GDN_LIMIT:
            order = order[:GDN_LIMIT]
        def chunk_gen(c):
                lat = c >= 2 and not red
                t0 = c * C
                l0 = t0 - NCTX
                x_c = xr.next(); BT_c = BTr.next(); CT_c = CTr.next(); B_c = Br.next(); sm_c = smr.next()
                k.dma("sp", x_c[:].rearrange("p h d -> p (h d)"), s_x[t0:t0 + C, :], r=[R_scr], w=[x_c.res], dres=x_c.res)
                if lat:
                    k.dma("sp", BT_c[:], BT_d[:, :, t0:t0 + C].rearrange("g p t -> p g t"), r=[R_scr], w=[BT_c.res], dres=BT_c.res)
                    k.dma("sp", CT_c[:], CT_d[:, :, t0:t0 + C].rearrange("g p t -> p g t"), r=[R_scr], w=[CT_c.res], dres=CT_c.res)
                k.dma("sp", B_c[:], s_B[t0:t0 + C, :].rearrange("t (g n) -> t g n", n=128), r=[R_scr], w=[B_c.res], dres=B_c.res)
                k.dma("sp", sm_c[:], s_sm[t0:t0 + C, :], r=[R_scr], w=[sm_c.res], dres=sm_c.res)
                y1_c = None
                if lat and not fwd:
                    y1_c = y1r.next()
                    k.dma("sp", y1_c[:], y1_d[l0:l0 + C, :], r=[R_y1[c]], w=[y1_c.res], dres=y1_c.res)
                dtc = sm_c[:, d0:d0 + 32]
                dA = dAr.next()
                k.ins("dve", lambda: nc.vector.tensor_tensor(out=dA[:], in0=dtc, in1=aB[:], op=ALU.mult),
                      r=[sm_c.res, aB.res], w=[dA.res])
                pb = banks.next()
                k.group("pe", [
                    lambda: nc.tensor.matmul(pb[:, 0:32], cumL, dA[:], start=True, stop=True),
                    lambda: nc.tensor.matmul(pb[:, 32:64], onesf[:], dA[:], start=True, stop=True)],
                    r=[tri.res, onesf.res, dA.res], w=[pb.res])
                s5 = s5r.next()
                ac, nac, tmp, ea, w2, eat = (s5[:, i, :] for i in range(6))
                k.ins("dve", lambda: nc.vector.tensor_copy(ac, pb[:, 0:32]), r=[pb.res], w=[s5.res])
                k.ins("dve", lambda: nc.vector.tensor_scalar(out=nac, in0=ac, scalar1=-1.0, scalar2=None, op0=ALU.mult),
                      r=[s5.res], w=[s5.res])
                k.ins("dve", lambda: nc.vector.tensor_tensor(out=tmp, in0=pb[:, 32:64], in1=ac, op=ALU.subtract),
                      r=[pb.res, s5.res], w=[s5.res])
                k.ins("act", lambda: nc.scalar.activation(out=ea, in_=ac, func=AF.Exp), r=[s5.res], w=[s5.res])
                k.ins("act", lambda: nc.scalar.activation(out=w2, in_=tmp, func=AF.Exp), r=[s5.res], w=[s5.res])
                k.ins("act", lambda: nc.scalar.activation(out=eat, in_=pb[:, 32:64], func=AF.Exp), r=[pb.res], w=[s5.res])
                k.ins("dve", lambda: nc.vector.tensor_tensor(out=w2, in0=w2, in1=dtc, op=ALU.mult), r=[s5.res, sm_c.res], w=[s5.res])
                bc32 = lambda ap: ap.unsqueeze(2).to_broadcast([128, 32, 64])
                xdts = xdtsr.next()
                k.ins("pool", lambda: nc.gpsimd.tensor_tensor(out=xdts[:], in0=x_c[:], in1=bc32(w2), op=ALU.mult),
                      r=[x_c.res, s5.res], w=[xdts.res])
                ytmp = None
                if lat:
                    xdt = xdtr.next()
                    k.ins("pool", lambda: nc.gpsimd.tensor_tensor(out=xdt[:], in0=x_c[:], in1=bc32(dtc), op=ALU.mult),
                          r=[x_c.res, sm_c.res], w=[xdt.res])
                    ytmp = ytr.next()
                    for g in range(4):
                        YO = banks.next()
                        k.ins("pe", lambda g=g, YO=YO: nc.tensor.matmul(
                            YO[:], CT_c[:, g, :], hTbf[:, g * 8:(g + 1) * 8, :].rearrange("p h d -> p (h d)"), start=True, stop=True),
                            r=[CT_c.res, hTbf.res], w=[YO.res])
                        k.ins("dve", lambda g=g, YO=YO: nc.vector.tensor_tensor(
                            out=ytmp[:, g * 8:(g + 1) * 8, :], in0=YO[:].rearrange("p (h d) -> p h d", d=64),
                            in1=ea[:, g * 8:(g + 1) * 8].unsqueeze(2).to_broadcast([128, 8, 64]), op=ALU.mult),
                            r=[YO.res, s5.res], w=[ytmp.res])
                k.ins("pool", lambda: nc.gpsimd.tensor_tensor(out=hT32[:], in0=hT32[:], in1=bc32(eat), op=ALU.mult),
                      r=[hT32.res, s5.res], w=[hT32.res])
                for g in range(4):
                    NS = banks.next()
                    k.ins("pe", lambda g=g, NS=NS: nc.tensor.matmul(
                        NS[:], B_c[:, g, :], xdts[:, g * 8:(g + 1) * 8, :].rearrange("p h d -> p (h d)"), start=True, stop=True),
                        r=[B_c.res, xdts.res], w=[NS.res])
                    hs = hT32[:, g * 8:(g + 1) * 8, :]
                    k.ins("dve", lambda g=g, NS=NS, hs=hs: nc.vector.tensor_tensor(
                        out=hs, in0=hs, in1=NS[:].rearrange("p (h d) -> p h d", d=64), op=ALU.add),
                        r=[hT32.res, NS.res], w=[hT32.res])
                k.ins("act", lambda: nc.scalar.copy(out=hTbf[:], in_=hT32[:]), r=[hT32.res], w=[hTbf.res])
                yield
                if not lat:
                    return
                GTb = banks.next()
                k.group("pe", [(lambda g=g: nc.tensor.matmul(GTb[:, g * 128:(g + 1) * 128], BT_c[:, g, :], CT_c[:, g, :],
                                                             start=True, stop=True)) for g in range(4)],
                        r=[BT_c.res, CT_c.res], w=[GTb.res])
                GT = GTr.next()
                k.ins("act", lambda: nc.scalar.copy(out=GT[:].rearrange("p g c -> p (g c)"), in_=GTb[:]), r=[GTb.res], w=[GT.res])
                y_c = ycr.next()

                def grp_gen(g):
                    YD = ydb[g]
                    for qd in range(2):
                        hq = g * 8 + qd * 4
                        Dq = banks.next()
                        fl = []
                        for j in range(4):
                            sl = slice(j * 128, (j + 1) * 128)
                            fl.append(lambda j=j, sl=sl, Dq=Dq: nc.tensor.matmul(
                                Dq[:, sl], dA[:, hq + j:hq + j + 1].to_broadcast([128, 128]), cumL, start=True, stop=False))
                            fl.append(lambda j=j, sl=sl, Dq=Dq: nc.tensor.matmul(Dq[:, sl], identb[:], mT_i, start=False, stop=True))
                        k.group("pe", fl, r=[dA.res, tri.res, identb.res, mskb.res], w=[Dq.res])
                        seg = segr.next()
                        for j in range(4):
                            k.ins("act", lambda j=j, seg=seg, Dq=Dq: nc.scalar.activation(
                                out=seg[:, j, :], in_=Dq[:, j * 128:(j + 1) * 128], func=AF.Exp, bias=s5[:, 1, hq + j:hq + j + 1]),
                                r=[Dq.res, s5.res], w=[seg.res])
                        MT = MTr.next()
                        k.ins("dve", lambda seg=seg, MT=MT, g=g: nc.vector.tensor_tensor(
                            out=MT[:], in0=seg[:], in1=GT[:, g:g + 1, :].to_broadcast([128, 4, 128]), op=ALU.mult),
                            r=[seg.res, GT.res], w=[MT.res])
                        k.group("pe", [(lambda j=j, MT=MT, YD=YD: nc.tensor.matmul(
                            YD[:, (qd * 4 + j) * 64:(qd * 4 + j + 1) * 64], MT[:, j, :], xdt[:, hq + j, :], start=True, stop=True))
                            for j in range(4)], r=[MT.res, xdt.res], w=[YD.res])
                        yield
                    ysl = y_c[:, g * 512:(g + 1) * 512]
                    k.ins("dve", lambda g=g, YD=YD, ysl=ysl: nc.vector.tensor_tensor(
                        out=ysl, in0=YD[:], in1=ytmp[:, g * 8:(g + 1) * 8, :].rearrange("p h d -> p (h d)"), op=ALU.add),
                        r=[YD.res, ytmp.res], w=[y_c.res])
                ggs = [grp_gen(g) for g in range(4)]
                while ggs:
                    for g_ in list(ggs):
                        try:
                            next(g_)
                        except StopIteration:
                            ggs.remove(g_)
                if fwd:
                    k.dma("sp", y1_d[l0:l0 + C, :], y_c[:], r=[y_c.res], w=[R_y1[c]], dres=y_c.res)
                else:
                    k.ins("pool", lambda: nc.gpsimd.tensor_tensor(out=y_c[:], in0=y_c[:], in1=y1_c[:], op=ALU.add),
                          r=[y_c.res, y1_c.res], w=[y_c.res])
                    k.dma("sp", y_d[l0:l0 + C, :], y_c[:], r=[y_c.res], w=[R_y], dres=y_c.res)

        def adv(g_, n):
            for _ in range(n):
                try:
                    next(g_)
                except StopIteration:
                    return

        for c in order:
            adv(chunk_gen(c), 100)
        if red:
            k.dma("sp", st2_d[:, 1024:3072], hT32[:].rearrange("p h d -> p (h d)"), r=[hT32.res], w=[R_st2], dres=hT32.res)
        cx.pop()

    with nc.allow_non_contiguous_dma("scan chunk relayout"):
        if "R" in phases:
            gdn_pass("red")
            ssd_pass("red")
        if "B1" in phases:
            gdn_pass("own1")
        if "S1" in phases:
            ssd_pass("own1")
        if "B2" in phases:
            gdn_pass("own2")
        if "S2" in phases:
            ssd_pass("own2")


    def silu_parts(src_ap, shape, ring_e, ring_z, src_res):
        ez = ring_e.next()
        zc = ring_z.next()
        k.ins("act", lambda: nc.scalar.activation(out=ez[:], in_=src_ap, func=AF.Exp, scale=-1.0), r=[src_res], w=[ez.res])
        k.ins("act", lambda: nc.scalar.copy(out=zc[:], in_=src_ap), r=[src_res], w=[zc.res])
        k.ins("dve", lambda: nc.vector.tensor_scalar(out=ez[:], in0=ez[:], scalar1=1.0, scalar2=None, op0=ALU.add),
              r=[ez.res], w=[ez.res])
        return zc, ez

    if "C1" in phases:
        cx.push()
        T1 = 128
        wZ = cx.sb("wZ", [128, 8, 3072], BF16)
        wbg = cx.sb("wbg", [128, 8, 1024], BF16)
        wbs = cx.sb("wbs", [128, 16, 1024], BF16)
        wo = cx.sb("wo", [128, 8, 1024], BF16)
        for kt in range(8):
            k.dma("pool", wZ[:, kt, :], w_in[kt * 128:(kt + 1) * 128, O_ZG:O_ZG + 3072], w=[wZ.res], dres=wZ.res)
            k.dma("pool", wbg[:, kt, :], wbg_d[kt * 128:(kt + 1) * 128, :], w=[wbg.res], dres=wbg.res)
            k.dma("pool", wo[:, kt, :], wo_d[kt * 128:(kt + 1) * 128, :], w=[wo.res], dres=wo.res)
        for kt in range(16):
            k.dma("pool", wbs[:, kt, :], wbs_d[kt * 128:(kt + 1) * 128, :], w=[wbs.res], dres=wbs.res)
        gnw = cx.sb("gnw", [128, 1], F32)
        k.dma("sp", gnw[:], gnw_d, w=[gnw.res], dres=gnw.res)
        dskB = cx.sb("dskB", [128, 2048], F32)
        snwB = cx.sb("snwB", [128, 2048], F32)
        g1B = cx.sb("g1B", [128, 1024], F32)
        k.dma("sp", dskB[:], dskip_d[0, :].partition_broadcast(128), w=[dskB.res], dres=dskB.res)
        k.dma("sp", snwB[:], snw_d[0, :].partition_broadcast(128), w=[snwB.res], dres=snwB.res)
        k.dma("sp", g1B[:], mod_d[0, 2048:3072].partition_broadcast(128), r=[R_mod], w=[g1B.res], dres=g1B.res)
        GB = [[cx.psum(f"c1g{i}{j}", [128, 512], F32) for j in range(2)] for i in range(2)]
        SB = [(cx.psum(f"c1s{i}", [128, 512], F32), cx.psum(f"c1t{i}", [128, 1024], BF16)) for i in range(2)]
        aTr = cx.sbring("c_aT", 2, [128, 8, T1], BF16)
        oTr_ = cx.sbring("c_oT", 1, [128, 8, T1], F32)
        yr_ = cx.sbring("c_y", 1, [128, 2048], F32)
        xsr_ = cx.sbring("c_xs", 1, [128, 2048], BF16)
        sgr_ = cx.sbring("c_sg", 1, [128, 16, T1], F32)
        xr_ = cx.sbring("c_x", 1, [128, 1024], F32)
        onTr = cx.sbring("c_on", 1, [128, 8, T1], BF16)
        ynTr = cx.sbring("c_yn", 1, [128, 16, T1], BF16)
        mTr = cx.sbring("c_mT", 1, [128, 8, T1], BF16)
        h1r = cx.sbring("c_h1", 1, [128, 1024], F32)
        TG = [dict(ez=cx.sb(f"g_ez{i}", [128, T1], F32), zc=cx.sb(f"g_zc{i}", [128, T1], F32),
                   o2=cx.sb(f"g_o2{i}", [128, T1], F32), rs=cx.sb(f"g_rs{i}", [128, T1], F32)) for i in range(2)]
        TS = [dict(ez=cx.sb(f"s_ez{i}", [128, 512], F32), zc=cx.sb(f"s_zc{i}", [128, 512], F32),
                   yy=cx.sb(f"s_yy{i}", [128, 512], F32), jk=cx.sb(f"s_jk{i}", [128, 512], BF16),
                   ynb=cx.sb(f"s_yn{i}", [128, 512], BF16), ss=cx.sb(f"s_ss{i}", [128, 2], F32)) for i in range(2)]

        def drive_c(gens):
            gens = list(gens)
            while gens:
                for g_ in list(gens):
                    try:
                        next(g_)
                    except StopIteration:
                        gens.remove(g_)

        for ti in range(NLAT // T1):
            l0 = ti * T1
            aT = aTr.next(); oT = oTr_.next(); yt = yr_.next(); xs_ = xsr_.next(); sg = sgr_.next(); xt = xr_.next()
            k.dma("sp", aT[:], aT_d[:, :, l0:l0 + T1].rearrange("kt p t -> p kt t"), r=[R_scr], w=[aT.res], dres=aT.res)
            k.dma("sp", oT[:], oT_d[:, :, l0:l0 + T1].rearrange("h p t -> p h t"), r=[R_oT], w=[oT.res], dres=oT.res)
            k.dma("sp", yt[:], y_d[l0:l0 + T1, :], r=[R_y], w=[yt.res], dres=yt.res)
            k.dma("sp", xs_[:], x_d[NCTX + l0:NCTX + l0 + T1, :], r=[R_scr], w=[xs_.res], dres=xs_.res)
            k.dma("sp", sg[:], sgT_d[:, :, l0:l0 + T1].rearrange("c p t -> p c t"), r=[R_scr], w=[sg.res], dres=sg.res)
            k.dma("sp", xt[:], xin[NCTX + l0:NCTX + l0 + T1, :], w=[xt.res], dres=xt.res)
            onT = onTr.next(); ynT = ynTr.next(); mT = mTr.next(); h1 = h1r.next()

            def gen_gdn(sid, hs):
                pz, pn = GB[sid]
                ez, zc, o2, rs = (TG[sid][n_] for n_ in ("ez", "zc", "o2", "rs"))
                for h in hs:
                    k.group("pe", [(lambda kt=kt: nc.tensor.matmul(pz[:, 0:T1], wZ[:, kt, h * 128:(h + 1) * 128], aT[:, kt, :],
                                                                   start=(kt == 0), stop=(kt == 7))) for kt in range(8)],
                            r=[wZ.res, aT.res], w=[pz.res])
                    k.ins("pool", lambda: nc.gpsimd.tensor_tensor(out=o2[:], in0=oT[:, h, :], in1=oT[:, h, :], op=ALU.mult),
                          r=[oT.res], w=[o2.res])
                    yield
                    k.ins("act", lambda: nc.scalar.activation(out=ez[:], in_=pz[:, 0:T1], func=AF.Exp, scale=-1.0), r=[pz.res], w=[ez.res])
                    k.ins("act", lambda: nc.scalar.copy(out=zc[:], in_=pz[:, 0:T1]), r=[pz.res], w=[zc.res])
                    k.ins("pe", lambda: nc.tensor.matmul(pn[:, 0:T1], onesf[:], o2[:], start=True, stop=True),
                          r=[onesf.res, o2.res], w=[pn.res])
                    yield
                    k.ins("dve", lambda: nc.vector.tensor_scalar(out=ez[:], in0=ez[:], scalar1=1.0, scalar2=None, op0=ALU.add),
                          r=[ez.res], w=[ez.res])
                    k.ins("act", lambda: nc.scalar.activation(out=rs[:], in_=pn[:, 0:T1], func=AF.Ln, scale=1.0 / 128, bias=epsc[:, 0:1]),
                          r=[pn.res, epsc.res], w=[rs.res])
                    yield
                    k.ins("dve", lambda: nc.vector.reciprocal(out=ez[:], in_=ez[:]), r=[ez.res], w=[ez.res])
                    k.ins("act", lambda: nc.scalar.activation(out=rs[:], in_=rs[:], func=AF.Exp, scale=-0.5), r=[rs.res], w=[rs.res])
                    yield
                    k.ins("pool", lambda: nc.gpsimd.tensor_tensor(out=zc[:], in0=zc[:], in1=ez[:], op=ALU.mult),
                          r=[zc.res, ez.res], w=[zc.res])
                    k.ins("dve", lambda: nc.vector.tensor_tensor(out=rs[:], in0=rs[:], in1=oT[:, h, :], op=ALU.mult),
                          r=[rs.res, oT.res], w=[rs.res])
                    yield
                    k.ins("dve", lambda: nc.vector.scalar_tensor_tensor(out=onT[:, h, :], in0=rs[:], scalar=gnw[:, 0:1], in1=zc[:],
                                                                        op0=ALU.mult, op1=ALU.mult),
                          r=[rs.res, gnw.res, zc.res], w=[onT.res])
                    yield

            def gen_ssd(sid, cgs):
                pz, pt = SB[sid]
                ez, zc, yy, jk, ynb, ss = (TS[sid][n_] for n_ in ("ez", "zc", "yy", "jk", "ynb", "ss"))
                for cg in cgs:
                    csl = slice(cg * 512, (cg + 1) * 512)
                    k.group("pe", [(lambda kt=kt: nc.tensor.matmul(pz[:], aT[:, kt, :], wZ[:, kt, 1024 + cg * 512:1024 + (cg + 1) * 512],
                                                                   start=(kt == 0), stop=(kt == 7))) for kt in range(8)],
                            r=[wZ.res, aT.res], w=[pz.res])
                    k.ins("pool", lambda: nc.gpsimd.tensor_tensor(out=yy[:], in0=xs_[:, csl], in1=dskB[:, csl], op=ALU.mult),
                          r=[xs_.res, dskB.res], w=[yy.res])
                    yield
                    k.ins("act", lambda: nc.scalar.activation(out=ez[:], in_=pz[:], func=AF.Exp, scale=-1.0), r=[pz.res], w=[ez.res])
                    k.ins("act", lambda: nc.scalar.copy(out=zc[:], in_=pz[:]), r=[pz.res], w=[zc.res])
                    k.ins("dve", lambda: nc.vector.tensor_tensor(out=yy[:], in0=yy[:], in1=yt[:, csl], op=ALU.add),
                          r=[yy.res, yt.res], w=[yy.res])
                    yield
                    k.ins("dve", lambda: nc.vector.tensor_scalar(out=ez[:], in0=ez[:], scalar1=1.0, scalar2=None, op0=ALU.add),
                          r=[ez.res], w=[ez.res])
                    yield
                    k.ins("dve", lambda: nc.vector.reciprocal(out=ez[:], in_=ez[:]), r=[ez.res], w=[ez.res])
                    yield
                    k.ins("pool", lambda: nc.gpsimd.tensor_tensor(out=zc[:], in0=zc[:], in1=ez[:], op=ALU.mult),
                          r=[zc.res, ez.res], w=[zc.res])
                    yield
                    k.ins("dve", lambda: nc.vector.tensor_tensor(out=yy[:], in0=yy[:], in1=zc[:], op=ALU.mult),
                          r=[yy.res, zc.res], w=[yy.res])
                    yield
                    k.ins("act", lambda: nc.scalar.activation(out=jk[:], in_=yy[:], func=AF.Square, accum_out=ss[:, 0:1]),
                          r=[yy.res], w=[jk.res, ss.res])
                    yield
                    k.ins("act", lambda: nc.scalar.activation(out=ss[:, 1:2], in_=ss[:, 0:1], func=AF.Ln, scale=1.0 / 512, bias=epsc[:, 0:1]),
                          r=[ss.res, epsc.res], w=[ss.res])
                    yield
                    k.ins("act", lambda: nc.scalar.activation(out=ss[:, 1:2], in_=ss[:, 1:2], func=AF.Exp, scale=-0.5),
                          r=[ss.res], w=[ss.res])
                    yield
                    k.ins("dve", lambda: nc.vector.scalar_tensor_tensor(out=ynb[:], in0=yy[:], scalar=ss[:, 1:2], in1=snwB[:, csl],
                                                                        op0=ALU.mult, op1=ALU.mult),
                          r=[yy.res, ss.res, snwB.res], w=[ynb.res])
                    yield
                    k.group("pe", [(lambda j=j: nc.tensor.transpose(pt[:, j * 128:(j + 1) * 128], ynb[:, j * 128:(j + 1) * 128], identb[:]))
                                   for j in range(4)], r=[ynb.res, identb.res], w=[pt.res])
                    yield
                    k.ins("act", lambda: nc.scalar.copy(out=ynT[:, cg * 4:(cg + 1) * 4, :],
                                                        in_=pt[:, 0:512].rearrange("p (j c) -> p j c", c=128)),
                          r=[pt.res], w=[ynT.res])
                    yield

            def gen_merge(sid, ocs):
                pg, pp = GB[sid]
                m1, m2 = TG[sid]["ez"], TG[sid]["zc"]
                for oc in ocs:
                    k.group("pe", [(lambda h=h: nc.tensor.matmul(pg[:, 0:T1], wbg[:, h, oc * 128:(oc + 1) * 128], onT[:, h, :],
                                                                 start=(h == 0), stop=(h == 7))) for h in range(8)],
                            r=[wbg.res, onT.res], w=[pg.res])
                    k.group("pe", [(lambda ct=ct: nc.tensor.matmul(pp[:, 0:T1], wbs[:, ct, oc * 128:(oc + 1) * 128], ynT[:, ct, :],
                                                                   start=(ct == 0), stop=(ct == 15))) for ct in range(16)],
                            r=[wbs.res, ynT.res], w=[pp.res])
                    yield
                    k.ins("dve", lambda: nc.vector.tensor_tensor(out=m1[:], in0=pg[:, 0:T1], in1=sg[:, oc, :], op=ALU.mult),
                          r=[pg.res, sg.res], w=[m1.res])
                    k.ins("dve", lambda: nc.vector.tensor_tensor(out=m2[:], in0=pp[:, 0:T1], in1=sg[:, 8 + oc, :], op=ALU.mult),
                          r=[pp.res, sg.res], w=[m2.res])
                    yield
                    k.ins("pool", lambda: nc.gpsimd.tensor_tensor(out=mT[:, oc, :], in0=m1[:], in1=m2[:], op=ALU.add),
                          r=[m1.res, m2.res], w=[mT.res])
                    yield

            def gen_out(sid, cg):
                pm = SB[sid][0]
                csl = slice(cg * 512, (cg + 1) * 512)
                k.group("pe", [(lambda kt=kt: nc.tensor.matmul(pm[:], mT[:, kt, :], wo[:, kt, csl],
                                                               start=(kt == 0), stop=(kt == 7))) for kt in range(8)],
                        r=[mT.res, wo.res], w=[pm.res])
                yield
                k.ins("dve", lambda: nc.vector.tensor_tensor(out=h1[:, csl], in0=pm[:], in1=g1B[:, csl], op=ALU.mult),
                      r=[pm.res, g1B.res], w=[h1.res])
                yield
                k.ins("pool", lambda: nc.gpsimd.tensor_tensor(out=h1[:, csl], in0=h1[:, csl], in1=xt[:, csl], op=ALU.add),
                      r=[h1.res, xt.res], w=[h1.res])
                yield

            drive_c([gen_gdn(0, range(0, 4)), gen_ssd(0, range(0, 2)), gen_gdn(1, range(4, 8)), gen_ssd(1, range(2, 4))])
            drive_c([gen_merge(0, range(0, 4)), gen_merge(1, range(4, 8))])
            drive_c([gen_out(0, 0), gen_out(1, 1)])
            k.dma("sp", h1_d[l0:l0 + T1, :], h1[:], r=[h1.res], w=[R_h1], dres=h1.res)
        cx.pop()

    if "C2" in phases:
        cx.push()
        T2 = 256
        NF = DFF // 128
        wfi = cx.sb("wfi", [128, 8, 2 * DFF], BF16)
        wfo = cx.sb("wfo", [128, NF, 1024], BF16)
        for kt in range(8):
            for c0 in range(0, 2 * DFF, 2816):
                k.dma("pool", wfi[:, kt, c0:c0 + 2816], wfi_d[kt * 128:(kt + 1) * 128, c0:c0 + 2816], w=[wfi.res], dres=wfi.res)
        for ft in range(NF):
            k.dma("pool", wfo[:, ft, :], wfo_d[ft * 128:(ft + 1) * 128, :], w=[wfo.res], dres=wfo.res)
        g2B = cx.sb("g2B", [128, 1024], F32)
        nfB = cx.sb("nfB", [128, 1024], F32)
        k.dma("sp", g2B[:], mod_d[0, 5120:6144].partition_broadcast(128), r=[R_mod], w=[g2B.res], dres=g2B.res)
        k.dma("sp", nfB[:], nfw_d[0, :].partition_broadcast(128), w=[nfB.res], dres=nfB.res)
        n2 = cx.sb("n2", [128, 8], F32)
        k.dma("sp", n2[:], n2w_d, w=[n2.res], dres=n2.res)
        S2 = cx.sb("S2", [128, 8], F32)
        k.ins("dve", lambda: nc.vector.scalar_tensor_tensor(out=S2[:], in0=modf[:, 0, 4, :], scalar=1.0, in1=n2[:],
                                                            op0=ALU.add, op1=ALU.mult), r=[modf.res, n2.res], w=[S2.res])
        neg1d = cx.sb("neg1d", [128, T2], F32)
        k.ins("pool", lambda: nc.gpsimd.memset(neg1d[:], -1.0), w=[neg1d.res])
        banks = Ring([cx.psum(f"c2b{i}", [128, 512], F32) for i in range(8)])
        h1r = cx.sbring("d_h1", 2, [128, 2, 1024], F32)
        xnr = cx.sbring("d_xn", 1, [128, 1024], F32)
        fTr = cx.sbring("d_fT", 2, [128, 8, T2], BF16)
        actr = cx.sbring("d_act", 1, [128, NF, T2], BF16)
        outr = cx.sbring("d_out", 1, [128, 2, 1024], F32)
        ssr2 = cx.sbring("d_ss", 4, [128, 2], F32)
        junk2 = cx.sb("junk2", [128, 1024], BF16)
        e2r = cx.sbring("d_e", 3, [128, T2], F32)
        a2r = cx.sbring("d_a", 3, [128, T2], F32)
        if debug:
            print("C2 sbuf remaining", nc.sbuf_bytes_remaining)
        def c2_pre(ti):
                l0 = ti * T2
                h1 = h1r.next()
                k.dma("sp", h1[:], h1_d[l0:l0 + T2, :].rearrange("(j p) d -> p j d", p=128), r=[R_h1], w=[h1.res], dres=h1.res)
                ss = ssr2.next()
                for j in range(2):
                    k.ins("act", lambda j=j: nc.scalar.activation(out=junk2[:], in_=h1[:, j, :], func=AF.Square, accum_out=ss[:, j:j + 1]),
                          r=[h1.res], w=[junk2.res, ss.res])
                k.ins("act", lambda: nc.scalar.activation(out=ss[:], in_=ss[:], func=AF.Ln, scale=1.0 / D, bias=epsc[:, 0:1]),
                      r=[ss.res, epsc.res], w=[ss.res])
                k.ins("act", lambda: nc.scalar.activation(out=ss[:], in_=ss[:], func=AF.Exp, scale=-0.5), r=[ss.res], w=[ss.res])
                fT = fTr.next()
                for j in range(2):
                    xn = xnr.next()
                    k.ins("act", lambda j=j, xn=xn: nc.scalar.activation(out=xn[:], in_=h1[:, j, :], func=AF.Identity, scale=ss[:, j:j + 1]),
                          r=[h1.res, ss.res], w=[xn.res])
                    for half in range(2):
                        pb = banks.next()
                        k.group("pe", [(lambda q=q, xn=xn, pb=pb: nc.tensor.transpose(
                            pb[:, q * 128:(q + 1) * 128], xn[:, (half * 4 + q) * 128:(half * 4 + q + 1) * 128], ident[:]))
                            for q in range(4)], r=[xn.res, ident.res], w=[pb.res])
                        for q in range(4):
                            kt = half * 4 + q
                            k.ins("act", lambda q=q, kt=kt, pb=pb, j=j: nc.scalar.activation(
                                out=fT[:, kt, j * 128:(j + 1) * 128], in_=pb[:, q * 128:(q + 1) * 128], func=AF.Identity,
                                scale=S2[:, kt:kt + 1], bias=modf[:, 0, 3, kt:kt + 1]),
                                r=[pb.res, S2.res, modf.res], w=[fT.res])
                return h1, fT

        nxt2 = c2_pre(0)
        for ti in range(NLAT // T2):
            l0 = ti * T2
            h1, fT = nxt2
            act = actr.next()
            for ft in range(NF):
                if ft == 10 and ti + 1 < NLAT // T2:
                    nxt2 = c2_pre(ti + 1)
                pgt = banks.next()
                k.group("pe", [(lambda kt=kt: nc.tensor.matmul(pgt[:, 0:T2], wfi[:, kt, ft * 128:(ft + 1) * 128], fT[:, kt, :],
                                                               start=(kt == 0), stop=(kt == 7))) for kt in range(8)],
                        r=[wfi.res, fT.res], w=[pgt.res])
                pup = banks.next()
                k.group("pe", [(lambda kt=kt: nc.tensor.matmul(pup[:, 0:T2], wfi[:, kt, DFF + ft * 128:DFF + (ft + 1) * 128], fT[:, kt, :],
                                                               start=(kt == 0), stop=(kt == 7))) for kt in range(8)],
                        r=[wfi.res, fT.res], w=[pup.res])
                ez = e2r.next()
                k.ins("act", lambda: nc.scalar.activation(out=ez[:], in_=pgt[:, 0:T2], func=AF.Exp, scale=-1.0), r=[pgt.res], w=[ez.res])
                k.ins("dve", lambda: nc.vector.tensor_scalar(out=ez[:], in0=ez[:], scalar1=1.0, scalar2=None, op0=ALU.add),
                      r=[ez.res], w=[ez.res])
                k.ins("dve", lambda: nc.vector.reciprocal(out=ez[:], in_=ez[:]), r=[ez.res], w=[ez.res])
                a1 = a2r.next()
                k.ins("dve", lambda: nc.vector.tensor_tensor(out=a1[:], in0=pgt[:, 0:T2], in1=ez[:], op=ALU.mult),
                      r=[pgt.res, ez.res], w=[a1.res])
                k.ins("dve", lambda: nc.vector.tensor_tensor(out=act[:, ft, :], in0=pup[:, 0:T2], in1=a1[:], op=ALU.mult),
                      r=[pup.res, a1.res], w=[act.res])
            ot = outr.next()
            ss2 = ssr2.next()
            for j in range(2):
                for cg in range(2):
                    csl = slice(cg * 512, (cg + 1) * 512)
                    pf = banks.next()
                    k.group("pe", [(lambda ft=ft: nc.tensor.matmul(pf[:], act[:, ft, j * 128:(j + 1) * 128], wfo[:, ft, csl],
                                                                   start=(ft == 0), stop=(ft == NF - 1))) for ft in range(NF)],
                            r=[act.res, wfo.res], w=[pf.res])
                    k.ins("dve", lambda: nc.vector.tensor_tensor(out=ot[:, j, csl], in0=pf[:], in1=g2B[:, csl], op=ALU.mult),
                          r=[pf.res, g2B.res], w=[ot.res])
                    k.ins("pool", lambda: nc.gpsimd.tensor_tensor(out=ot[:, j, csl], in0=ot[:, j, csl], in1=h1[:, j, csl], op=ALU.add),
                          r=[ot.res, h1.res], w=[ot.res])
                k.ins("act", lambda j=j: nc.scalar.activation(out=junk2[:], in_=ot[:, j, :], func=AF.Square, accum_out=ss2[:, j:j + 1]),
                      r=[ot.res], w=[junk2.res, ss2.res])
            k.ins("act", lambda: nc.scalar.activation(out=ss2[:], in_=ss2[:], func=AF.Ln, scale=1.0 / D, bias=epsc[:, 0:1]),
                  r=[ss2.res, epsc.res], w=[ss2.res])
            k.ins("act", lambda: nc.scalar.activation(out=ss2[:], in_=ss2[:], func=AF.Exp, scale=-0.5), r=[ss2.res], w=[ss2.res])
            for j in range(2):
                k.ins("dve", lambda j=j: nc.vector.scalar_tensor_tensor(out=ot[:, j, :], in0=ot[:, j, :], scalar=ss2[:, j:j + 1],
                                                                        in1=nfB[:], op0=ALU.mult, op1=ALU.mult),
                      r=[ot.res, ss2.res, nfB.res], w=[ot.res])
            k.dma("sp", out_d[l0:l0 + T2, :].rearrange("(j p) d -> p j d", p=128), ot[:], r=[ot.res], w=[R_out], dres=ot.res)
        cx.pop()

    k.wait_all("sp", [R_scr, R_mod, R_oT, R_st1, R_st2, R_y, R_h1, R_out] + R_o1 + R_y1)
    return nc


def _consts():
    p = np.arange(128)[:, None]
    f = np.arange(128)[None, :]
    U = (p <= f).astype(np.float32)
    Lo = (p >= f).astype(np.float32)
    tri = np.stack([U, Lo, -U, -Lo], axis=1)
    NEG = -30000.0
    m = lambda ok: np.where(ok, 0.0, NEG).astype(np.float32)
    msk = np.stack([m(p > f), m(p < f), m(p >= f), m(p <= f)], axis=1)
    return np.ascontiguousarray(tri), np.ascontiguousarray(msk)


TRI, MSK = _consts()


def host_prepare(inputs, core):
    b, s = core // 2, core % 2
    x = inputs["x"][b, s * NLAT:(s + 1) * NLAT]
    ctx = inputs["ctx"][b]
    if s == 1:
        x = x[::-1]
        ctx = ctx[::-1]
    xin = np.ascontiguousarray(np.concatenate([ctx, x], axis=0))
    x2 = inputs["x"][b, (1 - s) * NLAT:(2 - s) * NLAT]
    ctx2 = inputs["ctx"][b]
    if s == 0:
        x2 = x2[::-1]
        ctx2 = ctx2[::-1]
    xin2 = np.ascontiguousarray(np.concatenate([ctx2, x2], axis=0))
    cv = np.stack([inputs["c"][b], inputs["c_ctx"]], axis=-1)
    cvec = np.ascontiguousarray(cv.reshape(8, 128, 2).transpose(1, 0, 2))
    d1, d2 = (0, 1) if s == 0 else (1, 0)
    w = inputs["w_in"][0]
    offs = np.cumsum([0, 3072, 1024, 16, 16, 2048, 3072, 64, 2048])
    qkv = w[:, offs[0]:offs[1]]
    zg = w[:, offs[1]:offs[2]]
    a_ = w[:, offs[2]:offs[3]].reshape(D, 2, 8)
    b_ = w[:, offs[3]:offs[4]].reshape(D, 2, 8)
    zs = w[:, offs[4]:offs[5]]
    xbc = w[:, offs[5]:offs[6]]
    dt_ = w[:, offs[6]:offs[7]].reshape(D, 2, 32)
    gate = w[:, offs[7]:offs[8]]
    z16 = np.zeros((D, 16), np.float32)
    small = np.concatenate([a_[:, d1], a_[:, d2], z16, b_[:, d1], b_[:, d2], z16, dt_[:, d1], dt_[:, d2]], axis=1)
    w_perm = np.ascontiguousarray(np.concatenate([qkv, xbc, gate, small, zg, zs], axis=1))
    assert w_perm.shape[1] == W_IN_COLS
    cw = np.concatenate([inputs["gdn_conv_w"][0], inputs["ssm_conv_w"][0]], axis=1)
    cbias = np.concatenate([inputs["gdn_conv_b"][0], inputs["ssm_conv_b"][0]], axis=0)
    mk = lambda w_: np.ascontiguousarray(np.concatenate([w_, cbias[None]], axis=0).reshape(4, 48, 128).transpose(2, 1, 0))
    convp = mk(cw[::-1] if s == 1 else cw)
    convp2 = mk(cw[::-1] if s == 0 else cw)
    smallp = np.zeros((128, 4), np.float32)
    smallp[:, 0] = 1.0
    smallp[32:64, 0] = -1.0
    gb = inputs["gdn_dt_bias"][0]
    sbias = inputs["ssm_dt_bias"][0]
    smallp[0:8, 1] = gb[d1]; smallp[8:16, 1] = gb[d2]
    smallp[64:96, 1] = sbias[d1]; smallp[96:128, 1] = sbias[d2]
    ga = inputs["gdn_a_log"][0]
    smallp[0:8, 2] = ga[d1]; smallp[8:16, 2] = ga[d2]
    smallp[:, 3] = -1.0
    smallp[64:128, 3] = 1.0
    n1w = np.ascontiguousarray(inputs["norm1_w"][0].reshape(8, 128).T)
    return {
        "xin": xin, "xin2": xin2, "convp2": convp2, "cvec": cvec, "ada_w": np.ascontiguousarray(inputs["ada_w"][0]),
        "ada_b": np.ascontiguousarray(inputs["ada_b"][0][None]), "w_in": w_perm, "convp": convp,
        "smallp": smallp, "n1w": n1w, "ident": np.eye(128, dtype=np.float32),
        "tri": TRI, "msk": MSK,
        "w_brg": np.ascontiguousarray(inputs["w_br_gdn"][0]), "w_brs": np.ascontiguousarray(inputs["w_br_ssm"][0]),
        "w_o": np.ascontiguousarray(inputs["w_out"][0]), "w_fi": np.ascontiguousarray(inputs["w_ffn_in"][0]),
        "w_fo": np.ascontiguousarray(inputs["w_ffn_out"][0]),
        "gnw": np.ascontiguousarray(inputs["gdn_norm_w"][0].reshape(128, 1)),
        "dskip": np.ascontiguousarray(np.repeat(inputs["ssm_d"][0], 64)[None]),
        "snw": np.ascontiguousarray(inputs["ssm_norm_w"][0][None]),
        "n2w": np.ascontiguousarray(inputs["norm2_w"][0].reshape(8, 128).T),
        "nfw": np.ascontiguousarray(inputs["norm_f_w"][None]),
        "salog": np.ascontiguousarray(np.broadcast_to(inputs["ssm_a_log"][0][[d1, d2]][None], (128, 2, 32))),
    }


def kernel(**inputs):
    inputs = {k_: np.asarray(v) for k_, v in inputs.items()}
    nc = build_program()
    in_maps = [host_prepare(inputs, c) for c in range(8)]
    res = run_bass_kernel_spmd(nc, in_maps, core_ids=list(range(8)))
    out = np.zeros((4, 8192, D), np.float32)
    for c in range(8):
        b, s = c // 2, c % 2
        o = res.results[c]["out"]
        if s == 1:
            o = o[::-1]
        out[b, s * NLAT:(s + 1) * NLAT] = o
    return out
```

```python
import numpy as np
import ml_dtypes
from contextlib import ExitStack
import concourse.bass as bass
import concourse.mybir as mybir
from concourse.bass_utils import run_bass_kernel_spmd

F32 = mybir.dt.float32
BF16 = mybir.dt.bfloat16
AF = mybir.ActivationFunctionType
ALU = mybir.AluOpType
AX = mybir.AxisListType

D = 1024
NLAT = 4096
NCTX = 256
NTOK = NLAT + NCTX
C = 128
NCH = NTOK // C
EPS = 1e-6
DFF = 2816
O_QKV, O_XBC, O_GATE, O_SMALL, O_ZG, O_ZS = 0, 3072, 6144, 8192, 8320, 9344
W_IN_COLS = 11392
SEM_LIMIT = 30000
GDN_LIMIT = 0


class Sem:
    __slots__ = ("h", "count", "dma", "name")

    def __init__(self, h, dma, name):
        self.h, self.count, self.dma, self.name = h, 0, dma, name


class Res:
    __slots__ = ("name", "w", "r", "dsem")

    def __init__(self, name):
        self.name, self.w, self.r, self.dsem = name, None, {}, None


class Eng:
    def __init__(self, name, e, same_wait):
        self.name, self.e, self.sem, self.waited, self.same_wait = name, e, None, {}, same_wait
        self.nsem = 0


class K:
    def __init__(self, nc):
        self.nc = nc
        self.eng = {
            "pe": Eng("pe", nc.tensor, False),
            "act": Eng("act", nc.scalar, True),
            "dve": Eng("dve", nc.vector, True),
            "pool": Eng("pool", nc.gpsimd, True),
            "sp": Eng("sp", nc.sync, False),
        }
        self.nsem = 0
        self.ninst = 0
        self.all_sems = []
        self.free_dma = []

    def new_sem(self, dma, name):
        if dma and self.free_dma:
            self.free_dma.sort(key=lambda x: x.count)
            return self.free_dma.pop(0)
        self.nsem += 1
        h = self.nc.alloc_semaphore(f"s{self.nsem}_{name}")
        sm = Sem(h, dma, name)
        self.all_sems.append(sm)
        return sm

    def res(self, name):
        return Res(name)

    def _cur_sem(self, E):
        if E.sem is None or E.sem.count >= SEM_LIMIT:
            E.nsem += 1
            E.sem = self.new_sem(False, f"{E.name}{E.nsem}")
        return E.sem

    def _wait(self, E, evs):
        need = {}
        for (sem, val) in evs:
            if sem.dma:
                val = sem.count
            if need.get(sem, 0) < val:
                need[sem] = val
        for sem, val in need.items():
            if (not E.same_wait) and (sem is E.sem) and not sem.dma:
                continue
            if E.waited.get(sem, 0) >= val:
                continue
            E.e.wait_ge(sem.h, val)
            E.waited[sem] = val

    def _deps(self, r, w):
        evs = []
        for x in r:
            if x.w is not None:
                evs.append(x.w)
        for x in w:
            if x.w is not None:
                evs.append(x.w)
            evs.extend(x.r.items())
        return evs

    def _record(self, ev, r, w):
        sem, val = ev
        for x in r:
            if x.r.get(sem, 0) < val:
                x.r[sem] = val
        for x in w:
            x.w = ev
            x.r = {}

    def ins(self, en, fn, r=(), w=()):
        E = self.eng[en]
        self._wait(E, self._deps(r, w))
        inst = fn()
        sem = self._cur_sem(E)
        sem.count += 1
        inst.then_inc(sem.h, 1)
        self._record((sem, sem.count), r, w)
        self.ninst += 1
        return inst

    def group(self, en, fns, r=(), w=()):
        E = self.eng[en]
        self._wait(E, self._deps(r, w))
        inst = None
        for fn in fns:
            inst = fn()
        sem = self._cur_sem(E)
        sem.count += 1
        inst.then_inc(sem.h, 1)
        self._record((sem, sem.count), r, w)
        self.ninst += len(fns)
        return inst

    def dma(self, q, out, in_, r=(), w=(), dres=None):
        E = self.eng[q]
        self._wait(E, self._deps(r, w))
        if dres.dsem is None:
            dres.dsem = self.new_sem(True, "d" + dres.name)
        sem = dres.dsem
        inst = E.e.dma_start(out=out, in_=in_)
        sem.count += 16
        inst.then_inc(sem.h, 16)
        self._record((sem, sem.count), r, w)
        self.ninst += 1
        return inst

    def dmaop(self, q, fn, r=(), w=(), dres=None):
        E = self.eng[q]
        self._wait(E, self._deps(r, w))
        if dres.dsem is None:
            dres.dsem = self.new_sem(True, "d" + dres.name)
        sem = dres.dsem
        inst = fn()
        sem.count += 16
        inst.then_inc(sem.h, 16)
        self._record((sem, sem.count), r, w)
        self.ninst += 1
        return inst

    def barrier(self):
        sems = list(self.all_sems)
        for E in self.eng.values():
            for sem in sems:
                if sem.count == 0 or E.waited.get(sem, 0) >= sem.count:
                    continue
                if (sem is E.sem) and not E.same_wait:
                    continue
                E.e.wait_ge(sem.h, sem.count)
                E.waited[sem] = sem.count

    def wait_all(self, en, ress):
        E = self.eng[en]
        evs = []
        for x in ress:
            if x.w is not None:
                evs.append(x.w)
            evs.extend(x.r.items())
        self._wait(E, evs)


class Buf:
    def __init__(self, k, t, name):
        self.t, self.res, self.name = t, k.res(name), name

    def __getitem__(self, idx):
        return self.t[idx]


class Ring:
    def __init__(self, bufs):
        self.bufs, self.i = bufs, 0

    def next(self):
        b = self.bufs[self.i % len(self.bufs)]
        self.i += 1
        return b


class Ctx:
    def __init__(self, nc):
        self.nc = nc
        self.k = K(nc)
        self.stack = [ExitStack()]
        self.uid = 0
        self.phase_bufs = [[]]

    def push(self):
        self.stack.append(ExitStack())
        self.phase_bufs.append([])

    def pop(self):
        self.k.barrier()
        for b in self.phase_bufs.pop():
            if b.res.dsem is not None:
                self.k.free_dma.append(b.res.dsem)
                b.res.dsem = None
        self.stack.pop().close()

    def sb(self, name, shape, dtype):
        self.uid += 1
        t = self.stack[-1].enter_context(self.nc.sbuf_tensor(f"sb{self.uid}_{name}", list(shape), dtype))
        b = Buf(self.k, t, name)
        self.phase_bufs[-1].append(b)
        return b

    def psum(self, name, shape, dtype):
        self.uid += 1
        t = self.stack[-1].enter_context(self.nc.psum_tensor(f"ps{self.uid}_{name}", list(shape), dtype))
        return Buf(self.k, t, name)

    def sbring(self, name, n, shape, dtype):
        return Ring([self.sb(f"{name}{i}", shape, dtype) for i in range(n)])

    def dram(self, name, shape, dtype, kind="Internal"):
        t = self.nc.dram_tensor(name, list(shape), dtype, kind=kind)
        return t.ap()


def build_program(debug=False, phases=("0", "A", "R", "B1", "S1", "B2", "S2", "C1", "C2"), n_cores=8):
    nc = bass.Bass("TRN2", target_bir_lowering=False)
    cx = Ctx(nc)
    k = cx.k
    dk = "ExternalOutput" if debug else "Internal"

    xin = cx.dram("xin", [NTOK, D], F32, "ExternalInput")
    cvec = cx.dram("cvec", [128, 8, 2], F32, "ExternalInput")
    ada_w = cx.dram("ada_w", [D, 6 * D], F32, "ExternalInput")
    ada_b = cx.dram("ada_b", [1, 6 * D], F32, "ExternalInput")
    w_in = cx.dram("w_in", [D, W_IN_COLS], F32, "ExternalInput")
    xin2 = cx.dram("xin2", [NTOK, D], F32, "ExternalInput")
    convp2 = cx.dram("convp2", [128, 48, 4], F32, "ExternalInput")
    convp = cx.dram("convp", [128, 48, 4], F32, "ExternalInput")
    smallp = cx.dram("smallp", [128, 4], F32, "ExternalInput")
    n1w = cx.dram("n1w", [128, 8], F32, "ExternalInput")
    ident_d = cx.dram("ident", [128, 128], F32, "ExternalInput")
    wbg_d = cx.dram("w_brg", [1024, 1024], F32, "ExternalInput")
    wbs_d = cx.dram("w_brs", [2048, 1024], F32, "ExternalInput")
    wo_d = cx.dram("w_o", [1024, 1024], F32, "ExternalInput")
    wfi_d = cx.dram("w_fi", [1024, 2 * DFF], F32, "ExternalInput")
    wfo_d = cx.dram("w_fo", [DFF, 1024], F32, "ExternalInput")
    gnw_d = cx.dram("gnw", [128, 1], F32, "ExternalInput")
    dskip_d = cx.dram("dskip", [1, 2048], F32, "ExternalInput")
    snw_d = cx.dram("snw", [1, 2048], F32, "ExternalInput")
    n2w_d = cx.dram("n2w", [128, 8], F32, "ExternalInput")
    nfw_d = cx.dram("nfw", [1, 1024], F32, "ExternalInput")
    salog_d = cx.dram("salog", [128, 2, 32], F32, "ExternalInput")
    tri_d = cx.dram("tri", [128, 4, 128], F32, "ExternalInput")
    msk_d = cx.dram("msk", [128, 4, 128], F32, "ExternalInput")
    out_d = cx.dram("out", [NLAT, D], F32, "ExternalOutput")

    mod_d = cx.dram("mod_d", [2, 6 * D], F32, dk)
    qT_d = cx.dram("qT_d", [8, 128, NTOK], BF16, dk)
    kT_d = cx.dram("kT_d", [8, 128, NTOK], BF16, dk)
    k_d = cx.dram("k_d", [NTOK, 1024], BF16, dk)
    v_d = cx.dram("v_d", [NTOK, 1024], BF16, dk)
    x_d = cx.dram("x_d", [NTOK, 2048], BF16, dk)
    BT_d = cx.dram("BT_d", [4, 128, NTOK], BF16, dk)
    CT_d = cx.dram("CT_d", [4, 128, NTOK], BF16, dk)
    B_d = cx.dram("B_d", [NTOK, 512], BF16, dk)
    sm_d = cx.dram("sm_d", [NTOK, 128], F32, dk)
    kT2_d = cx.dram("kT2_d", [8, 128, NTOK], BF16)
    k2_d = cx.dram("k2_d", [NTOK, 1024], BF16)
    v2_d = cx.dram("v2_d", [NTOK, 1024], BF16)
    x2_d = cx.dram("x2_d", [NTOK, 2048], BF16)
    B2_d = cx.dram("B2_d", [NTOK, 512], BF16)
    sm2_d = cx.dram("sm2_d", [NTOK, 128], F32)
    sgT_d = cx.dram("sgT_d", [16, 128, NLAT], F32, dk)
    aT_d = cx.dram("aT_d", [8, 128, NLAT], BF16, dk)
    o1_d = cx.dram("o1_d", [8, 128, NLAT], F32, dk)
    oT_d = cx.dram("oT_d", [8, 128, NLAT], F32, dk)
    st1_d = cx.dram("st1_d", [128, 3072], F32)
    st2_d = cx.dram("st2_d", [128, 3072], F32)
    y1_d = cx.dram("y1_d", [NLAT, 2048], F32, dk)
    y_d = cx.dram("y_d", [NLAT, 2048], F32, dk)
    R_y1 = [k.res(f"y1_{c}") for c in range(NCH)]
    R_y = k.res("y")
    R_o1 = [k.res(f"o1_{c}") for c in range(NCH)]
    R_oT = k.res("oT")
    R_st1 = k.res("st1")
    R_st2 = k.res("st2")
    h1_d = cx.dram("h1_d", [NLAT, D], F32, dk)
    R_h1 = k.res("h1")
    R_out = k.res("out")
    R_scr = k.res("scratchA")
    R_mod = k.res("mod_d")

    ident = cx.sb("ident", [128, 128], F32)
    identb = cx.sb("identb", [128, 128], BF16)
    onesf = cx.sb("onesf", [128, 128], F32)
    k.dma("sp", ident[:], ident_d, w=[ident.res], dres=ident.res)
    k.ins("dve", lambda: nc.vector.tensor_copy(identb[:], ident[:]), r=[ident.res], w=[identb.res])
    k.ins("dve", lambda: nc.vector.memset(onesf[:], 1.0), w=[onesf.res])
    epsc = cx.sb("epsc", [128, 1], F32)
    k.ins("dve", lambda: nc.vector.memset(epsc[:], EPS), w=[epsc.res])


    if "0" in phases:
        cx.push()
        ps = [cx.psum(f"ps{i}", [128, 512], F32) for i in range(2)]
        cv = cx.sb("cv", [128, 8, 2], F32)
        cvs = cx.sb("cvs", [128, 8, 2], F32)
        k.dma("sp", cv[:], cvec, w=[cv.res], dres=cv.res)
        k.ins("act", lambda: nc.scalar.activation(out=cvs[:], in_=cv[:], func=AF.Exp, scale=-1.0), r=[cv.res], w=[cvs.res])
        k.ins("dve", lambda: nc.vector.tensor_scalar(out=cvs[:], in0=cvs[:], scalar1=1.0, scalar2=None, op0=ALU.add),
              r=[cvs.res], w=[cvs.res])
        k.ins("dve", lambda: nc.vector.reciprocal(out=cvs[:], in_=cvs[:]), r=[cvs.res], w=[cvs.res])
        k.ins("dve", lambda: nc.vector.tensor_tensor(out=cvs[:], in0=cvs[:], in1=cv[:], op=ALU.mult),
              r=[cvs.res, cv.res], w=[cvs.res])
        adab = cx.sb("adab", [2, 6 * D], F32)
        k.dma("sp", adab[0:1, :], ada_b, w=[adab.res], dres=adab.res)
        k.dma("sp", adab[1:2, :], ada_b, w=[adab.res], dres=adab.res)
        modsb = cx.sb("modsb", [2, 6 * D], F32)
        awring = cx.sbring("aw", 2, [128, 8, 512], F32)
        for cg in range(12):
            aw = awring.next()
            k.dma("sp", aw[:], ada_w[:, cg * 512:(cg + 1) * 512].rearrange("(kt p) n -> p kt n", p=128),
                  w=[aw.res], dres=aw.res)
            pb = ps[cg % 2]
            k.group("pe", [
                (lambda kt=kt, aw=aw, pb=pb: nc.tensor.matmul(pb[0:2, :], cvs[:, kt, :], aw[:, kt, :],
                                                               start=(kt == 0), stop=(kt == 7)))
                for kt in range(8)], r=[cvs.res, aw.res], w=[pb.res])
            k.ins("dve", lambda cg=cg, pb=pb: nc.vector.tensor_tensor(
                out=modsb[:, cg * 512:(cg + 1) * 512], in0=pb[0:2, :], in1=adab[:, cg * 512:(cg + 1) * 512],
                op=ALU.add), r=[pb.res, adab.res], w=[modsb.res])
        k.dma("sp", mod_d, modsb[:], r=[modsb.res], w=[R_mod], dres=modsb.res)
        cx.pop()

    modf = cx.sb("modf", [128, 2, 6, 8], F32)
    with nc.allow_non_contiguous_dma("small modulation vector relayout"):
        for r_ in range(2):
            k.dma("sp", modf[:, r_, :, :], mod_d[r_, :].rearrange("(j kt p) -> p j kt", p=128, kt=8),
                  r=[R_mod], w=[modf.res], dres=modf.res)
    n1 = cx.sb("n1", [128, 8], F32)
    k.dma("sp", n1[:], n1w, w=[n1.res], dres=n1.res)
    S1 = cx.sb("S1", [128, 2, 8], F32)
    for r_ in range(2):
        k.ins("dve", lambda r_=r_: nc.vector.scalar_tensor_tensor(
            out=S1[:, r_, :], in0=modf[:, r_, 1, :], scalar=1.0, in1=n1[:], op0=ALU.add, op1=ALU.mult),
            r=[modf.res, n1.res], w=[S1.res])

    if "A" in phases:
        cx.push()
        ps = [cx.psum(f"ps{i}", [128, 512], F32) for i in range(6)]
        NWC = O_ZG
        wA = cx.sb("wA", [128, 8, NWC], BF16)
        for kt in range(8):
            for c0 in range(0, NWC, 2080):
                k.dma("pool", wA[:, kt, c0:c0 + 2080], w_in[kt * 128:(kt + 1) * 128, c0:c0 + 2080],
                      w=[wA.res], dres=wA.res)
        cp = cx.sb("cp", [128, 48, 4], F32)
        k.dma("sp", cp[:], convp, w=[cp.res], dres=cp.res)
        smp = cx.sb("smp", [128, 4], F32)
        k.dma("sp", smp[:], smallp, w=[smp.res], dres=smp.res)
        smult = cx.sb("smult", [128, 1], F32)
        k.ins("act", lambda: nc.scalar.activation(out=smult[:], in_=smp[:, 2:3], func=AF.Exp),
              r=[smp.res], w=[smult.res])
        k.ins("dve", lambda: nc.vector.tensor_tensor(out=smult[:], in0=smult[:], in1=smp[:, 3:4], op=ALU.mult),
              r=[smp.res, smult.res], w=[smult.res])

        TT = 256
        xring = cx.sbring("xt", 1, [128, 2, D], F32)
        xnring = cx.sbring("xn", 1, [128, D], F32)
        junk = cx.sb("junk", [128, D], BF16)
        ssr = cx.sbring("ss", 2, [128, 2], F32)
        aTring = cx.sbring("aT", 2, [128, 8, TT], BF16)
        cring = cx.sbring("cv_", 6, [128, TT], F32)
        ering = cx.sbring("ee_", 4, [128, TT], F32)
        sfring = cx.sbring("sf_", 5, [128, TT], F32)
        sqring = cx.sbring("sq_", 3, [128, TT], F32)
        rsring = cx.sbring("rs_", 3, [128, TT], F32)
        sring = cx.sbring("so_", 8, [128, TT], BF16)
        sgring = cx.sbring("sg_", 4, [128, TT], F32)
        smring = cx.sbring("smf", 2, [128, TT], F32)
        ktok = cx.sbring("ktok", 1, [128, 2, 1024], BF16)
        vtok = cx.sbring("vtok", 1, [128, 2, 1024], BF16)
        xtok = cx.sbring("xtok", 1, [128, 2, 2048], BF16)
        btok = cx.sbring("btok", 1, [128, 2, 512], BF16)
        smtok = cx.sbring("smtok", 1, [128, 2, 128], F32)
        pst = ps[0]
        psa = Ring([ps[1], ps[2], ps[3], ps[4]])
        pss = ps[5]
        pso = Ring([cx.psum(f"pso{i}", [128, 1024], BF16) for i in range(2)])
        own = dict(qT_d=qT_d, kT_d=kT_d, k_d=k_d, v_d=v_d, x_d=x_d, BT_d=BT_d, CT_d=CT_d, B_d=B_d, sm_d=sm_d)
        par = dict(qT_d=None, kT_d=kT2_d, k_d=k2_d, v_d=v2_d, x_d=x2_d, BT_d=None, CT_d=None, B_d=B2_d, sm_d=sm2_d)
        cp2 = cx.sb("cp2", [128, 48, 4], F32)
        k.dma("sp", cp2[:], convp2, w=[cp2.res], dres=cp2.res)
        runs = [(False, xin, cp, own)]
        if "R" in phases:
            runs.append((True, xin2, cp2, par))

        def preamble(red, xsrc, ti):
            t0 = ti * TT
            is_ctx = ti == 0
            mr = 1 if is_ctx else 0
            xt = xring.next()
            k.dma("sp", xt[:], xsrc[t0:t0 + TT, :].rearrange("(j p) d -> p j d", p=128), w=[xt.res], dres=xt.res)
            ss = ssr.next()
            for j in range(2):
                k.ins("act", lambda j=j: nc.scalar.activation(out=junk[:], in_=xt[:, j, :], func=AF.Square,
                                                              accum_out=ss[:, j:j + 1]), r=[xt.res], w=[junk.res, ss.res])
            k.ins("act", lambda: nc.scalar.activation(out=ss[:], in_=ss[:], func=AF.Ln, scale=1.0 / D, bias=epsc[:, 0:1]),
                  r=[ss.res, epsc.res], w=[ss.res])
            k.ins("act", lambda: nc.scalar.activation(out=ss[:], in_=ss[:], func=AF.Exp, scale=-0.5), r=[ss.res], w=[ss.res])
            aT = aTring.next()
            for j in range(2):
                xn = xnring.next()
                k.ins("act", lambda j=j, xn=xn: nc.scalar.activation(out=xn[:], in_=xt[:, j, :], func=AF.Identity,
                                                                     scale=ss[:, j:j + 1]), r=[xt.res, ss.res], w=[xn.res])
                for half in range(2):
                    k.group("pe", [(lambda q=q, xn=xn, half=half: nc.tensor.transpose(
                        pst[:, q * 128:(q + 1) * 128], xn[:, (half * 4 + q) * 128:(half * 4 + q + 1) * 128], ident[:]))
                        for q in range(4)], r=[xn.res, ident.res], w=[pst.res])
                    for q in range(4):
                        kt = half * 4 + q
                        k.ins("act", lambda q=q, kt=kt, j=j: nc.scalar.activation(
                            out=aT[:, kt, j * 128:(j + 1) * 128], in_=pst[:, q * 128:(q + 1) * 128],
                            func=AF.Identity, scale=S1[:, mr, kt:kt + 1], bias=modf[:, mr, 0, kt:kt + 1]),
                            r=[pst.res, S1.res, modf.res], w=[aT.res])
            if not is_ctx and not red:
                l0 = t0 - NCTX
                k.dma("sp", aT_d[:, :, l0:l0 + TT].rearrange("kt p t -> p kt t"), aT[:], r=[aT.res], w=[R_scr], dres=aT.res)
            return dict(aT=aT, t0=t0, is_ctx=is_ctx, red=red)

        def make_job(tc, ct, cpt, DD, toks):
            aT, t0, is_ctx, red = tc["aT"], tc["t0"], tc["is_ctx"], tc["red"]
            l0 = t0 - NCTX
            rowlen = 256 if is_ctx else 64
            J = {}
            steps = []

            def s_mm():
                J["pa"] = pa = psa.next()
                k.group("pe", [(lambda kt=kt: nc.tensor.matmul(pa[:, 0:TT], wA[:, kt, ct * 128:(ct + 1) * 128], aT[:, kt, :],
                                                               start=(kt == 0), stop=(kt == 7))) for kt in range(8)],
                        r=[wA.res, aT.res], w=[pa.res])
            steps.append(s_mm)
            if ct < 48:
                def s_ident():
                    pa = J["pa"]
                    J["cb"] = cb = cring.next()
                    k.ins("dve", lambda: nc.vector.tensor_scalar(out=cb[:], in0=pa[:, 0:TT], scalar1=cpt[:, ct, 1:2],
                                                              scalar2=cpt[:, ct, 3:4], op0=ALU.mult, op1=ALU.add),
                          r=[pa.res, cpt.res], w=[cb.res])

                def s_taps():
                    pa, cb = J["pa"], J["cb"]
                    pv = pa[:, 0:TT].rearrange("p (r t) -> p r t", t=rowlen)
                    cv3 = cb[:].rearrange("p (r t) -> p r t", t=rowlen)
                    k.ins("dve", lambda: nc.vector.scalar_tensor_tensor(
                        out=cv3[:, :, 1:], in0=pv[:, :, 0:rowlen - 1], scalar=cpt[:, ct, 0:1], in1=cv3[:, :, 1:],
                        op0=ALU.mult, op1=ALU.add), r=[pa.res, cpt.res, cb.res], w=[cb.res])
                    k.ins("dve", lambda: nc.vector.scalar_tensor_tensor(
                        out=cv3[:, :, 0:rowlen - 1], in0=pv[:, :, 1:], scalar=cpt[:, ct, 2:3], in1=cv3[:, :, 0:rowlen - 1],
                        op0=ALU.mult, op1=ALU.add), r=[pa.res, cpt.res, cb.res], w=[cb.res])

                def s_exp():
                    cb = J["cb"]
                    J["ee"] = ee = ering.next()
                    k.ins("act", lambda: nc.scalar.activation(out=ee[:], in_=cb[:], func=AF.Exp, scale=-1.0), r=[cb.res], w=[ee.res])

                def s_recip():
                    ee = J["ee"]
                    k.ins("act", lambda: nc.scalar.activation(out=ee[:], in_=ee[:], func=AF.Ln, bias=1.0), r=[ee.res], w=[ee.res])

                def s_recip2():
                    ee = J["ee"]
                    k.ins("act", lambda: nc.scalar.activation(out=ee[:], in_=ee[:], func=AF.Exp, scale=-1.0), r=[ee.res], w=[ee.res])

                def s_mult():
                    cb, ee = J["cb"], J["ee"]
                    if ct < 16:
                        J["sf"] = sf = sfring.next()
                        k.ins("pool", lambda: nc.gpsimd.tensor_tensor(out=sf[:], in0=cb[:], in1=ee[:], op=ALU.mult),
                              r=[cb.res, ee.res], w=[sf.res])
                    else:
                        J["so"] = so = sring.next()
                        k.ins("pool", lambda: nc.gpsimd.tensor_tensor(out=so[:], in0=cb[:], in1=ee[:], op=ALU.mult),
                              r=[cb.res, ee.res], w=[so.res])
                steps.extend([s_ident, s_taps, s_exp, s_recip, s_recip2, s_mult])
                if ct < 16:
                    def s_sq():
                        sf = J["sf"]
                        J["sq"] = sq = sqring.next()
                        k.ins("pool", lambda: nc.gpsimd.tensor_tensor(out=sq[:], in0=sf[:], in1=sf[:], op=ALU.mult),
                              r=[sf.res], w=[sq.res])

                    def s_sum():
                        sq = J["sq"]
                        k.ins("pe", lambda: nc.tensor.matmul(pss[:, 0:TT], onesf[:], sq[:], start=True, stop=True),
                              r=[onesf.res, sq.res], w=[pss.res])
                        J["rs"] = rs = rsring.next()
                        k.ins("act", lambda: nc.scalar.activation(out=rs[:], in_=pss[:, 0:TT], func=AF.Ln, bias=epsc[:, 0:1]),
                              r=[pss.res, epsc.res], w=[rs.res])

                    def s_rs():
                        rs = J["rs"]
                        k.ins("act", lambda: nc.scalar.activation(out=rs[:], in_=rs[:], func=AF.Exp, scale=-0.5),
                              r=[rs.res], w=[rs.res])

                    def s_norm():
                        sf, rs = J["sf"], J["rs"]
                        J["so"] = so = sring.next()
                        qscale = (128 ** -0.5) if ct < 8 else 1.0
                        k.ins("dve", lambda: nc.vector.scalar_tensor_tensor(out=so[:], in0=sf[:], scalar=qscale, in1=rs[:],
                                                                            op0=ALU.mult, op1=ALU.mult),
                              r=[sf.res, rs.res], w=[so.res])
                    steps.extend([s_sq, s_sum, s_rs, s_norm])

                def s_out():
                    so = J["so"]
                    if ct < 8:
                        k.dma("sp", DD["qT_d"][ct, :, t0:t0 + TT], so[:], r=[so.res], w=[R_scr], dres=so.res)
                    elif ct < 16:
                        k.dma("sp", DD["kT_d"][ct - 8, :, t0:t0 + TT], so[:], r=[so.res], w=[R_scr], dres=so.res)
                    elif 40 <= ct < 44 and not red:
                        k.dma("sp", DD["BT_d"][ct - 40, :, t0:t0 + TT], so[:], r=[so.res], w=[R_scr], dres=so.res)
                    elif 44 <= ct < 48:
                        k.dma("sp", DD["CT_d"][ct - 44, :, t0:t0 + TT], so[:], r=[so.res], w=[R_scr], dres=so.res)
                    tgt = None
                    if 8 <= ct < 16:
                        tgt = (toks["k"], (ct - 8) * 128)
                    elif 16 <= ct < 24:
                        tgt = (toks["v"], (ct - 16) * 128)
                    elif 24 <= ct < 40:
                        tgt = (toks["x"], (ct - 24) * 128)
                    elif 40 <= ct < 44:
                        tgt = (toks["b"], (ct - 40) * 128)
                    J["tgt"] = tgt
                    if tgt is not None:
                        J["po"] = po = pso.next()
                        k.group("pe", [(lambda j=j: nc.tensor.transpose(po[:, j * 128:(j + 1) * 128], so[:, j * 128:(j + 1) * 128],
                                                                        identb[:])) for j in range(2)],
                                r=[so.res, identb.res], w=[po.res])

                def s_tcopy():
                    if J["tgt"] is not None:
                        tb, off = J["tgt"]
                        po = J["po"]
                        k.ins("act", lambda: nc.scalar.copy(out=tb[:, :, off:off + 128],
                                                            in_=po[:, 0:256].rearrange("p (j c) -> p j c", c=128)),
                              r=[po.res], w=[tb.res])
                steps.extend([s_out, s_tcopy])
            elif ct < 64:
                def g_exp():
                    pa = J["pa"]
                    J["sg"] = sg = sgring.next()
                    k.ins("act", lambda: nc.scalar.activation(out=sg[:], in_=pa[:, 0:TT], func=AF.Exp, scale=-1.0),
                          r=[pa.res], w=[sg.res])

                def g_recip():
                    sg = J["sg"]
                    k.ins("act", lambda: nc.scalar.activation(out=sg[:], in_=sg[:], func=AF.Ln, bias=1.0), r=[sg.res], w=[sg.res])

                def g_recip2():
                    sg = J["sg"]
                    k.ins("act", lambda: nc.scalar.activation(out=sg[:], in_=sg[:], func=AF.Exp, scale=-1.0), r=[sg.res], w=[sg.res])

                def g_out():
                    sg = J["sg"]
                    k.dma("sp", sgT_d[ct - 48, :, l0:l0 + TT], sg[:], r=[sg.res], w=[R_scr], dres=sg.res)
                steps.extend([g_exp, g_recip, g_recip2, g_out])
            else:
                st_ = toks["sm"]

                def m_exp():
                    pa = J["pa"]
                    J["sm"] = sm = smring.next()
                    k.ins("act", lambda: nc.scalar.activation(out=sm[:], in_=pa[:, 0:TT], func=AF.Exp, scale=smp[:, 0:1],
                                                              bias=smp[:, 1:2]), r=[pa.res, smp.res], w=[sm.res])

                def m_ln():
                    sm = J["sm"]
                    k.ins("act", lambda: nc.scalar.activation(out=sm[:], in_=sm[:], func=AF.Ln, bias=1.0), r=[sm.res], w=[sm.res])

                def m_mul():
                    sm = J["sm"]
                    k.ins("dve", lambda: nc.vector.tensor_scalar(out=sm[:], in0=sm[:], scalar1=smult[:, 0:1], scalar2=None,
                                                              op0=ALU.mult), r=[sm.res, smult.res], w=[sm.res])

                def m_tr():
                    sm = J["sm"]
                    k.group("pe", [(lambda j=j: nc.tensor.transpose(pst[:, j * 128:(j + 1) * 128], sm[:, j * 128:(j + 1) * 128],
                                                                    ident[:])) for j in range(2)],
                            r=[sm.res, ident.res], w=[pst.res])

                def m_copy():
                    k.ins("dve", lambda: nc.vector.tensor_copy(st_[:], pst[:, 0:256].rearrange("p (j c) -> p j c", c=128)),
                          r=[pst.res], w=[st_.res])
                steps.extend([m_exp, m_ln, m_mul, m_tr, m_copy])
            return steps

        def make_spill(tc, DD, toks):
            t0 = tc["t0"]

            def spill():
                rows = lambda d_: d_[t0:t0 + TT, :].rearrange("(j p) c -> p j c", p=128)
                for key, dn in (("k", "k_d"), ("v", "v_d"), ("x", "x_d"), ("b", "B_d"), ("sm", "sm_d")):
                    tb = toks[key]
                    k.dma("sp", rows(DD[dn]), tb[:], r=[tb.res], w=[R_scr], dres=tb.res)
            return spill

        NST = 14
        pipeline = []
        it = 0
        tiles = [(red, xsrc, cpt, DD, ti) for (red, xsrc, cpt, DD) in runs for ti in range(NTOK // TT)]
        pend_pre = {}

        def run_pipeline_until(limit):
            nonlocal it
            while it < limit:
                for (st0, steps) in pipeline:
                    sidx = it - st0
                    if 0 <= sidx < len(steps):
                        steps[sidx]()
                pipeline[:] = [(a, b) for (a, b) in pipeline if it - a < len(b) - 1]
                it += 1

        tcs = [None] * len(tiles)
        tcs[0] = preamble(tiles[0][0], tiles[0][1], tiles[0][4])
        for n, (red, xsrc, cpt, DD, ti) in enumerate(tiles):
            tc = tcs[n]
            toks = dict(k=ktok.next(), v=vtok.next(), x=xtok.next(), b=btok.next(), sm=smtok.next())
            cts = [ct for ct in range(65)
                   if not ((tc["is_ctx"] or red) and 48 <= ct < 64) and not (red and (ct < 8 or 44 <= ct < 48))]
            for idx, ct in enumerate(cts):
                pipeline.append((it, make_job(tc, ct, cpt, DD, toks)))
                run_pipeline_until(it + 1)
                if idx == 6 and n + 1 < len(tiles):
                    tcs[n + 1] = preamble(tiles[n + 1][0], tiles[n + 1][1], tiles[n + 1][4])
            pipeline.append((it + 6, [make_spill(tc, DD, toks)]))
        run_pipeline_until(it + NST + 2)
        cx.pop()


    def load_scan_consts():
        tri = cx.sb("tri", [128, 4, 128], F32)
        k.dma("sp", tri[:], tri_d, w=[tri.res], dres=tri.res)
        mskf = cx.sb("mskf", [128, 4, 128], F32)
        k.dma("sp", mskf[:], msk_d, w=[mskf.res], dres=mskf.res)
        mskb = cx.sb("mskb", [128, 4, 128], BF16)
        k.ins("dve", lambda: nc.vector.tensor_copy(mskb[:], mskf[:]), r=[mskf.res], w=[mskb.res])
        id4 = cx.sb("id4", [128, 4, 128], F32)
        for j in range(4):
            k.ins("dve", lambda j=j: nc.vector.tensor_copy(id4[:, j, :], ident[:]), r=[ident.res], w=[id4.res])
        return tri, mskb, id4

    def gdn_pass(mode):
        cx.push()
        fwd = mode != "own2"
        pi = 0 if mode == "own1" else 1
        red = mode == "red"
        s_kT, s_k, s_v, s_sm = (kT2_d, k2_d, v2_d, sm2_d) if red else (kT_d, k_d, v_d, sm_d)
        tri, mskb, id4 = load_scan_consts()
        cumL = tri[:, 0, :] if fwd else tri[:, 1, :]
        negR = tri[:, 2, :] if fwd else tri[:, 3, :]
        m_s = mskb[:, 0, :] if fwd else mskb[:, 1, :]
        mT_s = mskb[:, 1, :] if fwd else mskb[:, 0, :]
        mT_i = mskb[:, 3, :] if fwd else mskb[:, 2, :]
        g0 = pi * 8
        l0c = 32 + pi * 8
        banks = Ring([cx.psum(f"gb{i}", [128, 512], F32) for i in range(8)])
        S32 = cx.sb("S32", [128, 8, 128], F32)
        Sbf = cx.sb("Sbf", [128, 8, 128], BF16)
        if fwd:
            k.ins("pool", lambda: nc.gpsimd.memset(S32[:], 0.0), w=[S32.res])
        else:
            k.dma("sp", S32[:].rearrange("p h d -> p (h d)"), st2_d[:, 0:1024], r=[R_st2], w=[S32.res], dres=S32.res)
        k.ins("act", lambda: nc.scalar.copy(out=Sbf[:], in_=S32[:]), r=[S32.res], w=[Sbf.res])
        NB = 2
        qTr = cx.sbring("qTc", NB, [128, 8, 128], BF16)
        kTr = cx.sbring("kTc", NB, [128, 8, 128], BF16)
        kr = cx.sbring("kc", NB, [128, 8, 128], BF16)
        vr = cx.sbring("vc", NB, [128, 8, 128], BF16)
        smr = cx.sbring("smc", NB, [128, 128], F32)
        o1r = cx.sbring("o1c", NB, [128, 8, 128], F32)
        kbgr = cx.sbring("kbg", NB, [128, 8, 128], BF16)
        vbr = cx.sbring("vb", NB, [128, 8, 128], BF16)
        kdr = cx.sbring("kd", NB, [128, 8, 128], BF16)
        smallr = cx.sbring("gsm", NB, [128, 6, 8], F32)
        eglr = cx.sbring("egl", NB, [128, 8], F32)
        expr = cx.sbring("exps", NB, [128, 3, 8], F32)
        Ear = cx.sbring("Ea", 2, [128, 4, 128], F32)
        Ebr = cx.sbring("Eb", 2, [128, 4, 128], F32)
        Ecr = cx.sbring("Ec", 2, [128, 4, 128], F32)
        Edr = cx.sbring("Ed", 2, [128, 4, 128], F32)
        Pr = cx.sbring("Pp", 4, [128, 4, 128], F32)
        PTr = cx.sbring("PTp", 4, [128, 4, 128], F32)
        Yr = cx.sbring("Yp", 4, [128, 4, 128], F32)
        Yfr = cx.sbring("Yf", 2 * NB, [128, 4, 128], BF16)
        nWTr = cx.sbring("nWT", 2 * NB, [128, 4, 128], BF16)
        attr = cx.sbring("att", 2 * NB, [128, 4, 128], BF16)
        qgr = cx.sbring("qg", 2 * NB, [128, 4, 128], BF16)
        vnr = cx.sbring("vn", 2, [128, 4, 128], BF16)
        oTr = cx.sbring("oTs", 2, [128, 8, 128], F32)

        def prep(c):
            lat = c >= 2 and not red
            t0 = c * C
            l0 = t0 - NCTX
            qT_c = qTr.next(); kT_c = kTr.next(); k_c = kr.next(); v_c = vr.next(); sm_c = smr.next()
            if lat:
                k.dma("sp", qT_c[:], qT_d[:, :, t0:t0 + C].rearrange("h p t -> p h t"), r=[R_scr], w=[qT_c.res], dres=qT_c.res)
            k.dma("sp", kT_c[:], s_kT[:, :, t0:t0 + C].rearrange("h p t -> p h t"), r=[R_scr], w=[kT_c.res], dres=kT_c.res)
            k.dma("sp", k_c[:], s_k[t0:t0 + C, :].rearrange("t (h d) -> t h d", d=128), r=[R_scr], w=[k_c.res], dres=k_c.res)
            k.dma("sp", v_c[:], s_v[t0:t0 + C, :].rearrange("t (h d) -> t h d", d=128), r=[R_scr], w=[v_c.res], dres=v_c.res)
            k.dma("sp", sm_c[:], s_sm[t0:t0 + C, :], r=[R_scr], w=[sm_c.res], dres=sm_c.res)
            o1_c = None
            if lat and not fwd:
                o1_c = o1r.next()
                k.dma("sp", o1_c[:], o1_d[:, :, l0:l0 + C].rearrange("h p t -> p h t"), r=[R_o1[c]], w=[o1_c.res], dres=o1_c.res)
            gcols = sm_c[:, g0:g0 + 8]
            lcols = sm_c[:, l0c:l0c + 8]
            pb = banks.next()
            k.group("pe", [
                lambda: nc.tensor.matmul(pb[:, 0:8], cumL, gcols, start=True, stop=True),
                lambda: nc.tensor.matmul(pb[:, 8:16], onesf[:], gcols, start=True, stop=True)],
                r=[tri.res, onesf.res, sm_c.res], w=[pb.res])
            sm6 = smallr.next()
            gc, gcl, ngc, tmp = sm6[:, 0, :], sm6[:, 1, :], sm6[:, 2, :], sm6[:, 3, :]
            k.ins("dve", lambda: nc.vector.tensor_copy(gc, pb[:, 0:8]), r=[pb.res], w=[sm6.res])
            k.ins("dve", lambda: nc.vector.tensor_tensor(out=gcl, in0=gc, in1=lcols, op=ALU.add), r=[sm6.res, sm_c.res], w=[sm6.res])
            k.ins("dve", lambda: nc.vector.tensor_scalar(out=ngc, in0=gc, scalar1=-1.0, scalar2=None, op0=ALU.mult),
                  r=[sm6.res], w=[sm6.res])
            k.ins("dve", lambda: nc.vector.tensor_tensor(out=tmp, in0=pb[:, 8:16], in1=gc, op=ALU.subtract),
                  r=[pb.res, sm6.res], w=[sm6.res])
            ex = expr.next()
            egl = eglr.next()
            k.ins("act", lambda: nc.scalar.activation(out=ex[:, 0, :], in_=gcl, func=AF.Exp), r=[sm6.res], w=[ex.res])
            k.ins("act", lambda: nc.scalar.activation(out=ex[:, 1, :], in_=lcols, func=AF.Exp), r=[sm_c.res], w=[ex.res])
            k.ins("act", lambda: nc.scalar.activation(out=ex[:, 2, :], in_=tmp, func=AF.Exp), r=[sm6.res], w=[ex.res])
            k.ins("act", lambda: nc.scalar.activation(out=egl[:], in_=pb[:, 8:16], func=AF.Exp), r=[pb.res], w=[egl.res])
            kbg = kbgr.next(); vb = vbr.next(); kd = kdr.next()
            bc = lambda col: ex[:, col, :].unsqueeze(2).to_broadcast([128, 8, 128])
            k.ins("pool", lambda: nc.gpsimd.tensor_tensor(out=kbg[:], in0=k_c[:], in1=bc(0), op=ALU.mult),
                  r=[k_c.res, ex.res], w=[kbg.res])
            k.ins("pool", lambda: nc.gpsimd.tensor_tensor(out=vb[:], in0=v_c[:], in1=bc(1), op=ALU.mult),
                  r=[v_c.res, ex.res], w=[vb.res])
            k.ins("pool", lambda: nc.gpsimd.tensor_tensor(out=kd[:], in0=k_c[:], in1=bc(2), op=ALU.mult),
                  r=[k_c.res, ex.res], w=[kd.res])
            pp = dict(c=c, lat=lat, groups=[None, None], vb=vb, kd=kd, egl=egl, o1=o1_c)

            def grp(gi):
                h0 = gi * 4
                gb = lambda h: sm_c[:, g0 + h:g0 + h + 1].to_broadcast([128, 128])
                lb = lambda h: sm_c[:, l0c + h:l0c + h + 1].to_broadcast([128, 128])
                KK = banks.next()
                k.group("pe", [(lambda j=j: nc.tensor.matmul(KK[:, j * 128:(j + 1) * 128], kT_c[:, h0 + j, :], kT_c[:, h0 + j, :],
                                                             start=True, stop=True)) for j in range(4)],
                        r=[kT_c.res], w=[KK.res])
                Da = banks.next()
                fl = []
                for j in range(4):
                    fl.append(lambda j=j: nc.tensor.matmul(Da[:, j * 128:(j + 1) * 128], gb(h0 + j), negR, start=True, stop=False))
                    fl.append(lambda j=j: nc.tensor.matmul(Da[:, j * 128:(j + 1) * 128], identb[:], m_s, start=False, stop=True))
                k.group("pe", fl, r=[sm_c.res, tri.res, identb.res, mskb.res], w=[Da.res])
                Ea = Ear.next()
                for j in range(4):
                    k.ins("act", lambda j=j: nc.scalar.activation(out=Ea[:, j, :], in_=Da[:, j * 128:(j + 1) * 128], func=AF.Exp,
                                                                  bias=sm6[:, 1, h0 + j:h0 + j + 1]),
                          r=[Da.res, sm6.res], w=[Ea.res])
                Db = banks.next()
                fl = []
                for j in range(4):
                    sl = slice(j * 128, (j + 1) * 128)
                    fl.append(lambda j=j, sl=sl: nc.tensor.matmul(Db[:, sl], gb(h0 + j), cumL, start=True, stop=False))
                    fl.append(lambda j=j, sl=sl: nc.tensor.matmul(Db[:, sl], lb(h0 + j), ident[:], start=False, stop=False))
                    fl.append(lambda j=j, sl=sl: nc.tensor.matmul(Db[:, sl], identb[:], mT_s, start=False, stop=True))
                k.group("pe", fl, r=[sm_c.res, tri.res, ident.res, identb.res, mskb.res], w=[Db.res])
                Eb = Ebr.next()
                for j in range(4):
                    k.ins("act", lambda j=j: nc.scalar.activation(out=Eb[:, j, :], in_=Db[:, j * 128:(j + 1) * 128], func=AF.Exp,
                                                                  bias=sm6[:, 2, h0 + j:h0 + j + 1]),
                          r=[Db.res, sm6.res], w=[Eb.res])
                P0 = Pr.next(); P0T = PTr.next()
                KK3 = KK[:].rearrange("p (j c) -> p j c", c=128)
                k.ins("dve", lambda: nc.vector.scalar_tensor_tensor(out=P0[:], in0=KK3, scalar=-1.0, in1=Ea[:],
                                                                    op0=ALU.mult, op1=ALU.mult),
                      r=[KK.res, Ea.res], w=[P0.res])
                k.ins("dve", lambda: nc.vector.scalar_tensor_tensor(out=P0T[:], in0=KK3, scalar=-1.0, in1=Eb[:],
                                                                    op0=ALU.mult, op1=ALU.mult),
                      r=[KK.res, Eb.res], w=[P0T.res])
                yield
                att = None; qg = None
                if lat:
                    QK = banks.next()
                    k.group("pe", [(lambda j=j: nc.tensor.matmul(QK[:, j * 128:(j + 1) * 128], kT_c[:, h0 + j, :], qT_c[:, h0 + j, :],
                                                                 start=True, stop=True)) for j in range(4)],
                            r=[kT_c.res, qT_c.res], w=[QK.res])
                    Dc = banks.next()
                    fl = []
                    for j in range(4):
                        sl = slice(j * 128, (j + 1) * 128)
                        fl.append(lambda j=j, sl=sl: nc.tensor.matmul(Dc[:, sl], gb(h0 + j), cumL, start=True, stop=False))
                        fl.append(lambda j=j, sl=sl: nc.tensor.matmul(Dc[:, sl], identb[:], mT_i, start=False, stop=True))
                    k.group("pe", fl, r=[sm_c.res, tri.res, identb.res, mskb.res], w=[Dc.res])
                    Ec = Ecr.next()
                    for j in range(4):
                        k.ins("act", lambda j=j: nc.scalar.activation(out=Ec[:, j, :], in_=Dc[:, j * 128:(j + 1) * 128], func=AF.Exp,
                                                                      bias=sm6[:, 2, h0 + j:h0 + j + 1]),
                              r=[Dc.res, sm6.res], w=[Ec.res])
                    Dd = banks.next()
                    k.group("pe", [(lambda j=j: nc.tensor.matmul(Dd[:, j * 128:(j + 1) * 128], gb(h0 + j), cumL, start=True, stop=True))
                                   for j in range(4)], r=[sm_c.res, tri.res], w=[Dd.res])
                    Ed = Edr.next()
                    k.ins("act", lambda: nc.scalar.activation(out=Ed[:].rearrange("p j c -> p (j c)"), in_=Dd[:], func=AF.Exp),
                          r=[Dd.res], w=[Ed.res])
                    att = attr.next()
                    k.ins("dve", lambda: nc.vector.tensor_tensor(out=att[:], in0=QK[:].rearrange("p (j c) -> p j c", c=128),
                                                                 in1=Ec[:], op=ALU.mult), r=[QK.res, Ec.res], w=[att.res])
                    qg = qgr.next()
                    k.ins("pool", lambda: nc.gpsimd.tensor_tensor(out=qg[:], in0=qT_c[:, h0:h0 + 4, :], in1=Ed[:], op=ALU.mult),
                          r=[qT_c.res, Ed.res], w=[qg.res])
                Y = Yr.next()
                k.ins("pool", lambda: nc.gpsimd.tensor_tensor(out=Y[:], in0=P0T[:], in1=id4[:], op=ALU.add),
                      r=[P0T.res, id4.res], w=[Y.res])
                yield
                Pp, PTp = P0, P0T
                for lev in range(1, 7):
                    last = lev == 6
                    Pb = banks.next()
                    k.group("pe", [(lambda j=j, Pp=Pp, PTp=PTp, Pb=Pb: nc.tensor.matmul(
                        Pb[:, j * 128:(j + 1) * 128], PTp[:, j, :], Pp[:, j, :], start=True, stop=True)) for j in range(4)],
                        r=[Pp.res, PTp.res], w=[Pb.res])
                    Pn = Pr.next()
                    k.ins("act", lambda Pn=Pn, Pb=Pb: nc.scalar.copy(out=Pn[:].rearrange("p j c -> p (j c)"), in_=Pb[:]),
                          r=[Pb.res], w=[Pn.res])
                    PTn = None
                    if not last:
                        PTb = banks.next()
                        k.group("pe", [(lambda j=j, Pp=Pp, PTp=PTp, PTb=PTb: nc.tensor.matmul(
                            PTb[:, j * 128:(j + 1) * 128], Pp[:, j, :], PTp[:, j, :], start=True, stop=True)) for j in range(4)],
                            r=[Pp.res, PTp.res], w=[PTb.res])
                        PTn = PTr.next()
                        k.ins("act", lambda PTn=PTn, PTb=PTb: nc.scalar.copy(out=PTn[:].rearrange("p j c -> p (j c)"), in_=PTb[:]),
                              r=[PTb.res], w=[PTn.res])
                    yield
                    Yb = banks.next()
                    k.group("pe", [(lambda j=j, Yb=Yb, Y=Y, Pn=Pn: nc.tensor.matmul(
                        Yb[:, j * 128:(j + 1) * 128], Pn[:, j, :], Y[:, j, :], start=True, stop=True)) for j in range(4)],
                        r=[Y.res, Pn.res], w=[Yb.res])
                    if last:
                        Yn = Yfr.next()
                        k.ins("dve", lambda Yn=Yn, Yb=Yb, Y=Y: nc.vector.tensor_tensor(
                            out=Yn[:].rearrange("p j c -> p (j c)"), in0=Yb[:], in1=Y[:].rearrange("p j c -> p (j c)"), op=ALU.add),
                            r=[Yb.res, Y.res], w=[Yn.res])
                    else:
                        Yn = Yr.next()
                        k.ins("dve", lambda Yn=Yn, Yb=Yb, Y=Y: nc.vector.tensor_tensor(
                            out=Yn[:].rearrange("p j c -> p (j c)"), in0=Yb[:], in1=Y[:].rearrange("p j c -> p (j c)"), op=ALU.add),
                            r=[Yb.res, Y.res], w=[Yn.res])
                    Y = Yn
                    Pp, PTp = Pn, PTn
                    yield
                Wb = banks.next()
                k.group("pe", [(lambda j=j: nc.tensor.matmul(Wb[:, j * 128:(j + 1) * 128], kbg[:, h0 + j, :], Y[:, j, :],
                                                             start=True, stop=True)) for j in range(4)],
                        r=[kbg.res, Y.res], w=[Wb.res])
                nWT = nWTr.next()
                k.ins("act", lambda: nc.scalar.activation(out=nWT[:].rearrange("p j c -> p (j c)"), in_=Wb[:], func=AF.Identity,
                                                          scale=-1.0), r=[Wb.res], w=[nWT.res])
                pp["groups"][gi] = (Y, nWT, att, qg)
            pp["gens"] = [grp(0), grp(1)]
            return pp

        def chain(pp):
            c, lat = pp["c"], pp["lat"]
            l0 = c * C - NCTX
            vb, kd, egl = pp["vb"], pp["kd"], pp["egl"]
            oTs = oTr.next() if lat else None
            for gi in range(2):
                h0 = gi * 4
                Y, nWT, att, qg = pp["groups"][gi]
                VN = banks.next()
                fl = []
                for j in range(4):
                    sl = slice(j * 128, (j + 1) * 128)
                    fl.append(lambda j=j, sl=sl: nc.tensor.matmul(VN[:, sl], Y[:, j, :], vb[:, h0 + j, :], start=True, stop=False))
                    fl.append(lambda j=j, sl=sl: nc.tensor.matmul(VN[:, sl], nWT[:, j, :], Sbf[:, h0 + j, :], start=False, stop=True))
                k.group("pe", fl, r=[Y.res, vb.res, nWT.res, Sbf.res], w=[VN.res])
                vn = vnr.next()
                k.ins("dve", lambda: nc.vector.tensor_copy(vn[:].rearrange("p j c -> p (j c)"), VN[:]), r=[VN.res], w=[vn.res])
                yield
                if lat:
                    OT = banks.next()
                    fl = []
                    for j in range(4):
                        sl = slice(j * 128, (j + 1) * 128)
                        fl.append(lambda j=j, sl=sl: nc.tensor.matmul(OT[:, sl], Sbf[:, h0 + j, :], qg[:, j, :], start=True, stop=False))
                        fl.append(lambda j=j, sl=sl: nc.tensor.matmul(OT[:, sl], vn[:, j, :], att[:, j, :], start=False, stop=True))
                    k.group("pe", fl, r=[Sbf.res, qg.res, vn.res, att.res], w=[OT.res])
                    osl = oTs[:, h0:h0 + 4, :].rearrange("p j c -> p (j c)")
                    if fwd:
                        k.ins("act", lambda: nc.scalar.copy(out=osl, in_=OT[:]), r=[OT.res], w=[oTs.res])
                    else:
                        o1_c = pp["o1"]
                        k.ins("dve", lambda: nc.vector.tensor_tensor(
                            out=osl, in0=OT[:], in1=o1_c[:, h0:h0 + 4, :].rearrange("p j c -> p (j c)"), op=ALU.add),
                            r=[OT.res, o1_c.res], w=[oTs.res])
                DS = banks.next()
                k.group("pe", [(lambda j=j: nc.tensor.matmul(DS[:, j * 128:(j + 1) * 128], kd[:, h0 + j, :], vn[:, j, :],
                                                             start=True, stop=True)) for j in range(4)],
                        r=[kd.res, vn.res], w=[DS.res])
                ssl = S32[:, h0:h0 + 4, :]
                k.ins("pool", lambda: nc.gpsimd.tensor_tensor(
                    out=ssl, in0=ssl, in1=egl[:, h0:h0 + 4].unsqueeze(2).to_broadcast([128, 4, 128]), op=ALU.mult),
                    r=[S32.res, egl.res], w=[S32.res])
                k.ins("dve", lambda: nc.vector.tensor_tensor(out=ssl, in0=ssl, in1=DS[:].rearrange("p (j c) -> p j c", c=128),
                                                             op=ALU.add), r=[S32.res, DS.res], w=[S32.res])
                k.ins("act", lambda: nc.scalar.copy(out=Sbf[:, h0:h0 + 4, :], in_=ssl), r=[S32.res], w=[Sbf.res])
                yield
            if lat:
                if fwd:
                    k.dma("sp", o1_d[:, :, l0:l0 + C].rearrange("h p t -> p h t"), oTs[:], r=[oTs.res], w=[R_o1[c]], dres=oTs.res)
                else:
                    k.dma("sp", oT_d[:, :, l0:l0 + C].rearrange("h p t -> p h t"), oTs[:], r=[oTs.res], w=[R_oT], dres=oTs.res)

        order = list(range(NCH)) if fwd else list(range(NCH - 1, 1, -1))
        if GDN_LIMIT:
            order = order[:GDN_LIMIT]
        def drive(gens):
            gens = list(gens)
            while gens:
                for g_ in list(gens):
                    try:
                        next(g_)
                    except StopIteration:
                        gens.remove(g_)

        pend = prep(order[0])
        drive(pend["gens"])
        for idx in range(len(order)):
            nxt = prep(order[idx + 1]) if idx + 1 < len(order) else None
            drive([chain(pend)] + (nxt["gens"] if nxt is not None else []))
            pend = nxt
        if red:
            k.dma("sp", st2_d[:, 0:1024], S32[:].rearrange("p h d -> p (h d)"), r=[S32.res], w=[R_st2], dres=S32.res)
        cx.pop()

    def ssd_pass(mode):
        cx.push()
        fwd = mode != "own2"
        pi = 0 if mode == "own1" else 1
        red = mode == "red"
        s_x, s_B, s_sm = (x2_d, B2_d, sm2_d) if red else (x_d, B_d, sm_d)
        tri, mskb, id4 = load_scan_consts()
        cumL = tri[:, 0, :] if fwd else tri[:, 1, :]
        mT_i = mskb[:, 3, :] if fwd else mskb[:, 2, :]
        d0 = 64 + pi * 32
        banks = Ring([cx.psum(f"sbk{i}", [128, 512], F32) for i in range(4)])
        ydb = [cx.psum(f"syd{i}", [128, 512], F32) for i in range(4)]
        aB = cx.sb("aB", [128, 32], F32)
        k.dma("sp", aB[:], salog_d[:, pi, :], w=[aB.res], dres=aB.res)
        k.ins("act", lambda: nc.scalar.activation(out=aB[:], in_=aB[:], func=AF.Exp), r=[aB.res], w=[aB.res])
        k.ins("dve", lambda: nc.vector.tensor_scalar(out=aB[:], in0=aB[:], scalar1=-1.0, scalar2=None, op0=ALU.mult),
              r=[aB.res], w=[aB.res])
        hT32 = cx.sb("hT32", [128, 32, 64], F32)
        hTbf = cx.sb("hTbf", [128, 32, 64], BF16)
        if fwd:
            k.ins("pool", lambda: nc.gpsimd.memset(hT32[:], 0.0), w=[hT32.res])
        else:
            k.dma("sp", hT32[:].rearrange("p h d -> p (h d)"), st2_d[:, 1024:3072], r=[R_st2], w=[hT32.res], dres=hT32.res)
        k.ins("act", lambda: nc.scalar.copy(out=hTbf[:], in_=hT32[:]), r=[hT32.res], w=[hTbf.res])
        xr = cx.sbring("xc", 2, [128, 32, 64], BF16)
        BTr = cx.sbring("BTc", 2, [128, 4, 128], BF16)
        CTr = cx.sbring("CTc", 2, [128, 4, 128], BF16)
        Br = cx.sbring("Bc", 2, [128, 4, 128], BF16)
        smr = cx.sbring("smc", 2, [128, 128], F32)
        y1r = cx.sbring("y1c", 2, [128, 2048], F32)
        dAr = cx.sbring("dA", 2, [128, 32], F32)
        s5r = cx.sbring("s5", 2, [128, 6, 32], F32)
        xdtr = cx.sbring("xdt", 2, [128, 32, 64], BF16)
        xdtsr = cx.sbring("xdts", 2, [128, 32, 64], BF16)
        ytr = cx.sbring("ytmp", 2, [128, 32, 64], F32)
        ycr = cx.sbring("yc", 2, [128, 2048], F32)
        segr = cx.sbring("seg", 3, [128, 4, 128], F32)
        GTr = cx.sbring("GT", 2, [128, 4, 128], F32)
        MTr = cx.sbring("MT", 3, [128, 4, 128], BF16)
        order = list(range(NCH)) if fwd else list(range(NCH - 1, 1, -1))
        if GDN_LIMIT:
            order = order[:GDN_LIMIT]
        segs = [cx.sb(f"segd{i}", [128, 4, 128], F32) for i in range(2)]
        MTs = [cx.sb(f"MTd{i}", [128, 4, 128], BF16) for i in range(2)]

        def front_gen(c, cc):
            lat = c >= 2 and not red
            t0 = c * C
            l0 = t0 - NCTX
            x_c = xr.next(); BT_c = BTr.next(); CT_c = CTr.next(); B_c = Br.next(); sm_c = smr.next()
            k.dma("sp", x_c[:].rearrange("p h d -> p (h d)"), s_x[t0:t0 + C, :], r=[R_scr], w=[x_c.res], dres=x_c.res)
            if lat:
                k.dma("sp", BT_c[:], BT_d[:, :, t0:t0 + C].rearrange("g p t -> p g t"), r=[R_scr], w=[BT_c.res], dres=BT_c.res)
                k.dma("sp", CT_c[:], CT_d[:, :, t0:t0 + C].rearrange("g p t -> p g t"), r=[R_scr], w=[CT_c.res], dres=CT_c.res)
            k.dma("sp", B_c[:], s_B[t0:t0 + C, :].rearrange("t (g n) -> t g n", n=128), r=[R_scr], w=[B_c.res], dres=B_c.res)
            k.dma("sp", sm_c[:], s_sm[t0:t0 + C, :], r=[R_scr], w=[sm_c.res], dres=sm_c.res)
            y1_c = None
            if lat and not fwd:
                y1_c = y1r.next()
                k.dma("sp", y1_c[:], y1_d[l0:l0 + C, :], r=[R_y1[c]], w=[y1_c.res], dres=y1_c.res)
            dtc = sm_c[:, d0:d0 + 32]
            dA = dAr.next()
            k.ins("dve", lambda: nc.vector.tensor_tensor(out=dA[:], in0=dtc, in1=aB[:], op=ALU.mult),
                  r=[sm_c.res, aB.res], w=[dA.res])
            yield
            pb = banks.next()
            k.group("pe", [
                lambda: nc.tensor.matmul(pb[:, 0:32], cumL, dA[:], start=True, stop=True),
                lambda: nc.tensor.matmul(pb[:, 32:64], onesf[:], dA[:], start=True, stop=True)],
                r=[tri.res, onesf.res, dA.res], w=[pb.res])
            yield
            s5 = s5r.next()
            ac, nac, tmp, ea, w2, eat = (s5[:, i, :] for i in range(6))
            k.ins("dve", lambda: nc.vector.tensor_copy(ac, pb[:, 0:32]), r=[pb.res], w=[s5.res])
            k.ins("act", lambda: nc.scalar.activation(out=eat, in_=pb[:, 32:64], func=AF.Exp), r=[pb.res], w=[s5.res])
            yield
            k.ins("dve", lambda: nc.vector.tensor_scalar(out=nac, in0=ac, scalar1=-1.0, scalar2=None, op0=ALU.mult),
                  r=[s5.res], w=[s5.res])
            k.ins("dve", lambda: nc.vector.tensor_tensor(out=tmp, in0=pb[:, 32:64], in1=ac, op=ALU.subtract),
                  r=[pb.res, s5.res], w=[s5.res])
            yield
            k.ins("act", lambda: nc.scalar.activation(out=ea, in_=ac, func=AF.Exp), r=[s5.res], w=[s5.res])
            k.ins("act", lambda: nc.scalar.activation(out=w2, in_=tmp, func=AF.Exp), r=[s5.res], w=[s5.res])
            yield
            k.ins("dve", lambda: nc.vector.tensor_tensor(out=w2, in0=w2, in1=dtc, op=ALU.mult), r=[s5.res, sm_c.res], w=[s5.res])
            yield
            bc32 = lambda ap: ap.unsqueeze(2).to_broadcast([128, 32, 64])
            xdts = xdtsr.next()
            k.ins("pool", lambda: nc.gpsimd.tensor_tensor(out=xdts[:], in0=x_c[:], in1=bc32(w2), op=ALU.mult),
                  r=[x_c.res, s5.res], w=[xdts.res])
            ytmp = None; xdt = None
            if lat:
                xdt = xdtr.next()
                k.ins("pool", lambda: nc.gpsimd.tensor_tensor(out=xdt[:], in0=x_c[:], in1=bc32(dtc), op=ALU.mult),
                      r=[x_c.res, sm_c.res], w=[xdt.res])
                ytmp = ytr.next()
                for g in range(4):
                    YO = banks.next()
                    k.ins("pe", lambda g=g, YO=YO: nc.tensor.matmul(
                        YO[:], CT_c[:, g, :], hTbf[:, g * 8:(g + 1) * 8, :].rearrange("p h d -> p (h d)"), start=True, stop=True),
                        r=[CT_c.res, hTbf.res], w=[YO.res])
                    yield
                    k.ins("dve", lambda g=g, YO=YO: nc.vector.tensor_tensor(
                        out=ytmp[:, g * 8:(g + 1) * 8, :], in0=YO[:].rearrange("p (h d) -> p h d", d=64),
                        in1=ea[:, g * 8:(g + 1) * 8].unsqueeze(2).to_broadcast([128, 8, 64]), op=ALU.mult),
                        r=[YO.res, s5.res], w=[ytmp.res])
            yield
            k.ins("pool", lambda: nc.gpsimd.tensor_tensor(out=hT32[:], in0=hT32[:], in1=bc32(eat), op=ALU.mult),
                  r=[hT32.res, s5.res], w=[hT32.res])
            for g in range(4):
                NS = banks.next()
                k.ins("pe", lambda g=g, NS=NS: nc.tensor.matmul(
                    NS[:], B_c[:, g, :], xdts[:, g * 8:(g + 1) * 8, :].rearrange("p h d -> p (h d)"), start=True, stop=True),
                    r=[B_c.res, xdts.res], w=[NS.res])
                yield
                hs = hT32[:, g * 8:(g + 1) * 8, :]
                k.ins("dve", lambda g=g, NS=NS, hs=hs: nc.vector.tensor_tensor(
                    out=hs, in0=hs, in1=NS[:].rearrange("p (h d) -> p h d", d=64), op=ALU.add),
                    r=[hT32.res, NS.res], w=[hT32.res])
            yield
            k.ins("act", lambda: nc.scalar.copy(out=hTbf[:], in_=hT32[:]), r=[hT32.res], w=[hTbf.res])
            GT = None; y_c = None
            if lat:
                GTb = banks.next()
                k.group("pe", [(lambda g=g: nc.tensor.matmul(GTb[:, g * 128:(g + 1) * 128], BT_c[:, g, :], CT_c[:, g, :],
                                                             start=True, stop=True)) for g in range(4)],
                        r=[BT_c.res, CT_c.res], w=[GTb.res])
                yield
                GT = GTr.next()
                k.ins("act", lambda: nc.scalar.copy(out=GT[:].rearrange("p g c -> p (g c)"), in_=GTb[:]), r=[GTb.res], w=[GT.res])
                y_c = ycr.next()
            cc.update(lat=lat, l0=l0, c=c, dA=dA, s5=s5, xdt=xdt, ytmp=ytmp, GT=GT, y_c=y_c, y1_c=y1_c, done=[0])
            yield

        def group_gen(cc, sid, gs):
            dA, s5, xdt, ytmp, GT, y_c = (cc[n_] for n_ in ("dA", "s5", "xdt", "ytmp", "GT", "y_c"))
            Dq, YD = ydb[2 * sid], ydb[2 * sid + 1]
            seg, MT = segs[sid], MTs[sid]
            for g in gs:
                for qd in range(2):
                    hq = g * 8 + qd * 4
                    fl = []
                    for j in range(4):
                        sl = slice(j * 128, (j + 1) * 128)
                        fl.append(lambda j=j, sl=sl: nc.tensor.matmul(
                            Dq[:, sl], dA[:, hq + j:hq + j + 1].to_broadcast([128, 128]), cumL, start=True, stop=False))
                        fl.append(lambda j=j, sl=sl: nc.tensor.matmul(Dq[:, sl], identb[:], mT_i, start=False, stop=True))
                    k.group("pe", fl, r=[dA.res, tri.res, identb.res, mskb.res], w=[Dq.res])
                    yield
                    for j in range(4):
                        k.ins("act", lambda j=j: nc.scalar.activation(
                            out=seg[:, j, :], in_=Dq[:, j * 128:(j + 1) * 128], func=AF.Exp, bias=s5[:, 1, hq + j:hq + j + 1]),
                            r=[Dq.res, s5.res], w=[seg.res])
                    yield
                    k.ins("dve", lambda: nc.vector.tensor_tensor(
                        out=MT[:], in0=seg[:], in1=GT[:, g:g + 1, :].to_broadcast([128, 4, 128]), op=ALU.mult),
                        r=[seg.res, GT.res], w=[MT.res])
                    yield
                    k.group("pe", [(lambda j=j: nc.tensor.matmul(
                        YD[:, (qd * 4 + j) * 64:(qd * 4 + j + 1) * 64], MT[:, j, :], xdt[:, hq + j, :], start=True, stop=True))
                        for j in range(4)], r=[MT.res, xdt.res], w=[YD.res])
                    yield
                ysl = y_c[:, g * 512:(g + 1) * 512]
                k.ins("dve", lambda: nc.vector.tensor_tensor(
                    out=ysl, in0=YD[:], in1=ytmp[:, g * 8:(g + 1) * 8, :].rearrange("p h d -> p (h d)"), op=ALU.add),
                    r=[YD.res, ytmp.res], w=[y_c.res])
                yield
            cc["done"][0] += 1
            if cc["done"][0] == 2:
                l0, c = cc["l0"], cc["c"]
                if fwd:
                    k.dma("sp", y1_d[l0:l0 + C, :], y_c[:], r=[y_c.res], w=[R_y1[c]], dres=y_c.res)
                else:
                    y1_c = cc["y1_c"]
                    k.ins("pool", lambda: nc.gpsimd.tensor_tensor(out=y_c[:], in0=y_c[:], in1=y1_c[:], op=ALU.add),
                          r=[y_c.res, y1_c.res], w=[y_c.res])
                    k.dma("sp", y_d[l0:l0 + C, :], y_c[:], r=[y_c.res], w=[R_y], dres=y_c.res)

        def drive_s(gens):
            gens = list(gens)
            while gens:
                for g_ in list(gens):
                    try:
                        next(g_)
                    except StopIteration:
                        gens.remove(g_)

        prev_cc = None
        for c in order:
            cc = {}
            gens = [front_gen(c, cc)]
            if prev_cc is not None and prev_cc["lat"]:
                gens += [group_gen(prev_cc, 0, (0, 1)), group_gen(prev_cc, 1, (2, 3))]
            drive_s(gens)
            prev_cc = cc
        if prev_cc is not None and prev_cc["lat"]:
            drive_s([group_gen(prev_cc, 0, (0, 1)), group_gen(prev_cc, 1, (2, 3))])
        if red:
            k.dma("sp", st2_d[:, 1024:3072], hT32[:].rearrange("p h d -> p (h d)"), r=[hT32.res], w=[R_st2], dres=hT32.res)
        cx.pop()

    with nc.allow_non_contiguous_dma("scan chunk relayout"):
        if "R" in phases:
            gdn_pass("red")
            ssd_pass("red")
        if "B1" in phases:
            gdn_pass("own1")
        if "S1" in phases:
            ssd_pass("own1")
        if "B2" in phases:
            gdn_pass("own2")
        if "S2" in phases:
            ssd_pass("own2")


    def silu_parts(src_ap, shape, ring_e, ring_z, src_res):
        ez = ring_e.next()
        zc = ring_z.next()
        k.ins("act", lambda: nc.scalar.activation(out=ez[:], in_=src_ap, func=AF.Exp, scale=-1.0), r=[src_res], w=[ez.res])
        k.ins("act", lambda: nc.scalar.copy(out=zc[:], in_=src_ap), r=[src_res], w=[zc.res])
        k.ins("dve", lambda: nc.vector.tensor_scalar(out=ez[:], in0=ez[:], scalar1=1.0, scalar2=None, op0=ALU.add),
              r=[ez.res], w=[ez.res])
        return zc, ez

    if "C1" in phases:
        cx.push()
        T1 = 128
        wZ = cx.sb("wZ", [128, 8, 3072], BF16)
        wbg = cx.sb("wbg", [128, 8, 1024], BF16)
        wbs = cx.sb("wbs", [128, 16, 1024], BF16)
        wo = cx.sb("wo", [128, 8, 1024], BF16)
        for kt in range(8):
            k.dma("pool", wZ[:, kt, :], w_in[kt * 128:(kt + 1) * 128, O_ZG:O_ZG + 3072], w=[wZ.res], dres=wZ.res)
            k.dma("pool", wbg[:, kt, :], wbg_d[kt * 128:(kt + 1) * 128, :], w=[wbg.res], dres=wbg.res)
            k.dma("pool", wo[:, kt, :], wo_d[kt * 128:(kt + 1) * 128, :], w=[wo.res], dres=wo.res)
        for kt in range(16):
            k.dma("pool", wbs[:, kt, :], wbs_d[kt * 128:(kt + 1) * 128, :], w=[wbs.res], dres=wbs.res)
        gnw = cx.sb("gnw", [128, 1], F32)
        k.dma("sp", gnw[:], gnw_d, w=[gnw.res], dres=gnw.res)
        dskB = cx.sb("dskB", [128, 2048], F32)
        snwB = cx.sb("snwB", [128, 2048], F32)
        g1B = cx.sb("g1B", [128, 1024], F32)
        k.dma("sp", dskB[:], dskip_d[0, :].partition_broadcast(128), w=[dskB.res], dres=dskB.res)
        k.dma("sp", snwB[:], snw_d[0, :].partition_broadcast(128), w=[snwB.res], dres=snwB.res)
        k.dma("sp", g1B[:], mod_d[0, 2048:3072].partition_broadcast(128), r=[R_mod], w=[g1B.res], dres=g1B.res)
        GB = [[cx.psum(f"c1g{i}{j}", [128, 512], F32) for j in range(2)] for i in range(2)]
        SB = [(cx.psum(f"c1s{i}", [128, 512], F32), cx.psum(f"c1t{i}", [128, 1024], BF16)) for i in range(2)]
        aTr = cx.sbring("c_aT", 2, [128, 8, T1], BF16)
        oTr_ = cx.sbring("c_oT", 1, [128, 8, T1], F32)
        yr_ = cx.sbring("c_y", 1, [128, 2048], F32)
        xsr_ = cx.sbring("c_xs", 1, [128, 2048], BF16)
        sgr_ = cx.sbring("c_sg", 1, [128, 16, T1], F32)
        xr_ = cx.sbring("c_x", 1, [128, 1024], F32)
        onTr = cx.sbring("c_on", 1, [128, 8, T1], BF16)
        ynTr = cx.sbring("c_yn", 1, [128, 16, T1], BF16)
        mTr = cx.sbring("c_mT", 1, [128, 8, T1], BF16)
        h1r = cx.sbring("c_h1", 1, [128, 1024], F32)
        TG = [dict(ez=cx.sb(f"g_ez{i}", [128, T1], F32), zc=cx.sb(f"g_zc{i}", [128, T1], F32),
                   o2=cx.sb(f"g_o2{i}", [128, T1], F32), rs=cx.sb(f"g_rs{i}", [128, T1], F32)) for i in range(2)]
        TS = [dict(ez=cx.sb(f"s_ez{i}", [128, 512], F32), zc=cx.sb(f"s_zc{i}", [128, 512], F32),
                   yy=cx.sb(f"s_yy{i}", [128, 512], F32), jk=cx.sb(f"s_jk{i}", [128, 512], BF16),
                   ynb=cx.sb(f"s_yn{i}", [128, 512], BF16), ss=cx.sb(f"s_ss{i}", [128, 2], F32)) for i in range(2)]

        def drive_c(gens):
            gens = list(gens)
            while gens:
                for g_ in list(gens):
                    try:
                        next(g_)
                    except StopIteration:
                        gens.remove(g_)

        for ti in range(NLAT // T1):
            l0 = ti * T1
            aT = aTr.next(); oT = oTr_.next(); yt = yr_.next(); xs_ = xsr_.next(); sg = sgr_.next(); xt = xr_.next()
            k.dma("sp", aT[:], aT_d[:, :, l0:l0 + T1].rearrange("kt p t -> p kt t"), r=[R_scr], w=[aT.res], dres=aT.res)
            k.dma("sp", oT[:], oT_d[:, :, l0:l0 + T1].rearrange("h p t -> p h t"), r=[R_oT], w=[oT.res], dres=oT.res)
            k.dma("sp", yt[:], y_d[l0:l0 + T1, :], r=[R_y], w=[yt.res], dres=yt.res)
            k.dma("sp", xs_[:], x_d[NCTX + l0:NCTX + l0 + T1, :], r=[R_scr], w=[xs_.res], dres=xs_.res)
            k.dma("sp", sg[:], sgT_d[:, :, l0:l0 + T1].rearrange("c p t -> p c t"), r=[R_scr], w=[sg.res], dres=sg.res)
            k.dma("sp", xt[:], xin[NCTX + l0:NCTX + l0 + T1, :], w=[xt.res], dres=xt.res)
            onT = onTr.next(); ynT = ynTr.next(); mT = mTr.next(); h1 = h1r.next()

            def gen_gdn(sid, hs):
                pz, pn = GB[sid]
                ez, zc, o2, rs = (TG[sid][n_] for n_ in ("ez", "zc", "o2", "rs"))
                for h in hs:
                    k.group("pe", [(lambda kt=kt: nc.tensor.matmul(pz[:, 0:T1], wZ[:, kt, h * 128:(h + 1) * 128], aT[:, kt, :],
                                                                   start=(kt == 0), stop=(kt == 7))) for kt in range(8)],
                            r=[wZ.res, aT.res], w=[pz.res])
                    k.ins("pool", lambda: nc.gpsimd.tensor_tensor(out=o2[:], in0=oT[:, h, :], in1=oT[:, h, :], op=ALU.mult),
                          r=[oT.res], w=[o2.res])
                    yield
                    k.ins("act", lambda: nc.scalar.activation(out=ez[:], in_=pz[:, 0:T1], func=AF.Exp, scale=-1.0), r=[pz.res], w=[ez.res])
                    k.ins("act", lambda: nc.scalar.copy(out=zc[:], in_=pz[:, 0:T1]), r=[pz.res], w=[zc.res])
                    k.ins("pe", lambda: nc.tensor.matmul(pn[:, 0:T1], onesf[:], o2[:], start=True, stop=True),
                          r=[onesf.res, o2.res], w=[pn.res])
                    yield
                    k.ins("dve", lambda: nc.vector.tensor_scalar(out=ez[:], in0=ez[:], scalar1=1.0, scalar2=None, op0=ALU.add),
                          r=[ez.res], w=[ez.res])
                    k.ins("act", lambda: nc.scalar.activation(out=rs[:], in_=pn[:, 0:T1], func=AF.Ln, scale=1.0 / 128, bias=epsc[:, 0:1]),
                          r=[pn.res, epsc.res], w=[rs.res])
                    yield
                    k.ins("dve", lambda: nc.vector.reciprocal(out=ez[:], in_=ez[:]), r=[ez.res], w=[ez.res])
                    k.ins("act", lambda: nc.scalar.activation(out=rs[:], in_=rs[:], func=AF.Exp, scale=-0.5), r=[rs.res], w=[rs.res])
                    yield
                    k.ins("pool", lambda: nc.gpsimd.tensor_tensor(out=zc[:], in0=zc[:], in1=ez[:], op=ALU.mult),
                          r=[zc.res, ez.res], w=[zc.res])
                    k.ins("dve", lambda: nc.vector.tensor_tensor(out=rs[:], in0=rs[:], in1=oT[:, h, :], op=ALU.mult),
                          r=[rs.res, oT.res], w=[rs.res])
                    yield
                    k.ins("dve", lambda: nc.vector.scalar_tensor_tensor(out=onT[:, h, :], in0=rs[:], scalar=gnw[:, 0:1], in1=zc[:],
                                                                        op0=ALU.mult, op1=ALU.mult),
                          r=[rs.res, gnw.res, zc.res], w=[onT.res])
                    yield

            def gen_ssd(sid, cgs):
                pz, pt = SB[sid]
                ez, zc, yy, jk, ynb, ss = (TS[sid][n_] for n_ in ("ez", "zc", "yy", "jk", "ynb", "ss"))
                for cg in cgs:
                    csl = slice(cg * 512, (cg + 1) * 512)
                    k.group("pe", [(lambda kt=kt: nc.tensor.matmul(pz[:], aT[:, kt, :], wZ[:, kt, 1024 + cg * 512:1024 + (cg + 1) * 512],
                                                                   start=(kt == 0), stop=(kt == 7))) for kt in range(8)],
                            r=[wZ.res, aT.res], w=[pz.res])
                    k.ins("pool", lambda: nc.gpsimd.tensor_tensor(out=yy[:], in0=xs_[:, csl], in1=dskB[:, csl], op=ALU.mult),
                          r=[xs_.res, dskB.res], w=[yy.res])
                    yield
                    k.ins("act", lambda: nc.scalar.activation(out=ez[:], in_=pz[:], func=AF.Exp, scale=-1.0), r=[pz.res], w=[ez.res])
                    k.ins("act", lambda: nc.scalar.copy(out=zc[:], in_=pz[:]), r=[pz.res], w=[zc.res])
                    k.ins("dve", lambda: nc.vector.tensor_tensor(out=yy[:], in0=yy[:], in1=yt[:, csl], op=ALU.add),
                          r=[yy.res, yt.res], w=[yy.res])
                    yield
                    k.ins("dve", lambda: nc.vector.tensor_scalar(out=ez[:], in0=ez[:], scalar1=1.0, scalar2=None, op0=ALU.add),
                          r=[ez.res], w=[ez.res])
                    yield
                    k.ins("dve", lambda: nc.vector.reciprocal(out=ez[:], in_=ez[:]), r=[ez.res], w=[ez.res])
                    yield
                    k.ins("pool", lambda: nc.gpsimd.tensor_tensor(out=zc[:], in0=zc[:], in1=ez[:], op=ALU.mult),
                          r=[zc.res, ez.res], w=[zc.res])
                    yield
                    k.ins("dve", lambda: nc.vector.tensor_tensor(out=yy[:], in0=yy[:], in1=zc[:], op=ALU.mult),
                          r=[yy.res, zc.res], w=[yy.res])
                    yield
                    k.ins("act", lambda: nc.scalar.activation(out=jk[:], in_=yy[:], func=AF.Square, accum_out=ss[:, 0:1]),
                          r=[yy.res], w=[jk.res, ss.res])
                    yield
                    k.ins("act", lambda: nc.scalar.activation(out=ss[:, 1:2], in_=ss[:, 0:1], func=AF.Ln, scale=1.0 / 512, bias=epsc[:, 0:1]),
                          r=[ss.res, epsc.res], w=[ss.res])
                    yield
                    k.ins("act", lambda: nc.scalar.activation(out=ss[:, 1:2], in_=ss[:, 1:2], func=AF.Exp, scale=-0.5),
                          r=[ss.res], w=[ss.res])
                    yield
                    k.ins("dve", lambda: nc.vector.scalar_tensor_tensor(out=ynb[:], in0=yy[:], scalar=ss[:, 1:2], in1=snwB[:, csl],
                                                                        op0=ALU.mult, op1=ALU.mult),
                          r=[yy.res, ss.res, snwB.res], w=[ynb.res])
                    yield
                    k.group("pe", [(lambda j=j: nc.tensor.transpose(pt[:, j * 128:(j + 1) * 128], ynb[:, j * 128:(j + 1) * 128], identb[:]))
                                   for j in range(4)], r=[ynb.res, identb.res], w=[pt.res])
                    yield
                    k.ins("act", lambda: nc.scalar.copy(out=ynT[:, cg * 4:(cg + 1) * 4, :],
                                                        in_=pt[:, 0:512].rearrange("p (j c) -> p j c", c=128)),
                          r=[pt.res], w=[ynT.res])
                    yield

            def gen_merge(sid, ocs):
                pg, pp = GB[sid]
                m1, m2 = TG[sid]["ez"], TG[sid]["zc"]
                for oc in ocs:
                    k.group("pe", [(lambda h=h: nc.tensor.matmul(pg[:, 0:T1], wbg[:, h, oc * 128:(oc + 1) * 128], onT[:, h, :],
                                                                 start=(h == 0), stop=(h == 7))) for h in range(8)],
                            r=[wbg.res, onT.res], w=[pg.res])
                    k.group("pe", [(lambda ct=ct: nc.tensor.matmul(pp[:, 0:T1], wbs[:, ct, oc * 128:(oc + 1) * 128], ynT[:, ct, :],
                                                                   start=(ct == 0), stop=(ct == 15))) for ct in range(16)],
                            r=[wbs.res, ynT.res], w=[pp.res])
                    yield
                    k.ins("dve", lambda: nc.vector.tensor_tensor(out=m1[:], in0=pg[:, 0:T1], in1=sg[:, oc, :], op=ALU.mult),
                          r=[pg.res, sg.res], w=[m1.res])
                    k.ins("dve", lambda: nc.vector.tensor_tensor(out=m2[:], in0=pp[:, 0:T1], in1=sg[:, 8 + oc, :], op=ALU.mult),
                          r=[pp.res, sg.res], w=[m2.res])
                    yield
                    k.ins("pool", lambda: nc.gpsimd.tensor_tensor(out=mT[:, oc, :], in0=m1[:], in1=m2[:], op=ALU.add),
                          r=[m1.res, m2.res], w=[mT.res])
                    yield

            def gen_out(sid, cg):
                pm = SB[sid][0]
                csl = slice(cg * 512, (cg + 1) * 512)
                k.group("pe", [(lambda kt=kt: nc.tensor.matmul(pm[:], mT[:, kt, :], wo[:, kt, csl],
                                                               start=(kt == 0), stop=(kt == 7))) for kt in range(8)],
                        r=[mT.res, wo.res], w=[pm.res])
                yield
                k.ins("dve", lambda: nc.vector.tensor_tensor(out=h1[:, csl], in0=pm[:], in1=g1B[:, csl], op=ALU.mult),
                      r=[pm.res, g1B.res], w=[h1.res])
                yield
                k.ins("pool", lambda: nc.gpsimd.tensor_tensor(out=h1[:, csl], in0=h1[:, csl], in1=xt[:, csl], op=ALU.add),
                      r=[h1.res, xt.res], w=[h1.res])
                yield

            drive_c([gen_gdn(0, range(0, 4)), gen_ssd(0, range(0, 2)), gen_gdn(1, range(4, 8)), gen_ssd(1, range(2, 4))])
            drive_c([gen_merge(0, range(0, 4)), gen_merge(1, range(4, 8))])
            drive_c([gen_out(0, 0), gen_out(1, 1)])
            k.dma("sp", h1_d[l0:l0 + T1, :], h1[:], r=[h1.res], w=[R_h1], dres=h1.res)
        cx.pop()

    if "C2" in phases:
        cx.push()
        T2 = 256
        NF = DFF // 128
        wfi = cx.sb("wfi", [128, 8, 2 * DFF], BF16)
        wfo = cx.sb("wfo", [128, NF, 1024], BF16)
        for kt in range(8):
            for c0 in range(0, 2 * DFF, 2816):
                k.dma("pool", wfi[:, kt, c0:c0 + 2816], wfi_d[kt * 128:(kt + 1) * 128, c0:c0 + 2816], w=[wfi.res], dres=wfi.res)
        for ft in range(NF):
            k.dma("pool", wfo[:, ft, :], wfo_d[ft * 128:(ft + 1) * 128, :], w=[wfo.res], dres=wfo.res)
        g2B = cx.sb("g2B", [128, 1024], F32)
        nfB = cx.sb("nfB", [128, 1024], F32)
        k.dma("sp", g2B[:], mod_d[0, 5120:6144].partition_broadcast(128), r=[R_mod], w=[g2B.res], dres=g2B.res)
        k.dma("sp", nfB[:], nfw_d[0, :].partition_broadcast(128), w=[nfB.res], dres=nfB.res)
        n2 = cx.sb("n2", [128, 8], F32)
        k.dma("sp", n2[:], n2w_d, w=[n2.res], dres=n2.res)
        S2 = cx.sb("S2", [128, 8], F32)
        k.ins("dve", lambda: nc.vector.scalar_tensor_tensor(out=S2[:], in0=modf[:, 0, 4, :], scalar=1.0, in1=n2[:],
                                                            op0=ALU.add, op1=ALU.mult), r=[modf.res, n2.res], w=[S2.res])
        neg1d = cx.sb("neg1d", [128, T2], F32)
        k.ins("pool", lambda: nc.gpsimd.memset(neg1d[:], -1.0), w=[neg1d.res])
        banks = Ring([cx.psum(f"c2b{i}", [128, 512], F32) for i in range(8)])
        h1r = cx.sbring("d_h1", 2, [128, 2, 1024], F32)
        xnr = cx.sbring("d_xn", 1, [128, 1024], F32)
        fTr = cx.sbring("d_fT", 2, [128, 8, T2], BF16)
        actr = cx.sbring("d_act", 1, [128, NF, T2], BF16)
        outr = cx.sbring("d_out", 1, [128, 2, 1024], F32)
        ssr2 = cx.sbring("d_ss", 4, [128, 2], F32)
        junk2 = cx.sb("junk2", [128, 1024], BF16)
        e2r = cx.sbring("d_e", 3, [128, T2], F32)
        a2r = cx.sbring("d_a", 3, [128, T2], F32)
        if debug:
            print("C2 sbuf remaining", nc.sbuf_bytes_remaining)
        def c2_pre(ti):
                l0 = ti * T2
                h1 = h1r.next()
                k.dma("sp", h1[:], h1_d[l0:l0 + T2, :].rearrange("(j p) d -> p j d", p=128), r=[R_h1], w=[h1.res], dres=h1.res)
                ss = ssr2.next()
                for j in range(2):
                    k.ins("act", lambda j=j: nc.scalar.activation(out=junk2[:], in_=h1[:, j, :], func=AF.Square, accum_out=ss[:, j:j + 1]),
                          r=[h1.res], w=[junk2.res, ss.res])
                k.ins("act", lambda: nc.scalar.activation(out=ss[:], in_=ss[:], func=AF.Ln, scale=1.0 / D, bias=epsc[:, 0:1]),
                      r=[ss.res, epsc.res], w=[ss.res])
                k.ins("act", lambda: nc.scalar.activation(out=ss[:], in_=ss[:], func=AF.Exp, scale=-0.5), r=[ss.res], w=[ss.res])
                fT = fTr.next()
                for j in range(2):
                    xn = xnr.next()
                    k.ins("act", lambda j=j, xn=xn: nc.scalar.activation(out=xn[:], in_=h1[:, j, :], func=AF.Identity, scale=ss[:, j:j + 1]),
                          r=[h1.res, ss.res], w=[xn.res])
                    for half in range(2):
                        pb = banks.next()
                        k.group("pe", [(lambda q=q, xn=xn, pb=pb: nc.tensor.transpose(
                            pb[:, q * 128:(q + 1) * 128], xn[:, (half * 4 + q) * 128:(half * 4 + q + 1) * 128], ident[:]))
                            for q in range(4)], r=[xn.res, ident.res], w=[pb.res])
                        for q in range(4):
                            kt = half * 4 + q
                            k.ins("act", lambda q=q, kt=kt, pb=pb, j=j: nc.scalar.activation(
                                out=fT[:, kt, j * 128:(j + 1) * 128], in_=pb[:, q * 128:(q + 1) * 128], func=AF.Identity,
                                scale=S2[:, kt:kt + 1], bias=modf[:, 0, 3, kt:kt + 1]),
                                r=[pb.res, S2.res, modf.res], w=[fT.res])
                return h1, fT

        nxt2 = c2_pre(0)
        for ti in range(NLAT // T2):
            l0 = ti * T2
            h1, fT = nxt2
            act = actr.next()
            for ft in range(NF):
                if ft == 10 and ti + 1 < NLAT // T2:
                    nxt2 = c2_pre(ti + 1)
                pgt = banks.next()
                k.group("pe", [(lambda kt=kt: nc.tensor.matmul(pgt[:, 0:T2], wfi[:, kt, ft * 128:(ft + 1) * 128], fT[:, kt, :],
                                                               start=(kt == 0), stop=(kt == 7))) for kt in range(8)],
                        r=[wfi.res, fT.res], w=[pgt.res])
                pup = banks.next()
                k.group("pe", [(lambda kt=kt: nc.tensor.matmul(pup[:, 0:T2], wfi[:, kt, DFF + ft * 128:DFF + (ft + 1) * 128], fT[:, kt, :],
                                                               start=(kt == 0), stop=(kt == 7))) for kt in range(8)],
                        r=[wfi.res, fT.res], w=[pup.res])
                ez = e2r.next()
                k.ins("act", lambda: nc.scalar.activation(out=ez[:], in_=pgt[:, 0:T2], func=AF.Exp, scale=-1.0), r=[pgt.res], w=[ez.res])
                k.ins("dve", lambda: nc.vector.tensor_scalar(out=ez[:], in0=ez[:], scalar1=1.0, scalar2=None, op0=ALU.add),
                      r=[ez.res], w=[ez.res])
                k.ins("dve", lambda: nc.vector.reciprocal(out=ez[:], in_=ez[:]), r=[ez.res], w=[ez.res])
                a1 = a2r.next()
                k.ins("dve", lambda: nc.vector.tensor_tensor(out=a1[:], in0=pgt[:, 0:T2], in1=ez[:], op=ALU.mult),
                      r=[pgt.res, ez.res], w=[a1.res])
                k.ins("dve", lambda: nc.vector.tensor_tensor(out=act[:, ft, :], in0=pup[:, 0:T2], in1=a1[:], op=ALU.mult),
                      r=[pup.res, a1.res], w=[act.res])
            ot = outr.next()
            ss2 = ssr2.next()
            for j in range(2):
                for cg in range(2):
                    csl = slice(cg * 512, (cg + 1) * 512)
                    pf = banks.next()
                    k.group("pe", [(lambda ft=ft: nc.tensor.matmul(pf[:], act[:, ft, j * 128:(j + 1) * 128], wfo[:, ft, csl],
                                                                   start=(ft == 0), stop=(ft == NF - 1))) for ft in range(NF)],
                            r=[act.res, wfo.res], w=[pf.res])
                    k.ins("dve", lambda: nc.vector.tensor_tensor(out=ot[:, j, csl], in0=pf[:], in1=g2B[:, csl], op=ALU.mult),
                          r=[pf.res, g2B.res], w=[ot.res])
                    k.ins("pool", lambda: nc.gpsimd.tensor_tensor(out=ot[:, j, csl], in0=ot[:, j, csl], in1=h1[:, j, csl], op=ALU.add),
                          r=[ot.res, h1.res], w=[ot.res])
                k.ins("act", lambda j=j: nc.scalar.activation(out=junk2[:], in_=ot[:, j, :], func=AF.Square, accum_out=ss2[:, j:j + 1]),
                      r=[ot.res], w=[junk2.res, ss2.res])
            k.ins("act", lambda: nc.scalar.activation(out=ss2[:], in_=ss2[:], func=AF.Ln, scale=1.0 / D, bias=epsc[:, 0:1]),
                  r=[ss2.res, epsc.res], w=[ss2.res])
            k.ins("act", lambda: nc.scalar.activation(out=ss2[:], in_=ss2[:], func=AF.Exp, scale=-0.5), r=[ss2.res], w=[ss2.res])
            for j in range(2):
                k.ins("dve", lambda j=j: nc.vector.scalar_tensor_tensor(out=ot[:, j, :], in0=ot[:, j, :], scalar=ss2[:, j:j + 1],
                                                                        in1=nfB[:], op0=ALU.mult, op1=ALU.mult),
                      r=[ot.res, ss2.res, nfB.res], w=[ot.res])
            k.dma("sp", out_d[l0:l0 + T2, :].rearrange("(j p) d -> p j d", p=128), ot[:], r=[ot.res], w=[R_out], dres=ot.res)
        cx.pop()

    k.wait_all("sp", [R_scr, R_mod, R_oT, R_st1, R_st2, R_y, R_h1, R_out] + R_o1 + R_y1)
    return nc


def _consts():
    p = np.arange(128)[:, None]
    f = np.arange(128)[None, :]
    U = (p <= f).astype(np.float32)
    Lo = (p >= f).astype(np.float32)
    tri = np.stack([U, Lo, -U, -Lo], axis=1)
    NEG = -30000.0
    m = lambda ok: np.where(ok, 0.0, NEG).astype(np.float32)
    msk = np.stack([m(p > f), m(p < f), m(p >= f), m(p <= f)], axis=1)
    return np.ascontiguousarray(tri), np.ascontiguousarray(msk)


TRI, MSK = _consts()


def host_prepare(inputs, core):
    b, s = core // 2, core % 2
    x = inputs["x"][b, s * NLAT:(s + 1) * NLAT]
    ctx = inputs["ctx"][b]
    if s == 1:
        x = x[::-1]
        ctx = ctx[::-1]
    xin = np.ascontiguousarray(np.concatenate([ctx, x], axis=0))
    x2 = inputs["x"][b, (1 - s) * NLAT:(2 - s) * NLAT]
    ctx2 = inputs["ctx"][b]
    if s == 0:
        x2 = x2[::-1]
        ctx2 = ctx2[::-1]
    xin2 = np.ascontiguousarray(np.concatenate([ctx2, x2], axis=0))
    cv = np.stack([inputs["c"][b], inputs["c_ctx"]], axis=-1)
    cvec = np.ascontiguousarray(cv.reshape(8, 128, 2).transpose(1, 0, 2))
    d1, d2 = (0, 1) if s == 0 else (1, 0)
    w = inputs["w_in"][0]
    offs = np.cumsum([0, 3072, 1024, 16, 16, 2048, 3072, 64, 2048])
    qkv = w[:, offs[0]:offs[1]]
    zg = w[:, offs[1]:offs[2]]
    a_ = w[:, offs[2]:offs[3]].reshape(D, 2, 8)
    b_ = w[:, offs[3]:offs[4]].reshape(D, 2, 8)
    zs = w[:, offs[4]:offs[5]]
    xbc = w[:, offs[5]:offs[6]]
    dt_ = w[:, offs[6]:offs[7]].reshape(D, 2, 32)
    gate = w[:, offs[7]:offs[8]]
    z16 = np.zeros((D, 16), np.float32)
    small = np.concatenate([a_[:, d1], a_[:, d2], z16, b_[:, d1], b_[:, d2], z16, dt_[:, d1], dt_[:, d2]], axis=1)
    w_perm = np.ascontiguousarray(np.concatenate([qkv, xbc, gate, small, zg, zs], axis=1))
    assert w_perm.shape[1] == W_IN_COLS
    cw = np.concatenate([inputs["gdn_conv_w"][0], inputs["ssm_conv_w"][0]], axis=1)
    cbias = np.concatenate([inputs["gdn_conv_b"][0], inputs["ssm_conv_b"][0]], axis=0)
    mk = lambda w_: np.ascontiguousarray(np.concatenate([w_, cbias[None]], axis=0).reshape(4, 48, 128).transpose(2, 1, 0))
    convp = mk(cw[::-1] if s == 1 else cw)
    convp2 = mk(cw[::-1] if s == 0 else cw)
    smallp = np.zeros((128, 4), np.float32)
    smallp[:, 0] = 1.0
    smallp[32:64, 0] = -1.0
    gb = inputs["gdn_dt_bias"][0]
    sbias = inputs["ssm_dt_bias"][0]
    smallp[0:8, 1] = gb[d1]; smallp[8:16, 1] = gb[d2]
    smallp[64:96, 1] = sbias[d1]; smallp[96:128, 1] = sbias[d2]
    ga = inputs["gdn_a_log"][0]
    smallp[0:8, 2] = ga[d1]; smallp[8:16, 2] = ga[d2]
    smallp[:, 3] = -1.0
    smallp[64:128, 3] = 1.0
    n1w = np.ascontiguousarray(inputs["norm1_w"][0].reshape(8, 128).T)
    return {
        "xin": xin, "xin2": xin2, "convp2": convp2, "cvec": cvec, "ada_w": np.ascontiguousarray(inputs["ada_w"][0]),
        "ada_b": np.ascontiguousarray(inputs["ada_b"][0][None]), "w_in": w_perm, "convp": convp,
        "smallp": smallp, "n1w": n1w, "ident": np.eye(128, dtype=np.float32),
        "tri": TRI, "msk": MSK,
        "w_brg": np.ascontiguousarray(inputs["w_br_gdn"][0]), "w_brs": np.ascontiguousarray(inputs["w_br_ssm"][0]),
        "w_o": np.ascontiguousarray(inputs["w_out"][0]), "w_fi": np.ascontiguousarray(inputs["w_ffn_in"][0]),
        "w_fo": np.ascontiguousarray(inputs["w_ffn_out"][0]),
        "gnw": np.ascontiguousarray(inputs["gdn_norm_w"][0].reshape(128, 1)),
        "dskip": np.ascontiguousarray(np.repeat(inputs["ssm_d"][0], 64)[None]),
        "snw": np.ascontiguousarray(inputs["ssm_norm_w"][0][None]),
        "n2w": np.ascontiguousarray(inputs["norm2_w"][0].reshape(8, 128).T),
        "nfw": np.ascontiguousarray(inputs["norm_f_w"][None]),
        "salog": np.ascontiguousarray(np.broadcast_to(inputs["ssm_a_log"][0][[d1, d2]][None], (128, 2, 32))),
    }


def kernel(**inputs):
    inputs = {k_: np.asarray(v) for k_, v in inputs.items()}
    nc = build_program()
    in_maps = [host_prepare(inputs, c) for c in range(8)]
    res = run_bass_kernel_spmd(nc, in_maps, core_ids=list(range(8)))
    out = np.zeros((4, 8192, D), np.float32)
    for c in range(8):
        b, s = c // 2, c % 2
        o = res.results[c]["out"]
        if s == 1:
            o = o[::-1]
        out[b, s * NLAT:(s + 1) * NLAT] = o
    return out
```
